# Optimizing a Trainium2 kernel written in Bass

```python
import jax
import jax.numpy as jnp
from jax import lax
import numpy as np

D_MODEL = 1024
BATCH = 16
SEQ = 2048
DEPTH = 2

D_MIX = D_MODEL
N_GROUPS = 4
GROUP_WIDTH = D_MIX // N_GROUPS
A_HEADS = 4
A_DK = GROUP_WIDTH // A_HEADS
A_DV = GROUP_WIDTH // A_HEADS
A_CHUNK = 16
LB_FLOOR = 1e-30
B_HEADS = 4
B_Q_LORA = 256
B_KV_LORA = 128
B_NOPE = 64
B_ROPE = 32
B_V = GROUP_WIDTH // B_HEADS
ROPE_THETA = 10000.0
C_HEADS = 4
C_HEAD_DIM = GROUP_WIDTH // C_HEADS
FOX_GATE_BIAS = 3.0
D_GROUPS = 4
D_GROUP_DIM = GROUP_WIDTH // D_GROUPS
D_CHUNK = 128
Q_BLOCK = 128
D_FF = 2816
N_MOD = 9
ALPHA = (2 * DEPTH) ** 0.25
BETA = (8 * DEPTH) ** -0.25
LN_EPS = 1e-5
RMS_EPS = 1e-6
MIX_SPLIT_SIZES = (GROUP_WIDTH, GROUP_WIDTH, GROUP_WIDTH, GROUP_WIDTH,
                   B_Q_LORA, B_KV_LORA, B_ROPE,
                   GROUP_WIDTH, GROUP_WIDTH, GROUP_WIDTH, C_HEADS,
                   GROUP_WIDTH, GROUP_WIDTH)
MIX_IN_COLS = sum(MIX_SPLIT_SIZES)

kernel_name = "hybrid_hgrn2_mla_fox_gmlp_deepnorm_block"


def layer_norm(x, g, b):
    xf = x.astype(jnp.float32)
    mu = jnp.mean(xf, axis=-1, keepdims=True)
    var = jnp.mean(jnp.square(xf - mu), axis=-1, keepdims=True)
    return ((xf - mu) * lax.rsqrt(var + LN_EPS)).astype(x.dtype) * g + b


def rms_norm(x, g):
    xf = x.astype(jnp.float32)
    return (xf * lax.rsqrt(jnp.mean(xf * xf, axis=-1, keepdims=True) + RMS_EPS)).astype(x.dtype) * g


def swiglu_ffn(h, w_in, w_out):
    gate, up = jnp.split(h @ w_in, 2, axis=-1)
    return (jax.nn.silu(gate) * up) @ w_out


def rope(x, pos):
    half = x.shape[-1] // 2
    inv_freq = ROPE_THETA ** (-jnp.arange(half, dtype=jnp.float32) / half)
    ang = pos.astype(jnp.float32)[:, None] * inv_freq[None, :]
    cos = jnp.cos(ang)[None, :, None, :].astype(x.dtype)
    sin = jnp.sin(ang)[None, :, None, :].astype(x.dtype)
    x1, x2 = x[..., :half], x[..., half:]
    return jnp.concatenate([x1 * cos - x2 * sin, x1 * sin + x2 * cos], axis=-1)


def causal_softmax_attention(q, k, v, cum_log_f=None):
    b, s, h, dk = q.shape
    dv = v.shape[-1]
    n_blocks = s // Q_BLOCK
    scale = dk ** -0.5
    k_pos = jnp.arange(s)
    cum_t = None if cum_log_f is None else jnp.swapaxes(cum_log_f, 1, 2)

    def one_block(i):
        start = i * Q_BLOCK
        q_i = lax.dynamic_slice_in_dim(q, start, Q_BLOCK, axis=1)
        logits = jnp.einsum('bqhd,bkhd->bhqk', q_i, k,
                            preferred_element_type=jnp.float32) * scale
        if cum_t is not None:
            f_i = lax.dynamic_slice_in_dim(cum_t, start, Q_BLOCK, axis=2)
            logits = logits + (f_i[..., :, None] - cum_t[..., None, :])
        q_pos = start + jnp.arange(Q_BLOCK)
        logits = jnp.where(k_pos[None, :] <= q_pos[:, None], logits, -jnp.inf)
        p = jax.nn.softmax(logits, axis=-1).astype(v.dtype)
        return jnp.einsum('bhqk,bkhd->bqhd', p, v)

    out = lax.map(one_block, jnp.arange(n_blocks))
    return jnp.moveaxis(out, 0, 1).reshape(b, s, h * dv)


def hgrn2_mixer(q, f_logit, inp, g_out, lb, norm_g):
    b, s, _ = q.shape
    dt = q.dtype
    n_chunks = s // A_CHUNK
    lbf = lb.astype(jnp.float32)
    log_f = jnp.logaddexp(jnp.log(jnp.maximum(lbf, LB_FLOOR)),
                          jnp.log1p(-lbf) + jax.nn.log_sigmoid(f_logit.astype(jnp.float32)))
    k = -jnp.expm1(log_f)
    qf = jax.nn.silu(q.astype(jnp.float32))
    shp_k = (b, n_chunks, A_CHUNK, A_HEADS, A_DK)
    qf, k, log_f = qf.reshape(shp_k), k.reshape(shp_k), log_f.reshape(shp_k)
    v = inp.astype(jnp.float32).reshape(b, n_chunks, A_CHUNK, A_HEADS, A_DV)
    G = jnp.cumsum(log_f, axis=2)
    G_last = G[:, :, -1:]
    causal = jnp.tril(jnp.ones((A_CHUNK, A_CHUNK), dtype=bool))[None, None, :, :, None, None]
    rel = jnp.where(causal, G[:, :, :, None] - G[:, :, None, :], -jnp.inf)
    decay = jnp.exp(rel)
    scores = jnp.einsum('bctha,bcsha,bctsha->bchts', qf, k, decay)
    o_intra = jnp.einsum('bchts,bcshv->bcthv', scores, v)
    q_dec = qf * jnp.exp(G)
    k_to_end = k * jnp.exp(G_last - G)
    chunk_kv = jnp.einsum('bcsha,bcshv->cbhav', k_to_end, v)
    chunk_decay = jnp.transpose(jnp.exp(G_last[:, :, 0]), (1, 0, 2, 3))

    def step(state, xs):
        dec, kv = xs
        return dec[..., None] * state + kv, state

    state0 = jnp.zeros((b, A_HEADS, A_DK, A_DV), jnp.float32)
    _, state_in = lax.scan(step, state0, (chunk_decay, chunk_kv))
    o_inter = jnp.einsum('bctha,cbhav->bcthv', q_dec, state_in)
    o = (o_intra + o_inter).reshape(b, s, A_HEADS, A_DV)
    o = rms_norm(o, norm_g.astype(jnp.float32).reshape(A_HEADS, A_DV)).reshape(b, s, GROUP_WIDTH)
    return (o * jax.nn.silu(g_out.astype(jnp.float32))).astype(dt)


def mla_mixer(c_q, c_kv, k_rope, q_norm_g, kv_norm_g, w_uq, w_ukv, pos):
    b, s, _ = c_q.shape
    q = (rms_norm(c_q, q_norm_g) @ w_uq).reshape(b, s, B_HEADS, B_NOPE + B_ROPE)
    q_nope, q_rope = q[..., :B_NOPE], q[..., B_NOPE:]
    kv = (rms_norm(c_kv, kv_norm_g) @ w_ukv).reshape(b, s, B_HEADS, B_NOPE + B_V)
    k_nope, v = kv[..., :B_NOPE], kv[..., B_NOPE:]
    k_r = rope(k_rope[:, :, None, :], pos)
    q = jnp.concatenate([q_nope, rope(q_rope, pos)], axis=-1)
    k = jnp.concatenate([k_nope, jnp.broadcast_to(k_r, (b, s, B_HEADS, B_ROPE))], axis=-1)
    return causal_softmax_attention(q, k, v)


def fox_mixer(q, k, v, f_logit, b_f):
    b, s, _ = q.shape
    shp = (b, s, C_HEADS, C_HEAD_DIM)
    log_f = jax.nn.log_sigmoid(f_logit.astype(jnp.float32) + b_f.astype(jnp.float32))
    cum = jnp.cumsum(log_f, axis=1)
    return causal_softmax_attention(q.reshape(shp), k.reshape(shp), v.reshape(shp), cum)


def gmlp_mixer(u, v, ln_g, ln_b, w_s, b_s):
    b, s, _ = u.shape
    n_chunks = s // D_CHUNK
    u = jax.nn.gelu(u)
    v = layer_norm(jax.nn.gelu(v), ln_g, ln_b).reshape(b, n_chunks, D_CHUNK, D_GROUPS, D_GROUP_DIM)
    causal = jnp.tril(jnp.ones((D_CHUNK, D_CHUNK), dtype=bool))
    w = jnp.where(causal, w_s, 0.0)
    mixed = jnp.einsum('gts,bcsgd->bctgd', w, v) + jnp.swapaxes(b_s, 0, 1)[:, :, None]
    return u * mixed.reshape(b, s, GROUP_WIDTH)


def hybrid_token_mixer(h, w_in, w_out, lb, hgrn_norm_g, mla_q_norm_g, mla_kv_norm_g,
                       mla_w_uq, mla_w_ukv, fox_b_f, gmlp_ln_g, gmlp_ln_b, gmlp_w_s, gmlp_b_s):
    split_idx = [int(i) for i in np.cumsum(MIX_SPLIT_SIZES)[:-1]]
    proj = h @ w_in
    (a_q, a_f, a_i, a_g, b_cq, b_ckv, b_kr,
     c_q, c_k, c_v, c_f, d_u, d_v) = jnp.split(proj, split_idx, axis=-1)
    pos = jnp.arange(h.shape[1])
    o_a = hgrn2_mixer(a_q, a_f, a_i, a_g, lb, hgrn_norm_g)
    o_b = mla_mixer(b_cq, b_ckv, b_kr, mla_q_norm_g, mla_kv_norm_g, mla_w_uq, mla_w_ukv, pos)
    o_c = fox_mixer(c_q, c_k, c_v, c_f, fox_b_f)
    o_d = gmlp_mixer(d_u, d_v, gmlp_ln_g, gmlp_ln_b, gmlp_w_s, gmlp_b_s)
    return jnp.concatenate([o_a, o_b.astype(h.dtype), o_c.astype(h.dtype), o_d], axis=-1) @ w_out


def setup_inputs(seed: int = 0) -> dict:
    key = jax.random.key(seed)
    ks = jax.random.split(key, 23)
    L = DEPTH

    def nrm(k, shape, scale):
        return scale * jax.random.normal(k, shape, jnp.float32)

    return {
        'x': nrm(ks[0], (BATCH, SEQ, D_MODEL), 1.0),
        'c': nrm(ks[1], (BATCH, D_MODEL), 1.0),
        'ada_w': nrm(ks[2], (L, D_MODEL, N_MOD * D_MODEL), 0.1 * D_MODEL ** -0.5),
        'ada_b': nrm(ks[3], (L, N_MOD * D_MODEL), 0.01),
        'ln_g': 1.0 + nrm(ks[4], (L, 3, D_MODEL), 0.02),
        'ln_b': nrm(ks[5], (L, 3, D_MODEL), 0.02),
        'ffn1_w_in': nrm(ks[6], (L, D_MODEL, 2 * D_FF), D_MODEL ** -0.5),
        'ffn1_w_out': nrm(ks[7], (L, D_FF, D_MODEL), BETA * D_FF ** -0.5),
        'ffn2_w_in': nrm(ks[8], (L, D_MODEL, 2 * D_FF), D_MODEL ** -0.5),
        'ffn2_w_out': nrm(ks[9], (L, D_FF, D_MODEL), BETA * D_FF ** -0.5),
        'mix_w_in': nrm(ks[10], (L, D_MODEL, MIX_IN_COLS), D_MODEL ** -0.5),
        'mix_w_out': nrm(ks[11], (L, D_MIX, D_MODEL), BETA * D_MIX ** -0.5),
        'hgrn_lb_logits': nrm(ks[12], (L, GROUP_WIDTH), 0.5),
        'hgrn_norm_g': 1.0 + nrm(ks[13], (L, GROUP_WIDTH), 0.02),
        'mla_q_norm_g': 1.0 + nrm(ks[14], (L, B_Q_LORA), 0.02),
        'mla_kv_norm_g': 1.0 + nrm(ks[15], (L, B_KV_LORA), 0.02),
        'mla_w_uq': nrm(ks[16], (L, B_Q_LORA, B_HEADS * (B_NOPE + B_ROPE)), B_Q_LORA ** -0.5),
        'mla_w_ukv': nrm(ks[17], (L, B_KV_LORA, B_HEADS * (B_NOPE + B_V)), B_KV_LORA ** -0.5),
        'fox_b_f': FOX_GATE_BIAS + nrm(ks[18], (L, C_HEADS), 0.5),
        'gmlp_ln_g': 1.0 + nrm(ks[19], (L, GROUP_WIDTH), 0.02),
        'gmlp_ln_b': nrm(ks[20], (L, GROUP_WIDTH), 0.02),
        'gmlp_w_s': nrm(ks[21], (L, D_GROUPS, D_CHUNK, D_CHUNK), 0.5 * D_CHUNK ** -0.5),
        'gmlp_b_s': 1.0 + nrm(ks[22], (L, D_GROUPS, D_CHUNK), 0.02),
    }


def reference(x, c, ada_w, ada_b, ln_g, ln_b, ffn1_w_in, ffn1_w_out, ffn2_w_in, ffn2_w_out,
              mix_w_in, mix_w_out, hgrn_lb_logits, hgrn_norm_g, mla_q_norm_g, mla_kv_norm_g,
              mla_w_uq, mla_w_ukv, fox_b_f, gmlp_ln_g, gmlp_ln_b, gmlp_w_s, gmlp_b_s):
    lb_sm = jax.nn.softmax(hgrn_lb_logits.astype(jnp.float32), axis=0)
    lb_all = jnp.cumsum(lb_sm, axis=0) - lb_sm[0]
    c_act = jax.nn.silu(c)
    for l in range(DEPTH):
        mod = (c_act @ ada_w[l] + ada_b[l])[:, None, :]
        sh1, sc1, g1, sh2, sc2, g2, sh3, sc3, g3 = jnp.split(mod, N_MOD, axis=-1)
        h = x * (1.0 + sc1) + sh1
        x = layer_norm(ALPHA * x + 0.5 * (1.0 + g1) * swiglu_ffn(h, ffn1_w_in[l], ffn1_w_out[l]),
                       ln_g[l, 0], ln_b[l, 0])
        h = x * (1.0 + sc2) + sh2
        mixed = hybrid_token_mixer(h, mix_w_in[l], mix_w_out[l], lb_all[l], hgrn_norm_g[l],
                                   mla_q_norm_g[l], mla_kv_norm_g[l], mla_w_uq[l], mla_w_ukv[l],
                                   fox_b_f[l], gmlp_ln_g[l], gmlp_ln_b[l], gmlp_w_s[l], gmlp_b_s[l])
        x = layer_norm(ALPHA * x + (1.0 + g2) * mixed, ln_g[l, 1], ln_b[l, 1])
        h = x * (1.0 + sc3) + sh3
        x = layer_norm(ALPHA * x + 0.5 * (1.0 + g3) * swiglu_ffn(h, ffn2_w_in[l], ffn2_w_out[l]),
                       ln_g[l, 2], ln_b[l, 2])
    return x
```

```python
import numpy as np
from contextlib import ExitStack
import concourse.bass as bass
import concourse.mybir as mybir
from concourse.bass_utils import run_bass_kernel_spmd

F32 = mybir.dt.float32
BF16 = mybir.dt.bfloat16
AF = mybir.ActivationFunctionType
ALU = mybir.AluOpType

D = 1024
DFF = 2816
NJ = 22
MIXC = 2724
ALPHA = float(4 ** 0.25)
LN_EPS = 1e-5
RMS_EPS = 1e-6
NEG = -30000.0


class Buf:
    __slots__ = ("w", "r", "name")

    def __init__(self, name=""):
        self.w = None
        self.r = {}
        self.name = name


class Chan:
    def __init__(self, sem, name):
        self.sem = sem
        self.cnt = 0
        self.name = name


class Eng:
    def __init__(self, name, h, chan):
        self.name = name
        self.h = h
        self.chan = chan
        self.seen = {}


class FW:
    def __init__(self, nc, es):
        self.nc = nc
        self.es = es
        self.nsem = 0
        self.chans = []
        self.pe = Eng("pe", nc.tensor, self.new_chan("pe"))
        self.act = Eng("act", nc.scalar, self.new_chan("act"))
        self.dve = Eng("dve", nc.vector, self.new_chan("dve"))
        self.pool = Eng("pool", nc.gpsimd, self.new_chan("pool"))
        self.sp = Eng("sp", nc.sync, self.new_chan("sp"))
        self.engs = [self.pe, self.act, self.dve, self.pool, self.sp]
        self.n_inst = 0
        self.n_wait = 0
        self.uid = 0

    def new_chan(self, name):
        for ch in self.chans:
            if ch.name == name and name not in ("dbg",):
                return ch
        sem = self.es.enter_context(self.nc.semaphore(f"s_{name}_{self.nsem}"))
        self.nsem += 1
        ch = Chan(sem, name)
        self.chans.append(ch)
        return ch

    def _sync(self, E, reads, writes):
        need = {}
        for b in reads:
            if b.w is not None:
                ch, c = b.w
                if need.get(ch, 0) < c:
                    need[ch] = c
        for b in writes:
            if b.w is not None:
                ch, c = b.w
                if need.get(ch, 0) < c:
                    need[ch] = c
            for ch, c in b.r.items():
                if need.get(ch, 0) < c:
                    need[ch] = c
        for ch, c in need.items():
            if ch is E.chan and E.name == "pe":
                continue
            if E.seen.get(ch, 0) >= c:
                continue
            E.h.wait_ge(ch.sem, c)
            self.n_wait += 1
            E.seen[ch] = c

    def _mark(self, ch, mark, reads, writes):
        for b in reads:
            if b.r.get(ch, 0) < mark:
                b.r[ch] = mark
        for b in writes:
            b.w = (ch, mark)
            b.r = {}

    def op(self, E, emit, reads=(), writes=(), inc=True):
        self._sync(E, reads, writes)
        ins = emit(E.h)
        self.n_inst += 1
        ch = E.chan
        if inc:
            ch.cnt += 1
            ins.then_inc(ch.sem, 1)
            mark = ch.cnt
        else:
            mark = ch.cnt + 1
        self._mark(ch, mark, reads, writes)
        return ins

    def dma(self, Q, chan, out, in_, reads=(), writes=(), **kw):
        self._sync(Q, reads, writes)
        ins = Q.h.dma_start(out=out, in_=in_, **kw)
        self.n_inst += 1
        chan.cnt += 16
        ins.then_inc(chan.sem, 16)
        self._mark(chan, chan.cnt, reads, writes)
        return ins

    def barrier(self):
        for E in self.engs:
            for ch in self.chans:
                if ch.cnt == 0 or ch is E.chan:
                    continue
                if E.seen.get(ch, 0) >= ch.cnt:
                    continue
                E.h.wait_ge(ch.sem, ch.cnt)
                E.seen[ch] = ch.cnt

    def sb(self, es, name, shape, dt):
        self.uid += 1
        return es.enter_context(self.nc.sbuf_tensor(f"sb{self.uid}_{name}", list(shape), dt))

    def psum(self, es, name, shape, dt):
        self.uid += 1
        return es.enter_context(self.nc.psum_tensor(f"pp{self.uid}_{name}", list(shape), dt))


class Ring:
    def __init__(self, items):
        self.items = items
        self.i = 0

    def next(self):
        it = self.items[self.i % len(self.items)]
        self.i += 1
        return it


def host_consts(S):
    pos = np.arange(S, dtype=np.float32)
    half = 16
    inv_freq = (10000.0 ** (-np.arange(half, dtype=np.float32) / half)).astype(np.float32)
    ang = pos[None, :] * inv_freq[:, None]
    cos = np.cos(ang).astype(np.float32)
    sin = np.sin(ang).astype(np.float32)
    ropeC = np.ones((96, S), np.float32)
    ropeS = np.zeros((96, S), np.float32)
    ropeC[64:80] = cos
    ropeC[80:96] = cos
    ropeS[64:80] = -sin
    ropeS[80:96] = sin
    s = np.arange(128)[:, None]
    negmask = np.zeros((128, 4, 512), np.float32)
    t = np.arange(512)[None, :]
    for r in range(4):
        negmask[:, r, :] = np.where(128 * r + s <= t, 0.0, NEG)
    ident = np.eye(128, dtype=np.float32)
    t128 = np.arange(128)[None, :]
    same = (s // 16) == (t128 // 16)
    tri = (same & (s <= t128)).astype(np.float32)
    blk = same.astype(np.float32)
    csel = ((s // 16) == np.arange(8)[None, :]).astype(np.float32)
    caus = (s <= t128).astype(np.float32)
    sel65 = np.zeros((65, 64), np.float32)
    sel65[64, :] = 1.0
    ones64 = np.ones((64, 64), np.float32)
    return dict(ropeC=ropeC, ropeS=ropeS, negmask=negmask, ident=ident, tri=tri, blk=blk,
                csel=csel, caus=caus, sel65=sel65, ones64=ones64)


CONST_SHAPES = lambda S: dict(ropeC=[96, S], ropeS=[96, S], negmask=[128, 4, 512], ident=[128, 128],
                              tri=[128, 128], blk=[128, 128], csel=[128, 8], caus=[128, 128],
                              sel65=[65, 64], ones64=[64, 64])


def weight_shapes(L):
    return dict(
        ada_w=[L, D, 9 * D], ada_b=[L, 9 * D], ln_g=[L, 3, D], ln_b=[L, 3, D],
        ffn1_w_in=[L, D, 2 * DFF], ffn1_w_out=[L, DFF, D], ffn2_w_in=[L, D, 2 * DFF], ffn2_w_out=[L, DFF, D],
        mix_w_in=[L, D, MIXC], mix_w_out=[L, D, D], wkr=[L, D, 192],
        hgrn_lb_logits=[L, 256], hgrn_ng=[L, 64, 4], mla_q_norm_g=[L, 256], mla_kv_norm_g=[L, 128],
        mla_w_uq=[L, 256, 384], mla_w_uqp=[L, 256, 384], mla_w_ukv=[L, 128, 512],
        fox_b_f=[L, 4], gmlp_ln_g=[L, 256], gmlp_ln_b=[L, 256], gmlp_wsT=[L, 4, 128, 128], gmlp_b_s=[L, 4, 128],
    )


def host_weights(inp):
    L = inp["ada_w"].shape[0]
    f = lambda a: np.ascontiguousarray(np.asarray(a, dtype=np.float32))
    w = {k: f(inp[k]) for k in ["ada_w", "ada_b", "ln_g", "ln_b", "ffn1_w_in", "ffn1_w_out", "ffn2_w_in",
                                "ffn2_w_out", "mix_w_in", "mix_w_out", "hgrn_lb_logits", "mla_q_norm_g",
                                "mla_kv_norm_g", "mla_w_uq", "mla_w_ukv", "fox_b_f", "gmlp_ln_g", "gmlp_ln_b",
                                "gmlp_b_s"]}
    perm = np.concatenate([np.arange(16, 32), np.arange(0, 16)])
    kr = w["mix_w_in"][:, :, 1408:1440]
    wkr = np.zeros((L, D, 192), np.float32)
    wkr[:, :, 64:96] = kr
    wkr[:, :, 160:192] = kr[:, :, perm]
    w["wkr"] = wkr
    uq = w["mla_w_uq"].reshape(L, 256, 4, 96)
    uqp = np.zeros_like(uq)
    uqp[:, :, :, 64:96] = uq[:, :, :, 64:96][:, :, :, perm]
    w["mla_w_uqp"] = np.ascontiguousarray(uqp.reshape(L, 256, 384))
    w["hgrn_ng"] = np.ascontiguousarray(f(inp["hgrn_norm_g"]).reshape(L, 4, 64).transpose(0, 2, 1))
    w["gmlp_wsT"] = np.ascontiguousarray(f(inp["gmlp_w_s"]).transpose(0, 1, 3, 2))
    return w


class Builder:
    def __init__(self, S, L, stop=None, dumps=()):
        self.S = S
        self.L = L
        self.NT = S // 128
        self.NG = S // 512
        self.stop = stop
        self.dumps = set(dumps)
        self.dump_aps = {}

    def build(self):
        S, L = self.S, self.L
        nc = bass.Bass("TRN2", target_bir_lowering=False)
        self.nc = nc
        dr = {}
        dr["x"] = nc.dram_tensor("x", [2, S, D], F32, kind="ExternalInput").ap()
        dr["cT"] = nc.dram_tensor("cT", [128, 8, 2], F32, kind="ExternalInput").ap()
        for k, shp in weight_shapes(L).items():
            dr[k] = nc.dram_tensor(k, shp, F32, kind="ExternalInput").ap()
        for k, shp in CONST_SHAPES(S).items():
            dr[k] = nc.dram_tensor(k, shp, F32, kind="ExternalInput").ap()
        dr["out"] = nc.dram_tensor("out", [2, S, D], F32, kind="ExternalOutput").ap()
        dr["modd"] = nc.dram_tensor("modd", [2, L, 9 * D], F32, kind="Internal").ap()
        self.dr = dr
        with ExitStack() as es:
            self.fw = fw = FW(nc, es)
            self.es = es
            self.setup_global(es)
            self.prologue_mod()
            for s in range(2):
                self.run_sequence(s)
                if self.stop is not None and self.stop[0] == s:
                    break
            fw.barrier()
            print(f"[build] inst={fw.n_inst} waits={fw.n_wait} sems={fw.nsem} "
                  f"cnt={[(e.name, e.chan.cnt) for e in fw.engs]}", flush=True)
        return nc

    def V(self, emit, reads=(), writes=()):
        return self.fw.op(self.fw.dve, emit, reads, writes)

    def A(self, emit, reads=(), writes=()):
        return self.fw.op(self.fw.act, emit, reads, writes)

    def G(self, emit, reads=(), writes=()):
        return self.fw.op(self.fw.pool, emit, reads, writes)

    def MM(self, out, lhsT, rhs, start, stop, reads=(), writes=(), inc=None, skip=False):
        if inc is None:
            inc = bool(stop)
        kw = dict(skip_group_check=True) if skip else {}
        return self.fw.op(self.fw.pe, lambda h: h.matmul(out, lhsT=lhsT, rhs=rhs, start=start, stop=stop, **kw),
                          reads, writes, inc=inc)

    def TR(self, out, in_, reads=(), writes=(), inc=True):
        ident = self.ident
        n = in_.shape[0]
        return self.fw.op(self.fw.pe, lambda h: h.transpose(out, in_, ident[0:n, 0:n]),
                          list(reads) + [self.b_const], writes, inc=inc)

    def wdma(self, out, in_, chan, reads=(), writes=()):
        return self.fw.dma(self.fw.pool, chan, out, in_, reads, writes)

    def ldma(self, out, in_, chan, reads=(), writes=(), **kw):
        return self.fw.dma(self.fw.sp, chan, out, in_, reads, writes, **kw)

    def dump(self, name, ap, shape, reads):
        if name not in self.dumps or name in self.dump_aps:
            return
        d = self.nc.dram_tensor("dbg_" + name, list(shape), F32, kind="ExternalOutput").ap()
        self.dump_aps[name] = d
        ch = self.fw.new_chan("dbg")
        self.fw.dma(self.fw.sp, ch, d, ap, reads=reads)

    def setup_global(self, es):
        fw, dr, NT = self.fw, self.dr, self.NT
        self.x = fw.sb(es, "x", [128, NT, D], F32)
        self.bx = [Buf(f"x{t}") for t in range(NT)]
        self.hT = fw.sb(es, "hT", [128, 8, self.S], BF16)
        self.bhT = [Buf(f"hT{g}") for g in range(self.NG)]
        self.mv = fw.sb(es, "modv", [128, 5, D], F32)
        self.bmv = Buf("modv")
        self.ch_mv = fw.new_chan("modv")
        self.ident = fw.sb(es, "ident", [128, 128], BF16)
        self.b_const = Buf("const")
        ch = fw.new_chan("const")
        self.wdma(self.ident[:], dr["ident"], ch, writes=[self.b_const])
        self.pbank = [(fw.psum(es, f"b{i}", [128, 512], F32), Buf(f"ps{i}")) for i in range(6)]
        self.ptr = [(fw.psum(es, f"t{i}", [128, 1024], BF16), Buf(f"pt{i}")) for i in range(2)]
        self.ringA = Ring(self.pbank[0:4])
        self.ringB = Ring(self.pbank[4:6])
        self.ringT = Ring(self.ptr)
        self.ch_x = [fw.new_chan(f"x{i}") for i in range(4)]
        self.ch_out = [fw.new_chan(f"o{i}") for i in range(4)]
        self.b_modd = Buf("modd")

    def prologue_mod(self):
        fw, dr, L = self.fw, self.dr, self.L
        with ExitStack() as es:
            cT = fw.sb(es, "cT", [128, 8, 2], F32)
            cact = fw.sb(es, "cact", [128, 8, 2], BF16)
            b_c = Buf()
            ch = fw.new_chan("c")
            self.ldma(cT[:], dr["cT"], ch, writes=[b_c])
            self.A(lambda h: h.activation(out=cact[:], in_=cT[:], func=AF.Silu), [b_c], [b_c])
            slots = [(fw.sb(es, f"aw{i}", [128, 8, 512], BF16), Buf(), fw.new_chan(f"aw{i}")) for i in range(2)]
            bslots = [(fw.sb(es, f"ab{i}", [2, 512], F32), Buf(), fw.new_chan(f"ab{i}")) for i in range(2)]
            rows = [(fw.sb(es, f"mr{i}", [2, 512], F32), Buf(), fw.new_chan(f"mr{i}")) for i in range(2)]
            it = 0
            for l in range(L):
                awv = dr["ada_w"][l].rearrange("(kc p) n -> p kc n", p=128)
                for cb in range(18):
                    w, bw, chw = slots[it % 2]
                    ab, bab, chb = bslots[it % 2]
                    mr, bmr, chr_ = rows[it % 2]
                    it += 1
                    self.wdma(w[:], awv[:, :, cb * 512:(cb + 1) * 512], chw, writes=[bw])
                    self.ldma(ab[:], dr["ada_b"][l:l + 1, cb * 512:(cb + 1) * 512].broadcast_to([2, 512]), chb,
                              writes=[bab])
                    ps, bps = self.ringA.next()
                    for kc in range(8):
                        self.MM(ps[0:2, :], cact[:, kc, :], w[:, kc, :], kc == 0, kc == 7,
                                reads=[b_c, bw], writes=[bps])
                    seg = (cb * 512) // 1024
                    self.V(lambda h: h.tensor_tensor(out=mr[:], in0=ps[0:2, :], in1=ab[:], op=ALU.add),
                           [bps, bab], [bmr])
                    if seg in (1, 4, 7, 5):
                        self.V(lambda h: h.tensor_scalar_add(out=mr[:], in0=mr[:], scalar1=1.0), [bmr], [bmr])
                    elif seg in (2, 8):
                        self.V(lambda h: h.tensor_scalar(out=mr[:], in0=mr[:], scalar1=1.0, scalar2=0.5,
                                                         op0=ALU.add, op1=ALU.mult), [bmr], [bmr])
                    self.ldma(dr["modd"][:, l, cb * 512:(cb + 1) * 512], mr[:], chr_, reads=[bmr],
                              writes=[self.b_modd])
            fw.barrier()

    def load_modvec(self, s, l, j):
        dr = self.dr
        mv, b, ch = self.mv, self.bmv, self.ch_mv
        segs = [3 * j + 1, 3 * j + 0, 3 * j + 2]
        for i, sg in enumerate(segs):
            self.ldma(mv[:, i, :], dr["modd"][s:s + 1, l, sg * D:(sg + 1) * D].broadcast_to([128, D]), ch,
                      reads=[self.b_modd], writes=[b])
        self.ldma(mv[:, 3, :], dr["ln_g"][l, j:j + 1, :].broadcast_to([128, D]), ch, writes=[b])
        self.ldma(mv[:, 4, :], dr["ln_b"][l, j:j + 1, :].broadcast_to([128, D]), ch, writes=[b])

    def run_sequence(self, s):
        fw, dr, NT = self.fw, self.dr, self.NT
        xin = dr["x"][s].rearrange("(t p) d -> p t d", p=128)
        q = max(1, NT // 4)
        for i in range(0, NT, q):
            self.ldma(self.x[:, i:i + q, :], xin[:, i:i + q, :], self.ch_x[(i // q) % 4],
                      writes=self.bx[i:i + q])
        for l in range(self.L):
            self.load_modvec(s, l, 0)
            self.ffn(s, l, 0)
            if self.stop == (s, l, 0):
                break
            self.load_modvec(s, l, 1)
            self.mixer(s, l)
            if self.stop == (s, l, 1):
                break
            self.load_modvec(s, l, 2)
            self.ffn(s, l, 2)
            if self.stop == (s, l, 2):
                break
        xout = dr["out"][s].rearrange("(t p) d -> p t d", p=128)
        for t in range(NT):
            self.ldma(xout[:, t, :], self.x[:, t, :], self.ch_out[t % 4], reads=[self.bx[t]])

    def build_hT(self, es):
        fw, NT = self.fw, self.NT
        tmpf = [(fw.sb(es, f"hx{i}", [128, D], F32), Buf()) for i in range(2)]
        hb = [(fw.sb(es, f"hb{i}", [128, D], BF16), Buf()) for i in range(2)]
        mv, bmv = self.mv, self.bmv
        for t in range(NT):
            tf, btf = tmpf[t % 2]
            hbt, bhb = hb[t % 2]
            self.V(lambda h: h.tensor_tensor(out=tf[:], in0=self.x[:, t, :], in1=mv[:, 0, :], op=ALU.mult),
                   [self.bx[t], bmv], [btf])
            self.G(lambda h: h.tensor_tensor(out=hbt[:], in0=tf[:], in1=mv[:, 1, :], op=ALU.add),
                   [btf, bmv], [bhb])
            pt, bpt = self.ringT.next()
            for kc in range(8):
                self.TR(pt[:, kc * 128:(kc + 1) * 128], hbt[:, kc * 128:(kc + 1) * 128], [bhb], [bpt], inc=(kc == 7))
            self.A(lambda h: h.copy(out=self.hT[:, :, t * 128:(t + 1) * 128],
                                    in_=pt[:].rearrange("p (k c) -> p k c", k=8)),
                   [bpt], [self.bhT[t // 4]])

    def resid_update(self, t, half, ps, bps, first, tmp, btmp):
        sl = slice(half * 512, (half + 1) * 512)
        mv, bmv = self.mv, self.bmv
        self.V(lambda h: h.tensor_tensor(out=tmp[:], in0=ps[:], in1=mv[:, 2, sl], op=ALU.mult),
               [bps, bmv], [btmp])
        xs = self.x[:, t, sl]
        if first:
            self.V(lambda h: h.scalar_tensor_tensor(out=xs, in0=xs, scalar=ALPHA, in1=tmp[:],
                                                    op0=ALU.mult, op1=ALU.add), [btmp, self.bx[t]], [self.bx[t]])
        else:
            self.G(lambda h: h.tensor_tensor(out=xs, in0=xs, in1=tmp[:], op=ALU.add),
                   [btmp, self.bx[t]], [self.bx[t]])

    def layer_norm_x(self, es):
        fw, NT = self.fw, self.NT
        st = [(fw.sb(es, f"lnst{i}", [128, 2, 6], F32), fw.sb(es, f"lnmv{i}", [128, 4], F32), Buf()) for i in range(2)]
        mv, bmv = self.mv, self.bmv
        for t in range(NT):
            s6, m4, bs = st[t % 2]
            xt = self.x[:, t, :]
            bxt = self.bx[t]
            self.V(lambda h: h.bn_stats(out=s6[:, 0, :], in_=self.x[:, t, 0:512]), [bxt], [bs])
            self.V(lambda h: h.bn_stats(out=s6[:, 1, :], in_=self.x[:, t, 512:1024]), [bxt], [bs])
            self.V(lambda h: h.bn_aggr(out=m4[:, 0:2], in_=s6[:]), [bs], [bs])
            self.V(lambda h: h.tensor_scalar_add(out=m4[:, 2:3], in0=m4[:, 1:2], scalar1=LN_EPS), [bs], [bs])
            self.A(lambda h: h.sqrt(out=m4[:, 2:3], in_=m4[:, 2:3]), [bs], [bs])
            self.V(lambda h: h.reciprocal(out=m4[:, 2:3], in_=m4[:, 2:3]), [bs], [bs])
            self.V(lambda h: h.scalar_tensor_tensor(out=m4[:, 3:4], in0=m4[:, 0:1], scalar=-1.0, in1=m4[:, 2:3],
                                                    op0=ALU.mult, op1=ALU.mult), [bs], [bs])
            self.A(lambda h: h.activation(out=xt, in_=xt, func=AF.Identity, bias=m4[:, 3:4], scale=m4[:, 2:3]),
                   [bs, bxt], [bxt])
            self.V(lambda h: h.tensor_tensor(out=xt, in0=xt, in1=mv[:, 3, :], op=ALU.mult), [bxt, bmv], [bxt])
            self.G(lambda h: h.tensor_tensor(out=xt, in0=xt, in1=mv[:, 4, :], op=ALU.add), [bxt, bmv], [bxt])

    def ffn(self, s, l, j):
        fw, dr, NT, NG, S = self.fw, self.dr, self.NT, self.NG, self.S
        w_in = dr["ffn1_w_in" if j == 0 else "ffn2_w_in"][l].rearrange("(kc p) n -> p kc n", p=128)
        w_out = dr["ffn1_w_out" if j == 0 else "ffn2_w_out"][l].rearrange("(j p) n -> p j n", p=128)
        with ExitStack() as es:
            with ExitStack() as e0:
                self.build_hT(e0)
                fw.barrier()
            actT = fw.sb(es, "actT", [128, 6, S], BF16)
            bact = [Buf() for _ in range(NG)]
            wo = [(fw.sb(es, f"wo{i}", [128, 6, D], BF16), Buf(), fw.new_chan(f"wo{i}")) for i in range(2)]
            wi = [(fw.sb(es, f"wi{i}", [128, 8, 512], BF16), Buf(), fw.new_chan(f"wi{i}")) for i in range(2)]
            sgs = [(fw.sb(es, f"sg{i}", [128, 512], F32), Buf()) for i in range(2)]
            tmps = [(fw.sb(es, f"ft{i}", [128, 512], F32), Buf()) for i in range(2)]
            parts = [(0, 6), (6, 12), (12, 18), (18, 22)]
            iw = 0
            k = 0
            for pi, (j0, j1) in enumerate(parts):
                wot, bwo, chwo = wo[pi % 2]
                self.wdma(wot[:, 0:j1 - j0, :], w_out[:, j0:j1, :], chwo, writes=[bwo])
                for jj in range(j0, j1, 2):
                    wt, bw, chw = wi[iw % 2]
                    iw += 1
                    self.wdma(wt[:, :, 0:256], w_in[:, :, jj * 128:jj * 128 + 256], chw, writes=[bw])
                    self.wdma(wt[:, :, 256:512], w_in[:, :, DFF + jj * 128:DFF + jj * 128 + 256], chw, writes=[bw])
                    for c in range(2):
                        jl = jj + c - j0
                        for g in range(NG):
                            pg, bpg = self.ringA.next()
                            pu, bpu = self.ringA.next()
                            tok = slice(g * 512, (g + 1) * 512)
                            for kc in range(8):
                                self.MM(pg[:], wt[:, kc, c * 128:(c + 1) * 128], self.hT[:, kc, tok], kc == 0, kc == 7,
                                        [bw, self.bhT[g]], [bpg])
                            for kc in range(8):
                                self.MM(pu[:], wt[:, kc, 256 + c * 128:256 + (c + 1) * 128], self.hT[:, kc, tok],
                                        kc == 0, kc == 7, [bw, self.bhT[g]], [bpu])
                            sg, bsg = sgs[k % 2]
                            k += 1
                            self.A(lambda h: h.activation(out=sg[:], in_=pg[:], func=AF.Silu), [bpg], [bsg])
                            self.V(lambda h: h.tensor_tensor(out=actT[:, jl, tok], in0=sg[:], in1=pu[:], op=ALU.mult),
                                   [bsg, bpu], [bact[g]])
                nj = j1 - j0
                for t in range(NT):
                    for half in range(2):
                        ps, bps = self.ringB.next()
                        for jl in range(nj):
                            self.MM(ps[:], actT[:, jl, t * 128:(t + 1) * 128], wot[:, jl, half * 512:(half + 1) * 512],
                                    jl == 0, jl == nj - 1, [bact[t // 4], bwo], [bps])
                        tmp, btmp = tmps[(2 * t + half) % 2]
                        self.resid_update(t, half, ps, bps, pi == 0, tmp, btmp)
            self.layer_norm_x(es)
            fw.barrier()
        self.dump(f"x_{s}_{l}_{j}", self.x[:], [128, NT, D], self.bx)

    def mixer(self, s, l):
        fw = self.fw
        with ExitStack() as es:
            with ExitStack() as e0:
                self.build_hT(e0)
                fw.barrier()
            first = True
            for m, fn in enumerate([self.mix_hgrn, self.mix_mla, self.mix_fox, self.mix_gmlp]):
                if m in self.skip_mixers:
                    continue
                with ExitStack() as es2:
                    fn(es2, s, l, m, first)
                    fw.barrier()
                first = False
            if first:
                for t in range(self.NT):
                    self.V(lambda h: h.tensor_scalar_mul(out=self.x[:, t, :], in0=self.x[:, t, :], scalar1=ALPHA),
                           [self.bx[t]], [self.bx[t]])
            self.layer_norm_x(es)
            fw.barrier()
        self.dump(f"x_{s}_{l}_1", self.x[:], [128, self.NT, D], self.bx)

    skip_mixers = ()

    def out_proj(self, es, l, m, mixT, bmix, first):
        fw, dr, NT = self.fw, self.dr, self.NT
        wov = dr["mix_w_out"][l, m * 256:(m + 1) * 256, :].rearrange("(s p) n -> p s n", p=64)
        wo = fw.sb(es, "mwo", [64, 4, D], BF16)
        bwo = Buf()
        ch = fw.new_chan("mwo")
        self.wdma(wo[:], wov, ch, writes=[bwo])
        tmps = [(fw.sb(es, f"ot{i}", [128, 512], F32), Buf()) for i in range(2)]
        for t in range(NT):
            for half in range(2):
                ps, bps = self.ringB.next()
                for sl in range(4):
                    self.MM(ps[:], mixT[0:64, sl, t * 128:(t + 1) * 128], wo[0:64, sl, half * 512:(half + 1) * 512],
                            sl == 0, sl == 3, [bmix, bwo], [bps])
                tmp, btmp = tmps[(2 * t + half) % 2]
                self.resid_update(t, half, ps, bps, first, tmp, btmp)

    def attention(self, es, QT, KT, bqk, dk, Vaug, bv, scale, mixT, bmix):
        fw, dr, NG = self.fw, self.dr, self.NG
        negm = fw.sb(es, "negm", [128, 4, 512], BF16)
        sel = fw.sb(es, "sel65", [65, 64], F32)
        bc = Buf()
        ch = fw.new_chan("attc")
        self.wdma(negm[:], dr["negmask"], ch, writes=[bc])
        self.ldma(sel[:], dr["sel65"], ch, writes=[bc])
        tmpS = [(fw.sb(es, f"ts{i}", [128, 512], F32), Buf()) for i in range(2)]
        pTs = [(fw.sb(es, f"pT{i}", [128, 512], BF16), Buf()) for i in range(3)]
        osb = [(fw.sb(es, f"os{i}", [65, 512], F32), Buf()) for i in range(2)]
        rec = [(fw.sb(es, f"rc{i}", [64, 512], F32), Buf()) for i in range(2)]
        it = 0
        ip = 0
        for hd in range(4):
            for qg in range(NG):
                qs = slice(qg * 512, (qg + 1) * 512)
                nkb = 4 * (qg + 1)
                o_ps, bo = self.ringB.next()
                pend = []

                def qk(kb):
                    nonlocal ip
                    s_ps, bs = self.ringA.next()
                    self.MM(s_ps[:], KT[hd][0:dk, kb * 128:(kb + 1) * 128], QT[hd][0:dk, qs], True, True,
                            [bqk], [bs])
                    pT, bp = pTs[ip % 3]
                    ip += 1
                    r = kb - 4 * qg
                    if r >= 0:
                        tS, bt = tmpS[kb % 2]
                        self.V(lambda h: h.tensor_tensor(out=tS[:], in0=s_ps[:], in1=negm[:, r, :], op=ALU.add),
                               [bs, bc], [bt])
                        self.A(lambda h: h.activation(out=pT[:], in_=tS[:], func=AF.Exp, scale=scale), [bt], [bp])
                    else:
                        self.A(lambda h: h.activation(out=pT[:], in_=s_ps[:], func=AF.Exp, scale=scale), [bs], [bp])
                    return pT, bp

                def pv(kb, pT, bp):
                    self.MM(o_ps[0:65, :], Vaug[:, kb, hd, :], pT[:], kb == 0, kb == nkb - 1, [bv, bp], [bo])

                LA = 2
                for kb in range(nkb + LA):
                    if kb < nkb:
                        pend.append((kb,) + qk(kb))
                    if kb >= LA:
                        a = pend.pop(0)
                        pv(*a)
                o_sb, bos = osb[it % 2]
                rc, brc = rec[it % 2]
                it += 1
                self.A(lambda h: h.copy(out=o_sb[:], in_=o_ps[0:65, :]), [bo], [bos])
                d_ps, bd = self.ringB.next()
                self.MM(d_ps[0:64, :], sel[:], o_sb[:], True, True, [bc, bos], [bd])
                self.V(lambda h: h.reciprocal(out=rc[:], in_=d_ps[0:64, :]), [bd], [brc])
                self.V(lambda h: h.tensor_tensor(out=mixT[0:64, hd, qs], in0=o_sb[0:64, :], in1=rc[:], op=ALU.mult),
                       [bos, brc], [bmix])

    def mix_mla(self, es, s, l, m, first):
        fw, dr, NT, NG, S = self.fw, self.dr, self.NT, self.NG, self.S
        QT = [fw.sb(es, f"QT{h}", [96, S], BF16) for h in range(4)]
        KT = [fw.sb(es, f"KT{h}", [96, S], BF16) for h in range(4)]
        bqk = Buf()
        Vaug = fw.sb(es, "Vaug", [128, NT, 4, 65], BF16)
        bv = Buf()
        self.G(lambda h: h.memset(Vaug[:], 1.0), [], [bv])
        with ExitStack() as e1:
            wv = dr["mix_w_in"][l].rearrange("(kc p) n -> p kc n", p=128)
            wb = fw.sb(e1, "wB", [128, 8, 384], BF16)
            wkr = fw.sb(e1, "wkr", [128, 8, 192], BF16)
            wuq = fw.sb(e1, "wuq", [128, 2, 384], BF16)
            wuqp = fw.sb(e1, "wuqp", [128, 2, 384], BF16)
            wukv = fw.sb(e1, "wukv", [128, 512], BF16)
            gq = fw.sb(e1, "gq", [128, 256], F32)
            gkv = fw.sb(e1, "gkv", [128, 128], F32)
            bw = Buf()
            ch = fw.new_chan("mlaw")
            self.wdma(wb[:], wv[:, :, 1024:1408], ch, writes=[bw])
            self.wdma(wkr[:], dr["wkr"][l].rearrange("(kc p) n -> p kc n", p=128), ch, writes=[bw])
            self.wdma(wuq[:], dr["mla_w_uq"][l].rearrange("(kc p) n -> p kc n", p=128), ch, writes=[bw])
            self.wdma(wuqp[:], dr["mla_w_uqp"][l].rearrange("(kc p) n -> p kc n", p=128), ch, writes=[bw])
            self.wdma(wukv[:], dr["mla_w_ukv"][l], ch, writes=[bw])
            self.ldma(gq[:], dr["mla_q_norm_g"][l:l + 1, :].broadcast_to([128, 256]), ch, writes=[bw])
            self.ldma(gkv[:], dr["mla_kv_norm_g"][l:l + 1, :].broadcast_to([128, 128]), ch, writes=[bw])
            cqnT = fw.sb(e1, "cqnT", [128, 2, S], BF16)
            ckvT = fw.sb(e1, "ckvT", [128, S], BF16)
            bcT = [Buf() for _ in range(NG)]
            rc = [(fw.sb(e1, f"rC{i}", [96, 512], F32), fw.sb(e1, f"rS{i}", [96, 512], F32), Buf(),
                   fw.new_chan(f"rope{i}")) for i in range(2)]
            st = [(fw.sb(e1, f"mst{i}", [128, 2, 6], F32), fw.sb(e1, f"mmv{i}", [128, 8], F32), Buf()) for i in range(2)]
            cn = [(fw.sb(e1, f"cn{i}", [128, 384], BF16), Buf()) for i in range(2)]
            for t in range(NT):
                tok = slice(t * 128, (t + 1) * 128)
                ps, bps = self.ringA.next()
                for kc in range(8):
                    self.MM(ps[:, 0:384], self.hT[:, kc, tok], wb[:, kc, :], kc == 0, kc == 7, [self.bhT[t // 4], bw], [bps])
                s6, m8, bs = st[t % 2]
                cnt, bcn = cn[t % 2]
                self.V(lambda h: h.bn_stats(out=s6[:, 0, :], in_=ps[:, 0:256]), [bps], [bs])
                self.V(lambda h: h.bn_stats(out=s6[:, 1, :], in_=ps[:, 256:384]), [bps], [bs])
                for i in range(2):
                    self.V(lambda h: h.bn_aggr(out=m8[:, 4 * i:4 * i + 2], in_=s6[:, i:i + 1, :]), [bs], [bs])
                    self.V(lambda h: h.scalar_tensor_tensor(out=m8[:, 4 * i + 2:4 * i + 3], in0=m8[:, 4 * i:4 * i + 1],
                                                            scalar=m8[:, 4 * i:4 * i + 1], in1=m8[:, 4 * i + 1:4 * i + 2],
                                                            op0=ALU.mult, op1=ALU.add), [bs], [bs])
                    self.V(lambda h: h.tensor_scalar_add(out=m8[:, 4 * i + 2:4 * i + 3], in0=m8[:, 4 * i + 2:4 * i + 3],
                                                         scalar1=RMS_EPS), [bs], [bs])
                    self.A(lambda h: h.sqrt(out=m8[:, 4 * i + 2:4 * i + 3], in_=m8[:, 4 * i + 2:4 * i + 3]), [bs], [bs])
                    self.V(lambda h: h.reciprocal(out=m8[:, 4 * i + 3:4 * i + 4], in_=m8[:, 4 * i + 2:4 * i + 3]), [bs], [bs])
                self.V(lambda h: h.scalar_tensor_tensor(out=cnt[:, 0:256], in0=ps[:, 0:256], scalar=m8[:, 3:4], in1=gq[:],
                                                        op0=ALU.mult, op1=ALU.mult), [bps, bs, bw], [bcn])
                self.V(lambda h: h.scalar_tensor_tensor(out=cnt[:, 256:384], in0=ps[:, 256:384], scalar=m8[:, 7:8],
                                                        in1=gkv[:], op0=ALU.mult, op1=ALU.mult), [bps, bs, bw], [bcn])
                pt, bpt = self.ringT.next()
                for c in range(3):
                    self.TR(pt[:, c * 128:(c + 1) * 128], cnt[:, c * 128:(c + 1) * 128], [bcn], [bpt], inc=(c == 2))
                self.A(lambda h: h.copy(out=cqnT[:, :, tok], in_=pt[:, 0:256].rearrange("p (k c) -> p k c", k=2)),
                       [bpt], [bcT[t // 4]])
                self.A(lambda h: h.copy(out=ckvT[:, tok], in_=pt[:, 256:384]), [bpt], [bcT[t // 4]])
                pv_, bpv = self.ringB.next()
                self.MM(pv_[:, 0:256].rearrange("p (h x) -> p h x", h=4), ckvT[:, tok],
                        wukv[:].rearrange("p (h x) -> p h x", h=4)[:, :, 64:128], True, True,
                        [bcT[t // 4], bw], [bpv])
                self.A(lambda h: h.copy(out=Vaug[:, t, :, 0:64], in_=pv_[:, 0:256].rearrange("p (h x) -> p h x", h=4)),
                       [bpv], [bv])
            t1s = [(fw.sb(e1, f"r1{i}", [96, 512], F32), Buf()) for i in range(2)]
            t2s = [(fw.sb(e1, f"r2{i}", [96, 512], F32), Buf()) for i in range(2)]
            k = 0
            for g in range(NG):
                tok = slice(g * 512, (g + 1) * 512)
                rC, rS, brp, chr_ = rc[g % 2]
                self.ldma(rC[:], dr["ropeC"][:, tok], chr_, writes=[brp])
                self.ldma(rS[:], dr["ropeS"][:, tok], chr_, writes=[brp])
                for hd in range(4):
                    pq, bpq = self.ringA.next()
                    pp, bpp = self.ringA.next()
                    for kc in range(2):
                        self.MM(pq[0:96, :], wuq[:, kc, hd * 96:(hd + 1) * 96], cqnT[:, kc, tok], kc == 0, kc == 1,
                                [bw, bcT[g]], [bpq])
                    for kc in range(2):
                        self.MM(pp[0:96, :], wuqp[:, kc, hd * 96:(hd + 1) * 96], cqnT[:, kc, tok], kc == 0, kc == 1,
                                [bw, bcT[g]], [bpp])
                    t1, b1 = t1s[k % 2]
                    t2, b2 = t2s[k % 2]
                    k += 1
                    self.V(lambda h: h.tensor_tensor(out=t1[:], in0=pq[0:96, :], in1=rC[:], op=ALU.mult), [bpq, brp], [b1])
                    self.V(lambda h: h.tensor_tensor(out=t2[:], in0=pp[0:96, :], in1=rS[:], op=ALU.mult), [bpp, brp], [b2])
                    self.G(lambda h: h.tensor_tensor(out=QT[hd][:, tok], in0=t1[:], in1=t2[:], op=ALU.add), [b1, b2], [bqk])
                    pk, bpk = self.ringA.next()
                    self.MM(pk[0:64, :], wukv[:, hd * 128:hd * 128 + 64], ckvT[:, tok], True, True, [bw, bcT[g]], [bpk])
                    self.A(lambda h: h.copy(out=KT[hd][0:64, tok], in_=pk[0:64, :]), [bpk], [bqk])
                pq, bpq = self.ringA.next()
                pp, bpp = self.ringA.next()
                for kc in range(8):
                    self.MM(pq[0:96, :], wkr[:, kc, 0:96], self.hT[:, kc, tok], kc == 0, kc == 7, [bw, self.bhT[g]], [bpq])
                for kc in range(8):
                    self.MM(pp[0:96, :], wkr[:, kc, 96:192], self.hT[:, kc, tok], kc == 0, kc == 7, [bw, self.bhT[g]], [bpp])
                t1, b1 = t1s[k % 2]
                t2, b2 = t2s[k % 2]
                k += 1
                self.V(lambda h: h.tensor_tensor(out=t1[64:96, :], in0=pq[64:96, :], in1=rC[64:96, :], op=ALU.mult),
                       [bpq, brp], [b1])
                self.V(lambda h: h.tensor_tensor(out=t2[64:96, :], in0=pp[64:96, :], in1=rS[64:96, :], op=ALU.mult),
                       [bpp, brp], [b2])
                for hd in range(4):
                    self.G(lambda h: h.tensor_tensor(out=KT[hd][64:96, tok], in0=t1[64:96, :], in1=t2[64:96, :], op=ALU.add),
                           [b1, b2], [bqk])
            fw.barrier()
        if "mla_q" in self.dumps:
            self.dump_bf16(es, "mla_q", QT[0][:], [96, S], [bqk])
            self.dump_bf16(es, "mla_k", KT[0][:], [96, S], [bqk])
        with ExitStack() as e2:
            mixT = fw.sb(e2, "mixT", [64, 4, S], BF16)
            bmix = Buf()
            self.attention(e2, QT, KT, bqk, 96, Vaug, bv, float(96 ** -0.5), mixT, bmix)
            if "mla_o" in self.dumps:
                self.dump_bf16(e2, "mla_o", mixT[:], [64, 4, S], [bmix])
            self.out_proj(e2, l, m, mixT, bmix, first)
            fw.barrier()

    def dump_bf16(self, es, name, ap, shape, reads):
        if name not in self.dumps or name in self.dump_aps:
            return
        t = self.fw.sb(es, "dmp" + name, shape, F32)
        b = Buf()
        self.V(lambda h: h.tensor_copy(out=t[:], in_=ap), reads, [b])
        self.dump(name, t[:], shape, [b])
        self.fw.barrier()

    def mix_fox(self, es, s, l, m, first):
        fw, dr, NT, NG, S = self.fw, self.dr, self.NT, self.NG, self.S
        QT = [fw.sb(es, f"fQT{h}", [68, S], BF16) for h in range(4)]
        KT = [fw.sb(es, f"fKT{h}", [68, S], BF16) for h in range(4)]
        bqk = Buf()
        Vaug = fw.sb(es, "fVaug", [128, NT, 4, 65], BF16)
        bv = Buf()
        self.G(lambda h: h.memset(Vaug[:], 1.0), [], [bv])
        for hd in range(4):
            self.G(lambda h: h.memset(QT[hd][64:68, :], 1.0), [], [bqk])
            self.G(lambda h: h.memset(KT[hd][64:68, :], 1.0), [], [bqk])
        with ExitStack() as e1:
            wv = dr["mix_w_in"][l].rearrange("(kc p) n -> p kc n", p=128)
            wc = fw.sb(e1, "wC", [128, 8, 772], BF16)
            bw = Buf()
            ch = fw.new_chan("foxw")
            self.wdma(wc[:], wv[:, :, 1440:2212], ch, writes=[bw])
            nbf = fw.sb(e1, "nbf", [4, 1], F32)
            self.ldma(nbf[:], dr["fox_b_f"][l:l + 1, :].rearrange("o h -> h o"), ch, writes=[bw])
            self.V(lambda h: h.tensor_scalar_mul(out=nbf[:], in0=nbf[:], scalar1=-1.0), [bw], [bw])
            ones4 = fw.sb(e1, "ones4", [4, 512], F32)
            carry = fw.sb(e1, "carry", [4, 1], F32)
            bcar = Buf()
            self.G(lambda h: h.memset(ones4[:], 1.0), [], [bcar])
            self.G(lambda h: h.memset(carry[:], 0.0), [bcar], [bcar])
            ch_row = [fw.new_chan(f"frow{i}") for i in range(2)]
            for t in range(NT):
                tok = slice(t * 128, (t + 1) * 128)
                pv_, bpv = self.ringB.next()
                for kc in range(8):
                    self.MM(pv_[:, 0:256], self.hT[:, kc, tok], wc[:, kc, 512:768], kc == 0, kc == 7,
                            [self.bhT[t // 4], bw], [bpv])
                self.A(lambda h: h.copy(out=Vaug[:, t, :, 0:64], in_=pv_[:, 0:256].rearrange("p (h x) -> p h x", h=4)),
                       [bpv], [bv])
            ft = [dict(e=fw.sb(e1, f"fe{i}", [4, 512], F32), fn=fw.sb(e1, f"ffn{i}", [4, 512], F32),
                       hi=fw.sb(e1, f"fhi{i}", [4, 512], BF16), hif=fw.sb(e1, f"fhf{i}", [4, 512], F32),
                       lo=fw.sb(e1, f"flo{i}", [4, 512], BF16), nhi=fw.sb(e1, f"fnh{i}", [4, 512], BF16),
                       nlo=fw.sb(e1, f"fnl{i}", [4, 512], BF16), b=Buf()) for i in range(2)]
            for g in range(NG):
                tok = slice(g * 512, (g + 1) * 512)
                for hd in range(4):
                    for which, dst in ((0, QT), (1, KT)):
                        pq, bpq = self.ringA.next()
                        for kc in range(8):
                            self.MM(pq[0:64, :], wc[:, kc, which * 256 + hd * 64:which * 256 + (hd + 1) * 64],
                                    self.hT[:, kc, tok], kc == 0, kc == 7, [bw, self.bhT[g]], [bpq])
                        self.A(lambda h: h.copy(out=dst[hd][0:64, tok], in_=pq[0:64, :]), [bpq], [bqk])
                pf, bpf = self.ringA.next()
                for kc in range(8):
                    self.MM(pf[0:4, :], wc[:, kc, 768:772], self.hT[:, kc, tok], kc == 0, kc == 7, [bw, self.bhT[g]], [bpf])
                f = ft[g % 2]
                bf_ = f["b"]
                self.A(lambda h: h.activation(out=f["e"][:], in_=pf[0:4, :], func=AF.Exp, bias=nbf[:], scale=-1.0),
                       [bpf, bw], [bf_])
                self.A(lambda h: h.activation(out=f["e"][:], in_=f["e"][:], func=AF.Ln, bias=1.0, scale=1.0), [bf_], [bf_])
                self.V(lambda h: h.tensor_tensor_scan(out=f["fn"][:], data0=ones4[:], data1=f["e"][:], initial=carry[:],
                                                      op0=ALU.mult, op1=ALU.add), [bf_, bcar], [bf_])
                self.V(lambda h: h.tensor_copy(out=carry[:], in_=f["fn"][:, 511:512]), [bf_], [bcar])
                self.V(lambda h: h.tensor_scalar_mul(out=f["fn"][:], in0=f["fn"][:], scalar1=8.0), [bf_], [bf_])
                self.V(lambda h: h.tensor_copy(out=f["hi"][:], in_=f["fn"][:]), [bf_], [bf_])
                self.V(lambda h: h.tensor_copy(out=f["hif"][:], in_=f["hi"][:]), [bf_], [bf_])
                self.V(lambda h: h.tensor_tensor(out=f["lo"][:], in0=f["fn"][:], in1=f["hif"][:], op=ALU.subtract),
                       [bf_], [bf_])
                self.V(lambda h: h.tensor_scalar_mul(out=f["nhi"][:], in0=f["hi"][:], scalar1=-1.0), [bf_], [bf_])
                self.V(lambda h: h.tensor_scalar_mul(out=f["nlo"][:], in0=f["lo"][:], scalar1=-1.0), [bf_], [bf_])
                chr_ = ch_row[g % 2]
                for hd in range(4):
                    self.ldma(QT[hd][64:65, tok], f["nhi"][hd:hd + 1, :], chr_, reads=[bf_], writes=[bqk])
                    self.ldma(QT[hd][65:66, tok], f["nlo"][hd:hd + 1, :], chr_, reads=[bf_], writes=[bqk])
                    self.ldma(KT[hd][66:67, tok], f["hi"][hd:hd + 1, :], chr_, reads=[bf_], writes=[bqk])
                    self.ldma(KT[hd][67:68, tok], f["lo"][hd:hd + 1, :], chr_, reads=[bf_], writes=[bqk])
            fw.barrier()
        with ExitStack() as e2:
            mixT = fw.sb(e2, "fmixT", [64, 4, S], BF16)
            bmix = Buf()
            self.attention(e2, QT, KT, bqk, 68, Vaug, bv, 0.125, mixT, bmix)
            if "fox_o" in self.dumps:
                self.dump_bf16(e2, "fox_o", mixT[:], [64, 4, S], [bmix])
            self.out_proj(e2, l, m, mixT, bmix, first)
            fw.barrier()

    def mix_gmlp(self, es, s, l, m, first):
        fw, dr, NT, NG, S = self.fw, self.dr, self.NT, self.NG, self.S
        wv = dr["mix_w_in"][l].rearrange("(kc p) n -> p kc n", p=128)
        wd = fw.sb(es, "wD", [128, 8, 512], BF16)
        bw = Buf()
        ch = fw.new_chan("gmw")
        self.wdma(wd[:], wv[:, :, 2212:2724], ch, writes=[bw])
        wsf = fw.sb(es, "wsf", [128, 4, 128], F32)
        caus = fw.sb(es, "caus", [128, 128], F32)
        wsm = fw.sb(es, "wsm", [128, 4, 128], BF16)
        bsb = fw.sb(es, "bsb", [64, 4, 128], F32)
        lng = fw.sb(es, "glng", [128, 256], F32)
        lnb = fw.sb(es, "glnb", [128, 256], F32)
        self.ldma(wsf[:], dr["gmlp_wsT"][l].rearrange("g s t -> s g t"), ch, writes=[bw])
        self.ldma(caus[:], dr["caus"], ch, writes=[bw])
        self.ldma(bsb[:], dr["gmlp_b_s"][l:l + 1].broadcast_to([64, 4, 128]), ch, writes=[bw])
        self.ldma(lng[:], dr["gmlp_ln_g"][l:l + 1, :].broadcast_to([128, 256]), ch, writes=[bw])
        self.ldma(lnb[:], dr["gmlp_ln_b"][l:l + 1, :].broadcast_to([128, 256]), ch, writes=[bw])
        self.V(lambda h: h.tensor_tensor(out=wsm[:], in0=wsf[:], in1=caus[:].unsqueeze(1).broadcast_to([128, 4, 128]),
                                         op=ALU.mult), [bw], [bw])
        mixT = fw.sb(es, "gmixT", [64, 4, S], BF16)
        bmix = Buf()
        gv = [(fw.sb(es, f"gv{i}", [128, 256], F32), Buf()) for i in range(2)]
        vl = [(fw.sb(es, f"vl{i}", [128, 256], BF16), Buf()) for i in range(2)]
        st = [(fw.sb(es, f"gst{i}", [128, 6], F32), fw.sb(es, f"gmv{i}", [128, 4], F32), Buf()) for i in range(2)]
        gu = [(fw.sb(es, f"gu{i}", [64, 512], F32), Buf()) for i in range(2)]
        t1s = [(fw.sb(es, f"gt{i}", [64, 512], F32), Buf()) for i in range(2)]
        for t in range(NT):
            tok = slice(t * 128, (t + 1) * 128)
            bh = self.bhT[t // 4]
            pv_, bpv = self.ringA.next()
            for kc in range(8):
                self.MM(pv_[:, 0:256], self.hT[:, kc, tok], wd[:, kc, 256:512], kc == 0, kc == 7, [bh, bw], [bpv])
            g_, bg = gv[t % 2]
            v_, bvl = vl[t % 2]
            s6, m4, bs = st[t % 2]
            self.A(lambda h: h.activation(out=g_[:], in_=pv_[:, 0:256], func=AF.Gelu_apprx_tanh), [bpv], [bg])
            self.V(lambda h: h.bn_stats(out=s6[:], in_=g_[:]), [bg], [bs])
            self.V(lambda h: h.bn_aggr(out=m4[:, 0:2], in_=s6[:]), [bs], [bs])
            self.V(lambda h: h.tensor_scalar_add(out=m4[:, 2:3], in0=m4[:, 1:2], scalar1=LN_EPS), [bs], [bs])
            self.A(lambda h: h.sqrt(out=m4[:, 2:3], in_=m4[:, 2:3]), [bs], [bs])
            self.V(lambda h: h.reciprocal(out=m4[:, 2:3], in_=m4[:, 2:3]), [bs], [bs])
            self.V(lambda h: h.scalar_tensor_tensor(out=m4[:, 3:4], in0=m4[:, 0:1], scalar=-1.0, in1=m4[:, 2:3],
                                                    op0=ALU.mult, op1=ALU.mult), [bs], [bs])
            self.A(lambda h: h.activation(out=g_[:], in_=g_[:], func=AF.Identity, bias=m4[:, 3:4], scale=m4[:, 2:3]),
                   [bs, bg], [bg])
            self.V(lambda h: h.tensor_tensor(out=g_[:], in0=g_[:], in1=lng[:], op=ALU.mult), [bg, bw], [bg])
            self.G(lambda h: h.tensor_tensor(out=v_[:], in0=g_[:], in1=lnb[:], op=ALU.add), [bg, bw], [bvl])
            pm, bpm = self.ringA.next()
            for g in range(4):
                self.MM(pm[0:64, g * 128:(g + 1) * 128], v_[:, g * 64:(g + 1) * 64], wsm[:, g, :], g == 0, True,
                        [bvl, bw], [bpm], inc=(g == 3), skip=True)
            pu, bpu = self.ringA.next()
            for g in range(4):
                for kc in range(8):
                    self.MM(pu[0:64, g * 128:(g + 1) * 128], wd[:, kc, g * 64:(g + 1) * 64], self.hT[:, kc, tok],
                            kc == 0, kc == 7, [bh, bw], [bpu], inc=(g == 3 and kc == 7), skip=True)
            u_, bu = gu[t % 2]
            t1, b1 = t1s[t % 2]
            self.A(lambda h: h.activation(out=u_[:], in_=pu[0:64, :], func=AF.Gelu_apprx_tanh), [bpu], [bu])
            self.V(lambda h: h.tensor_tensor(out=t1[:], in0=pm[0:64, :], in1=bsb[:].rearrange("p g t -> p (g t)"),
                                             op=ALU.add), [bpm, bw], [b1])
            self.V(lambda h: h.tensor_tensor(out=mixT[0:64, :, tok], in0=t1[:].rearrange("p (g t) -> p g t", g=4),
                                             in1=u_[:].rearrange("p (g t) -> p g t", g=4), op=ALU.mult),
                   [b1, bu], [bmix])
        if "gmlp_o" in self.dumps:
            self.dump_bf16(es, "gmlp_o", mixT[:], [64, 4, S], [bmix])
        self.out_proj(es, l, m, mixT, bmix, first)

    def mix_hgrn(self, es, s, l, m, first):
        fw, dr, NT, NG, S = self.fw, self.dr, self.NT, self.NG, self.S
        wv = dr["mix_w_in"][l].rearrange("(kc p) n -> p kc n", p=128)
        wa = fw.sb(es, "wA", [128, 8, 1024], BF16)
        bw = Buf()
        ch = fw.new_chan("hgw")
        self.wdma(wa[:], wv[:, :, 0:1024], ch, writes=[bw])
        tri = fw.sb(es, "tri", [128, 128], F32)
        blk = fw.sb(es, "blk", [128, 128], F32)
        csel = fw.sb(es, "csel", [128, 8], F32)
        cselb = fw.sb(es, "cselb", [128, 8], BF16)
        ones64 = fw.sb(es, "ones64", [64, 64], F32)
        ng = fw.sb(es, "ng", [64, 4], F32)
        lb = fw.sb(es, "lb", [128, 256], F32)
        oml = fw.sb(es, "oml", [128, 256], F32)
        self.ldma(tri[:], dr["tri"], ch, writes=[bw])
        self.ldma(blk[:], dr["blk"], ch, writes=[bw])
        self.ldma(csel[:], dr["csel"], ch, writes=[bw])
        self.ldma(ones64[:], dr["ones64"], ch, writes=[bw])
        self.ldma(ng[:], dr["hgrn_ng"][l], ch, writes=[bw])
        self.V(lambda h: h.tensor_copy(out=cselb[:], in_=csel[:]), [bw], [bw])
        if l == 0:
            self.G(lambda h: h.memset(lb[:], 0.0), [], [bw])
        else:
            assert self.L == 2
            self.ldma(lb[:], dr["hgrn_lb_logits"][1:2, :].broadcast_to([128, 256]), ch, writes=[bw])
            self.ldma(oml[:], dr["hgrn_lb_logits"][0:1, :].broadcast_to([128, 256]), ch, writes=[bw])
            self.V(lambda h: h.tensor_tensor(out=lb[:], in0=lb[:], in1=oml[:], op=ALU.subtract), [bw], [bw])
            self.A(lambda h: h.activation(out=lb[:], in_=lb[:], func=AF.Sigmoid), [bw], [bw])
        self.V(lambda h: h.tensor_scalar(out=oml[:], in0=lb[:], scalar1=-1.0, scalar2=1.0, op0=ALU.mult, op1=ALU.add),
               [bw], [bw])
        mixT = fw.sb(es, "hmixT", [64, 4, S], BF16)
        bmix = Buf()
        Sst = fw.sb(es, "Sst", [64, 4, 64], F32)
        Sbf = fw.sb(es, "Sbf", [64, 4, 64], BF16)
        bS = Buf()
        bSb = Buf()
        self.G(lambda h: h.memset(Sst[:], 0.0), [], [bS])
        self.G(lambda h: h.memset(Sbf[:], 0.0), [], [bSb])

        def T(name, shape, dt, n=2):
            r = [(fw.sb(es, f"{name}{i}", shape, dt), Buf()) for i in range(n)]
            return r * (2 // n)
        f_ = T("hf", [128, 256], F32)
        lf_ = T("hlf", [128, 256], F32)
        kk_ = T("hkk", [128, 256], F32)
        eg_ = T("heg", [128, 768], F32)
        qf_ = T("hqf", [128, 256], F32)
        qt_ = T("hqt", [128, 256], BF16)
        kh_ = T("hkh", [128, 256], F32)
        khb_ = T("hkhb", [128, 256], BF16)
        ke_ = T("hke", [128, 256], BF16)
        kem_ = T("hkem", [128, 8, 256], BF16, 1)
        vb_ = T("hvb", [128, 256], BF16)
        sgb_ = T("hsg", [128, 256], BF16)
        Dt_ = T("hDt", [64, 4, 8], F32)
        qkT_ = T("hqkT", [64, 8, 128], BF16)
        gT_ = T("hgT", [64, 4, 128], BF16)
        scm_ = T("hscm", [128, 4, 128], BF16)
        osq_ = T("hosq", [64, 512], F32, 1)
        rs_ = T("hrs", [64, 512], F32, 1)
        t1_ = T("ht1", [64, 512], F32, 1)
        for t in range(NT):
            tok = slice(t * 128, (t + 1) * 128)
            bh = self.bhT[t // 4]
            i2 = t % 2
            pa, bpa = self.ringA.next()
            pb, bpb = self.ringA.next()
            for kc in range(8):
                self.MM(pa[:], self.hT[:, kc, tok], wa[:, kc, 0:512], kc == 0, kc == 7, [bh, bw], [bpa])
            for kc in range(8):
                self.MM(pb[:], self.hT[:, kc, tok], wa[:, kc, 512:1024], kc == 0, kc == 7, [bh, bw], [bpb])
            f, bf_ = f_[i2]
            lf, blf = lf_[i2]
            kk, bkk = kk_[i2]
            eg, beg = eg_[i2]
            qf, bqf = qf_[i2]
            qt, bqt = qt_[i2]
            kh, bkh = kh_[i2]
            khb, bkhb = khb_[i2]
            ke, bke = ke_[i2]
            kem, bkem = kem_[i2]
            vb, bvb = vb_[i2]
            sgb, bsgb = sgb_[i2]
            Dt, bDt = Dt_[i2]
            qkT, bqkT = qkT_[i2]
            gT, bgT = gT_[i2]
            scm, bscm = scm_[i2]
            osq, bosq = osq_[i2]
            rs, brs = rs_[i2]
            t1, bt1 = t1_[i2]
            self.A(lambda h: h.activation(out=f[:], in_=pa[:, 256:512], func=AF.Sigmoid), [bpa], [bf_])
            self.A(lambda h: h.activation(out=qf[:], in_=pa[:, 0:256], func=AF.Silu), [bpa], [bqf])
            self.A(lambda h: h.copy(out=vb[:], in_=pb[:, 0:256]), [bpb], [bvb])
            self.A(lambda h: h.activation(out=sgb[:], in_=pb[:, 256:512], func=AF.Silu), [bpb], [bsgb])
            self.V(lambda h: h.tensor_tensor(out=f[:], in0=f[:], in1=oml[:], op=ALU.mult), [bf_, bw], [bf_])
            self.V(lambda h: h.tensor_tensor(out=f[:], in0=f[:], in1=lb[:], op=ALU.add), [bf_, bw], [bf_])
            self.A(lambda h: h.activation(out=lf[:], in_=f[:], func=AF.Ln), [bf_], [blf])
            self.V(lambda h: h.tensor_scalar(out=kk[:], in0=f[:], scalar1=-1.0, scalar2=1.0, op0=ALU.mult, op1=ALU.add),
                   [bf_], [bkk])
            pg, bpg = self.ringA.next()
            self.MM(pg[:, 0:256], tri[:], lf[:], True, True, [bw, blf], [bpg], inc=False)
            self.MM(pg[:, 256:512], blk[:], lf[:], True, True, [bw, blf], [bpg], skip=True)
            pd, bpd = self.ringB.next()
            for hd in range(4):
                self.MM(pd[0:64, hd * 8:(hd + 1) * 8], lf[:, hd * 64:(hd + 1) * 64], csel[:], hd == 0, True,
                        [bw, blf], [bpd], inc=(hd == 3), skip=True)
            self.A(lambda h: h.activation(out=eg[:, 0:256], in_=pg[:, 0:256], func=AF.Exp), [bpg], [beg])
            self.A(lambda h: h.activation(out=eg[:, 256:512], in_=pg[:, 0:256], func=AF.Exp, scale=-1.0), [bpg], [beg])
            self.A(lambda h: h.activation(out=eg[:, 512:768], in_=pg[:, 256:512], func=AF.Exp), [bpg], [beg])
            self.A(lambda h: h.activation(out=Dt[:].rearrange("p h c -> p (h c)"), in_=pd[0:64, 0:32], func=AF.Exp),
                   [bpd], [bDt])
            self.V(lambda h: h.tensor_tensor(out=qt[:], in0=qf[:], in1=eg[:, 0:256], op=ALU.mult), [bqf, beg], [bqt])
            self.V(lambda h: h.tensor_tensor(out=kh[:], in0=kk[:], in1=eg[:, 256:512], op=ALU.mult), [bkk, beg], [bkh])
            self.G(lambda h: h.tensor_copy(out=khb[:], in_=kh[:]), [bkh], [bkhb])
            self.G(lambda h: h.tensor_tensor(out=ke[:], in0=kh[:], in1=eg[:, 512:768], op=ALU.mult), [bkh, beg], [bke])
            self.G(lambda h: h.tensor_tensor(out=kem[:], in0=ke[:].unsqueeze(1).broadcast_to([128, 8, 256]),
                                             in1=cselb[:].unsqueeze(2).broadcast_to([128, 8, 256]), op=ALU.mult),
                   [bke, bw], [bkem])
            pt, bpt = self.ringT.next()
            for hd in range(4):
                self.TR(pt[0:64, hd * 128:(hd + 1) * 128], qt[:, hd * 64:(hd + 1) * 64], [bqt], [bpt], inc=False)
            for hd in range(4):
                self.TR(pt[0:64, 512 + hd * 128:512 + (hd + 1) * 128], khb[:, hd * 64:(hd + 1) * 64], [bkhb], [bpt],
                        inc=(hd == 3))
            self.A(lambda h: h.copy(out=qkT[:].rearrange("p a t -> p (a t)"), in_=pt[0:64, :]), [bpt], [bqkT])
            pt2, bpt2 = self.ringT.next()
            for hd in range(4):
                self.TR(pt2[0:64, hd * 128:(hd + 1) * 128], sgb[:, hd * 64:(hd + 1) * 64], [bsgb], [bpt2], inc=(hd == 3))
            self.A(lambda h: h.copy(out=gT[:].rearrange("p a t -> p (a t)"), in_=pt2[0:64, 0:512]), [bpt2], [bgT])
            psc, bpsc = self.ringA.next()
            for hd in range(4):
                self.MM(psc[:, hd * 128:(hd + 1) * 128], qkT[:, 4 + hd, :], qkT[:, hd, :], hd == 0, True,
                        [bqkT], [bpsc], inc=(hd == 3), skip=True)
            self.V(lambda h: h.tensor_tensor(out=scm[:], in0=psc[:].rearrange("p (a t) -> p a t", a=4),
                                             in1=tri[:].unsqueeze(1).broadcast_to([128, 4, 128]), op=ALU.mult),
                   [bpsc, bw], [bscm])
            po, bpo = self.ringB.next()
            firstmm = True
            for ci in range(8):
                c = t * 8 + ci
                if c > 0:
                    for hd in range(4):
                        self.MM(po[0:64, hd * 128 + ci * 16:hd * 128 + (ci + 1) * 16], Sbf[:, hd, :],
                                qkT[:, hd, ci * 16:(ci + 1) * 16], firstmm, False, [bSb, bqkT], [bpo], inc=False, skip=True)
                        firstmm = False
                pkv, bpkv = self.ringA.next()
                for hd in range(4):
                    self.MM(pkv[0:64, hd * 64:(hd + 1) * 64], kem[:, ci, hd * 64:(hd + 1) * 64], vb[:, hd * 64:(hd + 1) * 64],
                            hd == 0, True, [bkem, bvb], [bpkv], inc=(hd == 3), skip=True)
                for hd in range(4):
                    self.V(lambda h: h.scalar_tensor_tensor(out=Sst[:, hd, :], in0=Sst[:, hd, :], scalar=Dt[:, hd, ci:ci + 1],
                                                            in1=pkv[0:64, hd * 64:(hd + 1) * 64], op0=ALU.mult, op1=ALU.add),
                           [bS, bDt, bpkv], [bS])
                self.A(lambda h: h.copy(out=Sbf[:], in_=Sst[:]), [bS], [bSb])
            for hd in range(4):
                self.MM(po[0:64, hd * 128:(hd + 1) * 128], vb[:, hd * 64:(hd + 1) * 64], scm[:, hd, :], firstmm, True,
                        [bvb, bscm], [bpo], inc=(hd == 3), skip=True)
                firstmm = False
            self.A(lambda h: h.activation(out=osq[:], in_=po[0:64, :], func=AF.Square), [bpo], [bosq])
            pss, bpss = self.ringA.next()
            self.MM(pss[0:64, :], ones64[:], osq[:], True, True, [bw, bosq], [bpss])
            self.V(lambda h: h.tensor_scalar(out=rs[:], in0=pss[0:64, :], scalar1=1.0 / 64.0, scalar2=RMS_EPS,
                                             op0=ALU.mult, op1=ALU.add), [bpss], [brs])
            self.A(lambda h: h.sqrt(out=rs[:], in_=rs[:]), [brs], [brs])
            self.V(lambda h: h.reciprocal(out=rs[:], in_=rs[:]), [brs], [brs])
            self.V(lambda h: h.tensor_tensor(out=t1[:], in0=po[0:64, :], in1=rs[:], op=ALU.mult), [bpo, brs], [bt1])
            for hd in range(4):
                self.V(lambda h: h.scalar_tensor_tensor(out=mixT[0:64, hd, tok], in0=t1[:, hd * 128:(hd + 1) * 128],
                                                        scalar=ng[:, hd:hd + 1], in1=gT[:, hd, :],
                                                        op0=ALU.mult, op1=ALU.mult), [bt1, bw, bgT], [bmix])
        if "hgrn_o" in self.dumps:
            self.dump_bf16(es, "hgrn_o", mixT[:], [64, 4, S], [bmix])
        self.out_proj(es, l, m, mixT, bmix, first)


_NC_CACHE = {}


def _get_nc(S, L):
    key = (S, L)
    if key not in _NC_CACHE:
        _NC_CACHE[key] = Builder(S, L).build()
    return _NC_CACHE[key]


def make_in_maps(inputs, n_cores, S):
    w = host_weights(inputs)
    consts = host_consts(S)
    x = np.asarray(inputs["x"], dtype=np.float32)
    c = np.asarray(inputs["c"], dtype=np.float32)
    maps = []
    for i in range(n_cores):
        cc = c[2 * i:2 * i + 2]
        cT = np.ascontiguousarray(cc.reshape(2, 8, 128).transpose(2, 1, 0))
        mp = {"x": np.ascontiguousarray(x[2 * i:2 * i + 2]), "cT": cT}
        mp.update(w)
        mp.update(consts)
        maps.append(mp)
    return maps


def kernel(**inputs):
    x = np.asarray(inputs["x"])
    B, S, _ = x.shape
    L = np.asarray(inputs["ada_w"]).shape[0]
    n = B // 2
    nc = _get_nc(S, L)
    maps = make_in_maps(inputs, n, S)
    res = run_bass_kernel_spmd(nc, maps, core_ids=list(range(n)))
    out = np.concatenate([r["out"] for r in res.results], axis=0)
    return out.astype(np.float32)
```

```python
import numpy as np
from contextlib import ExitStack
import concourse.bass as bass
import concourse.mybir as mybir
from concourse.bass_utils import run_bass_kernel_spmd

F32 = mybir.dt.float32
BF16 = mybir.dt.bfloat16
AF = mybir.ActivationFunctionType
ALU = mybir.AluOpType

D = 1024
DFF = 2816
NJ = 22
MIXC = 2724
ALPHA = float(4 ** 0.25)
LN_EPS = 1e-5
RMS_EPS = 1e-6
NEG = -30000.0


class Buf:
    __slots__ = ("w", "r", "name")

    def __init__(self, name=""):
        self.w = {}
        self.r = {}
        self.name = name


class Chan:
    def __init__(self, sem, name):
        self.sem = sem
        self.cnt = 0
        self.name = name


class Eng:
    def __init__(self, name, h, chan):
        self.name = name
        self.h = h
        self.chan = chan
        self.seen = {}


class FW:
    def __init__(self, nc, es):
        self.nc = nc
        self.es = es
        self.nsem = 0
        self.chans = []
        self.pe = Eng("pe", nc.tensor, self.new_chan("pe"))
        self.act = Eng("act", nc.scalar, self.new_chan("act"))
        self.dve = Eng("dve", nc.vector, self.new_chan("dve"))
        self.pool = Eng("pool", nc.gpsimd, self.new_chan("pool"))
        self.sp = Eng("sp", nc.sync, self.new_chan("sp"))
        self.engs = [self.pe, self.act, self.dve, self.pool, self.sp]
        self.n_inst = 0
        self.n_wait = 0
        self.uid = 0

    def new_chan(self, name):
        for ch in self.chans:
            if ch.name == name and name not in ("dbg",):
                return ch
        sem = self.es.enter_context(self.nc.semaphore(f"s_{name}_{self.nsem}"))
        self.nsem += 1
        ch = Chan(sem, name)
        self.chans.append(ch)
        return ch

    def _sync(self, E, reads, writes):
        need = {}
        for b in reads:
            for ch, c in b.w.items():
                if need.get(ch, 0) < c:
                    need[ch] = c
        for b in writes:
            for ch, c in b.w.items():
                if need.get(ch, 0) < c:
                    need[ch] = c
            for ch, c in b.r.items():
                if need.get(ch, 0) < c:
                    need[ch] = c
        for ch, c in need.items():
            if ch is E.chan and E.name == "pe":
                continue
            if E.seen.get(ch, 0) >= c:
                continue
            E.h.wait_ge(ch.sem, c)
            self.n_wait += 1
            E.seen[ch] = c

    def _mark(self, ch, mark, reads, writes):
        for b in reads:
            if b.r.get(ch, 0) < mark:
                b.r[ch] = mark
        for b in writes:
            b.w[ch] = mark
            b.r = {}

    def op(self, E, emit, reads=(), writes=(), inc=True):
        self._sync(E, reads, writes)
        ins = emit(E.h)
        self.n_inst += 1
        ch = E.chan
        if inc:
            ch.cnt += 1
            ins.then_inc(ch.sem, 1)
            mark = ch.cnt
        else:
            mark = ch.cnt + 1
        self._mark(ch, mark, reads, writes)
        return ins

    def dma(self, Q, chan, out, in_, reads=(), writes=(), **kw):
        self._sync(Q, reads, writes)
        ins = Q.h.dma_start(out=out, in_=in_, **kw)
        self.n_inst += 1
        chan.cnt += 16
        ins.then_inc(chan.sem, 16)
        self._mark(chan, chan.cnt, reads, writes)
        return ins

    def barrier(self):
        for E in self.engs:
            for ch in self.chans:
                if ch.cnt == 0 or ch is E.chan:
                    continue
                if E.seen.get(ch, 0) >= ch.cnt:
                    continue
                E.h.wait_ge(ch.sem, ch.cnt)
                E.seen[ch] = ch.cnt

    def sb(self, es, name, shape, dt):
        self.uid += 1
        return es.enter_context(self.nc.sbuf_tensor(f"sb{self.uid}_{name}", list(shape), dt))

    def psum(self, es, name, shape, dt):
        self.uid += 1
        return es.enter_context(self.nc.psum_tensor(f"pp{self.uid}_{name}", list(shape), dt))


class Ring:
    def __init__(self, items):
        self.items = items
        self.i = 0

    def next(self):
        it = self.items[self.i % len(self.items)]
        self.i += 1
        return it


def host_consts(S):
    pos = np.arange(S, dtype=np.float32)
    half = 16
    inv_freq = (10000.0 ** (-np.arange(half, dtype=np.float32) / half)).astype(np.float32)
    ang = pos[None, :] * inv_freq[:, None]
    cos = np.cos(ang).astype(np.float32)
    sin = np.sin(ang).astype(np.float32)
    ropeC = np.ones((96, S), np.float32)
    ropeS = np.zeros((96, S), np.float32)
    ropeC[64:80] = cos
    ropeC[80:96] = cos
    ropeS[64:80] = -sin
    ropeS[80:96] = sin
    s = np.arange(128)[:, None]
    negmask = np.zeros((128, 4, 512), np.float32)
    t = np.arange(512)[None, :]
    for r in range(4):
        negmask[:, r, :] = np.where(128 * r + s <= t, 0.0, NEG)
    ident = np.eye(128, dtype=np.float32)
    t128 = np.arange(128)[None, :]
    same = (s // 16) == (t128 // 16)
    tri = (same & (s <= t128)).astype(np.float32)
    blk = same.astype(np.float32)
    csel = ((s // 16) == np.arange(8)[None, :]).astype(np.float32)
    caus = (s <= t128).astype(np.float32)
    sel65 = np.zeros((65, 64), np.float32)
    sel65[64, :] = 1.0
    ones64 = np.ones((64, 64), np.float32)
    return dict(ropeC=ropeC, ropeS=ropeS, negmask=negmask, ident=ident, tri=tri, blk=blk,
                csel=csel, caus=caus, sel65=sel65, ones64=ones64)


CONST_SHAPES = lambda S: dict(ropeC=[96, S], ropeS=[96, S], negmask=[128, 4, 512], ident=[128, 128],
                              tri=[128, 128], blk=[128, 128], csel=[128, 8], caus=[128, 128],
                              sel65=[65, 64], ones64=[64, 64])


def weight_shapes(L):
    return dict(
        ada_w=[L, D, 9 * D], ada_b=[L, 9 * D], ln_g=[L, 3, D], ln_b=[L, 3, D],
        ffn1_w_in=[L, D, 2 * DFF], ffn1_w_out=[L, DFF, D], ffn2_w_in=[L, D, 2 * DFF], ffn2_w_out=[L, DFF, D],
        mix_w_in=[L, D, MIXC], mix_w_out=[L, D, D], wkr=[L, D, 192],
        hgrn_lb_logits=[L, 256], hgrn_ng=[L, 64, 4], mla_q_norm_g=[L, 256], mla_kv_norm_g=[L, 128],
        mla_w_uq=[L, 256, 384], mla_w_uqp=[L, 256, 384], mla_w_ukv=[L, 128, 512],
        fox_b_f=[L, 4], gmlp_ln_g=[L, 256], gmlp_ln_b=[L, 256], gmlp_wsT=[L, 4, 128, 128], gmlp_b_s=[L, 4, 128],
    )


def host_weights(inp):
    L = inp["ada_w"].shape[0]
    f = lambda a: np.ascontiguousarray(np.asarray(a, dtype=np.float32))
    w = {k: f(inp[k]) for k in ["ada_w", "ada_b", "ln_g", "ln_b", "ffn1_w_in", "ffn1_w_out", "ffn2_w_in",
                                "ffn2_w_out", "mix_w_in", "mix_w_out", "hgrn_lb_logits", "mla_q_norm_g",
                                "mla_kv_norm_g", "mla_w_uq", "mla_w_ukv", "fox_b_f", "gmlp_ln_g", "gmlp_ln_b",
                                "gmlp_b_s"]}
    perm = np.concatenate([np.arange(16, 32), np.arange(0, 16)])
    kr = w["mix_w_in"][:, :, 1408:1440]
    wkr = np.zeros((L, D, 192), np.float32)
    wkr[:, :, 64:96] = kr
    wkr[:, :, 160:192] = kr[:, :, perm]
    w["wkr"] = wkr
    uq = w["mla_w_uq"].reshape(L, 256, 4, 96)
    uqp = np.zeros_like(uq)
    uqp[:, :, :, 64:96] = uq[:, :, :, 64:96][:, :, :, perm]
    w["mla_w_uqp"] = np.ascontiguousarray(uqp.reshape(L, 256, 384))
    w["hgrn_ng"] = np.ascontiguousarray(f(inp["hgrn_norm_g"]).reshape(L, 4, 64).transpose(0, 2, 1))
    w["gmlp_wsT"] = np.ascontiguousarray(f(inp["gmlp_w_s"]).transpose(0, 1, 3, 2))
    return w


class Builder:
    def __init__(self, S, L, stop=None, dumps=()):
        self.S = S
        self.L = L
        self.NT = S // 128
        self.NG = S // 512
        self.stop = stop
        self.dumps = set(dumps)
        self.dump_aps = {}

    def build(self):
        S, L = self.S, self.L
        nc = bass.Bass("TRN2", target_bir_lowering=False)
        self.nc = nc
        dr = {}
        dr["x"] = nc.dram_tensor("x", [2, S, D], F32, kind="ExternalInput").ap()
        dr["cT"] = nc.dram_tensor("cT", [128, 8, 2], F32, kind="ExternalInput").ap()
        for k, shp in weight_shapes(L).items():
            dr[k] = nc.dram_tensor(k, shp, F32, kind="ExternalInput").ap()
        for k, shp in CONST_SHAPES(S).items():
            dr[k] = nc.dram_tensor(k, shp, F32, kind="ExternalInput").ap()
        dr["out"] = nc.dram_tensor("out", [2, S, D], F32, kind="ExternalOutput").ap()
        dr["modd"] = nc.dram_tensor("modd", [2, L, 9 * D], F32, kind="Internal").ap()
        self.dr = dr
        with ExitStack() as es:
            self.fw = fw = FW(nc, es)
            self.es = es
            self.setup_global(es)
            self.prologue_mod()
            for s in range(2):
                self.run_sequence(s)
                if self.stop is not None and self.stop[0] == s:
                    break
            fw.barrier()
            print(f"[build] inst={fw.n_inst} waits={fw.n_wait} sems={fw.nsem} "
                  f"cnt={[(e.name, e.chan.cnt) for e in fw.engs]}", flush=True)
        return nc

    def V(self, emit, reads=(), writes=()):
        return self.fw.op(self.fw.dve, emit, reads, writes)

    def A(self, emit, reads=(), writes=()):
        return self.fw.op(self.fw.act, emit, reads, writes)

    def G(self, emit, reads=(), writes=()):
        return self.fw.op(self.fw.pool, emit, reads, writes)

    def MM(self, out, lhsT, rhs, start, stop, reads=(), writes=(), inc=None, skip=False):
        if inc is None:
            inc = bool(stop)
        kw = dict(skip_group_check=True) if skip else {}
        return self.fw.op(self.fw.pe, lambda h: h.matmul(out, lhsT=lhsT, rhs=rhs, start=start, stop=stop, **kw),
                          reads, writes, inc=inc)

    def TR(self, out, in_, reads=(), writes=(), inc=True):
        ident = self.ident
        n = in_.shape[0]
        return self.fw.op(self.fw.pe, lambda h: h.transpose(out, in_, ident[0:n, 0:n]),
                          list(reads) + [self.b_const], writes, inc=inc)

    def wdma(self, out, in_, chan, reads=(), writes=()):
        return self.fw.dma(self.fw.pool, self.fw.new_chan(chan.name + "_sw"), out, in_, reads, writes)

    def ldma(self, out, in_, chan, reads=(), writes=(), **kw):
        return self.fw.dma(self.fw.sp, chan, out, in_, reads, writes, **kw)

    def dump(self, name, ap, shape, reads):
        if name not in self.dumps or name in self.dump_aps:
            return
        d = self.nc.dram_tensor("dbg_" + name, list(shape), F32, kind="ExternalOutput").ap()
        self.dump_aps[name] = d
        ch = self.fw.new_chan("dbg")
        self.fw.dma(self.fw.sp, ch, d, ap, reads=reads)

    def setup_global(self, es):
        fw, dr, NT = self.fw, self.dr, self.NT
        self.x = fw.sb(es, "x", [128, NT, D], F32)
        self.bx = [Buf(f"x{t}") for t in range(NT)]
        self.hT = fw.sb(es, "hT", [128, 8, self.S], BF16)
        self.bhT = [Buf(f"hT{g}") for g in range(self.NG)]
        self.mv = fw.sb(es, "modv", [128, 5, D], F32)
        self.bmv = Buf("modv")
        self.ch_mv = fw.new_chan("modv")
        self.bab = Buf("modab")
        self.ch_ab = fw.new_chan("modab")
        self.ident = fw.sb(es, "ident", [128, 128], BF16)
        self.b_const = Buf("const")
        ch = fw.new_chan("const")
        self.wdma(self.ident[:], dr["ident"], ch, writes=[self.b_const])
        self.pbank = [(fw.psum(es, f"b{i}", [128, 512], F32), Buf(f"ps{i}")) for i in range(6)]
        self.ptr = [(fw.psum(es, f"t{i}", [128, 1024], BF16), Buf(f"pt{i}")) for i in range(2)]
        self.ringA = Ring(self.pbank[0:4])
        self.ringB = Ring(self.pbank[4:6])
        self.ringT = Ring(self.ptr)
        self.ch_x = [fw.new_chan(f"x{i}") for i in range(4)]
        self.ch_out = [fw.new_chan(f"o{i}") for i in range(4)]
        self.b_modd = Buf("modd")

    def prologue_mod(self):
        fw, dr, L = self.fw, self.dr, self.L
        with ExitStack() as es:
            cT = fw.sb(es, "cT", [128, 8, 2], F32)
            cact = fw.sb(es, "cact", [128, 8, 2], BF16)
            b_c = Buf()
            ch = fw.new_chan("c")
            self.ldma(cT[:], dr["cT"], ch, writes=[b_c])
            self.A(lambda h: h.activation(out=cact[:], in_=cT[:], func=AF.Silu), [b_c], [b_c])
            slots = [(fw.sb(es, f"aw{i}", [128, 8, 512], BF16), Buf(), fw.new_chan(f"aw{i}")) for i in range(2)]
            bslots = [(fw.sb(es, f"ab{i}", [2, 512], F32), Buf(), fw.new_chan(f"ab{i}")) for i in range(2)]
            rows = [(fw.sb(es, f"mr{i}", [2, 512], F32), Buf(), fw.new_chan(f"mr{i}")) for i in range(2)]
            it = 0
            for l in range(L):
                awv = dr["ada_w"][l].rearrange("(kc p) n -> p kc n", p=128)
                for cb in range(18):
                    w, bw, chw = slots[it % 2]
                    ab, bab, chb = bslots[it % 2]
                    mr, bmr, chr_ = rows[it % 2]
                    it += 1
                    self.wdma(w[:], awv[:, :, cb * 512:(cb + 1) * 512], chw, writes=[bw])
                    self.ldma(ab[:], dr["ada_b"][l:l + 1, cb * 512:(cb + 1) * 512].broadcast_to([2, 512]), chb,
                              writes=[bab])
                    ps, bps = self.ringA.next()
                    for kc in range(8):
                        self.MM(ps[0:2, :], cact[:, kc, :], w[:, kc, :], kc == 0, kc == 7,
                                reads=[b_c, bw], writes=[bps])
                    seg = (cb * 512) // 1024
                    self.V(lambda h: h.tensor_tensor(out=mr[:], in0=ps[0:2, :], in1=ab[:], op=ALU.add),
                           [bps, bab], [bmr])
                    if seg in (1, 4, 7, 5):
                        self.V(lambda h: h.tensor_scalar_add(out=mr[:], in0=mr[:], scalar1=1.0), [bmr], [bmr])
                    elif seg in (2, 8):
                        self.V(lambda h: h.tensor_scalar(out=mr[:], in0=mr[:], scalar1=1.0, scalar2=0.5,
                                                         op0=ALU.add, op1=ALU.mult), [bmr], [bmr])
                    self.ldma(dr["modd"][:, l, cb * 512:(cb + 1) * 512], mr[:], chr_, reads=[bmr],
                              writes=[self.b_modd])
            fw.barrier()

    def load_ab(self, s, l, j):
        dr = self.dr
        for i, sg in enumerate([3 * j + 1, 3 * j + 0]):
            self.ldma(self.mv[:, i, :], dr["modd"][s:s + 1, l, sg * D:(sg + 1) * D].broadcast_to([128, D]), self.ch_ab,
                      reads=[self.b_modd], writes=[self.bab])

    def load_modvec(self, s, l, j):
        dr = self.dr
        mv, b, ch = self.mv, self.bmv, self.ch_mv
        sg = 3 * j + 2
        self.ldma(mv[:, 2, :], dr["modd"][s:s + 1, l, sg * D:(sg + 1) * D].broadcast_to([128, D]), ch,
                  reads=[self.b_modd], writes=[b])
        self.ldma(mv[:, 3, :], dr["ln_g"][l, j:j + 1, :].broadcast_to([128, D]), ch, writes=[b])
        self.ldma(mv[:, 4, :], dr["ln_b"][l, j:j + 1, :].broadcast_to([128, D]), ch, writes=[b])

    def run_sequence(self, s):
        fw, dr, NT = self.fw, self.dr, self.NT
        xin = dr["x"][s].rearrange("(t p) d -> p t d", p=128)
        q = max(1, NT // 4)
        for i in range(0, NT, q):
            self.ldma(self.x[:, i:i + q, :], xin[:, i:i + q, :], self.ch_x[(i // q) % 4],
                      writes=self.bx[i:i + q])
        subs = [(l, j) for l in range(self.L) for j in range(3)]
        if self.stop is not None and self.stop[0] == s:
            subs = subs[:subs.index((self.stop[1], self.stop[2])) + 1]
        self.xout = dr["out"][s].rearrange("(t p) d -> p t d", p=128)
        self.load_ab(s, *subs[0])
        with ExitStack() as e0:
            self.build_hT(e0)
            fw.barrier()
        for k, (l, j) in enumerate(subs):
            self.load_modvec(s, l, j)
            self.has_next = k + 1 < len(subs)
            if self.has_next:
                self.load_ab(s, *subs[k + 1])
            if j == 1:
                self.mixer(s, l)
            else:
                self.ffn(s, l, j)

    def build_hT(self, es):
        fw, NT = self.fw, self.NT
        tmpf = [(fw.sb(es, f"hx{i}", [128, D], F32), Buf()) for i in range(2)]
        hb = [(fw.sb(es, f"hb{i}", [128, D], BF16), Buf()) for i in range(2)]
        mv, bmv = self.mv, self.bab
        for t in range(NT):
            tf, btf = tmpf[t % 2]
            hbt, bhb = hb[t % 2]
            self.V(lambda h: h.tensor_tensor(out=tf[:], in0=self.x[:, t, :], in1=mv[:, 0, :], op=ALU.mult),
                   [self.bx[t], bmv], [btf])
            self.G(lambda h: h.tensor_tensor(out=hbt[:], in0=tf[:], in1=mv[:, 1, :], op=ALU.add),
                   [btf, bmv], [bhb])
            self.hT_transpose(t, hbt, bhb)

    def hT_transpose(self, t, hbt, bhb):
        pt, bpt = self.ringT.next()
        for kc in range(8):
            self.TR(pt[:, kc * 128:(kc + 1) * 128], hbt[:, kc * 128:(kc + 1) * 128], [bhb], [bpt], inc=(kc == 7))
        self.A(lambda h: h.copy(out=self.hT[:, :, t * 128:(t + 1) * 128],
                                in_=pt[:].rearrange("p (k c) -> p k c", k=8)),
               [bpt], [self.bhT[t // 4]])

    def begin_tail(self, es):
        fw = self.fw
        self.t_st = [(fw.sb(es, f"lnst{i}", [128, 2, 6], F32), fw.sb(es, f"lnmv{i}", [128, 4], F32), Buf()) for i in range(2)]
        self.t_q = []
        if self.has_next:
            self.t_tf = [(fw.sb(es, f"thx{i}", [128, D], F32), Buf()) for i in range(1)]
            self.t_hb = [(fw.sb(es, f"thb{i}", [128, D], BF16), Buf()) for i in range(3)]

    def tail_tile(self, t):
        self.ln_tile(t, *self.t_st[t % 2])
        if self.has_next:
            tf, btf = self.t_tf[0]
            hbt, bhb = self.t_hb[t % 3]
            mv = self.mv
            self.V(lambda h: h.tensor_tensor(out=tf[:], in0=self.x[:, t, :], in1=mv[:, 0, :], op=ALU.mult),
                   [self.bx[t], self.bab], [btf])
            self.G(lambda h: h.tensor_tensor(out=hbt[:], in0=tf[:], in1=mv[:, 1, :], op=ALU.add),
                   [btf, self.bab], [bhb])
            self.t_q.append((t, hbt, bhb))
            if len(self.t_q) > 2:
                self.hT_transpose(*self.t_q.pop(0))
        else:
            self.ldma(self.xout[:, t, :], self.x[:, t, :], self.ch_out[t % 4], reads=[self.bx[t]])

    def end_tail(self):
        while self.t_q:
            self.hT_transpose(*self.t_q.pop(0))

    def resid_update(self, t, half, ps, bps, first, tmp, btmp):
        sl = slice(half * 512, (half + 1) * 512)
        mv, bmv = self.mv, self.bmv
        self.V(lambda h: h.tensor_tensor(out=tmp[:], in0=ps[:], in1=mv[:, 2, sl], op=ALU.mult),
               [bps, bmv], [btmp])
        xs = self.x[:, t, sl]
        if first:
            self.V(lambda h: h.scalar_tensor_tensor(out=xs, in0=xs, scalar=ALPHA, in1=tmp[:],
                                                    op0=ALU.mult, op1=ALU.add), [btmp, self.bx[t]], [self.bx[t]])
        else:
            self.G(lambda h: h.tensor_tensor(out=xs, in0=xs, in1=tmp[:], op=ALU.add),
                   [btmp, self.bx[t]], [self.bx[t]])

    def ln_tile(self, t, s6, m4, bs):
        mv, bmv = self.mv, self.bmv
        if True:
            xt = self.x[:, t, :]
            bxt = self.bx[t]
            self.V(lambda h: h.bn_stats(out=s6[:, 0, :], in_=self.x[:, t, 0:512]), [bxt], [bs])
            self.V(lambda h: h.bn_stats(out=s6[:, 1, :], in_=self.x[:, t, 512:1024]), [bxt], [bs])
            self.V(lambda h: h.bn_aggr(out=m4[:, 0:2], in_=s6[:]), [bs], [bs])
            self.V(lambda h: h.tensor_scalar_add(out=m4[:, 2:3], in0=m4[:, 1:2], scalar1=LN_EPS), [bs], [bs])
            self.A(lambda h: h.sqrt(out=m4[:, 2:3], in_=m4[:, 2:3]), [bs], [bs])
            self.V(lambda h: h.reciprocal(out=m4[:, 2:3], in_=m4[:, 2:3]), [bs], [bs])
            self.V(lambda h: h.scalar_tensor_tensor(out=m4[:, 3:4], in0=m4[:, 0:1], scalar=-1.0, in1=m4[:, 2:3],
                                                    op0=ALU.mult, op1=ALU.mult), [bs], [bs])
            self.A(lambda h: h.activation(out=xt, in_=xt, func=AF.Identity, bias=m4[:, 3:4], scale=m4[:, 2:3]),
                   [bs, bxt], [bxt])
            self.V(lambda h: h.tensor_tensor(out=xt, in0=xt, in1=mv[:, 3, :], op=ALU.mult), [bxt, bmv], [bxt])
            self.G(lambda h: h.tensor_tensor(out=xt, in0=xt, in1=mv[:, 4, :], op=ALU.add), [bxt, bmv], [bxt])

    def ffn(self, s, l, j):
        fw, dr, NT, NG, S = self.fw, self.dr, self.NT, self.NG, self.S
        w_in = dr["ffn1_w_in" if j == 0 else "ffn2_w_in"][l].rearrange("(kc p) n -> p kc n", p=128)
        w_out = dr["ffn1_w_out" if j == 0 else "ffn2_w_out"][l].rearrange("(j p) n -> p j n", p=128)
        with ExitStack() as es:
            self.begin_tail(es)
            actT = fw.sb(es, "actT", [128, 6, S], BF16)
            bact = [Buf() for _ in range(NG)]
            wo = [(fw.sb(es, f"wo{i}", [128, 6, D], BF16), Buf(), fw.new_chan(f"wo{i}")) for i in range(2)]
            wi = [(fw.sb(es, f"wi{i}", [128, 8, 512], BF16), Buf(), fw.new_chan(f"wi{i}")) for i in range(2)]
            sgs = [(fw.sb(es, f"sg{i}", [128, 512], F32), Buf()) for i in range(2)]
            tmps = [(fw.sb(es, f"ft{i}", [128, 512], F32), Buf()) for i in range(2)]
            parts = [(0, 6), (6, 12), (12, 18), (18, 22)]
            iw = 0
            k = 0
            for pi, (j0, j1) in enumerate(parts):
                wot, bwo, chwo = wo[pi % 2]
                self.wdma(wot[:, 0:j1 - j0, :], w_out[:, j0:j1, :], chwo, writes=[bwo])
                for jj in range(j0, j1, 2):
                    wt, bw, chw = wi[iw % 2]
                    iw += 1
                    self.wdma(wt[:, :, 0:256], w_in[:, :, jj * 128:jj * 128 + 256], chw, writes=[bw])
                    self.wdma(wt[:, :, 256:512], w_in[:, :, DFF + jj * 128:DFF + jj * 128 + 256], chw, writes=[bw])
                    for c in range(2):
                        jl = jj + c - j0
                        for g in range(NG):
                            pg, bpg = self.ringA.next()
                            pu, bpu = self.ringA.next()
                            tok = slice(g * 512, (g + 1) * 512)
                            for kc in range(8):
                                self.MM(pg[:], wt[:, kc, c * 128:(c + 1) * 128], self.hT[:, kc, tok], kc == 0, kc == 7,
                                        [bw, self.bhT[g]], [bpg])
                            for kc in range(8):
                                self.MM(pu[:], wt[:, kc, 256 + c * 128:256 + (c + 1) * 128], self.hT[:, kc, tok],
                                        kc == 0, kc == 7, [bw, self.bhT[g]], [bpu])
                            sg, bsg = sgs[k % 2]
                            k += 1
                            self.A(lambda h: h.activation(out=sg[:], in_=pg[:], func=AF.Silu), [bpg], [bsg])
                            self.V(lambda h: h.tensor_tensor(out=actT[:, jl, tok], in0=sg[:], in1=pu[:], op=ALU.mult),
                                   [bsg, bpu], [bact[g]])
                nj = j1 - j0
                for t in range(NT):
                    for half in range(2):
                        ps, bps = self.ringB.next()
                        for jl in range(nj):
                            self.MM(ps[:], actT[:, jl, t * 128:(t + 1) * 128], wot[:, jl, half * 512:(half + 1) * 512],
                                    jl == 0, jl == nj - 1, [bact[t // 4], bwo], [bps])
                        tmp, btmp = tmps[(2 * t + half) % 2]
                        self.resid_update(t, half, ps, bps, pi == 0, tmp, btmp)
                    if pi == len(parts) - 1:
                        self.tail_tile(t)
            self.end_tail()
            fw.barrier()
        self.dump(f"x_{s}_{l}_{j}", self.x[:], [128, NT, D], self.bx)

    def mixer(self, s, l):
        fw = self.fw
        with ExitStack() as es:
            first = True
            active = [m for m in range(4) if m not in self.skip_mixers]
            fns = [self.mix_hgrn, self.mix_mla, self.mix_fox, self.mix_gmlp]
            for m in active:
                self.tail_on = (m == active[-1])
                with ExitStack() as es2:
                    fns[m](es2, s, l, m, first)
                    fw.barrier()
                first = False
        self.dump(f"x_{s}_{l}_1", self.x[:], [128, self.NT, D], self.bx)

    skip_mixers = ()

    def out_proj(self, es, l, m, mixT, bmix, first):
        fw, dr, NT = self.fw, self.dr, self.NT
        wov = dr["mix_w_out"][l, m * 256:(m + 1) * 256, :].rearrange("(s p) n -> p s n", p=64)
        wo = fw.sb(es, "mwo", [64, 4, D], BF16)
        bwo = Buf()
        ch = fw.new_chan("mwo")
        self.wdma(wo[:], wov, ch, writes=[bwo])
        tmps = [(fw.sb(es, f"ot{i}", [128, 512], F32), Buf()) for i in range(2)]
        if self.tail_on:
            self.begin_tail(es)
        for t in range(NT):
            for half in range(2):
                ps, bps = self.ringB.next()
                for sl in range(4):
                    self.MM(ps[:], mixT[0:64, sl, t * 128:(t + 1) * 128], wo[0:64, sl, half * 512:(half + 1) * 512],
                            sl == 0, sl == 3, [bmix, bwo], [bps])
                tmp, btmp = tmps[(2 * t + half) % 2]
                self.resid_update(t, half, ps, bps, first, tmp, btmp)
            if self.tail_on:
                self.tail_tile(t)
        if self.tail_on:
            self.end_tail()

    def attention(self, es, QT, KT, bqk, dk, Vaug, bv, scale, mixT, bmix):
        fw, dr, NG = self.fw, self.dr, self.NG
        negm = fw.sb(es, "negm", [128, 4, 512], BF16)
        sel = fw.sb(es, "sel65", [65, 64], F32)
        bc = Buf()
        ch = fw.new_chan("attc")
        self.wdma(negm[:], dr["negmask"], ch, writes=[bc])
        self.ldma(sel[:], dr["sel65"], ch, writes=[bc])
        tmpS = [(fw.sb(es, f"ts{i}", [128, 512], F32), Buf()) for i in range(2)]
        pTs = [(fw.sb(es, f"pT{i}", [128, 512], BF16), Buf()) for i in range(3)]
        osb = [(fw.sb(es, f"os{i}", [65, 512], F32), Buf()) for i in range(2)]
        rec = [(fw.sb(es, f"rc{i}", [64, 512], F32), Buf()) for i in range(2)]
        it = 0
        ip = 0
        for hd in range(4):
            for qg in range(NG):
                qs = slice(qg * 512, (qg + 1) * 512)
                nkb = 4 * (qg + 1)
                o_ps, bo = self.ringB.next()
                pend = []

                def qk(kb):
                    nonlocal ip
                    s_ps, bs = self.ringA.next()
                    self.MM(s_ps[:], KT[hd][0:dk, kb * 128:(kb + 1) * 128], QT[hd][0:dk, qs], True, True,
                            [bqk], [bs])
                    pT, bp = pTs[ip % 3]
                    ip += 1
                    r = kb - 4 * qg
                    if r >= 0:
                        tS, bt = tmpS[kb % 2]
                        self.V(lambda h: h.tensor_tensor(out=tS[:], in0=s_ps[:], in1=negm[:, r, :], op=ALU.add),
                               [bs, bc], [bt])
                        self.A(lambda h: h.activation(out=pT[:], in_=tS[:], func=AF.Exp, scale=scale), [bt], [bp])
                    else:
                        self.A(lambda h: h.activation(out=pT[:], in_=s_ps[:], func=AF.Exp, scale=scale), [bs], [bp])
                    return pT, bp

                def pv(kb, pT, bp):
                    self.MM(o_ps[0:65, :], Vaug[:, kb, hd, :], pT[:], kb == 0, kb == nkb - 1, [bv, bp], [bo])

                LA = 2
                for kb in range(nkb + LA):
                    if kb < nkb:
                        pend.append((kb,) + qk(kb))
                    if kb >= LA:
                        a = pend.pop(0)
                        pv(*a)
                o_sb, bos = osb[it % 2]
                rc, brc = rec[it % 2]
                it += 1
                self.A(lambda h: h.copy(out=o_sb[:], in_=o_ps[0:65, :]), [bo], [bos])
                d_ps, bd = self.ringB.next()
                self.A(lambda h: h.activation(out=o_sb[64:65, :], in_=o_sb[64:65, :], func=AF.Ln), [bos], [bos])
                self.A(lambda h: h.activation(out=o_sb[64:65, :], in_=o_sb[64:65, :], func=AF.Exp, scale=-1.0), [bos], [bos])
                self.MM(d_ps[0:64, :], sel[:], o_sb[:], True, True, [bc, bos], [bd])
                self.V(lambda h: h.tensor_tensor(out=mixT[0:64, hd, qs], in0=o_sb[0:64, :], in1=d_ps[0:64, :],
                                                 op=ALU.mult), [bos, bd], [bmix])

    def mix_mla(self, es, s, l, m, first):
        fw, dr, NT, NG, S = self.fw, self.dr, self.NT, self.NG, self.S
        QT = [fw.sb(es, f"QT{h}", [96, S], BF16) for h in range(4)]
        KT = [fw.sb(es, f"KT{h}", [96, S], BF16) for h in range(4)]
        bqk = Buf()
        Vaug = fw.sb(es, "Vaug", [128, NT, 4, 65], BF16)
        bv = Buf()
        self.G(lambda h: h.memset(Vaug[:], 1.0), [], [bv])
        with ExitStack() as e1:
            wv = dr["mix_w_in"][l].rearrange("(kc p) n -> p kc n", p=128)
            wb = fw.sb(e1, "wB", [128, 8, 384], BF16)
            wkr = fw.sb(e1, "wkr", [128, 8, 192], BF16)
            wuq = fw.sb(e1, "wuq", [128, 2, 384], BF16)
            wuqp = fw.sb(e1, "wuqp", [128, 2, 384], BF16)
            wukv = fw.sb(e1, "wukv", [128, 512], BF16)
            gq = fw.sb(e1, "gq", [128, 256], F32)
            gkv = fw.sb(e1, "gkv", [128, 128], F32)
            bw = Buf()
            ch = fw.new_chan("mlaw")
            self.wdma(wb[:], wv[:, :, 1024:1408], ch, writes=[bw])
            self.wdma(wkr[:], dr["wkr"][l].rearrange("(kc p) n -> p kc n", p=128), ch, writes=[bw])
            self.wdma(wuq[:], dr["mla_w_uq"][l].rearrange("(kc p) n -> p kc n", p=128), ch, writes=[bw])
            self.wdma(wuqp[:], dr["mla_w_uqp"][l].rearrange("(kc p) n -> p kc n", p=128), ch, writes=[bw])
            self.wdma(wukv[:], dr["mla_w_ukv"][l], ch, writes=[bw])
            self.ldma(gq[:], dr["mla_q_norm_g"][l:l + 1, :].broadcast_to([128, 256]), ch, writes=[bw])
            self.ldma(gkv[:], dr["mla_kv_norm_g"][l:l + 1, :].broadcast_to([128, 128]), ch, writes=[bw])
            cqnT = fw.sb(e1, "cqnT", [128, 2, S], BF16)
            ckvT = fw.sb(e1, "ckvT", [128, S], BF16)
            bcT = [Buf() for _ in range(NG)]
            rc = [(fw.sb(e1, f"rC{i}", [96, 512], F32), fw.sb(e1, f"rS{i}", [96, 512], F32), Buf(),
                   fw.new_chan(f"rope{i}")) for i in range(2)]
            st = [(fw.sb(e1, f"mst{i}", [128, 2, 6], F32), fw.sb(e1, f"mmv{i}", [128, 8], F32), Buf()) for i in range(2)]
            cn = [(fw.sb(e1, f"cn{i}", [128, 384], BF16), Buf()) for i in range(2)]
            for t in range(NT):
                tok = slice(t * 128, (t + 1) * 128)
                ps, bps = self.ringA.next()
                for kc in range(8):
                    self.MM(ps[:, 0:384], self.hT[:, kc, tok], wb[:, kc, :], kc == 0, kc == 7, [self.bhT[t // 4], bw], [bps])
                s6, m8, bs = st[t % 2]
                cnt, bcn = cn[t % 2]
                self.V(lambda h: h.bn_stats(out=s6[:, 0, :], in_=ps[:, 0:256]), [bps], [bs])
                self.V(lambda h: h.bn_stats(out=s6[:, 1, :], in_=ps[:, 256:384]), [bps], [bs])
                for i in range(2):
                    self.V(lambda h: h.bn_aggr(out=m8[:, 4 * i:4 * i + 2], in_=s6[:, i:i + 1, :]), [bs], [bs])
                    self.V(lambda h: h.scalar_tensor_tensor(out=m8[:, 4 * i + 2:4 * i + 3], in0=m8[:, 4 * i:4 * i + 1],
                                                            scalar=m8[:, 4 * i:4 * i + 1], in1=m8[:, 4 * i + 1:4 * i + 2],
                                                            op0=ALU.mult, op1=ALU.add), [bs], [bs])
                    self.V(lambda h: h.tensor_scalar_add(out=m8[:, 4 * i + 2:4 * i + 3], in0=m8[:, 4 * i + 2:4 * i + 3],
                                                         scalar1=RMS_EPS), [bs], [bs])
                    self.A(lambda h: h.sqrt(out=m8[:, 4 * i + 2:4 * i + 3], in_=m8[:, 4 * i + 2:4 * i + 3]), [bs], [bs])
                    self.V(lambda h: h.reciprocal(out=m8[:, 4 * i + 3:4 * i + 4], in_=m8[:, 4 * i + 2:4 * i + 3]), [bs], [bs])
                self.V(lambda h: h.scalar_tensor_tensor(out=cnt[:, 0:256], in0=ps[:, 0:256], scalar=m8[:, 3:4], in1=gq[:],
                                                        op0=ALU.mult, op1=ALU.mult), [bps, bs, bw], [bcn])
                self.V(lambda h: h.scalar_tensor_tensor(out=cnt[:, 256:384], in0=ps[:, 256:384], scalar=m8[:, 7:8],
                                                        in1=gkv[:], op0=ALU.mult, op1=ALU.mult), [bps, bs, bw], [bcn])
                pt, bpt = self.ringT.next()
                for c in range(3):
                    self.TR(pt[:, c * 128:(c + 1) * 128], cnt[:, c * 128:(c + 1) * 128], [bcn], [bpt], inc=(c == 2))
                self.A(lambda h: h.copy(out=cqnT[:, :, tok], in_=pt[:, 0:256].rearrange("p (k c) -> p k c", k=2)),
                       [bpt], [bcT[t // 4]])
                self.A(lambda h: h.copy(out=ckvT[:, tok], in_=pt[:, 256:384]), [bpt], [bcT[t // 4]])
                pv_, bpv = self.ringB.next()
                self.MM(pv_[:, 0:256].rearrange("p (h x) -> p h x", h=4), ckvT[:, tok],
                        wukv[:].rearrange("p (h x) -> p h x", h=4)[:, :, 64:128], True, True,
                        [bcT[t // 4], bw], [bpv])
                self.A(lambda h: h.copy(out=Vaug[:, t, :, 0:64], in_=pv_[:, 0:256].rearrange("p (h x) -> p h x", h=4)),
                       [bpv], [bv])
            t1s = [(fw.sb(e1, f"r1{i}", [96, 512], F32), Buf()) for i in range(2)]
            t2s = [(fw.sb(e1, f"r2{i}", [96, 512], F32), Buf()) for i in range(2)]
            k = 0
            for g in range(NG):
                tok = slice(g * 512, (g + 1) * 512)
                rC, rS, brp, chr_ = rc[g % 2]
                self.ldma(rC[:], dr["ropeC"][:, tok], chr_, writes=[brp])
                self.ldma(rS[:], dr["ropeS"][:, tok], chr_, writes=[brp])
                for hd in range(4):
                    pq, bpq = self.ringA.next()
                    pp, bpp = self.ringA.next()
                    for kc in range(2):
                        self.MM(pq[0:96, :], wuq[:, kc, hd * 96:(hd + 1) * 96], cqnT[:, kc, tok], kc == 0, kc == 1,
                                [bw, bcT[g]], [bpq])
                    for kc in range(2):
                        self.MM(pp[0:96, :], wuqp[:, kc, hd * 96:(hd + 1) * 96], cqnT[:, kc, tok], kc == 0, kc == 1,
                                [bw, bcT[g]], [bpp])
                    t1, b1 = t1s[k % 2]
                    t2, b2 = t2s[k % 2]
                    k += 1
                    self.V(lambda h: h.tensor_tensor(out=t1[:], in0=pq[0:96, :], in1=rC[:], op=ALU.mult), [bpq, brp], [b1])
                    self.V(lambda h: h.tensor_tensor(out=t2[:], in0=pp[0:96, :], in1=rS[:], op=ALU.mult), [bpp, brp], [b2])
                    self.G(lambda h: h.tensor_tensor(out=QT[hd][:, tok], in0=t1[:], in1=t2[:], op=ALU.add), [b1, b2], [bqk])
                    pk, bpk = self.ringA.next()
                    self.MM(pk[0:64, :], wukv[:, hd * 128:hd * 128 + 64], ckvT[:, tok], True, True, [bw, bcT[g]], [bpk])
                    self.A(lambda h: h.copy(out=KT[hd][0:64, tok], in_=pk[0:64, :]), [bpk], [bqk])
                pq, bpq = self.ringA.next()
                pp, bpp = self.ringA.next()
                for kc in range(8):
                    self.MM(pq[0:96, :], wkr[:, kc, 0:96], self.hT[:, kc, tok], kc == 0, kc == 7, [bw, self.bhT[g]], [bpq])
                for kc in range(8):
                    self.MM(pp[0:96, :], wkr[:, kc, 96:192], self.hT[:, kc, tok], kc == 0, kc == 7, [bw, self.bhT[g]], [bpp])
                t1, b1 = t1s[k % 2]
                t2, b2 = t2s[k % 2]
                k += 1
                self.V(lambda h: h.tensor_tensor(out=t1[64:96, :], in0=pq[64:96, :], in1=rC[64:96, :], op=ALU.mult),
                       [bpq, brp], [b1])
                self.V(lambda h: h.tensor_tensor(out=t2[64:96, :], in0=pp[64:96, :], in1=rS[64:96, :], op=ALU.mult),
                       [bpp, brp], [b2])
                for hd in range(4):
                    self.G(lambda h: h.tensor_tensor(out=KT[hd][64:96, tok], in0=t1[64:96, :], in1=t2[64:96, :], op=ALU.add),
                           [b1, b2], [bqk])
            fw.barrier()
        if "mla_q" in self.dumps:
            self.dump_bf16(es, "mla_q", QT[0][:], [96, S], [bqk])
            self.dump_bf16(es, "mla_k", KT[0][:], [96, S], [bqk])
        with ExitStack() as e2:
            mixT = fw.sb(e2, "mixT", [64, 4, S], BF16)
            bmix = Buf()
            self.attention(e2, QT, KT, bqk, 96, Vaug, bv, float(96 ** -0.5), mixT, bmix)
            if "mla_o" in self.dumps:
                self.dump_bf16(e2, "mla_o", mixT[:], [64, 4, S], [bmix])
            self.out_proj(e2, l, m, mixT, bmix, first)
            fw.barrier()

    def dump_bf16(self, es, name, ap, shape, reads):
        if name not in self.dumps or name in self.dump_aps:
            return
        t = self.fw.sb(es, "dmp" + name, shape, F32)
        b = Buf()
        self.V(lambda h: h.tensor_copy(out=t[:], in_=ap), reads, [b])
        self.dump(name, t[:], shape, [b])
        self.fw.barrier()

    def mix_fox(self, es, s, l, m, first):
        fw, dr, NT, NG, S = self.fw, self.dr, self.NT, self.NG, self.S
        QT = [fw.sb(es, f"fQT{h}", [68, S], BF16) for h in range(4)]
        KT = [fw.sb(es, f"fKT{h}", [68, S], BF16) for h in range(4)]
        bqk = Buf()
        Vaug = fw.sb(es, "fVaug", [128, NT, 4, 65], BF16)
        bv = Buf()
        self.G(lambda h: h.memset(Vaug[:], 1.0), [], [bv])
        for hd in range(4):
            self.G(lambda h: h.memset(QT[hd][64:68, :], 1.0), [], [bqk])
            self.G(lambda h: h.memset(KT[hd][64:68, :], 1.0), [], [bqk])
        with ExitStack() as e1:
            wv = dr["mix_w_in"][l].rearrange("(kc p) n -> p kc n", p=128)
            wc = fw.sb(e1, "wC", [128, 8, 772], BF16)
            bw = Buf()
            ch = fw.new_chan("foxw")
            self.wdma(wc[:], wv[:, :, 1440:2212], ch, writes=[bw])
            nbf = fw.sb(e1, "nbf", [4, 1], F32)
            self.ldma(nbf[:], dr["fox_b_f"][l:l + 1, :].rearrange("o h -> h o"), ch, writes=[bw])
            self.V(lambda h: h.tensor_scalar_mul(out=nbf[:], in0=nbf[:], scalar1=-1.0), [bw], [bw])
            ones4 = fw.sb(e1, "ones4", [4, 512], F32)
            carry = fw.sb(e1, "carry", [4, 1], F32)
            bcar = Buf()
            self.G(lambda h: h.memset(ones4[:], 1.0), [], [bcar])
            self.G(lambda h: h.memset(carry[:], 0.0), [bcar], [bcar])
            ch_row = [fw.new_chan(f"frow{i}") for i in range(2)]
            for t in range(NT):
                tok = slice(t * 128, (t + 1) * 128)
                pv_, bpv = self.ringB.next()
                for kc in range(8):
                    self.MM(pv_[:, 0:256], self.hT[:, kc, tok], wc[:, kc, 512:768], kc == 0, kc == 7,
                            [self.bhT[t // 4], bw], [bpv])
                self.A(lambda h: h.copy(out=Vaug[:, t, :, 0:64], in_=pv_[:, 0:256].rearrange("p (h x) -> p h x", h=4)),
                       [bpv], [bv])
            ft = [dict(e=fw.sb(e1, f"fe{i}", [4, 512], F32), fn=fw.sb(e1, f"ffn{i}", [4, 512], F32),
                       hi=fw.sb(e1, f"fhi{i}", [4, 512], BF16), hif=fw.sb(e1, f"fhf{i}", [4, 512], F32),
                       lo=fw.sb(e1, f"flo{i}", [4, 512], BF16), nhi=fw.sb(e1, f"fnh{i}", [4, 512], BF16),
                       nlo=fw.sb(e1, f"fnl{i}", [4, 512], BF16), b=Buf()) for i in range(2)]
            for g in range(NG):
                tok = slice(g * 512, (g + 1) * 512)
                for hd in range(4):
                    for which, dst in ((0, QT), (1, KT)):
                        pq, bpq = self.ringA.next()
                        for kc in range(8):
                            self.MM(pq[0:64, :], wc[:, kc, which * 256 + hd * 64:which * 256 + (hd + 1) * 64],
                                    self.hT[:, kc, tok], kc == 0, kc == 7, [bw, self.bhT[g]], [bpq])
                        self.A(lambda h: h.copy(out=dst[hd][0:64, tok], in_=pq[0:64, :]), [bpq], [bqk])
                pf, bpf = self.ringA.next()
                for kc in range(8):
                    self.MM(pf[0:4, :], wc[:, kc, 768:772], self.hT[:, kc, tok], kc == 0, kc == 7, [bw, self.bhT[g]], [bpf])
                f = ft[g % 2]
                bf_ = f["b"]
                self.A(lambda h: h.activation(out=f["e"][:], in_=pf[0:4, :], func=AF.Exp, bias=nbf[:], scale=-1.0),
                       [bpf, bw], [bf_])
                self.A(lambda h: h.activation(out=f["e"][:], in_=f["e"][:], func=AF.Ln, bias=1.0, scale=1.0), [bf_], [bf_])
                self.V(lambda h: h.tensor_tensor_scan(out=f["fn"][:], data0=ones4[:], data1=f["e"][:], initial=carry[:],
                                                      op0=ALU.mult, op1=ALU.add), [bf_, bcar], [bf_])
                self.V(lambda h: h.tensor_copy(out=carry[:], in_=f["fn"][:, 511:512]), [bf_], [bcar])
                self.V(lambda h: h.tensor_scalar_mul(out=f["fn"][:], in0=f["fn"][:], scalar1=8.0), [bf_], [bf_])
                self.V(lambda h: h.tensor_copy(out=f["hi"][:], in_=f["fn"][:]), [bf_], [bf_])
                self.V(lambda h: h.tensor_copy(out=f["hif"][:], in_=f["hi"][:]), [bf_], [bf_])
                self.V(lambda h: h.tensor_tensor(out=f["lo"][:], in0=f["fn"][:], in1=f["hif"][:], op=ALU.subtract),
                       [bf_], [bf_])
                self.V(lambda h: h.tensor_scalar_mul(out=f["nhi"][:], in0=f["hi"][:], scalar1=-1.0), [bf_], [bf_])
                self.V(lambda h: h.tensor_scalar_mul(out=f["nlo"][:], in0=f["lo"][:], scalar1=-1.0), [bf_], [bf_])
                chr_ = ch_row[g % 2]
                for hd in range(4):
                    self.ldma(QT[hd][64:65, tok], f["nhi"][hd:hd + 1, :], chr_, reads=[bf_], writes=[bqk])
                    self.ldma(QT[hd][65:66, tok], f["nlo"][hd:hd + 1, :], chr_, reads=[bf_], writes=[bqk])
                    self.ldma(KT[hd][66:67, tok], f["hi"][hd:hd + 1, :], chr_, reads=[bf_], writes=[bqk])
                    self.ldma(KT[hd][67:68, tok], f["lo"][hd:hd + 1, :], chr_, reads=[bf_], writes=[bqk])
            fw.barrier()
        with ExitStack() as e2:
            mixT = fw.sb(e2, "fmixT", [64, 4, S], BF16)
            bmix = Buf()
            self.attention(e2, QT, KT, bqk, 68, Vaug, bv, 0.125, mixT, bmix)
            if "fox_o" in self.dumps:
                self.dump_bf16(e2, "fox_o", mixT[:], [64, 4, S], [bmix])
            self.out_proj(e2, l, m, mixT, bmix, first)
            fw.barrier()

    def mix_gmlp(self, es, s, l, m, first):
        fw, dr, NT, NG, S = self.fw, self.dr, self.NT, self.NG, self.S
        wv = dr["mix_w_in"][l].rearrange("(kc p) n -> p kc n", p=128)
        wd = fw.sb(es, "wD", [128, 8, 512], BF16)
        bw = Buf()
        ch = fw.new_chan("gmw")
        self.wdma(wd[:], wv[:, :, 2212:2724], ch, writes=[bw])
        wsf = fw.sb(es, "wsf", [128, 4, 128], F32)
        caus = fw.sb(es, "caus", [128, 128], F32)
        wsm = fw.sb(es, "wsm", [128, 4, 128], BF16)
        bsb = fw.sb(es, "bsb", [64, 4, 128], F32)
        lng = fw.sb(es, "glng", [128, 256], F32)
        lnb = fw.sb(es, "glnb", [128, 256], F32)
        self.ldma(wsf[:], dr["gmlp_wsT"][l].rearrange("g s t -> s g t"), ch, writes=[bw])
        self.ldma(caus[:], dr["caus"], ch, writes=[bw])
        self.ldma(bsb[:], dr["gmlp_b_s"][l:l + 1].broadcast_to([64, 4, 128]), ch, writes=[bw])
        self.ldma(lng[:], dr["gmlp_ln_g"][l:l + 1, :].broadcast_to([128, 256]), ch, writes=[bw])
        self.ldma(lnb[:], dr["gmlp_ln_b"][l:l + 1, :].broadcast_to([128, 256]), ch, writes=[bw])
        self.V(lambda h: h.tensor_tensor(out=wsm[:], in0=wsf[:], in1=caus[:].unsqueeze(1).broadcast_to([128, 4, 128]),
                                         op=ALU.mult), [bw], [bw])
        mixT = fw.sb(es, "gmixT", [64, 4, S], BF16)
        bmix = Buf()
        gv = [(fw.sb(es, f"gv{i}", [128, 256], F32), Buf()) for i in range(2)]
        vl = [(fw.sb(es, f"vl{i}", [128, 256], BF16), Buf()) for i in range(2)]
        st = [(fw.sb(es, f"gst{i}", [128, 6], F32), fw.sb(es, f"gmv{i}", [128, 4], F32), Buf()) for i in range(2)]
        gu = [(fw.sb(es, f"gu{i}", [64, 512], F32), Buf()) for i in range(2)]
        t1s = [(fw.sb(es, f"gt{i}", [64, 512], F32), Buf()) for i in range(2)]
        for t in range(NT):
            tok = slice(t * 128, (t + 1) * 128)
            bh = self.bhT[t // 4]
            pv_, bpv = self.ringA.next()
            for kc in range(8):
                self.MM(pv_[:, 0:256], self.hT[:, kc, tok], wd[:, kc, 256:512], kc == 0, kc == 7, [bh, bw], [bpv])
            g_, bg = gv[t % 2]
            v_, bvl = vl[t % 2]
            s6, m4, bs = st[t % 2]
            self.A(lambda h: h.activation(out=g_[:], in_=pv_[:, 0:256], func=AF.Gelu_apprx_tanh), [bpv], [bg])
            self.V(lambda h: h.bn_stats(out=s6[:], in_=g_[:]), [bg], [bs])
            self.V(lambda h: h.bn_aggr(out=m4[:, 0:2], in_=s6[:]), [bs], [bs])
            self.V(lambda h: h.tensor_scalar_add(out=m4[:, 2:3], in0=m4[:, 1:2], scalar1=LN_EPS), [bs], [bs])
            self.A(lambda h: h.sqrt(out=m4[:, 2:3], in_=m4[:, 2:3]), [bs], [bs])
            self.V(lambda h: h.reciprocal(out=m4[:, 2:3], in_=m4[:, 2:3]), [bs], [bs])
            self.V(lambda h: h.scalar_tensor_tensor(out=m4[:, 3:4], in0=m4[:, 0:1], scalar=-1.0, in1=m4[:, 2:3],
                                                    op0=ALU.mult, op1=ALU.mult), [bs], [bs])
            self.A(lambda h: h.activation(out=g_[:], in_=g_[:], func=AF.Identity, bias=m4[:, 3:4], scale=m4[:, 2:3]),
                   [bs, bg], [bg])
            self.V(lambda h: h.tensor_tensor(out=g_[:], in0=g_[:], in1=lng[:], op=ALU.mult), [bg, bw], [bg])
            self.G(lambda h: h.tensor_tensor(out=v_[:], in0=g_[:], in1=lnb[:], op=ALU.add), [bg, bw], [bvl])
            pm, bpm = self.ringA.next()
            for g in range(4):
                self.MM(pm[0:64, g * 128:(g + 1) * 128], v_[:, g * 64:(g + 1) * 64], wsm[:, g, :], g == 0, True,
                        [bvl, bw], [bpm], inc=(g == 3), skip=True)
            pu, bpu = self.ringA.next()
            for g in range(4):
                for kc in range(8):
                    self.MM(pu[0:64, g * 128:(g + 1) * 128], wd[:, kc, g * 64:(g + 1) * 64], self.hT[:, kc, tok],
                            kc == 0, kc == 7, [bh, bw], [bpu], inc=(g == 3 and kc == 7), skip=True)
            u_, bu = gu[t % 2]
            t1, b1 = t1s[t % 2]
            self.A(lambda h: h.activation(out=u_[:], in_=pu[0:64, :], func=AF.Gelu_apprx_tanh), [bpu], [bu])
            self.V(lambda h: h.tensor_tensor(out=t1[:], in0=pm[0:64, :], in1=bsb[:].rearrange("p g t -> p (g t)"),
                                             op=ALU.add), [bpm, bw], [b1])
            self.V(lambda h: h.tensor_tensor(out=mixT[0:64, :, tok], in0=t1[:].rearrange("p (g t) -> p g t", g=4),
                                             in1=u_[:].rearrange("p (g t) -> p g t", g=4), op=ALU.mult),
                   [b1, bu], [bmix])
        if "gmlp_o" in self.dumps:
            self.dump_bf16(es, "gmlp_o", mixT[:], [64, 4, S], [bmix])
        self.out_proj(es, l, m, mixT, bmix, first)

    def mix_hgrn(self, es, s, l, m, first):
        fw, dr, NT, NG, S = self.fw, self.dr, self.NT, self.NG, self.S
        wv = dr["mix_w_in"][l].rearrange("(kc p) n -> p kc n", p=128)
        wa = fw.sb(es, "wA", [128, 8, 1024], BF16)
        bw = Buf()
        ch = fw.new_chan("hgw")
        self.wdma(wa[:], wv[:, :, 0:1024], ch, writes=[bw])
        tri = fw.sb(es, "tri", [128, 128], F32)
        blk = fw.sb(es, "blk", [128, 128], F32)
        csel = fw.sb(es, "csel", [128, 8], F32)
        cselb = fw.sb(es, "cselb", [128, 8], BF16)
        ones64 = fw.sb(es, "ones64", [64, 64], F32)
        ng = fw.sb(es, "ng", [64, 4], F32)
        lb = fw.sb(es, "lb", [128, 256], F32)
        oml = fw.sb(es, "oml", [128, 256], F32)
        self.ldma(tri[:], dr["tri"], ch, writes=[bw])
        self.ldma(blk[:], dr["blk"], ch, writes=[bw])
        self.ldma(csel[:], dr["csel"], ch, writes=[bw])
        self.ldma(ones64[:], dr["ones64"], ch, writes=[bw])
        self.ldma(ng[:], dr["hgrn_ng"][l], ch, writes=[bw])
        self.V(lambda h: h.tensor_copy(out=cselb[:], in_=csel[:]), [bw], [bw])
        if l == 0:
            self.G(lambda h: h.memset(lb[:], 0.0), [], [bw])
        else:
            assert self.L == 2
            self.ldma(lb[:], dr["hgrn_lb_logits"][1:2, :].broadcast_to([128, 256]), ch, writes=[bw])
            self.ldma(oml[:], dr["hgrn_lb_logits"][0:1, :].broadcast_to([128, 256]), ch, writes=[bw])
            self.V(lambda h: h.tensor_tensor(out=lb[:], in0=lb[:], in1=oml[:], op=ALU.subtract), [bw], [bw])
            self.A(lambda h: h.activation(out=lb[:], in_=lb[:], func=AF.Sigmoid), [bw], [bw])
        self.V(lambda h: h.tensor_scalar(out=oml[:], in0=lb[:], scalar1=-1.0, scalar2=1.0, op0=ALU.mult, op1=ALU.add),
               [bw], [bw])
        mixT = fw.sb(es, "hmixT", [64, 4, S], BF16)
        bmix = Buf()
        Sst = fw.sb(es, "Sst", [64, 4, 64], F32)
        Sbf = fw.sb(es, "Sbf", [64, 4, 64], BF16)
        bS = Buf()
        bSb = Buf()
        self.G(lambda h: h.memset(Sst[:], 0.0), [], [bS])
        self.G(lambda h: h.memset(Sbf[:], 0.0), [], [bSb])

        def T(name, shape, dt, n=2):
            r = [(fw.sb(es, f"{name}{i}", shape, dt), Buf()) for i in range(n)]
            return r * (2 // n)
        f_ = T("hf", [128, 256], F32)
        lf_ = T("hlf", [128, 256], F32)
        kk_ = T("hkk", [128, 256], F32)
        eg_ = T("heg", [128, 768], F32)
        qf_ = T("hqf", [128, 256], F32)
        qt_ = T("hqt", [128, 256], BF16)
        kh_ = T("hkh", [128, 256], F32)
        khb_ = T("hkhb", [128, 256], BF16)
        ke_ = T("hke", [128, 256], BF16)
        kem_ = T("hkem", [128, 8, 256], BF16, 1)
        vb_ = T("hvb", [128, 256], BF16)
        sgb_ = T("hsg", [128, 256], BF16)
        Dt_ = T("hDt", [64, 4, 8], F32)
        qkT_ = T("hqkT", [64, 8, 128], BF16)
        gT_ = T("hgT", [64, 4, 128], BF16)
        scm_ = T("hscm", [128, 4, 128], BF16)
        osq_ = T("hosq", [64, 512], F32, 1)
        rs_ = T("hrs", [64, 512], F32, 1)
        t1_ = T("ht1", [64, 512], F32, 1)
        for t in range(NT):
            tok = slice(t * 128, (t + 1) * 128)
            bh = self.bhT[t // 4]
            i2 = t % 2
            pa, bpa = self.ringA.next()
            pb, bpb = self.ringA.next()
            for kc in range(8):
                self.MM(pa[:], self.hT[:, kc, tok], wa[:, kc, 0:512], kc == 0, kc == 7, [bh, bw], [bpa])
            for kc in range(8):
                self.MM(pb[:], self.hT[:, kc, tok], wa[:, kc, 512:1024], kc == 0, kc == 7, [bh, bw], [bpb])
            f, bf_ = f_[i2]
            lf, blf = lf_[i2]
            kk, bkk = kk_[i2]
            eg, beg = eg_[i2]
            qf, bqf = qf_[i2]
            qt, bqt = qt_[i2]
            kh, bkh = kh_[i2]
            khb, bkhb = khb_[i2]
            ke, bke = ke_[i2]
            kem, bkem = kem_[i2]
            vb, bvb = vb_[i2]
            sgb, bsgb = sgb_[i2]
            Dt, bDt = Dt_[i2]
            qkT, bqkT = qkT_[i2]
            gT, bgT = gT_[i2]
            scm, bscm = scm_[i2]
            osq, bosq = osq_[i2]
            rs, brs = rs_[i2]
            t1, bt1 = t1_[i2]
            self.A(lambda h: h.activation(out=f[:], in_=pa[:, 256:512], func=AF.Sigmoid), [bpa], [bf_])
            self.A(lambda h: h.activation(out=qf[:], in_=pa[:, 0:256], func=AF.Silu), [bpa], [bqf])
            self.A(lambda h: h.copy(out=vb[:], in_=pb[:, 0:256]), [bpb], [bvb])
            self.A(lambda h: h.activation(out=sgb[:], in_=pb[:, 256:512], func=AF.Silu), [bpb], [bsgb])
            self.V(lambda h: h.tensor_tensor(out=f[:], in0=f[:], in1=oml[:], op=ALU.mult), [bf_, bw], [bf_])
            self.V(lambda h: h.tensor_tensor(out=f[:], in0=f[:], in1=lb[:], op=ALU.add), [bf_, bw], [bf_])
            self.A(lambda h: h.activation(out=lf[:], in_=f[:], func=AF.Ln), [bf_], [blf])
            self.V(lambda h: h.tensor_scalar(out=kk[:], in0=f[:], scalar1=-1.0, scalar2=1.0, op0=ALU.mult, op1=ALU.add),
                   [bf_], [bkk])
            pg, bpg = self.ringA.next()
            self.MM(pg[:, 0:256], tri[:], lf[:], True, True, [bw, blf], [bpg], inc=False)
            self.MM(pg[:, 256:512], blk[:], lf[:], True, True, [bw, blf], [bpg], skip=True)
            pd, bpd = self.ringB.next()
            for hd in range(4):
                self.MM(pd[0:64, hd * 8:(hd + 1) * 8], lf[:, hd * 64:(hd + 1) * 64], csel[:], hd == 0, True,
                        [bw, blf], [bpd], inc=(hd == 3), skip=True)
            self.A(lambda h: h.activation(out=eg[:, 0:256], in_=pg[:, 0:256], func=AF.Exp), [bpg], [beg])
            self.A(lambda h: h.activation(out=eg[:, 256:512], in_=pg[:, 0:256], func=AF.Exp, scale=-1.0), [bpg], [beg])
            self.A(lambda h: h.activation(out=eg[:, 512:768], in_=pg[:, 256:512], func=AF.Exp), [bpg], [beg])
            self.A(lambda h: h.activation(out=Dt[:].rearrange("p h c -> p (h c)"), in_=pd[0:64, 0:32], func=AF.Exp),
                   [bpd], [bDt])
            self.V(lambda h: h.tensor_tensor(out=qt[:], in0=qf[:], in1=eg[:, 0:256], op=ALU.mult), [bqf, beg], [bqt])
            self.V(lambda h: h.tensor_tensor(out=kh[:], in0=kk[:], in1=eg[:, 256:512], op=ALU.mult), [bkk, beg], [bkh])
            self.G(lambda h: h.tensor_copy(out=khb[:], in_=kh[:]), [bkh], [bkhb])
            self.G(lambda h: h.tensor_tensor(out=ke[:], in0=kh[:], in1=eg[:, 512:768], op=ALU.mult), [bkh, beg], [bke])
            self.G(lambda h: h.tensor_tensor(out=kem[:], in0=ke[:].unsqueeze(1).broadcast_to([128, 8, 256]),
                                             in1=cselb[:].unsqueeze(2).broadcast_to([128, 8, 256]), op=ALU.mult),
                   [bke, bw], [bkem])
            pt, bpt = self.ringT.next()
            for hd in range(4):
                self.TR(pt[0:64, hd * 128:(hd + 1) * 128], qt[:, hd * 64:(hd + 1) * 64], [bqt], [bpt], inc=False)
            for hd in range(4):
                self.TR(pt[0:64, 512 + hd * 128:512 + (hd + 1) * 128], khb[:, hd * 64:(hd + 1) * 64], [bkhb], [bpt],
                        inc=(hd == 3))
            self.A(lambda h: h.copy(out=qkT[:].rearrange("p a t -> p (a t)"), in_=pt[0:64, :]), [bpt], [bqkT])
            pt2, bpt2 = self.ringT.next()
            for hd in range(4):
                self.TR(pt2[0:64, hd * 128:(hd + 1) * 128], sgb[:, hd * 64:(hd + 1) * 64], [bsgb], [bpt2], inc=(hd == 3))
            self.A(lambda h: h.copy(out=gT[:].rearrange("p a t -> p (a t)"), in_=pt2[0:64, 0:512]), [bpt2], [bgT])
            psc, bpsc = self.ringA.next()
            for hd in range(4):
                self.MM(psc[:, hd * 128:(hd + 1) * 128], qkT[:, 4 + hd, :], qkT[:, hd, :], hd == 0, True,
                        [bqkT], [bpsc], inc=(hd == 3), skip=True)
            self.V(lambda h: h.tensor_tensor(out=scm[:], in0=psc[:].rearrange("p (a t) -> p a t", a=4),
                                             in1=tri[:].unsqueeze(1).broadcast_to([128, 4, 128]), op=ALU.mult),
                   [bpsc, bw], [bscm])
            po, bpo = self.ringB.next()
            firstmm = True
            for ci in range(8):
                c = t * 8 + ci
                if c > 0:
                    for hd in range(4):
                        self.MM(po[0:64, hd * 128 + ci * 16:hd * 128 + (ci + 1) * 16], Sbf[:, hd, :],
                                qkT[:, hd, ci * 16:(ci + 1) * 16], firstmm, False, [bSb, bqkT], [bpo], inc=False, skip=True)
                        firstmm = False
                pkv, bpkv = self.ringA.next()
                for hd in range(4):
                    self.MM(pkv[0:64, hd * 64:(hd + 1) * 64], kem[:, ci, hd * 64:(hd + 1) * 64], vb[:, hd * 64:(hd + 1) * 64],
                            hd == 0, True, [bkem, bvb], [bpkv], inc=(hd == 3), skip=True)
                self.V(lambda h: h.tensor_tensor(out=Sst[:], in0=Sst[:], in1=Dt[:, :, ci:ci + 1].broadcast_to([64, 4, 64]),
                                                 op=ALU.mult), [bS, bDt], [bS])
                self.V(lambda h: h.tensor_tensor(out=Sst[:], in0=Sst[:], in1=pkv[0:64, 0:256].rearrange("p (a b) -> p a b", a=4),
                                                 op=ALU.add), [bS, bpkv], [bS])
                self.V(lambda h: h.tensor_copy(out=Sbf[:], in_=Sst[:]), [bS], [bSb])
            for hd in range(4):
                self.MM(po[0:64, hd * 128:(hd + 1) * 128], vb[:, hd * 64:(hd + 1) * 64], scm[:, hd, :], firstmm, True,
                        [bvb, bscm], [bpo], inc=(hd == 3), skip=True)
                firstmm = False
            self.A(lambda h: h.activation(out=osq[:], in_=po[0:64, :], func=AF.Square), [bpo], [bosq])
            pss, bpss = self.ringA.next()
            self.MM(pss[0:64, :], ones64[:], osq[:], True, True, [bw, bosq], [bpss])
            self.V(lambda h: h.tensor_scalar(out=rs[:], in0=pss[0:64, :], scalar1=1.0 / 64.0, scalar2=RMS_EPS,
                                             op0=ALU.mult, op1=ALU.add), [bpss], [brs])
            self.A(lambda h: h.activation(out=rs[:], in_=rs[:], func=AF.Ln), [brs], [brs])
            self.A(lambda h: h.activation(out=rs[:], in_=rs[:], func=AF.Exp, scale=-0.5), [brs], [brs])
            self.V(lambda h: h.tensor_tensor(out=t1[:], in0=po[0:64, :], in1=rs[:], op=ALU.mult), [bpo, brs], [bt1])
            for hd in range(4):
                self.V(lambda h: h.scalar_tensor_tensor(out=mixT[0:64, hd, tok], in0=t1[:, hd * 128:(hd + 1) * 128],
                                                        scalar=ng[:, hd:hd + 1], in1=gT[:, hd, :],
                                                        op0=ALU.mult, op1=ALU.mult), [bt1, bw, bgT], [bmix])
        if "hgrn_o" in self.dumps:
            self.dump_bf16(es, "hgrn_o", mixT[:], [64, 4, S], [bmix])
        self.out_proj(es, l, m, mixT, bmix, first)


_NC_CACHE = {}


def _get_nc(S, L):
    key = (S, L)
    if key not in _NC_CACHE:
        _NC_CACHE[key] = Builder(S, L).build()
    return _NC_CACHE[key]


def make_in_maps(inputs, n_cores, S):
    w = host_weights(inputs)
    consts = host_consts(S)
    x = np.asarray(inputs["x"], dtype=np.float32)
    c = np.asarray(inputs["c"], dtype=np.float32)
    maps = []
    for i in range(n_cores):
        cc = c[2 * i:2 * i + 2]
        cT = np.ascontiguousarray(cc.reshape(2, 8, 128).transpose(2, 1, 0))
        mp = {"x": np.ascontiguousarray(x[2 * i:2 * i + 2]), "cT": cT}
        mp.update(w)
        mp.update(consts)
        maps.append(mp)
    return maps


def kernel(**inputs):
    x = np.asarray(inputs["x"])
    B, S, _ = x.shape
    L = np.asarray(inputs["ada_w"]).shape[0]
    n = B // 2
    nc = _get_nc(S, L)
    maps = make_in_maps(inputs, n, S)
    res = run_bass_kernel_spmd(nc, maps, core_ids=list(range(n)))
    out = np.concatenate([r["out"] for r in res.results], axis=0)
    return out.astype(np.float32)
```

```python
import numpy as np
from contextlib import ExitStack
import concourse.bass as bass
import concourse.mybir as mybir
from concourse.bass_utils import run_bass_kernel_spmd

F32 = mybir.dt.float32
BF16 = mybir.dt.bfloat16
AF = mybir.ActivationFunctionType
ALU = mybir.AluOpType

D = 1024
DFF = 2816
NJ = 22
MIXC = 2724
ALPHA = float(4 ** 0.25)
LN_EPS = 1e-5
RMS_EPS = 1e-6
NEG = -30000.0


class Buf:
    __slots__ = ("w", "r", "name")

    def __init__(self, name=""):
        self.w = {}
        self.r = {}
        self.name = name


class Chan:
    def __init__(self, sem, name):
        self.sem = sem
        self.cnt = 0
        self.name = name


class Eng:
    def __init__(self, name, h, chan):
        self.name = name
        self.h = h
        self.chan = chan
        self.seen = {}


class FW:
    def __init__(self, nc, es):
        self.nc = nc
        self.es = es
        self.nsem = 0
        self.chans = []
        self.pe = Eng("pe", nc.tensor, self.new_chan("pe"))
        self.act = Eng("act", nc.scalar, self.new_chan("act"))
        self.dve = Eng("dve", nc.vector, self.new_chan("dve"))
        self.pool = Eng("pool", nc.gpsimd, self.new_chan("pool"))
        self.sp = Eng("sp", nc.sync, self.new_chan("sp"))
        self.engs = [self.pe, self.act, self.dve, self.pool, self.sp]
        self.n_inst = 0
        self.n_wait = 0
        self.uid = 0

    def new_chan(self, name):
        for ch in self.chans:
            if ch.name == name and name not in ("dbg",):
                return ch
        sem = self.es.enter_context(self.nc.semaphore(f"s_{name}_{self.nsem}"))
        self.nsem += 1
        ch = Chan(sem, name)
        self.chans.append(ch)
        return ch

    def _sync(self, E, reads, writes):
        need = {}
        for b in reads:
            for ch, c in b.w.items():
                if need.get(ch, 0) < c:
                    need[ch] = c
        for b in writes:
            for ch, c in b.w.items():
                if need.get(ch, 0) < c:
                    need[ch] = c
            for ch, c in b.r.items():
                if need.get(ch, 0) < c:
                    need[ch] = c
        for ch, c in need.items():
            if ch is E.chan and E.name == "pe":
                continue
            if E.seen.get(ch, 0) >= c:
                continue
            E.h.wait_ge(ch.sem, c)
            self.n_wait += 1
            E.seen[ch] = c

    def _mark(self, ch, mark, reads, writes):
        for b in reads:
            if b.r.get(ch, 0) < mark:
                b.r[ch] = mark
        for b in writes:
            b.w[ch] = mark
            b.r = {}

    def op(self, E, emit, reads=(), writes=(), inc=True):
        self._sync(E, reads, writes)
        ins = emit(E.h)
        self.n_inst += 1
        ch = E.chan
        if inc:
            ch.cnt += 1
            ins.then_inc(ch.sem, 1)
            mark = ch.cnt
        else:
            mark = ch.cnt + 1
        self._mark(ch, mark, reads, writes)
        return ins

    def dma(self, Q, chan, out, in_, reads=(), writes=(), **kw):
        self._sync(Q, reads, writes)
        ins = Q.h.dma_start(out=out, in_=in_, **kw)
        self.n_inst += 1
        chan.cnt += 16
        ins.then_inc(chan.sem, 16)
        self._mark(chan, chan.cnt, reads, writes)
        return ins

    def barrier(self):
        for E in self.engs:
            for ch in self.chans:
                if ch.cnt == 0 or ch is E.chan:
                    continue
                if E.seen.get(ch, 0) >= ch.cnt:
                    continue
                E.h.wait_ge(ch.sem, ch.cnt)
                E.seen[ch] = ch.cnt

    def sb(self, es, name, shape, dt):
        self.uid += 1
        return es.enter_context(self.nc.sbuf_tensor(f"sb{self.uid}_{name}", list(shape), dt))

    def psum(self, es, name, shape, dt):
        self.uid += 1
        return es.enter_context(self.nc.psum_tensor(f"pp{self.uid}_{name}", list(shape), dt))


class Ring:
    def __init__(self, items):
        self.items = items
        self.i = 0

    def next(self):
        it = self.items[self.i % len(self.items)]
        self.i += 1
        return it


def host_consts(S):
    pos = np.arange(S, dtype=np.float32)
    half = 16
    inv_freq = (10000.0 ** (-np.arange(half, dtype=np.float32) / half)).astype(np.float32)
    ang = pos[None, :] * inv_freq[:, None]
    cos = np.cos(ang).astype(np.float32)
    sin = np.sin(ang).astype(np.float32)
    ropeC = np.ones((96, S), np.float32)
    ropeS = np.zeros((96, S), np.float32)
    ropeC[64:80] = cos
    ropeC[80:96] = cos
    ropeS[64:80] = -sin
    ropeS[80:96] = sin
    s = np.arange(128)[:, None]
    negmask = np.zeros((128, 4, 512), np.float32)
    t = np.arange(512)[None, :]
    for r in range(4):
        negmask[:, r, :] = np.where(128 * r + s <= t, 0.0, NEG)
    ident = np.eye(128, dtype=np.float32)
    t128 = np.arange(128)[None, :]
    same = (s // 16) == (t128 // 16)
    tri = (same & (s <= t128)).astype(np.float32)
    blk = same.astype(np.float32)
    csel = ((s // 16) == np.arange(8)[None, :]).astype(np.float32)
    caus = (s <= t128).astype(np.float32)
    sel65 = np.zeros((65, 64), np.float32)
    sel65[64, :] = 1.0
    ones64 = np.ones((64, 64), np.float32)
    return dict(ropeC=ropeC, ropeS=ropeS, negmask=negmask, ident=ident, tri=tri, blk=blk,
                csel=csel, caus=caus, sel65=sel65, ones64=ones64)


CONST_SHAPES = lambda S: dict(ropeC=[96, S], ropeS=[96, S], negmask=[128, 4, 512], ident=[128, 128],
                              tri=[128, 128], blk=[128, 128], csel=[128, 8], caus=[128, 128],
                              sel65=[65, 64], ones64=[64, 64])


def weight_shapes(L):
    return dict(
        ada_w=[L, D, 9 * D], ada_b=[L, 9 * D], ln_g=[L, 3, D], ln_b=[L, 3, D],
        ffn1_w_in=[L, D, 2 * DFF], ffn1_w_out=[L, DFF, D], ffn2_w_in=[L, D, 2 * DFF], ffn2_w_out=[L, DFF, D],
        mix_w_in=[L, D, MIXC], mix_w_out=[L, D, D], wkr=[L, D, 192],
        hgrn_lb_logits=[L, 256], hgrn_ng=[L, 64, 4], mla_q_norm_g=[L, 256], mla_kv_norm_g=[L, 128],
        mla_w_uq=[L, 256, 384], mla_w_uqp=[L, 256, 384], mla_w_ukv=[L, 128, 512],
        fox_b_f=[L, 4], gmlp_ln_g=[L, 256], gmlp_ln_b=[L, 256], gmlp_wsT=[L, 4, 128, 128], gmlp_b_s=[L, 4, 128],
    )


def host_weights(inp):
    L = inp["ada_w"].shape[0]
    f = lambda a: np.ascontiguousarray(np.asarray(a, dtype=np.float32))
    w = {k: f(inp[k]) for k in ["ada_w", "ada_b", "ln_g", "ln_b", "ffn1_w_in", "ffn1_w_out", "ffn2_w_in",
                                "ffn2_w_out", "mix_w_in", "mix_w_out", "hgrn_lb_logits", "mla_q_norm_g",
                                "mla_kv_norm_g", "mla_w_uq", "mla_w_ukv", "fox_b_f", "gmlp_ln_g", "gmlp_ln_b",
                                "gmlp_b_s"]}
    perm = np.concatenate([np.arange(16, 32), np.arange(0, 16)])
    kr = w["mix_w_in"][:, :, 1408:1440]
    wkr = np.zeros((L, D, 192), np.float32)
    wkr[:, :, 64:96] = kr
    wkr[:, :, 160:192] = kr[:, :, perm]
    w["wkr"] = wkr
    uq = w["mla_w_uq"].reshape(L, 256, 4, 96)
    uqp = np.zeros_like(uq)
    uqp[:, :, :, 64:96] = uq[:, :, :, 64:96][:, :, :, perm]
    w["mla_w_uqp"] = np.ascontiguousarray(uqp.reshape(L, 256, 384))
    w["hgrn_ng"] = np.ascontiguousarray(f(inp["hgrn_norm_g"]).reshape(L, 4, 64).transpose(0, 2, 1))
    w["gmlp_wsT"] = np.ascontiguousarray(f(inp["gmlp_w_s"]).transpose(0, 1, 3, 2))
    return w


class Builder:
    def __init__(self, S, L, stop=None, dumps=()):
        self.S = S
        self.L = L
        self.NT = S // 128
        self.NG = S // 512
        self.stop = stop
        self.dumps = set(dumps)
        self.dump_aps = {}

    def build(self):
        S, L = self.S, self.L
        nc = bass.Bass("TRN2", target_bir_lowering=False)
        self.nc = nc
        dr = {}
        dr["x"] = nc.dram_tensor("x", [2, S, D], F32, kind="ExternalInput").ap()
        dr["cT"] = nc.dram_tensor("cT", [128, 8, 2], F32, kind="ExternalInput").ap()
        for k, shp in weight_shapes(L).items():
            dr[k] = nc.dram_tensor(k, shp, F32, kind="ExternalInput").ap()
        for k, shp in CONST_SHAPES(S).items():
            dr[k] = nc.dram_tensor(k, shp, F32, kind="ExternalInput").ap()
        dr["out"] = nc.dram_tensor("out", [2, S, D], F32, kind="ExternalOutput").ap()
        dr["modd"] = nc.dram_tensor("modd", [2, L, 9 * D], F32, kind="Internal").ap()
        self.dr = dr
        with ExitStack() as es:
            self.fw = fw = FW(nc, es)
            self.es = es
            self.setup_global(es)
            self.prologue_mod()
            for s in range(2):
                self.run_sequence(s)
                if self.stop is not None and self.stop[0] == s:
                    break
            fw.barrier()
            print(f"[build] inst={fw.n_inst} waits={fw.n_wait} sems={fw.nsem} "
                  f"cnt={[(e.name, e.chan.cnt) for e in fw.engs]}", flush=True)
        return nc

    def V(self, emit, reads=(), writes=()):
        return self.fw.op(self.fw.dve, emit, reads, writes)

    def A(self, emit, reads=(), writes=()):
        return self.fw.op(self.fw.act, emit, reads, writes)

    def G(self, emit, reads=(), writes=()):
        return self.fw.op(self.fw.pool, emit, reads, writes)

    def MM(self, out, lhsT, rhs, start, stop, reads=(), writes=(), inc=None, skip=False):
        if inc is None:
            inc = bool(stop)
        kw = dict(skip_group_check=True) if skip else {}
        return self.fw.op(self.fw.pe, lambda h: h.matmul(out, lhsT=lhsT, rhs=rhs, start=start, stop=stop, **kw),
                          reads, writes, inc=inc)

    def TR(self, out, in_, reads=(), writes=(), inc=True):
        ident = self.ident
        n = in_.shape[0]
        return self.fw.op(self.fw.pe, lambda h: h.transpose(out, in_, ident[0:n, 0:n]),
                          list(reads) + [self.b_const], writes, inc=inc)

    def wdma(self, out, in_, chan, reads=(), writes=()):
        return self.fw.dma(self.fw.pool, self.fw.new_chan(chan.name + "_sw"), out, in_, reads, writes)

    def ldma(self, out, in_, chan, reads=(), writes=(), **kw):
        return self.fw.dma(self.fw.sp, chan, out, in_, reads, writes, **kw)

    def dump(self, name, ap, shape, reads):
        if name not in self.dumps or name in self.dump_aps:
            return
        d = self.nc.dram_tensor("dbg_" + name, list(shape), F32, kind="ExternalOutput").ap()
        self.dump_aps[name] = d
        ch = self.fw.new_chan("dbg")
        self.fw.dma(self.fw.sp, ch, d, ap, reads=reads)

    def setup_global(self, es):
        fw, dr, NT = self.fw, self.dr, self.NT
        self.x = fw.sb(es, "x", [128, NT, D], F32)
        self.bx = [Buf(f"x{t}") for t in range(NT)]
        self.hT = fw.sb(es, "hT", [128, 8, self.S], BF16)
        self.bhT = [Buf(f"hT{g}") for g in range(self.NG)]
        self.mv = fw.sb(es, "modv", [128, 5, D], F32)
        self.bmv = Buf("modv")
        self.ch_mv = fw.new_chan("modv")
        self.bab = Buf("modab")
        self.ch_ab = fw.new_chan("modab")
        self.ident = fw.sb(es, "ident", [128, 128], BF16)
        self.b_const = Buf("const")
        ch = fw.new_chan("const")
        self.wdma(self.ident[:], dr["ident"], ch, writes=[self.b_const])
        self.pbank = [(fw.psum(es, f"b{i}", [128, 512], F32), Buf(f"ps{i}")) for i in range(6)]
        self.ptr = [(fw.psum(es, f"t{i}", [128, 1024], BF16), Buf(f"pt{i}")) for i in range(2)]
        self.ringA = Ring(self.pbank[0:4])
        self.ringB = Ring(self.pbank[4:6])
        self.ringT = Ring(self.ptr)
        self.ch_x = [fw.new_chan(f"x{i}") for i in range(4)]
        self.ch_out = [fw.new_chan(f"o{i}") for i in range(4)]
        self.b_modd = Buf("modd")

    def mod_alloc(self, es):
        fw = self.fw
        self.m_slots = [(fw.sb(es, f"aw{i}", [128, 8, 512], BF16), Buf(), fw.new_chan(f"aw{i}")) for i in range(2)]
        self.m_bslots = [(fw.sb(es, f"ab{i}", [2, 512], F32), Buf(), fw.new_chan(f"ab{i}")) for i in range(2)]
        self.m_rows = [(fw.sb(es, f"mr{i}", [2, 512], F32), Buf(), fw.new_chan(f"mr{i}")) for i in range(2)]

    def mod_block(self, l, cb):
        dr = self.dr
        it = self.m_it
        self.m_it += 1
        awv = dr["ada_w"][l].rearrange("(kc p) n -> p kc n", p=128)
        w, bw, chw = self.m_slots[it % 2]
        ab, bab, chb = self.m_bslots[it % 2]
        mr, bmr, chr_ = self.m_rows[it % 2]
        cact, b_c = self.cact, self.b_cact
        self.wdma(w[:], awv[:, :, cb * 512:(cb + 1) * 512], chw, writes=[bw])
        self.ldma(ab[:], dr["ada_b"][l:l + 1, cb * 512:(cb + 1) * 512].broadcast_to([2, 512]), chb, writes=[bab])
        ps, bps = self.ringA.next()
        for kc in range(8):
            self.MM(ps[0:2, :], cact[:, kc, :], w[:, kc, :], kc == 0, kc == 7, reads=[b_c, bw], writes=[bps])
        seg = (cb * 512) // 1024
        self.V(lambda h: h.tensor_tensor(out=mr[:], in0=ps[0:2, :], in1=ab[:], op=ALU.add), [bps, bab], [bmr])
        if seg in (1, 4, 7, 5):
            self.V(lambda h: h.tensor_scalar_add(out=mr[:], in0=mr[:], scalar1=1.0), [bmr], [bmr])
        elif seg in (2, 8):
            self.V(lambda h: h.tensor_scalar(out=mr[:], in0=mr[:], scalar1=1.0, scalar2=0.5,
                                             op0=ALU.add, op1=ALU.mult), [bmr], [bmr])
        self.ldma(dr["modd"][:, l, cb * 512:(cb + 1) * 512], mr[:], chr_, reads=[bmr], writes=[self.b_modd])

    def prologue_mod(self):
        fw, dr = self.fw, self.dr
        self.m_it = 0
        self.cact = fw.sb(self.es, "cact", [128, 8, 2], BF16)
        self.b_cact = Buf()
        self.mod_deferred = (self.L > 1 and self.stop is None and not self.skip_mixers)
        with ExitStack() as es:
            cT = fw.sb(es, "cT", [128, 8, 2], F32)
            ch = fw.new_chan("c")
            self.ldma(cT[:], dr["cT"], ch, writes=[self.b_cact])
            self.A(lambda h: h.activation(out=self.cact[:], in_=cT[:], func=AF.Silu), [self.b_cact], [self.b_cact])
            self.mod_alloc(es)
            for l in range(1 if self.mod_deferred else self.L):
                for cb in range(18):
                    self.mod_block(l, cb)
            fw.barrier()

    def load_ab(self, s, l, j):
        dr = self.dr
        for i, sg in enumerate([3 * j + 1, 3 * j + 0]):
            self.ldma(self.mv[:, i, :], dr["modd"][s:s + 1, l, sg * D:(sg + 1) * D].broadcast_to([128, D]), self.ch_ab,
                      reads=[self.b_modd], writes=[self.bab])

    def load_modvec(self, s, l, j):
        dr = self.dr
        mv, b, ch = self.mv, self.bmv, self.ch_mv
        sg = 3 * j + 2
        self.ldma(mv[:, 2, :], dr["modd"][s:s + 1, l, sg * D:(sg + 1) * D].broadcast_to([128, D]), ch,
                  reads=[self.b_modd], writes=[b])
        self.ldma(mv[:, 3, :], dr["ln_g"][l, j:j + 1, :].broadcast_to([128, D]), ch, writes=[b])
        self.ldma(mv[:, 4, :], dr["ln_b"][l, j:j + 1, :].broadcast_to([128, D]), ch, writes=[b])

    def run_sequence(self, s):
        fw, dr, NT = self.fw, self.dr, self.NT
        xin = dr["x"][s].rearrange("(t p) d -> p t d", p=128)
        q = max(1, NT // 4)
        for i in range(0, NT, q):
            self.ldma(self.x[:, i:i + q, :], xin[:, i:i + q, :], self.ch_x[(i // q) % 4],
                      writes=self.bx[i:i + q])
        subs = [(l, j) for l in range(self.L) for j in range(3)]
        if self.stop is not None and self.stop[0] == s:
            subs = subs[:subs.index((self.stop[1], self.stop[2])) + 1]
        self.xout = dr["out"][s].rearrange("(t p) d -> p t d", p=128)
        self.load_ab(s, *subs[0])
        with ExitStack() as e0:
            self.build_hT(e0)
            fw.barrier()
        for k, (l, j) in enumerate(subs):
            self.load_modvec(s, l, j)
            self.has_next = k + 1 < len(subs)
            if self.has_next:
                self.load_ab(s, *subs[k + 1])
            if j == 1:
                self.mixer(s, l)
            else:
                self.ffn(s, l, j)

    def build_hT(self, es):
        fw, NT = self.fw, self.NT
        tmpf = [(fw.sb(es, f"hx{i}", [128, D], F32), Buf()) for i in range(2)]
        hb = [(fw.sb(es, f"hb{i}", [128, D], BF16), Buf()) for i in range(2)]
        mv, bmv = self.mv, self.bab
        for t in range(NT):
            tf, btf = tmpf[t % 2]
            hbt, bhb = hb[t % 2]
            self.V(lambda h: h.tensor_tensor(out=tf[:], in0=self.x[:, t, :], in1=mv[:, 0, :], op=ALU.mult),
                   [self.bx[t], bmv], [btf])
            self.G(lambda h: h.tensor_tensor(out=hbt[:], in0=tf[:], in1=mv[:, 1, :], op=ALU.add),
                   [btf, bmv], [bhb])
            self.hT_transpose(t, hbt, bhb)

    def hT_transpose(self, t, hbt, bhb):
        pt, bpt = self.ringT.next()
        for kc in range(8):
            self.TR(pt[:, kc * 128:(kc + 1) * 128], hbt[:, kc * 128:(kc + 1) * 128], [bhb], [bpt], inc=(kc == 7))
        self.A(lambda h: h.copy(out=self.hT[:, :, t * 128:(t + 1) * 128],
                                in_=pt[:].rearrange("p (k c) -> p k c", k=8)),
               [bpt], [self.bhT[t // 4]])

    def begin_tail(self, es):
        fw = self.fw
        self.t_st = [(fw.sb(es, f"lnst{i}", [128, 2, 6], F32), fw.sb(es, f"lnmv{i}", [128, 4], F32), Buf()) for i in range(2)]
        self.t_q = []
        if self.has_next:
            self.t_tf = [(fw.sb(es, f"thx{i}", [128, D], F32), Buf()) for i in range(1)]
            self.t_hb = [(fw.sb(es, f"thb{i}", [128, D], BF16), Buf()) for i in range(3)]

    def tail_tile(self, t):
        self.ln_tile(t, *self.t_st[t % 2])
        if self.has_next:
            tf, btf = self.t_tf[0]
            hbt, bhb = self.t_hb[t % 3]
            mv = self.mv
            self.V(lambda h: h.tensor_tensor(out=tf[:], in0=self.x[:, t, :], in1=mv[:, 0, :], op=ALU.mult),
                   [self.bx[t], self.bab], [btf])
            self.G(lambda h: h.tensor_tensor(out=hbt[:], in0=tf[:], in1=mv[:, 1, :], op=ALU.add),
                   [btf, self.bab], [bhb])
            self.t_q.append((t, hbt, bhb))
            if len(self.t_q) > 2:
                self.hT_transpose(*self.t_q.pop(0))
        else:
            self.ldma(self.xout[:, t, :], self.x[:, t, :], self.ch_out[t % 4], reads=[self.bx[t]])

    def end_tail(self):
        while self.t_q:
            self.hT_transpose(*self.t_q.pop(0))

    def resid_update(self, t, half, ps, bps, first, tmp, btmp):
        sl = slice(half * 512, (half + 1) * 512)
        mv, bmv = self.mv, self.bmv
        self.V(lambda h: h.tensor_tensor(out=tmp[:], in0=ps[:], in1=mv[:, 2, sl], op=ALU.mult),
               [bps, bmv], [btmp])
        xs = self.x[:, t, sl]
        if first:
            self.V(lambda h: h.scalar_tensor_tensor(out=xs, in0=xs, scalar=ALPHA, in1=tmp[:],
                                                    op0=ALU.mult, op1=ALU.add), [btmp, self.bx[t]], [self.bx[t]])
        else:
            self.G(lambda h: h.tensor_tensor(out=xs, in0=xs, in1=tmp[:], op=ALU.add),
                   [btmp, self.bx[t]], [self.bx[t]])

    def ln_tile(self, t, s6, m4, bs):
        mv, bmv = self.mv, self.bmv
        if True:
            xt = self.x[:, t, :]
            bxt = self.bx[t]
            self.V(lambda h: h.bn_stats(out=s6[:, 0, :], in_=self.x[:, t, 0:512]), [bxt], [bs])
            self.V(lambda h: h.bn_stats(out=s6[:, 1, :], in_=self.x[:, t, 512:1024]), [bxt], [bs])
            self.V(lambda h: h.bn_aggr(out=m4[:, 0:2], in_=s6[:]), [bs], [bs])
            self.V(lambda h: h.tensor_scalar_add(out=m4[:, 2:3], in0=m4[:, 1:2], scalar1=LN_EPS), [bs], [bs])
            self.A(lambda h: h.sqrt(out=m4[:, 2:3], in_=m4[:, 2:3]), [bs], [bs])
            self.V(lambda h: h.reciprocal(out=m4[:, 2:3], in_=m4[:, 2:3]), [bs], [bs])
            self.V(lambda h: h.scalar_tensor_tensor(out=m4[:, 3:4], in0=m4[:, 0:1], scalar=-1.0, in1=m4[:, 2:3],
                                                    op0=ALU.mult, op1=ALU.mult), [bs], [bs])
            self.A(lambda h: h.activation(out=xt, in_=xt, func=AF.Identity, bias=m4[:, 3:4], scale=m4[:, 2:3]),
                   [bs, bxt], [bxt])
            self.V(lambda h: h.tensor_tensor(out=xt, in0=xt, in1=mv[:, 3, :], op=ALU.mult), [bxt, bmv], [bxt])
            self.G(lambda h: h.tensor_tensor(out=xt, in0=xt, in1=mv[:, 4, :], op=ALU.add), [bxt, bmv], [bxt])

    def ffn(self, s, l, j):
        fw, dr, NT, NG, S = self.fw, self.dr, self.NT, self.NG, self.S
        w_in = dr["ffn1_w_in" if j == 0 else "ffn2_w_in"][l].rearrange("(kc p) n -> p kc n", p=128)
        w_out = dr["ffn1_w_out" if j == 0 else "ffn2_w_out"][l].rearrange("(j p) n -> p j n", p=128)
        with ExitStack() as es:
            self.begin_tail(es)
            actT = fw.sb(es, "actT", [128, 6, S], BF16)
            bact = [Buf() for _ in range(NG)]
            wo = [(fw.sb(es, f"wo{i}", [128, 6, D], BF16), Buf(), fw.new_chan(f"wo{i}")) for i in range(2)]
            wi = [(fw.sb(es, f"wi{i}", [128, 8, 512], BF16), Buf(), fw.new_chan(f"wi{i}")) for i in range(2)]
            sgs = [(fw.sb(es, f"sg{i}", [128, 512], F32), Buf()) for i in range(2)]
            tmps = [(fw.sb(es, f"ft{i}", [128, 512], F32), Buf()) for i in range(2)]
            parts = [(0, 6), (6, 12), (12, 18), (18, 22)]
            iw = 0
            k = 0
            for pi, (j0, j1) in enumerate(parts):
                wot, bwo, chwo = wo[pi % 2]
                self.wdma(wot[:, 0:j1 - j0, :], w_out[:, j0:j1, :], chwo, writes=[bwo])
                for jj in range(j0, j1, 2):
                    wt, bw, chw = wi[iw % 2]
                    iw += 1
                    self.wdma(wt[:, :, 0:256], w_in[:, :, jj * 128:jj * 128 + 256], chw, writes=[bw])
                    self.wdma(wt[:, :, 256:512], w_in[:, :, DFF + jj * 128:DFF + jj * 128 + 256], chw, writes=[bw])
                    for c in range(2):
                        jl = jj + c - j0
                        for g in range(NG):
                            pg, bpg = self.ringA.next()
                            pu, bpu = self.ringA.next()
                            tok = slice(g * 512, (g + 1) * 512)
                            for kc in range(8):
                                self.MM(pg[:], wt[:, kc, c * 128:(c + 1) * 128], self.hT[:, kc, tok], kc == 0, kc == 7,
                                        [bw, self.bhT[g]], [bpg])
                            for kc in range(8):
                                self.MM(pu[:], wt[:, kc, 256 + c * 128:256 + (c + 1) * 128], self.hT[:, kc, tok],
                                        kc == 0, kc == 7, [bw, self.bhT[g]], [bpu])
                            sg, bsg = sgs[k % 2]
                            k += 1
                            self.A(lambda h: h.activation(out=sg[:], in_=pg[:], func=AF.Silu), [bpg], [bsg])
                            self.V(lambda h: h.tensor_tensor(out=actT[:, jl, tok], in0=sg[:], in1=pu[:], op=ALU.mult),
                                   [bsg, bpu], [bact[g]])
                nj = j1 - j0
                for t in range(NT):
                    for half in range(2):
                        ps, bps = self.ringB.next()
                        for jl in range(nj):
                            self.MM(ps[:], actT[:, jl, t * 128:(t + 1) * 128], wot[:, jl, half * 512:(half + 1) * 512],
                                    jl == 0, jl == nj - 1, [bact[t // 4], bwo], [bps])
                        tmp, btmp = tmps[(2 * t + half) % 2]
                        self.resid_update(t, half, ps, bps, pi == 0, tmp, btmp)
                    if pi == len(parts) - 1:
                        self.tail_tile(t)
            self.end_tail()
            fw.barrier()
        self.dump(f"x_{s}_{l}_{j}", self.x[:], [128, NT, D], self.bx)

    def mixer(self, s, l):
        fw = self.fw
        with ExitStack() as es:
            first = True
            active = [m for m in range(4) if m not in self.skip_mixers]
            fns = [self.mix_hgrn, self.mix_mla, self.mix_fox, self.mix_gmlp]
            for m in active:
                self.tail_on = (m == active[-1])
                with ExitStack() as es2:
                    fns[m](es2, s, l, m, first)
                    fw.barrier()
                first = False
        self.dump(f"x_{s}_{l}_1", self.x[:], [128, self.NT, D], self.bx)

    skip_mixers = ()

    def out_proj(self, es, l, m, mixT, bmix, first):
        fw, dr, NT = self.fw, self.dr, self.NT
        wov = dr["mix_w_out"][l, m * 256:(m + 1) * 256, :].rearrange("(s p) n -> p s n", p=64)
        wo = fw.sb(es, "mwo", [64, 4, D], BF16)
        bwo = Buf()
        ch = fw.new_chan("mwo")
        self.wdma(wo[:], wov, ch, writes=[bwo])
        tmps = [(fw.sb(es, f"ot{i}", [128, 512], F32), Buf()) for i in range(2)]
        if self.tail_on:
            self.begin_tail(es)
        for t in range(NT):
            for half in range(2):
                ps, bps = self.ringB.next()
                for sl in range(4):
                    self.MM(ps[:], mixT[0:64, sl, t * 128:(t + 1) * 128], wo[0:64, sl, half * 512:(half + 1) * 512],
                            sl == 0, sl == 3, [bmix, bwo], [bps])
                tmp, btmp = tmps[(2 * t + half) % 2]
                self.resid_update(t, half, ps, bps, first, tmp, btmp)
            if self.tail_on:
                self.tail_tile(t)
        if self.tail_on:
            self.end_tail()

    def attention(self, es, QT, KT, bq, bk, dk, Vaug, bv, scale, mixT, bmix):
        fw, dr, NG = self.fw, self.dr, self.NG
        negm = fw.sb(es, "negm", [128, 4, 512], BF16)
        sel = fw.sb(es, "sel65", [65, 64], F32)
        bc = Buf()
        ch = fw.new_chan("attc")
        self.wdma(negm[:], dr["negmask"], ch, writes=[bc])
        self.ldma(sel[:], dr["sel65"], ch, writes=[bc])
        tmpS = [(fw.sb(es, f"ts{i}", [128, 512], F32), Buf()) for i in range(2)]
        pTs = [(fw.sb(es, f"pT{i}", [128, 512], BF16), Buf()) for i in range(3)]
        osb = [(fw.sb(es, f"os{i}", [65, 512], F32), Buf()) for i in range(2)]
        rec = [(fw.sb(es, f"rc{i}", [64, 512], F32), Buf()) for i in range(2)]
        it = 0
        ip = 0
        for hd in range(4):
            for qg in range(NG):
                qs = slice(qg * 512, (qg + 1) * 512)
                nkb = 4 * (qg + 1)
                o_ps, bo = self.ringB.next()
                pend = []

                def qk(kb):
                    nonlocal ip
                    s_ps, bs = self.ringA.next()
                    self.MM(s_ps[:], KT[hd][0:dk, kb * 128:(kb + 1) * 128], QT[hd][0:dk, qs], True, True,
                            bk[kb // 4] + bq[qg], [bs])
                    pT, bp = pTs[ip % 3]
                    ip += 1
                    r = kb - 4 * qg
                    if r >= 0:
                        tS, bt = tmpS[kb % 2]
                        self.V(lambda h: h.tensor_tensor(out=tS[:], in0=s_ps[:], in1=negm[:, r, :], op=ALU.add),
                               [bs, bc], [bt])
                        self.A(lambda h: h.activation(out=pT[:], in_=tS[:], func=AF.Exp, scale=scale), [bt], [bp])
                    else:
                        self.A(lambda h: h.activation(out=pT[:], in_=s_ps[:], func=AF.Exp, scale=scale), [bs], [bp])
                    return pT, bp

                def pv(kb, pT, bp):
                    self.MM(o_ps[0:65, :], Vaug[:, kb, hd, :], pT[:], kb == 0, kb == nkb - 1, [bv, bp], [bo])

                LA = 2
                for kb in range(nkb + LA):
                    if kb < nkb:
                        pend.append((kb,) + qk(kb))
                    if kb >= LA:
                        a = pend.pop(0)
                        pv(*a)
                o_sb, bos = osb[it % 2]
                rc, brc = rec[it % 2]
                it += 1
                self.A(lambda h: h.copy(out=o_sb[:], in_=o_ps[0:65, :]), [bo], [bos])
                d_ps, bd = self.ringB.next()
                self.A(lambda h: h.activation(out=o_sb[64:65, :], in_=o_sb[64:65, :], func=AF.Ln), [bos], [bos])
                self.A(lambda h: h.activation(out=o_sb[64:65, :], in_=o_sb[64:65, :], func=AF.Exp, scale=-1.0), [bos], [bos])
                self.MM(d_ps[0:64, :], sel[:], o_sb[:], True, True, [bc, bos], [bd])
                self.V(lambda h: h.tensor_tensor(out=mixT[0:64, hd, qs], in0=o_sb[0:64, :], in1=d_ps[0:64, :],
                                                 op=ALU.mult), [bos, bd], [bmix])

    def mix_mla(self, es, s, l, m, first):
        fw, dr, NT, NG, S = self.fw, self.dr, self.NT, self.NG, self.S
        QT = [fw.sb(es, f"QT{h}", [96, S], BF16) for h in range(4)]
        KT = [fw.sb(es, f"KT{h}", [96, S], BF16) for h in range(4)]
        one = Buf()
        bq = [[one] for _ in range(NG)]
        bk = [[one] for _ in range(NG)]
        Vaug = fw.sb(es, "Vaug", [128, NT, 4, 65], BF16)
        bv = Buf()
        self.G(lambda h: h.memset(Vaug[:], 1.0), [], [bv])
        with ExitStack() as e1:
            wv = dr["mix_w_in"][l].rearrange("(kc p) n -> p kc n", p=128)
            wb = fw.sb(e1, "wB", [128, 8, 384], BF16)
            wkr = fw.sb(e1, "wkr", [128, 8, 192], BF16)
            wuq = fw.sb(e1, "wuq", [128, 2, 384], BF16)
            wuqp = fw.sb(e1, "wuqp", [128, 2, 384], BF16)
            wukv = fw.sb(e1, "wukv", [128, 512], BF16)
            gq = fw.sb(e1, "gq", [128, 256], F32)
            gkv = fw.sb(e1, "gkv", [128, 128], F32)
            bw = Buf()
            ch = fw.new_chan("mlaw")
            self.wdma(wb[:], wv[:, :, 1024:1408], ch, writes=[bw])
            self.wdma(wkr[:], dr["wkr"][l].rearrange("(kc p) n -> p kc n", p=128), ch, writes=[bw])
            self.wdma(wuq[:], dr["mla_w_uq"][l].rearrange("(kc p) n -> p kc n", p=128), ch, writes=[bw])
            self.wdma(wuqp[:], dr["mla_w_uqp"][l].rearrange("(kc p) n -> p kc n", p=128), ch, writes=[bw])
            self.wdma(wukv[:], dr["mla_w_ukv"][l], ch, writes=[bw])
            self.ldma(gq[:], dr["mla_q_norm_g"][l:l + 1, :].broadcast_to([128, 256]), ch, writes=[bw])
            self.ldma(gkv[:], dr["mla_kv_norm_g"][l:l + 1, :].broadcast_to([128, 128]), ch, writes=[bw])
            cqnT = fw.sb(e1, "cqnT", [128, 2, S], BF16)
            ckvT = fw.sb(e1, "ckvT", [128, S], BF16)
            bcT = [Buf() for _ in range(NG)]
            rc = [(fw.sb(e1, f"rC{i}", [96, 512], F32), fw.sb(e1, f"rS{i}", [96, 512], F32), Buf(),
                   fw.new_chan(f"rope{i}")) for i in range(2)]
            st = [(fw.sb(e1, f"mst{i}", [128, 2, 6], F32), fw.sb(e1, f"mmv{i}", [128, 8], F32), Buf()) for i in range(2)]
            cn = [(fw.sb(e1, f"cn{i}", [128, 384], BF16), Buf()) for i in range(2)]
            for t in range(NT):
                tok = slice(t * 128, (t + 1) * 128)
                ps, bps = self.ringA.next()
                for kc in range(8):
                    self.MM(ps[:, 0:384], self.hT[:, kc, tok], wb[:, kc, :], kc == 0, kc == 7, [self.bhT[t // 4], bw], [bps])
                s6, m8, bs = st[t % 2]
                cnt, bcn = cn[t % 2]
                self.V(lambda h: h.bn_stats(out=s6[:, 0, :], in_=ps[:, 0:256]), [bps], [bs])
                self.V(lambda h: h.bn_stats(out=s6[:, 1, :], in_=ps[:, 256:384]), [bps], [bs])
                for i in range(2):
                    self.V(lambda h: h.bn_aggr(out=m8[:, 4 * i:4 * i + 2], in_=s6[:, i:i + 1, :]), [bs], [bs])
                    self.V(lambda h: h.scalar_tensor_tensor(out=m8[:, 4 * i + 2:4 * i + 3], in0=m8[:, 4 * i:4 * i + 1],
                                                            scalar=m8[:, 4 * i:4 * i + 1], in1=m8[:, 4 * i + 1:4 * i + 2],
                                                            op0=ALU.mult, op1=ALU.add), [bs], [bs])
                    self.V(lambda h: h.tensor_scalar_add(out=m8[:, 4 * i + 2:4 * i + 3], in0=m8[:, 4 * i + 2:4 * i + 3],
                                                         scalar1=RMS_EPS), [bs], [bs])
                    self.A(lambda h: h.sqrt(out=m8[:, 4 * i + 2:4 * i + 3], in_=m8[:, 4 * i + 2:4 * i + 3]), [bs], [bs])
                    self.V(lambda h: h.reciprocal(out=m8[:, 4 * i + 3:4 * i + 4], in_=m8[:, 4 * i + 2:4 * i + 3]), [bs], [bs])
                self.V(lambda h: h.scalar_tensor_tensor(out=cnt[:, 0:256], in0=ps[:, 0:256], scalar=m8[:, 3:4], in1=gq[:],
                                                        op0=ALU.mult, op1=ALU.mult), [bps, bs, bw], [bcn])
                self.V(lambda h: h.scalar_tensor_tensor(out=cnt[:, 256:384], in0=ps[:, 256:384], scalar=m8[:, 7:8],
                                                        in1=gkv[:], op0=ALU.mult, op1=ALU.mult), [bps, bs, bw], [bcn])
                pt, bpt = self.ringT.next()
                for c in range(3):
                    self.TR(pt[:, c * 128:(c + 1) * 128], cnt[:, c * 128:(c + 1) * 128], [bcn], [bpt], inc=(c == 2))
                self.A(lambda h: h.copy(out=cqnT[:, :, tok], in_=pt[:, 0:256].rearrange("p (k c) -> p k c", k=2)),
                       [bpt], [bcT[t // 4]])
                self.A(lambda h: h.copy(out=ckvT[:, tok], in_=pt[:, 256:384]), [bpt], [bcT[t // 4]])
                pv_, bpv = self.ringB.next()
                self.MM(pv_[:, 0:256].rearrange("p (h x) -> p h x", h=4), ckvT[:, tok],
                        wukv[:].rearrange("p (h x) -> p h x", h=4)[:, :, 64:128], True, True,
                        [bcT[t // 4], bw], [bpv])
                self.A(lambda h: h.copy(out=Vaug[:, t, :, 0:64], in_=pv_[:, 0:256].rearrange("p (h x) -> p h x", h=4)),
                       [bpv], [bv])
            t1s = [(fw.sb(e1, f"r1{i}", [96, 512], F32), Buf()) for i in range(2)]
            t2s = [(fw.sb(e1, f"r2{i}", [96, 512], F32), Buf()) for i in range(2)]
            k = 0
            for g in range(NG):
                tok = slice(g * 512, (g + 1) * 512)
                rC, rS, brp, chr_ = rc[g % 2]
                self.ldma(rC[:], dr["ropeC"][:, tok], chr_, writes=[brp])
                self.ldma(rS[:], dr["ropeS"][:, tok], chr_, writes=[brp])
                for hd in range(4):
                    pq, bpq = self.ringA.next()
                    pp, bpp = self.ringA.next()
                    for kc in range(2):
                        self.MM(pq[0:96, :], wuq[:, kc, hd * 96:(hd + 1) * 96], cqnT[:, kc, tok], kc == 0, kc == 1,
                                [bw, bcT[g]], [bpq])
                    for kc in range(2):
                        self.MM(pp[0:96, :], wuqp[:, kc, hd * 96:(hd + 1) * 96], cqnT[:, kc, tok], kc == 0, kc == 1,
                                [bw, bcT[g]], [bpp])
                    t1, b1 = t1s[k % 2]
                    t2, b2 = t2s[k % 2]
                    k += 1
                    self.V(lambda h: h.tensor_tensor(out=t1[:], in0=pq[0:96, :], in1=rC[:], op=ALU.mult), [bpq, brp], [b1])
                    self.V(lambda h: h.tensor_tensor(out=t2[:], in0=pp[0:96, :], in1=rS[:], op=ALU.mult), [bpp, brp], [b2])
                    self.G(lambda h: h.tensor_tensor(out=QT[hd][:, tok], in0=t1[:], in1=t2[:], op=ALU.add), [b1, b2], bq[g])
                    pk, bpk = self.ringA.next()
                    self.MM(pk[0:64, :], wukv[:, hd * 128:hd * 128 + 64], ckvT[:, tok], True, True, [bw, bcT[g]], [bpk])
                    self.A(lambda h: h.copy(out=KT[hd][0:64, tok], in_=pk[0:64, :]), [bpk], bk[g])
                pq, bpq = self.ringA.next()
                pp, bpp = self.ringA.next()
                for kc in range(8):
                    self.MM(pq[0:96, :], wkr[:, kc, 0:96], self.hT[:, kc, tok], kc == 0, kc == 7, [bw, self.bhT[g]], [bpq])
                for kc in range(8):
                    self.MM(pp[0:96, :], wkr[:, kc, 96:192], self.hT[:, kc, tok], kc == 0, kc == 7, [bw, self.bhT[g]], [bpp])
                t1, b1 = t1s[k % 2]
                t2, b2 = t2s[k % 2]
                k += 1
                self.V(lambda h: h.tensor_tensor(out=t1[64:96, :], in0=pq[64:96, :], in1=rC[64:96, :], op=ALU.mult),
                       [bpq, brp], [b1])
                self.V(lambda h: h.tensor_tensor(out=t2[64:96, :], in0=pp[64:96, :], in1=rS[64:96, :], op=ALU.mult),
                       [bpp, brp], [b2])
                for hd in range(4):
                    self.G(lambda h: h.tensor_tensor(out=KT[hd][64:96, tok], in0=t1[64:96, :], in1=t2[64:96, :], op=ALU.add),
                           [b1, b2], bk[g])
            fw.barrier()
        with ExitStack() as e2:
            mixT = fw.sb(e2, "mixT", [64, 4, S], BF16)
            bmix = Buf()
            self.attention(e2, QT, KT, bq, bk, 96, Vaug, bv, float(96 ** -0.5), mixT, bmix)
            if "mla_o" in self.dumps:
                self.dump_bf16(e2, "mla_o", mixT[:], [64, 4, S], [bmix])
            self.out_proj(e2, l, m, mixT, bmix, first)
            fw.barrier()

    def dump_bf16(self, es, name, ap, shape, reads):
        if name not in self.dumps or name in self.dump_aps:
            return
        t = self.fw.sb(es, "dmp" + name, shape, F32)
        b = Buf()
        self.V(lambda h: h.tensor_copy(out=t[:], in_=ap), reads, [b])
        self.dump(name, t[:], shape, [b])
        self.fw.barrier()

    def mix_fox(self, es, s, l, m, first):
        fw, dr, NT, NG, S = self.fw, self.dr, self.NT, self.NG, self.S
        QT = [fw.sb(es, f"fQT{h}", [68, S], BF16) for h in range(4)]
        KT = [fw.sb(es, f"fKT{h}", [68, S], BF16) for h in range(4)]
        one = Buf()
        bq = [[one, one] for _ in range(NG)]
        bk = [[one, one] for _ in range(NG)]
        allaug = [one]
        Vaug = fw.sb(es, "fVaug", [128, NT, 4, 65], BF16)
        bv = Buf()
        self.G(lambda h: h.memset(Vaug[:], 1.0), [], [bv])
        for hd in range(4):
            self.G(lambda h: h.memset(QT[hd][64:68, :], 1.0), [], allaug)
            self.G(lambda h: h.memset(KT[hd][64:68, :], 1.0), [], allaug)
        with ExitStack() as e1:
            wv = dr["mix_w_in"][l].rearrange("(kc p) n -> p kc n", p=128)
            wc = fw.sb(e1, "wC", [128, 8, 772], BF16)
            bw = Buf()
            ch = fw.new_chan("foxw")
            self.wdma(wc[:], wv[:, :, 1440:2212], ch, writes=[bw])
            nbf = fw.sb(e1, "nbf", [4, 1], F32)
            self.ldma(nbf[:], dr["fox_b_f"][l:l + 1, :].rearrange("o h -> h o"), ch, writes=[bw])
            self.V(lambda h: h.tensor_scalar_mul(out=nbf[:], in0=nbf[:], scalar1=-1.0), [bw], [bw])
            ones4 = fw.sb(e1, "ones4", [4, 512], F32)
            carry = fw.sb(e1, "carry", [4, 1], F32)
            bcar = Buf()
            self.G(lambda h: h.memset(ones4[:], 1.0), [], [bcar])
            self.G(lambda h: h.memset(carry[:], 0.0), [bcar], [bcar])
            ch_row = [fw.new_chan(f"frow{i}") for i in range(2)]
            for t in range(NT):
                tok = slice(t * 128, (t + 1) * 128)
                pv_, bpv = self.ringB.next()
                for kc in range(8):
                    self.MM(pv_[:, 0:256], self.hT[:, kc, tok], wc[:, kc, 512:768], kc == 0, kc == 7,
                            [self.bhT[t // 4], bw], [bpv])
                self.A(lambda h: h.copy(out=Vaug[:, t, :, 0:64], in_=pv_[:, 0:256].rearrange("p (h x) -> p h x", h=4)),
                       [bpv], [bv])
            ft = [dict(e=fw.sb(e1, f"fe{i}", [4, 512], F32), fn=fw.sb(e1, f"ffn{i}", [4, 512], F32),
                       hi=fw.sb(e1, f"fhi{i}", [4, 512], BF16), hif=fw.sb(e1, f"fhf{i}", [4, 512], F32),
                       lo=fw.sb(e1, f"flo{i}", [4, 512], BF16), nhi=fw.sb(e1, f"fnh{i}", [4, 512], BF16),
                       nlo=fw.sb(e1, f"fnl{i}", [4, 512], BF16), b=Buf()) for i in range(2)]
            for g in range(NG):
                tok = slice(g * 512, (g + 1) * 512)
                for hd in range(4):
                    for which, dst, bdst in ((0, QT, bq[g][0]), (1, KT, bk[g][0])):
                        pq, bpq = self.ringA.next()
                        for kc in range(8):
                            self.MM(pq[0:64, :], wc[:, kc, which * 256 + hd * 64:which * 256 + (hd + 1) * 64],
                                    self.hT[:, kc, tok], kc == 0, kc == 7, [bw, self.bhT[g]], [bpq])
                        self.A(lambda h: h.copy(out=dst[hd][0:64, tok], in_=pq[0:64, :]), [bpq], [bdst])
                pf, bpf = self.ringA.next()
                for kc in range(8):
                    self.MM(pf[0:4, :], wc[:, kc, 768:772], self.hT[:, kc, tok], kc == 0, kc == 7, [bw, self.bhT[g]], [bpf])
                f = ft[g % 2]
                bf_ = f["b"]
                self.A(lambda h: h.activation(out=f["e"][:], in_=pf[0:4, :], func=AF.Exp, bias=nbf[:], scale=-1.0),
                       [bpf, bw], [bf_])
                self.A(lambda h: h.activation(out=f["e"][:], in_=f["e"][:], func=AF.Ln, bias=1.0, scale=1.0), [bf_], [bf_])
                self.V(lambda h: h.tensor_tensor_scan(out=f["fn"][:], data0=ones4[:], data1=f["e"][:], initial=carry[:],
                                                      op0=ALU.mult, op1=ALU.add), [bf_, bcar], [bf_])
                self.V(lambda h: h.tensor_copy(out=carry[:], in_=f["fn"][:, 511:512]), [bf_], [bcar])
                self.V(lambda h: h.tensor_scalar_mul(out=f["fn"][:], in0=f["fn"][:], scalar1=8.0), [bf_], [bf_])
                self.V(lambda h: h.tensor_copy(out=f["hi"][:], in_=f["fn"][:]), [bf_], [bf_])
                self.V(lambda h: h.tensor_copy(out=f["hif"][:], in_=f["hi"][:]), [bf_], [bf_])
                self.V(lambda h: h.tensor_tensor(out=f["lo"][:], in0=f["fn"][:], in1=f["hif"][:], op=ALU.subtract),
                       [bf_], [bf_])
                self.V(lambda h: h.tensor_scalar_mul(out=f["nhi"][:], in0=f["hi"][:], scalar1=-1.0), [bf_], [bf_])
                self.V(lambda h: h.tensor_scalar_mul(out=f["nlo"][:], in0=f["lo"][:], scalar1=-1.0), [bf_], [bf_])
                chr_ = ch_row[g % 2]
                for hd in range(4):
                    self.ldma(QT[hd][64:65, tok], f["nhi"][hd:hd + 1, :], chr_, reads=[bf_], writes=[bq[g][1]])
                    self.ldma(QT[hd][65:66, tok], f["nlo"][hd:hd + 1, :], chr_, reads=[bf_], writes=[bq[g][1]])
                    self.ldma(KT[hd][66:67, tok], f["hi"][hd:hd + 1, :], chr_, reads=[bf_], writes=[bk[g][1]])
                    self.ldma(KT[hd][67:68, tok], f["lo"][hd:hd + 1, :], chr_, reads=[bf_], writes=[bk[g][1]])
            fw.barrier()
        with ExitStack() as e2:
            mixT = fw.sb(e2, "fmixT", [64, 4, S], BF16)
            bmix = Buf()
            self.attention(e2, QT, KT, bq, bk, 68, Vaug, bv, 0.125, mixT, bmix)
            if "fox_o" in self.dumps:
                self.dump_bf16(e2, "fox_o", mixT[:], [64, 4, S], [bmix])
            self.out_proj(e2, l, m, mixT, bmix, first)
            fw.barrier()

    def mix_gmlp(self, es, s, l, m, first):
        fw, dr, NT, NG, S = self.fw, self.dr, self.NT, self.NG, self.S
        wv = dr["mix_w_in"][l].rearrange("(kc p) n -> p kc n", p=128)
        wd = fw.sb(es, "wD", [128, 8, 512], BF16)
        bw = Buf()
        ch = fw.new_chan("gmw")
        self.wdma(wd[:], wv[:, :, 2212:2724], ch, writes=[bw])
        wsf = fw.sb(es, "wsf", [128, 4, 128], F32)
        caus = fw.sb(es, "caus", [128, 128], F32)
        wsm = fw.sb(es, "wsm", [128, 4, 128], BF16)
        bsb = fw.sb(es, "bsb", [64, 4, 128], F32)
        lng = fw.sb(es, "glng", [128, 256], F32)
        lnb = fw.sb(es, "glnb", [128, 256], F32)
        self.ldma(wsf[:], dr["gmlp_wsT"][l].rearrange("g s t -> s g t"), ch, writes=[bw])
        self.ldma(caus[:], dr["caus"], ch, writes=[bw])
        self.ldma(bsb[:], dr["gmlp_b_s"][l:l + 1].broadcast_to([64, 4, 128]), ch, writes=[bw])
        self.ldma(lng[:], dr["gmlp_ln_g"][l:l + 1, :].broadcast_to([128, 256]), ch, writes=[bw])
        self.ldma(lnb[:], dr["gmlp_ln_b"][l:l + 1, :].broadcast_to([128, 256]), ch, writes=[bw])
        self.V(lambda h: h.tensor_tensor(out=wsm[:], in0=wsf[:], in1=caus[:].unsqueeze(1).broadcast_to([128, 4, 128]),
                                         op=ALU.mult), [bw], [bw])
        mixT = fw.sb(es, "gmixT", [64, 4, S], BF16)
        bmix = Buf()
        self.mod_pending = []
        if s == 0 and l == 0 and self.mod_deferred:
            self.mod_alloc(es)
            self.mod_pending = [(ll, cb) for ll in range(1, self.L) for cb in range(18)]
        gv = [(fw.sb(es, f"gv{i}", [128, 256], F32), Buf()) for i in range(2)]
        vl = [(fw.sb(es, f"vl{i}", [128, 256], BF16), Buf()) for i in range(2)]
        st = [(fw.sb(es, f"gst{i}", [128, 6], F32), fw.sb(es, f"gmv{i}", [128, 4], F32), Buf()) for i in range(2)]
        gu = [(fw.sb(es, f"gu{i}", [64, 512], F32), Buf()) for i in range(2)]
        t1s = [(fw.sb(es, f"gt{i}", [64, 512], F32), Buf()) for i in range(2)]
        def stageA(t):
            tok = slice(t * 128, (t + 1) * 128)
            bh = self.bhT[t // 4]
            pv_, bpv = self.ringA.next()
            for kc in range(8):
                self.MM(pv_[:, 0:256], self.hT[:, kc, tok], wd[:, kc, 256:512], kc == 0, kc == 7, [bh, bw], [bpv])
            g_, bg = gv[t % 2]
            v_, bvl = vl[t % 2]
            s6, m4, bs = st[t % 2]
            self.A(lambda h: h.activation(out=g_[:], in_=pv_[:, 0:256], func=AF.Gelu_apprx_tanh), [bpv], [bg])
            self.V(lambda h: h.bn_stats(out=s6[:], in_=g_[:]), [bg], [bs])
            self.V(lambda h: h.bn_aggr(out=m4[:, 0:2], in_=s6[:]), [bs], [bs])
            self.V(lambda h: h.tensor_scalar_add(out=m4[:, 2:3], in0=m4[:, 1:2], scalar1=LN_EPS), [bs], [bs])
            self.A(lambda h: h.sqrt(out=m4[:, 2:3], in_=m4[:, 2:3]), [bs], [bs])
            self.V(lambda h: h.reciprocal(out=m4[:, 2:3], in_=m4[:, 2:3]), [bs], [bs])
            self.V(lambda h: h.scalar_tensor_tensor(out=m4[:, 3:4], in0=m4[:, 0:1], scalar=-1.0, in1=m4[:, 2:3],
                                                    op0=ALU.mult, op1=ALU.mult), [bs], [bs])
            self.A(lambda h: h.activation(out=g_[:], in_=g_[:], func=AF.Identity, bias=m4[:, 3:4], scale=m4[:, 2:3]),
                   [bs, bg], [bg])
            self.V(lambda h: h.tensor_tensor(out=g_[:], in0=g_[:], in1=lng[:], op=ALU.mult), [bg, bw], [bg])
            self.G(lambda h: h.tensor_tensor(out=v_[:], in0=g_[:], in1=lnb[:], op=ALU.add), [bg, bw], [bvl])

        def stageB(t):
            tok = slice(t * 128, (t + 1) * 128)
            bh = self.bhT[t // 4]
            v_, bvl = vl[t % 2]
            pm, bpm = self.ringA.next()
            for g in range(4):
                self.MM(pm[0:64, g * 128:(g + 1) * 128], v_[:, g * 64:(g + 1) * 64], wsm[:, g, :], g == 0, True,
                        [bvl, bw], [bpm], inc=(g == 3), skip=True)
            pu, bpu = self.ringA.next()
            for g in range(4):
                for kc in range(8):
                    self.MM(pu[0:64, g * 128:(g + 1) * 128], wd[:, kc, g * 64:(g + 1) * 64], self.hT[:, kc, tok],
                            kc == 0, kc == 7, [bh, bw], [bpu], inc=(g == 3 and kc == 7), skip=True)
            u_, bu = gu[t % 2]
            t1, b1 = t1s[t % 2]
            self.A(lambda h: h.activation(out=u_[:], in_=pu[0:64, :], func=AF.Gelu_apprx_tanh), [bpu], [bu])
            self.V(lambda h: h.tensor_tensor(out=t1[:], in0=pm[0:64, :], in1=bsb[:].rearrange("p g t -> p (g t)"),
                                             op=ALU.add), [bpm, bw], [b1])
            self.V(lambda h: h.tensor_tensor(out=mixT[0:64, :, tok], in0=t1[:].rearrange("p (g t) -> p g t", g=4),
                                             in1=u_[:].rearrange("p (g t) -> p g t", g=4), op=ALU.mult),
                   [b1, bu], [bmix])
            for _ in range(2):
                if self.mod_pending:
                    self.mod_block(*self.mod_pending.pop(0))

        stageA(0)
        for t in range(NT):
            if t + 1 < NT:
                stageA(t + 1)
            stageB(t)
        while self.mod_pending:
            self.mod_block(*self.mod_pending.pop(0))
        if "gmlp_o" in self.dumps:
            self.dump_bf16(es, "gmlp_o", mixT[:], [64, 4, S], [bmix])
        self.out_proj(es, l, m, mixT, bmix, first)

    def mix_hgrn(self, es, s, l, m, first):
        fw, dr, NT, NG, S = self.fw, self.dr, self.NT, self.NG, self.S
        wv = dr["mix_w_in"][l].rearrange("(kc p) n -> p kc n", p=128)
        wa = fw.sb(es, "wA", [128, 8, 1024], BF16)
        bw = Buf()
        ch = fw.new_chan("hgw")
        self.wdma(wa[:], wv[:, :, 0:1024], ch, writes=[bw])
        tri = fw.sb(es, "tri", [128, 128], F32)
        blk = fw.sb(es, "blk", [128, 128], F32)
        csel = fw.sb(es, "csel", [128, 8], F32)
        cselb = fw.sb(es, "cselb", [128, 8], BF16)
        ones64 = fw.sb(es, "ones64", [64, 64], F32)
        ng = fw.sb(es, "ng", [64, 4], F32)
        lb = fw.sb(es, "lb", [128, 256], F32)
        oml = fw.sb(es, "oml", [128, 256], F32)
        self.ldma(tri[:], dr["tri"], ch, writes=[bw])
        self.ldma(blk[:], dr["blk"], ch, writes=[bw])
        self.ldma(csel[:], dr["csel"], ch, writes=[bw])
        self.ldma(ones64[:], dr["ones64"], ch, writes=[bw])
        self.ldma(ng[:], dr["hgrn_ng"][l], ch, writes=[bw])
        self.V(lambda h: h.tensor_copy(out=cselb[:], in_=csel[:]), [bw], [bw])
        if l == 0:
            self.G(lambda h: h.memset(lb[:], 0.0), [], [bw])
        else:
            assert self.L == 2
            self.ldma(lb[:], dr["hgrn_lb_logits"][1:2, :].broadcast_to([128, 256]), ch, writes=[bw])
            self.ldma(oml[:], dr["hgrn_lb_logits"][0:1, :].broadcast_to([128, 256]), ch, writes=[bw])
            self.V(lambda h: h.tensor_tensor(out=lb[:], in0=lb[:], in1=oml[:], op=ALU.subtract), [bw], [bw])
            self.A(lambda h: h.activation(out=lb[:], in_=lb[:], func=AF.Sigmoid), [bw], [bw])
        self.V(lambda h: h.tensor_scalar(out=oml[:], in0=lb[:], scalar1=-1.0, scalar2=1.0, op0=ALU.mult, op1=ALU.add),
               [bw], [bw])
        mixT = fw.sb(es, "hmixT", [64, 4, S], BF16)
        bmix = Buf()
        Sst = fw.sb(es, "Sst", [64, 4, 64], F32)
        Sbf = fw.sb(es, "Sbf", [64, 4, 64], BF16)
        bS = Buf()
        bSb = Buf()
        self.G(lambda h: h.memset(Sst[:], 0.0), [], [bS])
        self.G(lambda h: h.memset(Sbf[:], 0.0), [], [bSb])

        def T(name, shape, dt, n=2):
            r = [(fw.sb(es, f"{name}{i}", shape, dt), Buf()) for i in range(n)]
            return r * (2 // n)
        f_ = T("hf", [128, 256], F32)
        lf_ = T("hlf", [128, 256], F32)
        kk_ = T("hkk", [128, 256], F32)
        eg_ = T("heg", [128, 768], F32)
        qf_ = T("hqf", [128, 256], F32)
        qt_ = T("hqt", [128, 256], BF16)
        kh_ = T("hkh", [128, 256], F32)
        khb_ = T("hkhb", [128, 256], BF16)
        ke_ = T("hke", [128, 256], BF16)
        kem_ = T("hkem", [128, 8, 256], BF16, 1)
        vb_ = T("hvb", [128, 256], BF16)
        sgb_ = T("hsg", [128, 256], BF16)
        Dt_ = T("hDt", [64, 4, 8], F32)
        qkT_ = T("hqkT", [64, 8, 128], BF16)
        gT_ = T("hgT", [64, 4, 128], BF16)
        scm_ = T("hscm", [128, 4, 128], BF16)
        osq_ = T("hosq", [64, 512], F32, 1)
        rs_ = T("hrs", [64, 512], F32, 1)
        t1_ = T("ht1", [64, 512], F32, 1)
        for t in range(NT):
            tok = slice(t * 128, (t + 1) * 128)
            bh = self.bhT[t // 4]
            i2 = t % 2
            pa, bpa = self.ringA.next()
            pb, bpb = self.ringA.next()
            for kc in range(8):
                self.MM(pa[:], self.hT[:, kc, tok], wa[:, kc, 0:512], kc == 0, kc == 7, [bh, bw], [bpa])
            for kc in range(8):
                self.MM(pb[:], self.hT[:, kc, tok], wa[:, kc, 512:1024], kc == 0, kc == 7, [bh, bw], [bpb])
            f, bf_ = f_[i2]
            lf, blf = lf_[i2]
            kk, bkk = kk_[i2]
            eg, beg = eg_[i2]
            qf, bqf = qf_[i2]
            qt, bqt = qt_[i2]
            kh, bkh = kh_[i2]
            khb, bkhb = khb_[i2]
            ke, bke = ke_[i2]
            kem, bkem = kem_[i2]
            vb, bvb = vb_[i2]
            sgb, bsgb = sgb_[i2]
            Dt, bDt = Dt_[i2]
            qkT, bqkT = qkT_[i2]
            gT, bgT = gT_[i2]
            scm, bscm = scm_[i2]
            osq, bosq = osq_[i2]
            rs, brs = rs_[i2]
            t1, bt1 = t1_[i2]
            self.A(lambda h: h.activation(out=f[:], in_=pa[:, 256:512], func=AF.Sigmoid), [bpa], [bf_])
            self.A(lambda h: h.activation(out=qf[:], in_=pa[:, 0:256], func=AF.Silu), [bpa], [bqf])
            self.A(lambda h: h.copy(out=vb[:], in_=pb[:, 0:256]), [bpb], [bvb])
            self.A(lambda h: h.activation(out=sgb[:], in_=pb[:, 256:512], func=AF.Silu), [bpb], [bsgb])
            self.V(lambda h: h.tensor_tensor(out=f[:], in0=f[:], in1=oml[:], op=ALU.mult), [bf_, bw], [bf_])
            self.V(lambda h: h.tensor_tensor(out=f[:], in0=f[:], in1=lb[:], op=ALU.add), [bf_, bw], [bf_])
            self.A(lambda h: h.activation(out=lf[:], in_=f[:], func=AF.Ln), [bf_], [blf])
            self.V(lambda h: h.tensor_scalar(out=kk[:], in0=f[:], scalar1=-1.0, scalar2=1.0, op0=ALU.mult, op1=ALU.add),
                   [bf_], [bkk])
            pg, bpg = self.ringA.next()
            self.MM(pg[:, 0:256], tri[:], lf[:], True, True, [bw, blf], [bpg], inc=False)
            self.MM(pg[:, 256:512], blk[:], lf[:], True, True, [bw, blf], [bpg], skip=True)
            pd, bpd = self.ringB.next()
            for hd in range(4):
                self.MM(pd[0:64, hd * 8:(hd + 1) * 8], lf[:, hd * 64:(hd + 1) * 64], csel[:], hd == 0, True,
                        [bw, blf], [bpd], inc=(hd == 3), skip=True)
            self.A(lambda h: h.activation(out=eg[:, 0:256], in_=pg[:, 0:256], func=AF.Exp), [bpg], [beg])
            self.A(lambda h: h.activation(out=eg[:, 256:512], in_=pg[:, 0:256], func=AF.Exp, scale=-1.0), [bpg], [beg])
            self.A(lambda h: h.activation(out=eg[:, 512:768], in_=pg[:, 256:512], func=AF.Exp), [bpg], [beg])
            self.A(lambda h: h.activation(out=Dt[:].rearrange("p h c -> p (h c)"), in_=pd[0:64, 0:32], func=AF.Exp),
                   [bpd], [bDt])
            self.V(lambda h: h.tensor_tensor(out=qt[:], in0=qf[:], in1=eg[:, 0:256], op=ALU.mult), [bqf, beg], [bqt])
            self.V(lambda h: h.tensor_tensor(out=kh[:], in0=kk[:], in1=eg[:, 256:512], op=ALU.mult), [bkk, beg], [bkh])
            self.G(lambda h: h.tensor_copy(out=khb[:], in_=kh[:]), [bkh], [bkhb])
            self.G(lambda h: h.tensor_tensor(out=ke[:], in0=kh[:], in1=eg[:, 512:768], op=ALU.mult), [bkh, beg], [bke])
            self.G(lambda h: h.tensor_tensor(out=kem[:], in0=ke[:].unsqueeze(1).broadcast_to([128, 8, 256]),
                                             in1=cselb[:].unsqueeze(2).broadcast_to([128, 8, 256]), op=ALU.mult),
                   [bke, bw], [bkem])
            pt, bpt = self.ringT.next()
            for hd in range(4):
                self.TR(pt[0:64, hd * 128:(hd + 1) * 128], qt[:, hd * 64:(hd + 1) * 64], [bqt], [bpt], inc=False)
            for hd in range(4):
                self.TR(pt[0:64, 512 + hd * 128:512 + (hd + 1) * 128], khb[:, hd * 64:(hd + 1) * 64], [bkhb], [bpt],
                        inc=(hd == 3))
            self.A(lambda h: h.copy(out=qkT[:].rearrange("p a t -> p (a t)"), in_=pt[0:64, :]), [bpt], [bqkT])
            pt2, bpt2 = self.ringT.next()
            for hd in range(4):
                self.TR(pt2[0:64, hd * 128:(hd + 1) * 128], sgb[:, hd * 64:(hd + 1) * 64], [bsgb], [bpt2], inc=(hd == 3))
            self.A(lambda h: h.copy(out=gT[:].rearrange("p a t -> p (a t)"), in_=pt2[0:64, 0:512]), [bpt2], [bgT])
            psc, bpsc = self.ringA.next()
            for hd in range(4):
                self.MM(psc[:, hd * 128:(hd + 1) * 128], qkT[:, 4 + hd, :], qkT[:, hd, :], hd == 0, True,
                        [bqkT], [bpsc], inc=(hd == 3), skip=True)
            self.V(lambda h: h.tensor_tensor(out=scm[:], in0=psc[:].rearrange("p (a t) -> p a t", a=4),
                                             in1=tri[:].unsqueeze(1).broadcast_to([128, 4, 128]), op=ALU.mult),
                   [bpsc, bw], [bscm])
            po, bpo = self.ringB.next()
            firstmm = True
            for ci in range(8):
                c = t * 8 + ci
                if c > 0:
                    for hd in range(4):
                        self.MM(po[0:64, hd * 128 + ci * 16:hd * 128 + (ci + 1) * 16], Sbf[:, hd, :],
                                qkT[:, hd, ci * 16:(ci + 1) * 16], firstmm, False, [bSb, bqkT], [bpo], inc=False, skip=True)
                        firstmm = False
                pkv, bpkv = self.ringA.next()
                for hd in range(4):
                    self.MM(pkv[0:64, hd * 64:(hd + 1) * 64], kem[:, ci, hd * 64:(hd + 1) * 64], vb[:, hd * 64:(hd + 1) * 64],
                            hd == 0, True, [bkem, bvb], [bpkv], inc=(hd == 3), skip=True)
                self.V(lambda h: h.tensor_tensor(out=Sst[:], in0=Sst[:], in1=Dt[:, :, ci:ci + 1].broadcast_to([64, 4, 64]),
                                                 op=ALU.mult), [bS, bDt], [bS])
                self.V(lambda h: h.tensor_tensor(out=Sst[:], in0=Sst[:], in1=pkv[0:64, 0:256].rearrange("p (a b) -> p a b", a=4),
                                                 op=ALU.add), [bS, bpkv], [bS])
                self.V(lambda h: h.tensor_copy(out=Sbf[:], in_=Sst[:]), [bS], [bSb])
            for hd in range(4):
                self.MM(po[0:64, hd * 128:(hd + 1) * 128], vb[:, hd * 64:(hd + 1) * 64], scm[:, hd, :], firstmm, True,
                        [bvb, bscm], [bpo], inc=(hd == 3), skip=True)
                firstmm = False
            self.A(lambda h: h.activation(out=osq[:], in_=po[0:64, :], func=AF.Square), [bpo], [bosq])
            pss, bpss = self.ringA.next()
            self.MM(pss[0:64, :], ones64[:], osq[:], True, True, [bw, bosq], [bpss])
            self.V(lambda h: h.tensor_scalar(out=rs[:], in0=pss[0:64, :], scalar1=1.0 / 64.0, scalar2=RMS_EPS,
                                             op0=ALU.mult, op1=ALU.add), [bpss], [brs])
            self.A(lambda h: h.activation(out=rs[:], in_=rs[:], func=AF.Ln), [brs], [brs])
            self.A(lambda h: h.activation(out=rs[:], in_=rs[:], func=AF.Exp, scale=-0.5), [brs], [brs])
            self.V(lambda h: h.tensor_tensor(out=t1[:], in0=po[0:64, :], in1=rs[:], op=ALU.mult), [bpo, brs], [bt1])
            for hd in range(4):
                self.V(lambda h: h.scalar_tensor_tensor(out=mixT[0:64, hd, tok], in0=t1[:, hd * 128:(hd + 1) * 128],
                                                        scalar=ng[:, hd:hd + 1], in1=gT[:, hd, :],
                                                        op0=ALU.mult, op1=ALU.mult), [bt1, bw, bgT], [bmix])
        if "hgrn_o" in self.dumps:
            self.dump_bf16(es, "hgrn_o", mixT[:], [64, 4, S], [bmix])
        self.out_proj(es, l, m, mixT, bmix, first)


_NC_CACHE = {}


def _get_nc(S, L):
    key = (S, L)
    if key not in _NC_CACHE:
        _NC_CACHE[key] = Builder(S, L).build()
    return _NC_CACHE[key]


def make_in_maps(inputs, n_cores, S):
    w = host_weights(inputs)
    consts = host_consts(S)
    x = np.asarray(inputs["x"], dtype=np.float32)
    c = np.asarray(inputs["c"], dtype=np.float32)
    maps = []
    for i in range(n_cores):
        cc = c[2 * i:2 * i + 2]
        cT = np.ascontiguousarray(cc.reshape(2, 8, 128).transpose(2, 1, 0))
        mp = {"x": np.ascontiguousarray(x[2 * i:2 * i + 2]), "cT": cT}
        mp.update(w)
        mp.update(consts)
        maps.append(mp)
    return maps


def kernel(**inputs):
    x = np.asarray(inputs["x"])
    B, S, _ = x.shape
    L = np.asarray(inputs["ada_w"]).shape[0]
    n = B // 2
    nc = _get_nc(S, L)
    maps = make_in_maps(inputs, n, S)
    res = run_bass_kernel_spmd(nc, maps, core_ids=list(range(n)))
    out = np.concatenate([r["out"] for r in res.results], axis=0)
    return out.astype(np.float32)
```

```python
import numpy as np
from contextlib import ExitStack
import concourse.bass as bass
import concourse.mybir as mybir
from concourse.bass_utils import run_bass_kernel_spmd

F32 = mybir.dt.float32
BF16 = mybir.dt.bfloat16
AF = mybir.ActivationFunctionType
ALU = mybir.AluOpType

D = 1024
DFF = 2816
NJ = 22
MIXC = 2724
ALPHA = float(4 ** 0.25)
LN_EPS = 1e-5
RMS_EPS = 1e-6
NEG = -30000.0


class Buf:
    __slots__ = ("w", "r", "name")

    def __init__(self, name=""):
        self.w = {}
        self.r = {}
        self.name = name


class Chan:
    def __init__(self, sem, name):
        self.sem = sem
        self.cnt = 0
        self.name = name


class Eng:
    def __init__(self, name, h, chan):
        self.name = name
        self.h = h
        self.chan = chan
        self.seen = {}


class FW:
    def __init__(self, nc, es):
        self.nc = nc
        self.es = es
        self.nsem = 0
        self.chans = []
        self.pe = Eng("pe", nc.tensor, self.new_chan("pe"))
        self.act = Eng("act", nc.scalar, self.new_chan("act"))
        self.dve = Eng("dve", nc.vector, self.new_chan("dve"))
        self.pool = Eng("pool", nc.gpsimd, self.new_chan("pool"))
        self.sp = Eng("sp", nc.sync, self.new_chan("sp"))
        self.engs = [self.pe, self.act, self.dve, self.pool, self.sp]
        self.n_inst = 0
        self.n_wait = 0
        self.uid = 0

    def new_chan(self, name):
        for ch in self.chans:
            if ch.name == name and name not in ("dbg",):
                return ch
        sem = self.es.enter_context(self.nc.semaphore(f"s_{name}_{self.nsem}"))
        self.nsem += 1
        ch = Chan(sem, name)
        self.chans.append(ch)
        return ch

    def _sync(self, E, reads, writes):
        need = {}
        for b in reads:
            for ch, c in b.w.items():
                if need.get(ch, 0) < c:
                    need[ch] = c
        for b in writes:
            for ch, c in b.w.items():
                if need.get(ch, 0) < c:
                    need[ch] = c
            for ch, c in b.r.items():
                if need.get(ch, 0) < c:
                    need[ch] = c
        for ch, c in need.items():
            if ch is E.chan and E.name == "pe":
                continue
            if E.seen.get(ch, 0) >= c:
                continue
            E.h.wait_ge(ch.sem, c)
            self.n_wait += 1
            E.seen[ch] = c

    def _mark(self, ch, mark, reads, writes):
        for b in reads:
            if b.r.get(ch, 0) < mark:
                b.r[ch] = mark
        for b in writes:
            b.w[ch] = mark
            b.r = {}

    def op(self, E, emit, reads=(), writes=(), inc=True):
        self._sync(E, reads, writes)
        ins = emit(E.h)
        self.n_inst += 1
        ch = E.chan
        if inc:
            ch.cnt += 1
            ins.then_inc(ch.sem, 1)
            mark = ch.cnt
        else:
            mark = ch.cnt + 1
        self._mark(ch, mark, reads, writes)
        return ins

    def dma(self, Q, chan, out, in_, reads=(), writes=(), **kw):
        self._sync(Q, reads, writes)
        ins = Q.h.dma_start(out=out, in_=in_, **kw)
        self.n_inst += 1
        chan.cnt += 16
        ins.then_inc(chan.sem, 16)
        self._mark(chan, chan.cnt, reads, writes)
        return ins

    def barrier(self):
        for E in self.engs:
            for ch in self.chans:
                if ch.cnt == 0 or ch is E.chan:
                    continue
                if E.seen.get(ch, 0) >= ch.cnt:
                    continue
                E.h.wait_ge(ch.sem, ch.cnt)
                E.seen[ch] = ch.cnt

    def sb(self, es, name, shape, dt):
        self.uid += 1
        return es.enter_context(self.nc.sbuf_tensor(f"sb{self.uid}_{name}", list(shape), dt))

    def psum(self, es, name, shape, dt):
        self.uid += 1
        return es.enter_context(self.nc.psum_tensor(f"pp{self.uid}_{name}", list(shape), dt))


class Ring:
    def __init__(self, items):
        self.items = items
        self.i = 0

    def next(self):
        it = self.items[self.i % len(self.items)]
        self.i += 1
        return it


def host_consts(S):
    pos = np.arange(S, dtype=np.float32)
    half = 16
    inv_freq = (10000.0 ** (-np.arange(half, dtype=np.float32) / half)).astype(np.float32)
    ang = pos[None, :] * inv_freq[:, None]
    cos = np.cos(ang).astype(np.float32)
    sin = np.sin(ang).astype(np.float32)
    ropeC = np.ones((96, S), np.float32)
    ropeS = np.zeros((96, S), np.float32)
    ropeC[64:80] = cos
    ropeC[80:96] = cos
    ropeS[64:80] = -sin
    ropeS[80:96] = sin
    s = np.arange(128)[:, None]
    negmask = np.zeros((128, 4, 512), np.float32)
    t = np.arange(512)[None, :]
    for r in range(4):
        negmask[:, r, :] = np.where(128 * r + s <= t, 0.0, NEG)
    ident = np.eye(128, dtype=np.float32)
    t128 = np.arange(128)[None, :]
    same = (s // 16) == (t128 // 16)
    tri = (same & (s <= t128)).astype(np.float32)
    blk = same.astype(np.float32)
    csel = ((s // 16) == np.arange(8)[None, :]).astype(np.float32)
    caus = (s <= t128).astype(np.float32)
    sel65 = np.zeros((65, 64), np.float32)
    sel65[64, :] = 1.0
    ones64 = np.ones((64, 64), np.float32)
    return dict(ropeC=ropeC, ropeS=ropeS, negmask=negmask, ident=ident, tri=tri, blk=blk,
                csel=csel, caus=caus, sel65=sel65, ones64=ones64)


CONST_SHAPES = lambda S: dict(ropeC=[96, S], ropeS=[96, S], negmask=[128, 4, 512], ident=[128, 128],
                              tri=[128, 128], blk=[128, 128], csel=[128, 8], caus=[128, 128],
                              sel65=[65, 64], ones64=[64, 64])


def weight_shapes(L):
    return dict(
        ada_w=[L, D, 9 * D], ada_b=[L, 9 * D], ln_g=[L, 3, D], ln_b=[L, 3, D],
        ffn1_w_in=[L, D, 2 * DFF], ffn1_w_out=[L, DFF, D], ffn2_w_in=[L, D, 2 * DFF], ffn2_w_out=[L, DFF, D],
        mix_w_in=[L, D, MIXC], mix_w_out=[L, D, D], wkr=[L, D, 192],
        hgrn_lb_logits=[L, 256], hgrn_ng=[L, 64, 4], mla_q_norm_g=[L, 256], mla_kv_norm_g=[L, 128],
        mla_w_uq=[L, 256, 384], mla_w_uqp=[L, 256, 384], mla_w_ukv=[L, 128, 512],
        fox_b_f=[L, 4], gmlp_ln_g=[L, 256], gmlp_ln_b=[L, 256], gmlp_wsT=[L, 4, 128, 128], gmlp_b_s=[L, 4, 128],
    )


def host_weights(inp):
    L = inp["ada_w"].shape[0]
    f = lambda a: np.ascontiguousarray(np.asarray(a, dtype=np.float32))
    w = {k: f(inp[k]) for k in ["ada_w", "ada_b", "ln_g", "ln_b", "ffn1_w_in", "ffn1_w_out", "ffn2_w_in",
                                "ffn2_w_out", "mix_w_in", "mix_w_out", "hgrn_lb_logits", "mla_q_norm_g",
                                "mla_kv_norm_g", "mla_w_uq", "mla_w_ukv", "fox_b_f", "gmlp_ln_g", "gmlp_ln_b",
                                "gmlp_b_s"]}
    perm = np.concatenate([np.arange(16, 32), np.arange(0, 16)])
    kr = w["mix_w_in"][:, :, 1408:1440]
    wkr = np.zeros((L, D, 192), np.float32)
    wkr[:, :, 64:96] = kr
    wkr[:, :, 160:192] = kr[:, :, perm]
    w["wkr"] = wkr
    uq = w["mla_w_uq"].reshape(L, 256, 4, 96)
    uqp = np.zeros_like(uq)
    uqp[:, :, :, 64:96] = uq[:, :, :, 64:96][:, :, :, perm]
    w["mla_w_uqp"] = np.ascontiguousarray(uqp.reshape(L, 256, 384))
    w["hgrn_ng"] = np.ascontiguousarray(f(inp["hgrn_norm_g"]).reshape(L, 4, 64).transpose(0, 2, 1))
    w["gmlp_wsT"] = np.ascontiguousarray(f(inp["gmlp_w_s"]).transpose(0, 1, 3, 2))
    return w


class Builder:
    def __init__(self, S, L, stop=None, dumps=()):
        self.S = S
        self.L = L
        self.NT = S // 128
        self.NG = S // 512
        self.stop = stop
        self.dumps = set(dumps)
        self.dump_aps = {}

    def build(self):
        S, L = self.S, self.L
        nc = bass.Bass("TRN2", target_bir_lowering=False)
        self.nc = nc
        dr = {}
        dr["x"] = nc.dram_tensor("x", [2, S, D], F32, kind="ExternalInput").ap()
        dr["cT"] = nc.dram_tensor("cT", [128, 8, 2], F32, kind="ExternalInput").ap()
        for k, shp in weight_shapes(L).items():
            dr[k] = nc.dram_tensor(k, shp, F32, kind="ExternalInput").ap()
        for k, shp in CONST_SHAPES(S).items():
            dr[k] = nc.dram_tensor(k, shp, F32, kind="ExternalInput").ap()
        dr["out"] = nc.dram_tensor("out", [2, S, D], F32, kind="ExternalOutput").ap()
        dr["modd"] = nc.dram_tensor("modd", [2, L, 9 * D], F32, kind="Internal").ap()
        self.dr = dr
        with ExitStack() as es:
            self.fw = fw = FW(nc, es)
            self.es = es
            self.setup_global(es)
            self.prologue_mod()
            for s in range(2):
                self.run_sequence(s)
                if self.stop is not None and self.stop[0] == s:
                    break
            fw.barrier()
            print(f"[build] inst={fw.n_inst} waits={fw.n_wait} sems={fw.nsem} "
                  f"cnt={[(e.name, e.chan.cnt) for e in fw.engs]}", flush=True)
        return nc

    def V(self, emit, reads=(), writes=()):
        return self.fw.op(self.fw.dve, emit, reads, writes)

    def A(self, emit, reads=(), writes=()):
        return self.fw.op(self.fw.act, emit, reads, writes)

    def G(self, emit, reads=(), writes=()):
        return self.fw.op(self.fw.pool, emit, reads, writes)

    def MM(self, out, lhsT, rhs, start, stop, reads=(), writes=(), inc=None, skip=False):
        if inc is None:
            inc = bool(stop)
        kw = dict(skip_group_check=True) if skip else {}
        return self.fw.op(self.fw.pe, lambda h: h.matmul(out, lhsT=lhsT, rhs=rhs, start=start, stop=stop, **kw),
                          reads, writes, inc=inc)

    def TR(self, out, in_, reads=(), writes=(), inc=True):
        ident = self.ident
        n = in_.shape[0]
        return self.fw.op(self.fw.pe, lambda h: h.transpose(out, in_, ident[0:n, 0:n]),
                          list(reads) + [self.b_const], writes, inc=inc)

    def wdma(self, out, in_, chan, reads=(), writes=()):
        return self.fw.dma(self.fw.pool, self.fw.new_chan(chan.name + "_sw"), out, in_, reads, writes)

    def ldma(self, out, in_, chan, reads=(), writes=(), **kw):
        return self.fw.dma(self.fw.sp, chan, out, in_, reads, writes, **kw)

    def dump(self, name, ap, shape, reads):
        if name not in self.dumps or name in self.dump_aps:
            return
        d = self.nc.dram_tensor("dbg_" + name, list(shape), F32, kind="ExternalOutput").ap()
        self.dump_aps[name] = d
        ch = self.fw.new_chan("dbg")
        self.fw.dma(self.fw.sp, ch, d, ap, reads=reads)

    def setup_global(self, es):
        fw, dr, NT = self.fw, self.dr, self.NT
        self.x = fw.sb(es, "x", [128, NT, D], F32)
        self.bx = [Buf(f"x{t}") for t in range(NT)]
        self.hT = fw.sb(es, "hT", [128, 8, self.S], BF16)
        self.bhT = [Buf(f"hT{g}") for g in range(self.NG)]
        self.mv = fw.sb(es, "modv", [128, 5, D], F32)
        self.bmv = Buf("modv")
        self.ch_mv = fw.new_chan("modv")
        self.bab = Buf("modab")
        self.ch_ab = fw.new_chan("modab")
        self.ident = fw.sb(es, "ident", [128, 128], BF16)
        self.b_const = Buf("const")
        ch = fw.new_chan("const")
        self.wdma(self.ident[:], dr["ident"], ch, writes=[self.b_const])
        self.pbank = [(fw.psum(es, f"b{i}", [128, 512], F32), Buf(f"ps{i}")) for i in range(6)]
        self.ptr = [(fw.psum(es, f"t{i}", [128, 1024], BF16), Buf(f"pt{i}")) for i in range(2)]
        self.ringA = Ring(self.pbank[0:4])
        self.ringB = Ring(self.pbank[4:6])
        self.ringT = Ring(self.ptr)
        self.ch_x = [fw.new_chan(f"x{i}") for i in range(4)]
        self.ch_out = [fw.new_chan(f"o{i}") for i in range(4)]
        self.b_modd = Buf("modd")

    def mod_alloc(self, es):
        fw = self.fw
        self.m_slots = [(fw.sb(es, f"aw{i}", [128, 8, 512], BF16), Buf(), fw.new_chan(f"aw{i}")) for i in range(2)]
        self.m_bslots = [(fw.sb(es, f"ab{i}", [2, 512], F32), Buf(), fw.new_chan(f"ab{i}")) for i in range(2)]
        self.m_rows = [(fw.sb(es, f"mr{i}", [2, 512], F32), Buf(), fw.new_chan(f"mr{i}")) for i in range(2)]

    def mod_block(self, l, cb):
        dr = self.dr
        it = self.m_it
        self.m_it += 1
        awv = dr["ada_w"][l].rearrange("(kc p) n -> p kc n", p=128)
        w, bw, chw = self.m_slots[it % 2]
        ab, bab, chb = self.m_bslots[it % 2]
        mr, bmr, chr_ = self.m_rows[it % 2]
        cact, b_c = self.cact, self.b_cact
        self.wdma(w[:], awv[:, :, cb * 512:(cb + 1) * 512], chw, writes=[bw])
        self.ldma(ab[:], dr["ada_b"][l:l + 1, cb * 512:(cb + 1) * 512].broadcast_to([2, 512]), chb, writes=[bab])
        ps, bps = self.ringA.next()
        for kc in range(8):
            self.MM(ps[0:2, :], cact[:, kc, :], w[:, kc, :], kc == 0, kc == 7, reads=[b_c, bw], writes=[bps])
        seg = (cb * 512) // 1024
        self.V(lambda h: h.tensor_tensor(out=mr[:], in0=ps[0:2, :], in1=ab[:], op=ALU.add), [bps, bab], [bmr])
        if seg in (1, 4, 7, 5):
            self.V(lambda h: h.tensor_scalar_add(out=mr[:], in0=mr[:], scalar1=1.0), [bmr], [bmr])
        elif seg in (2, 8):
            self.V(lambda h: h.tensor_scalar(out=mr[:], in0=mr[:], scalar1=1.0, scalar2=0.5,
                                             op0=ALU.add, op1=ALU.mult), [bmr], [bmr])
        self.ldma(dr["modd"][:, l, cb * 512:(cb + 1) * 512], mr[:], chr_, reads=[bmr], writes=[self.b_modd])

    def prologue_mod(self):
        fw, dr = self.fw, self.dr
        self.m_it = 0
        self.cact = fw.sb(self.es, "cact", [128, 8, 2], BF16)
        self.b_cact = Buf()
        self.mod_deferred = (self.L > 1 and self.stop is None and not self.skip_mixers)
        with ExitStack() as es:
            cT = fw.sb(es, "cT", [128, 8, 2], F32)
            ch = fw.new_chan("c")
            self.ldma(cT[:], dr["cT"], ch, writes=[self.b_cact])
            self.A(lambda h: h.activation(out=self.cact[:], in_=cT[:], func=AF.Silu), [self.b_cact], [self.b_cact])
            self.mod_alloc(es)
            for l in range(1 if self.mod_deferred else self.L):
                for cb in range(18):
                    self.mod_block(l, cb)
            fw.barrier()

    def load_ab(self, s, l, j):
        dr = self.dr
        for i, sg in enumerate([3 * j + 1, 3 * j + 0]):
            self.ldma(self.mv[:, i, :], dr["modd"][s:s + 1, l, sg * D:(sg + 1) * D].broadcast_to([128, D]), self.ch_ab,
                      reads=[self.b_modd], writes=[self.bab])

    def load_modvec(self, s, l, j):
        dr = self.dr
        mv, b, ch = self.mv, self.bmv, self.ch_mv
        sg = 3 * j + 2
        self.ldma(mv[:, 2, :], dr["modd"][s:s + 1, l, sg * D:(sg + 1) * D].broadcast_to([128, D]), ch,
                  reads=[self.b_modd], writes=[b])
        self.ldma(mv[:, 3, :], dr["ln_g"][l, j:j + 1, :].broadcast_to([128, D]), ch, writes=[b])
        self.ldma(mv[:, 4, :], dr["ln_b"][l, j:j + 1, :].broadcast_to([128, D]), ch, writes=[b])

    def run_sequence(self, s):
        fw, dr, NT = self.fw, self.dr, self.NT
        xin = dr["x"][s].rearrange("(t p) d -> p t d", p=128)
        q = max(1, NT // 4)
        for i in range(0, NT, q):
            self.ldma(self.x[:, i:i + q, :], xin[:, i:i + q, :], self.ch_x[(i // q) % 4],
                      writes=self.bx[i:i + q])
        subs = [(l, j) for l in range(self.L) for j in range(3)]
        if self.stop is not None and self.stop[0] == s:
            subs = subs[:subs.index((self.stop[1], self.stop[2])) + 1]
        self.xout = dr["out"][s].rearrange("(t p) d -> p t d", p=128)
        self.load_ab(s, *subs[0])
        with ExitStack() as e0:
            self.build_hT(e0)
            fw.barrier()
        for k, (l, j) in enumerate(subs):
            self.load_modvec(s, l, j)
            self.has_next = k + 1 < len(subs)
            if self.has_next:
                self.load_ab(s, *subs[k + 1])
            if j == 1:
                self.mixer(s, l)
            else:
                self.ffn(s, l, j)

    def build_hT(self, es):
        fw, NT = self.fw, self.NT
        tmpf = [(fw.sb(es, f"hx{i}", [128, D], F32), Buf()) for i in range(2)]
        hb = [(fw.sb(es, f"hb{i}", [128, D], BF16), Buf()) for i in range(2)]
        mv, bmv = self.mv, self.bab
        for t in range(NT):
            tf, btf = tmpf[t % 2]
            hbt, bhb = hb[t % 2]
            self.V(lambda h: h.tensor_tensor(out=tf[:], in0=self.x[:, t, :], in1=mv[:, 0, :], op=ALU.mult),
                   [self.bx[t], bmv], [btf])
            self.G(lambda h: h.tensor_tensor(out=hbt[:], in0=tf[:], in1=mv[:, 1, :], op=ALU.add),
                   [btf, bmv], [bhb])
            self.hT_transpose(t, hbt, bhb)

    def hT_transpose(self, t, hbt, bhb):
        pt, bpt = self.ringT.next()
        for kc in range(8):
            self.TR(pt[:, kc * 128:(kc + 1) * 128], hbt[:, kc * 128:(kc + 1) * 128], [bhb], [bpt], inc=(kc == 7))
        self.A(lambda h: h.copy(out=self.hT[:, :, t * 128:(t + 1) * 128],
                                in_=pt[:].rearrange("p (k c) -> p k c", k=8)),
               [bpt], [self.bhT[t // 4]])

    def begin_tail(self, es):
        fw = self.fw
        self.t_st = [(fw.sb(es, f"lnst{i}", [128, 2, 6], F32), fw.sb(es, f"lnmv{i}", [128, 4], F32), Buf()) for i in range(2)]
        self.t_q = []
        if self.has_next:
            self.t_tf = [(fw.sb(es, f"thx{i}", [128, D], F32), Buf()) for i in range(1)]
            self.t_hb = [(fw.sb(es, f"thb{i}", [128, D], BF16), Buf()) for i in range(3)]

    def tail_tile(self, t):
        self.ln_tile(t, *self.t_st[t % 2])
        if self.has_next:
            tf, btf = self.t_tf[0]
            hbt, bhb = self.t_hb[t % 3]
            mv = self.mv
            self.V(lambda h: h.tensor_tensor(out=tf[:], in0=self.x[:, t, :], in1=mv[:, 0, :], op=ALU.mult),
                   [self.bx[t], self.bab], [btf])
            self.G(lambda h: h.tensor_tensor(out=hbt[:], in0=tf[:], in1=mv[:, 1, :], op=ALU.add),
                   [btf, self.bab], [bhb])
            self.t_q.append((t, hbt, bhb))
            if len(self.t_q) > 2:
                self.hT_transpose(*self.t_q.pop(0))
        else:
            self.ldma(self.xout[:, t, :], self.x[:, t, :], self.ch_out[t % 4], reads=[self.bx[t]])

    def end_tail(self):
        while self.t_q:
            self.hT_transpose(*self.t_q.pop(0))

    def resid_update(self, t, half, ps, bps, first, tmp, btmp):
        sl = slice(half * 512, (half + 1) * 512)
        mv, bmv = self.mv, self.bmv
        self.V(lambda h: h.tensor_tensor(out=tmp[:], in0=ps[:], in1=mv[:, 2, sl], op=ALU.mult),
               [bps, bmv], [btmp])
        xs = self.x[:, t, sl]
        if first:
            self.V(lambda h: h.scalar_tensor_tensor(out=xs, in0=xs, scalar=ALPHA, in1=tmp[:],
                                                    op0=ALU.mult, op1=ALU.add), [btmp, self.bx[t]], [self.bx[t]])
        else:
            self.G(lambda h: h.tensor_tensor(out=xs, in0=xs, in1=tmp[:], op=ALU.add),
                   [btmp, self.bx[t]], [self.bx[t]])

    def ln_tile(self, t, s6, m4, bs):
        mv, bmv = self.mv, self.bmv
        if True:
            xt = self.x[:, t, :]
            bxt = self.bx[t]
            self.V(lambda h: h.bn_stats(out=s6[:, 0, :], in_=self.x[:, t, 0:512]), [bxt], [bs])
            self.V(lambda h: h.bn_stats(out=s6[:, 1, :], in_=self.x[:, t, 512:1024]), [bxt], [bs])
            self.V(lambda h: h.bn_aggr(out=m4[:, 0:2], in_=s6[:]), [bs], [bs])
            self.V(lambda h: h.tensor_scalar_add(out=m4[:, 2:3], in0=m4[:, 1:2], scalar1=LN_EPS), [bs], [bs])
            self.A(lambda h: h.sqrt(out=m4[:, 2:3], in_=m4[:, 2:3]), [bs], [bs])
            self.V(lambda h: h.reciprocal(out=m4[:, 2:3], in_=m4[:, 2:3]), [bs], [bs])
            self.V(lambda h: h.scalar_tensor_tensor(out=m4[:, 3:4], in0=m4[:, 0:1], scalar=-1.0, in1=m4[:, 2:3],
                                                    op0=ALU.mult, op1=ALU.mult), [bs], [bs])
            self.A(lambda h: h.activation(out=xt, in_=xt, func=AF.Identity, bias=m4[:, 3:4], scale=m4[:, 2:3]),
                   [bs, bxt], [bxt])
            self.V(lambda h: h.tensor_tensor(out=xt, in0=xt, in1=mv[:, 3, :], op=ALU.mult), [bxt, bmv], [bxt])
            self.G(lambda h: h.tensor_tensor(out=xt, in0=xt, in1=mv[:, 4, :], op=ALU.add), [bxt, bmv], [bxt])

    def ffn(self, s, l, j):
        fw, dr, NT, NG, S = self.fw, self.dr, self.NT, self.NG, self.S
        w_in = dr["ffn1_w_in" if j == 0 else "ffn2_w_in"][l].rearrange("(kc p) n -> p kc n", p=128)
        w_out = dr["ffn1_w_out" if j == 0 else "ffn2_w_out"][l].rearrange("(j p) n -> p j n", p=128)
        with ExitStack() as es:
            self.begin_tail(es)
            actT = fw.sb(es, "actT", [128, 6, S], BF16)
            bact = [Buf() for _ in range(NG)]
            wo = [(fw.sb(es, f"wo{i}", [128, 6, D], BF16), Buf(), fw.new_chan(f"wo{i}")) for i in range(2)]
            wi = [(fw.sb(es, f"wi{i}", [128, 8, 512], BF16), Buf(), fw.new_chan(f"wi{i}")) for i in range(2)]
            sgs = [(fw.sb(es, f"sg{i}", [128, 512], F32), Buf()) for i in range(2)]
            tmps = [(fw.sb(es, f"ft{i}", [128, 512], F32), Buf()) for i in range(2)]
            parts = [(0, 6), (6, 12), (12, 18), (18, 22)]
            iw = 0
            k = 0
            for pi, (j0, j1) in enumerate(parts):
                wot, bwo, chwo = wo[pi % 2]
                self.wdma(wot[:, 0:j1 - j0, :], w_out[:, j0:j1, :], chwo, writes=[bwo])
                for jj in range(j0, j1, 2):
                    wt, bw, chw = wi[iw % 2]
                    iw += 1
                    self.wdma(wt[:, :, 0:256], w_in[:, :, jj * 128:jj * 128 + 256], chw, writes=[bw])
                    self.wdma(wt[:, :, 256:512], w_in[:, :, DFF + jj * 128:DFF + jj * 128 + 256], chw, writes=[bw])
                    for c in range(2):
                        jl = jj + c - j0
                        for g in range(NG):
                            pg, bpg = self.ringA.next()
                            pu, bpu = self.ringA.next()
                            tok = slice(g * 512, (g + 1) * 512)
                            for kc in range(8):
                                self.MM(pg[:], wt[:, kc, c * 128:(c + 1) * 128], self.hT[:, kc, tok], kc == 0, kc == 7,
                                        [bw, self.bhT[g]], [bpg])
                            for kc in range(8):
                                self.MM(pu[:], wt[:, kc, 256 + c * 128:256 + (c + 1) * 128], self.hT[:, kc, tok],
                                        kc == 0, kc == 7, [bw, self.bhT[g]], [bpu])
                            sg, bsg = sgs[k % 2]
                            k += 1
                            self.A(lambda h: h.activation(out=sg[:], in_=pg[:], func=AF.Silu), [bpg], [bsg])
                            self.V(lambda h: h.tensor_tensor(out=actT[:, jl, tok], in0=sg[:], in1=pu[:], op=ALU.mult),
                                   [bsg, bpu], [bact[g]])
                nj = j1 - j0
                for t in range(NT):
                    for half in range(2):
                        ps, bps = self.ringB.next()
                        for jl in range(nj):
                            self.MM(ps[:], actT[:, jl, t * 128:(t + 1) * 128], wot[:, jl, half * 512:(half + 1) * 512],
                                    jl == 0, jl == nj - 1, [bact[t // 4], bwo], [bps])
                        tmp, btmp = tmps[(2 * t + half) % 2]
                        self.resid_update(t, half, ps, bps, pi == 0, tmp, btmp)
                    if pi == len(parts) - 1:
                        self.tail_tile(t)
            self.end_tail()
            fw.barrier()
        self.dump(f"x_{s}_{l}_{j}", self.x[:], [128, NT, D], self.bx)

    def mixer(self, s, l):
        fw = self.fw
        with ExitStack() as es:
            first = True
            active = [m for m in range(4) if m not in self.skip_mixers]
            fns = [self.mix_hgrn, self.mix_mla, self.mix_fox, self.mix_gmlp]
            for m in active:
                self.tail_on = (m == active[-1])
                with ExitStack() as es2:
                    fns[m](es2, s, l, m, first)
                    fw.barrier()
                first = False
        self.dump(f"x_{s}_{l}_1", self.x[:], [128, self.NT, D], self.bx)

    skip_mixers = ()

    def out_proj(self, es, l, m, mixT, bmix, first):
        fw, dr, NT = self.fw, self.dr, self.NT
        wov = dr["mix_w_out"][l, m * 256:(m + 1) * 256, :].rearrange("(s p) n -> p s n", p=64)
        wo = fw.sb(es, "mwo", [64, 4, D], BF16)
        bwo = Buf()
        ch = fw.new_chan("mwo")
        self.wdma(wo[:], wov, ch, writes=[bwo])
        tmps = [(fw.sb(es, f"ot{i}", [128, 512], F32), Buf()) for i in range(2)]
        if self.tail_on:
            self.begin_tail(es)
        for t in range(NT):
            for half in range(2):
                ps, bps = self.ringB.next()
                for sl in range(4):
                    self.MM(ps[:], mixT[0:64, sl, t * 128:(t + 1) * 128], wo[0:64, sl, half * 512:(half + 1) * 512],
                            sl == 0, sl == 3, [bmix, bwo], [bps])
                tmp, btmp = tmps[(2 * t + half) % 2]
                self.resid_update(t, half, ps, bps, first, tmp, btmp)
            if self.tail_on:
                self.tail_tile(t)
        if self.tail_on:
            self.end_tail()

    def attention(self, es, QT, KT, bq, bk, dk, Vaug, bv, scale, mixT, bmix):
        fw, dr, NG = self.fw, self.dr, self.NG
        negm = fw.sb(es, "negm", [128, 4, 512], BF16)
        sel = fw.sb(es, "sel65", [65, 64], F32)
        bc = Buf()
        ch = fw.new_chan("attc")
        self.wdma(negm[:], dr["negmask"], ch, writes=[bc])
        self.ldma(sel[:], dr["sel65"], ch, writes=[bc])
        tmpS = [(fw.sb(es, f"ts{i}", [128, 512], F32), Buf()) for i in range(2)]
        pTs = [(fw.sb(es, f"pT{i}", [128, 512], BF16), Buf()) for i in range(3)]
        osb = [(fw.sb(es, f"os{i}", [65, 512], F32), Buf()) for i in range(2)]
        it = 0
        ip = 0
        fin_pending = [None]
        for hd in range(4):
            for qg in range(NG):
                qs = slice(qg * 512, (qg + 1) * 512)
                nkb = 4 * (qg + 1)
                o_ps, bo = self.ringB.next()
                pend = []

                def qk(kb):
                    nonlocal ip
                    s_ps, bs = self.ringA.next()
                    self.MM(s_ps[:], KT[hd][0:dk, kb * 128:(kb + 1) * 128], QT[hd][0:dk, qs], True, True,
                            bk[kb // 4] + bq[qg], [bs])
                    pT, bp = pTs[ip % 3]
                    ip += 1
                    r = kb - 4 * qg
                    if r >= 0:
                        tS, bt = tmpS[kb % 2]
                        self.V(lambda h: h.tensor_tensor(out=tS[:], in0=s_ps[:], in1=negm[:, r, :], op=ALU.add),
                               [bs, bc], [bt])
                        self.A(lambda h: h.activation(out=pT[:], in_=tS[:], func=AF.Exp, scale=scale), [bt], [bp])
                    else:
                        self.A(lambda h: h.activation(out=pT[:], in_=s_ps[:], func=AF.Exp, scale=scale), [bs], [bp])
                    return pT, bp

                def pv(kb, pT, bp):
                    self.MM(o_ps[0:65, :], Vaug[:, kb, hd, :], pT[:], kb == 0, kb == nkb - 1, [bv, bp], [bo])

                LA = 2
                for kb in range(nkb + LA):
                    if kb < nkb:
                        pend.append((kb,) + qk(kb))
                    if kb == 1 and fin_pending[0] is not None:
                        fin_pending[0]()
                        fin_pending[0] = None
                    if kb >= LA:
                        a = pend.pop(0)
                        pv(*a)
                o_sb, bos = osb[it % 2]
                it += 1
                self.A(lambda h: h.copy(out=o_sb[:], in_=o_ps[0:65, :]), [bo], [bos])
                self.A(lambda h: h.activation(out=o_sb[64:65, :], in_=o_sb[64:65, :], func=AF.Ln), [bos], [bos])
                self.A(lambda h: h.activation(out=o_sb[64:65, :], in_=o_sb[64:65, :], func=AF.Exp, scale=-1.0), [bos], [bos])

                def fin(o_sb=o_sb, bos=bos, hd=hd, qs=qs):
                    d_ps, bd = self.ringB.next()
                    self.MM(d_ps[0:64, :], sel[:], o_sb[:], True, True, [bc, bos], [bd])
                    self.V(lambda h: h.tensor_tensor(out=mixT[0:64, hd, qs], in0=o_sb[0:64, :], in1=d_ps[0:64, :],
                                                     op=ALU.mult), [bos, bd], [bmix])
                fin_pending[0] = fin
        if fin_pending[0] is not None:
            fin_pending[0]()

    def mix_mla(self, es, s, l, m, first):
        fw, dr, NT, NG, S = self.fw, self.dr, self.NT, self.NG, self.S
        QT = [fw.sb(es, f"QT{h}", [96, S], BF16) for h in range(4)]
        KT = [fw.sb(es, f"KT{h}", [96, S], BF16) for h in range(4)]
        one = Buf()
        bq = [[one] for _ in range(NG)]
        bk = [[one] for _ in range(NG)]
        Vaug = fw.sb(es, "Vaug", [128, NT, 4, 65], BF16)
        bv = Buf()
        self.G(lambda h: h.memset(Vaug[:], 1.0), [], [bv])
        with ExitStack() as e1:
            wv = dr["mix_w_in"][l].rearrange("(kc p) n -> p kc n", p=128)
            wb = fw.sb(e1, "wB", [128, 8, 384], BF16)
            wkr = fw.sb(e1, "wkr", [128, 8, 192], BF16)
            wuq = fw.sb(e1, "wuq", [128, 2, 384], BF16)
            wuqp = fw.sb(e1, "wuqp", [128, 2, 384], BF16)
            wukv = fw.sb(e1, "wukv", [128, 512], BF16)
            gq = fw.sb(e1, "gq", [128, 256], F32)
            gkv = fw.sb(e1, "gkv", [128, 128], F32)
            bw = Buf()
            ch = fw.new_chan("mlaw")
            self.wdma(wb[:], wv[:, :, 1024:1408], ch, writes=[bw])
            self.wdma(wkr[:], dr["wkr"][l].rearrange("(kc p) n -> p kc n", p=128), ch, writes=[bw])
            self.wdma(wuq[:], dr["mla_w_uq"][l].rearrange("(kc p) n -> p kc n", p=128), ch, writes=[bw])
            self.wdma(wuqp[:], dr["mla_w_uqp"][l].rearrange("(kc p) n -> p kc n", p=128), ch, writes=[bw])
            self.wdma(wukv[:], dr["mla_w_ukv"][l], ch, writes=[bw])
            self.ldma(gq[:], dr["mla_q_norm_g"][l:l + 1, :].broadcast_to([128, 256]), ch, writes=[bw])
            self.ldma(gkv[:], dr["mla_kv_norm_g"][l:l + 1, :].broadcast_to([128, 128]), ch, writes=[bw])
            cqnT = fw.sb(e1, "cqnT", [128, 2, S], BF16)
            ckvT = fw.sb(e1, "ckvT", [128, S], BF16)
            bcT = [Buf() for _ in range(NG)]
            rc = [(fw.sb(e1, f"rC{i}", [96, 512], F32), fw.sb(e1, f"rS{i}", [96, 512], F32), Buf(),
                   fw.new_chan(f"rope{i}")) for i in range(2)]
            st = [(fw.sb(e1, f"mst{i}", [128, 2, 6], F32), fw.sb(e1, f"mmv{i}", [128, 8], F32), Buf()) for i in range(2)]
            cn = [(fw.sb(e1, f"cn{i}", [128, 384], BF16), Buf()) for i in range(2)]
            for t in range(NT):
                tok = slice(t * 128, (t + 1) * 128)
                ps, bps = self.ringA.next()
                for kc in range(8):
                    self.MM(ps[:, 0:384], self.hT[:, kc, tok], wb[:, kc, :], kc == 0, kc == 7, [self.bhT[t // 4], bw], [bps])
                s6, m8, bs = st[t % 2]
                cnt, bcn = cn[t % 2]
                self.V(lambda h: h.bn_stats(out=s6[:, 0, :], in_=ps[:, 0:256]), [bps], [bs])
                self.V(lambda h: h.bn_stats(out=s6[:, 1, :], in_=ps[:, 256:384]), [bps], [bs])
                for i in range(2):
                    self.V(lambda h: h.bn_aggr(out=m8[:, 4 * i:4 * i + 2], in_=s6[:, i:i + 1, :]), [bs], [bs])
                    self.V(lambda h: h.scalar_tensor_tensor(out=m8[:, 4 * i + 2:4 * i + 3], in0=m8[:, 4 * i:4 * i + 1],
                                                            scalar=m8[:, 4 * i:4 * i + 1], in1=m8[:, 4 * i + 1:4 * i + 2],
                                                            op0=ALU.mult, op1=ALU.add), [bs], [bs])
                    self.V(lambda h: h.tensor_scalar_add(out=m8[:, 4 * i + 2:4 * i + 3], in0=m8[:, 4 * i + 2:4 * i + 3],
                                                         scalar1=RMS_EPS), [bs], [bs])
                    self.A(lambda h: h.sqrt(out=m8[:, 4 * i + 2:4 * i + 3], in_=m8[:, 4 * i + 2:4 * i + 3]), [bs], [bs])
                    self.V(lambda h: h.reciprocal(out=m8[:, 4 * i + 3:4 * i + 4], in_=m8[:, 4 * i + 2:4 * i + 3]), [bs], [bs])
                self.V(lambda h: h.scalar_tensor_tensor(out=cnt[:, 0:256], in0=ps[:, 0:256], scalar=m8[:, 3:4], in1=gq[:],
                                                        op0=ALU.mult, op1=ALU.mult), [bps, bs, bw], [bcn])
                self.V(lambda h: h.scalar_tensor_tensor(out=cnt[:, 256:384], in0=ps[:, 256:384], scalar=m8[:, 7:8],
                                                        in1=gkv[:], op0=ALU.mult, op1=ALU.mult), [bps, bs, bw], [bcn])
                pt, bpt = self.ringT.next()
                for c in range(3):
                    self.TR(pt[:, c * 128:(c + 1) * 128], cnt[:, c * 128:(c + 1) * 128], [bcn], [bpt], inc=(c == 2))
                self.A(lambda h: h.copy(out=cqnT[:, :, tok], in_=pt[:, 0:256].rearrange("p (k c) -> p k c", k=2)),
                       [bpt], [bcT[t // 4]])
                self.A(lambda h: h.copy(out=ckvT[:, tok], in_=pt[:, 256:384]), [bpt], [bcT[t // 4]])
                pv_, bpv = self.ringB.next()
                self.MM(pv_[:, 0:256].rearrange("p (h x) -> p h x", h=4), ckvT[:, tok],
                        wukv[:].rearrange("p (h x) -> p h x", h=4)[:, :, 64:128], True, True,
                        [bcT[t // 4], bw], [bpv])
                self.A(lambda h: h.copy(out=Vaug[:, t, :, 0:64], in_=pv_[:, 0:256].rearrange("p (h x) -> p h x", h=4)),
                       [bpv], [bv])
            t1s = [(fw.sb(e1, f"r1{i}", [96, 512], F32), Buf()) for i in range(2)]
            t2s = [(fw.sb(e1, f"r2{i}", [96, 512], F32), Buf()) for i in range(2)]
            k = 0
            for g in range(NG):
                tok = slice(g * 512, (g + 1) * 512)
                rC, rS, brp, chr_ = rc[g % 2]
                self.ldma(rC[:], dr["ropeC"][:, tok], chr_, writes=[brp])
                self.ldma(rS[:], dr["ropeS"][:, tok], chr_, writes=[brp])
                for hd in range(4):
                    pq, bpq = self.ringA.next()
                    pp, bpp = self.ringA.next()
                    for kc in range(2):
                        self.MM(pq[0:96, :], wuq[:, kc, hd * 96:(hd + 1) * 96], cqnT[:, kc, tok], kc == 0, kc == 1,
                                [bw, bcT[g]], [bpq])
                    for kc in range(2):
                        self.MM(pp[0:96, :], wuqp[:, kc, hd * 96:(hd + 1) * 96], cqnT[:, kc, tok], kc == 0, kc == 1,
                                [bw, bcT[g]], [bpp])
                    t1, b1 = t1s[k % 2]
                    t2, b2 = t2s[k % 2]
                    k += 1
                    self.V(lambda h: h.tensor_tensor(out=t1[:], in0=pq[0:96, :], in1=rC[:], op=ALU.mult), [bpq, brp], [b1])
                    self.V(lambda h: h.tensor_tensor(out=t2[:], in0=pp[0:96, :], in1=rS[:], op=ALU.mult), [bpp, brp], [b2])
                    self.G(lambda h: h.tensor_tensor(out=QT[hd][:, tok], in0=t1[:], in1=t2[:], op=ALU.add), [b1, b2], bq[g])
                    pk, bpk = self.ringA.next()
                    self.MM(pk[0:64, :], wukv[:, hd * 128:hd * 128 + 64], ckvT[:, tok], True, True, [bw, bcT[g]], [bpk])
                    self.A(lambda h: h.copy(out=KT[hd][0:64, tok], in_=pk[0:64, :]), [bpk], bk[g])
                pq, bpq = self.ringA.next()
                pp, bpp = self.ringA.next()
                for kc in range(8):
                    self.MM(pq[0:96, :], wkr[:, kc, 0:96], self.hT[:, kc, tok], kc == 0, kc == 7, [bw, self.bhT[g]], [bpq])
                for kc in range(8):
                    self.MM(pp[0:96, :], wkr[:, kc, 96:192], self.hT[:, kc, tok], kc == 0, kc == 7, [bw, self.bhT[g]], [bpp])
                t1, b1 = t1s[k % 2]
                t2, b2 = t2s[k % 2]
                k += 1
                self.V(lambda h: h.tensor_tensor(out=t1[64:96, :], in0=pq[64:96, :], in1=rC[64:96, :], op=ALU.mult),
                       [bpq, brp], [b1])
                self.V(lambda h: h.tensor_tensor(out=t2[64:96, :], in0=pp[64:96, :], in1=rS[64:96, :], op=ALU.mult),
                       [bpp, brp], [b2])
                for hd in range(4):
                    self.G(lambda h: h.tensor_tensor(out=KT[hd][64:96, tok], in0=t1[64:96, :], in1=t2[64:96, :], op=ALU.add),
                           [b1, b2], bk[g])
            fw.barrier()
        with ExitStack() as e2:
            mixT = fw.sb(e2, "mixT", [64, 4, S], BF16)
            bmix = Buf()
            self.attention(e2, QT, KT, bq, bk, 96, Vaug, bv, float(96 ** -0.5), mixT, bmix)
            if "mla_o" in self.dumps:
                self.dump_bf16(e2, "mla_o", mixT[:], [64, 4, S], [bmix])
            self.out_proj(e2, l, m, mixT, bmix, first)
            fw.barrier()

    def dump_bf16(self, es, name, ap, shape, reads):
        if name not in self.dumps or name in self.dump_aps:
            return
        t = self.fw.sb(es, "dmp" + name, shape, F32)
        b = Buf()
        self.V(lambda h: h.tensor_copy(out=t[:], in_=ap), reads, [b])
        self.dump(name, t[:], shape, [b])
        self.fw.barrier()

    def mix_fox(self, es, s, l, m, first):
        fw, dr, NT, NG, S = self.fw, self.dr, self.NT, self.NG, self.S
        QT = [fw.sb(es, f"fQT{h}", [68, S], BF16) for h in range(4)]
        KT = [fw.sb(es, f"fKT{h}", [68, S], BF16) for h in range(4)]
        one = Buf()
        bq = [[one, one] for _ in range(NG)]
        bk = [[one, one] for _ in range(NG)]
        allaug = [one]
        Vaug = fw.sb(es, "fVaug", [128, NT, 4, 65], BF16)
        bv = Buf()
        self.G(lambda h: h.memset(Vaug[:], 1.0), [], [bv])
        for hd in range(4):
            self.G(lambda h: h.memset(QT[hd][64:68, :], 1.0), [], allaug)
            self.G(lambda h: h.memset(KT[hd][64:68, :], 1.0), [], allaug)
        with ExitStack() as e1:
            wv = dr["mix_w_in"][l].rearrange("(kc p) n -> p kc n", p=128)
            wc = fw.sb(e1, "wC", [128, 8, 772], BF16)
            bw = Buf()
            ch = fw.new_chan("foxw")
            self.wdma(wc[:], wv[:, :, 1440:2212], ch, writes=[bw])
            nbf = fw.sb(e1, "nbf", [4, 1], F32)
            self.ldma(nbf[:], dr["fox_b_f"][l:l + 1, :].rearrange("o h -> h o"), ch, writes=[bw])
            self.V(lambda h: h.tensor_scalar_mul(out=nbf[:], in0=nbf[:], scalar1=-1.0), [bw], [bw])
            ones4 = fw.sb(e1, "ones4", [4, 512], F32)
            carry = fw.sb(e1, "carry", [4, 1], F32)
            bcar = Buf()
            self.G(lambda h: h.memset(ones4[:], 1.0), [], [bcar])
            self.G(lambda h: h.memset(carry[:], 0.0), [bcar], [bcar])
            ch_row = [fw.new_chan(f"frow{i}") for i in range(2)]
            for t in range(NT):
                tok = slice(t * 128, (t + 1) * 128)
                pv_, bpv = self.ringB.next()
                for kc in range(8):
                    self.MM(pv_[:, 0:256], self.hT[:, kc, tok], wc[:, kc, 512:768], kc == 0, kc == 7,
                            [self.bhT[t // 4], bw], [bpv])
                self.A(lambda h: h.copy(out=Vaug[:, t, :, 0:64], in_=pv_[:, 0:256].rearrange("p (h x) -> p h x", h=4)),
                       [bpv], [bv])
            ft = [dict(e=fw.sb(e1, f"fe{i}", [4, 512], F32), fn=fw.sb(e1, f"ffn{i}", [4, 512], F32),
                       hi=fw.sb(e1, f"fhi{i}", [4, 512], BF16), hif=fw.sb(e1, f"fhf{i}", [4, 512], F32),
                       lo=fw.sb(e1, f"flo{i}", [4, 512], BF16), nhi=fw.sb(e1, f"fnh{i}", [4, 512], BF16),
                       nlo=fw.sb(e1, f"fnl{i}", [4, 512], BF16), b=Buf()) for i in range(2)]
            for g in range(NG):
                tok = slice(g * 512, (g + 1) * 512)
                for hd in range(4):
                    for which, dst, bdst in ((0, QT, bq[g][0]), (1, KT, bk[g][0])):
                        pq, bpq = self.ringA.next()
                        for kc in range(8):
                            self.MM(pq[0:64, :], wc[:, kc, which * 256 + hd * 64:which * 256 + (hd + 1) * 64],
                                    self.hT[:, kc, tok], kc == 0, kc == 7, [bw, self.bhT[g]], [bpq])
                        self.A(lambda h: h.copy(out=dst[hd][0:64, tok], in_=pq[0:64, :]), [bpq], [bdst])
                pf, bpf = self.ringA.next()
                for kc in range(8):
                    self.MM(pf[0:4, :], wc[:, kc, 768:772], self.hT[:, kc, tok], kc == 0, kc == 7, [bw, self.bhT[g]], [bpf])
                f = ft[g % 2]
                bf_ = f["b"]
                self.A(lambda h: h.activation(out=f["e"][:], in_=pf[0:4, :], func=AF.Exp, bias=nbf[:], scale=-1.0),
                       [bpf, bw], [bf_])
                self.A(lambda h: h.activation(out=f["e"][:], in_=f["e"][:], func=AF.Ln, bias=1.0, scale=1.0), [bf_], [bf_])
                self.V(lambda h: h.tensor_tensor_scan(out=f["fn"][:], data0=ones4[:], data1=f["e"][:], initial=carry[:],
                                                      op0=ALU.mult, op1=ALU.add), [bf_, bcar], [bf_])
                self.V(lambda h: h.tensor_copy(out=carry[:], in_=f["fn"][:, 511:512]), [bf_], [bcar])
                self.V(lambda h: h.tensor_scalar_mul(out=f["fn"][:], in0=f["fn"][:], scalar1=8.0), [bf_], [bf_])
                self.V(lambda h: h.tensor_copy(out=f["hi"][:], in_=f["fn"][:]), [bf_], [bf_])
                self.V(lambda h: h.tensor_copy(out=f["hif"][:], in_=f["hi"][:]), [bf_], [bf_])
                self.V(lambda h: h.tensor_tensor(out=f["lo"][:], in0=f["fn"][:], in1=f["hif"][:], op=ALU.subtract),
                       [bf_], [bf_])
                self.V(lambda h: h.tensor_scalar_mul(out=f["nhi"][:], in0=f["hi"][:], scalar1=-1.0), [bf_], [bf_])
                self.V(lambda h: h.tensor_scalar_mul(out=f["nlo"][:], in0=f["lo"][:], scalar1=-1.0), [bf_], [bf_])
                chr_ = ch_row[g % 2]
                for hd in range(4):
                    self.ldma(QT[hd][64:65, tok], f["nhi"][hd:hd + 1, :], chr_, reads=[bf_], writes=[bq[g][1]])
                    self.ldma(QT[hd][65:66, tok], f["nlo"][hd:hd + 1, :], chr_, reads=[bf_], writes=[bq[g][1]])
                    self.ldma(KT[hd][66:67, tok], f["hi"][hd:hd + 1, :], chr_, reads=[bf_], writes=[bk[g][1]])
                    self.ldma(KT[hd][67:68, tok], f["lo"][hd:hd + 1, :], chr_, reads=[bf_], writes=[bk[g][1]])
            fw.barrier()
        with ExitStack() as e2:
            mixT = fw.sb(e2, "fmixT", [64, 4, S], BF16)
            bmix = Buf()
            self.attention(e2, QT, KT, bq, bk, 68, Vaug, bv, 0.125, mixT, bmix)
            if "fox_o" in self.dumps:
                self.dump_bf16(e2, "fox_o", mixT[:], [64, 4, S], [bmix])
            self.out_proj(e2, l, m, mixT, bmix, first)
            fw.barrier()

    def mix_gmlp(self, es, s, l, m, first):
        fw, dr, NT, NG, S = self.fw, self.dr, self.NT, self.NG, self.S
        wv = dr["mix_w_in"][l].rearrange("(kc p) n -> p kc n", p=128)
        wd = fw.sb(es, "wD", [128, 8, 512], BF16)
        bw = Buf()
        ch = fw.new_chan("gmw")
        self.wdma(wd[:], wv[:, :, 2212:2724], ch, writes=[bw])
        wsf = fw.sb(es, "wsf", [128, 4, 128], F32)
        caus = fw.sb(es, "caus", [128, 128], F32)
        wsm = fw.sb(es, "wsm", [128, 4, 128], BF16)
        bsb = fw.sb(es, "bsb", [64, 4, 128], F32)
        lng = fw.sb(es, "glng", [128, 256], F32)
        lnb = fw.sb(es, "glnb", [128, 256], F32)
        self.ldma(wsf[:], dr["gmlp_wsT"][l].rearrange("g s t -> s g t"), ch, writes=[bw])
        self.ldma(caus[:], dr["caus"], ch, writes=[bw])
        self.ldma(bsb[:], dr["gmlp_b_s"][l:l + 1].broadcast_to([64, 4, 128]), ch, writes=[bw])
        self.ldma(lng[:], dr["gmlp_ln_g"][l:l + 1, :].broadcast_to([128, 256]), ch, writes=[bw])
        self.ldma(lnb[:], dr["gmlp_ln_b"][l:l + 1, :].broadcast_to([128, 256]), ch, writes=[bw])
        self.V(lambda h: h.tensor_tensor(out=wsm[:], in0=wsf[:], in1=caus[:].unsqueeze(1).broadcast_to([128, 4, 128]),
                                         op=ALU.mult), [bw], [bw])
        mixT = fw.sb(es, "gmixT", [64, 4, S], BF16)
        bmix = Buf()
        self.mod_pending = []
        if s == 0 and l == 0 and self.mod_deferred:
            self.mod_alloc(es)
            self.mod_pending = [(ll, cb) for ll in range(1, self.L) for cb in range(18)]
        gv = [(fw.sb(es, f"gv{i}", [128, 256], F32), Buf()) for i in range(2)]
        vl = [(fw.sb(es, f"vl{i}", [128, 256], BF16), Buf()) for i in range(2)]
        st = [(fw.sb(es, f"gst{i}", [128, 6], F32), fw.sb(es, f"gmv{i}", [128, 4], F32), Buf()) for i in range(2)]
        gu = [(fw.sb(es, f"gu{i}", [64, 512], F32), Buf()) for i in range(2)]
        t1s = [(fw.sb(es, f"gt{i}", [64, 512], F32), Buf()) for i in range(2)]
        def stageA(t):
            tok = slice(t * 128, (t + 1) * 128)
            bh = self.bhT[t // 4]
            pv_, bpv = self.ringA.next()
            for kc in range(8):
                self.MM(pv_[:, 0:256], self.hT[:, kc, tok], wd[:, kc, 256:512], kc == 0, kc == 7, [bh, bw], [bpv])
            g_, bg = gv[t % 2]
            v_, bvl = vl[t % 2]
            s6, m4, bs = st[t % 2]
            self.A(lambda h: h.activation(out=g_[:], in_=pv_[:, 0:256], func=AF.Gelu_apprx_tanh), [bpv], [bg])
            self.V(lambda h: h.bn_stats(out=s6[:], in_=g_[:]), [bg], [bs])
            self.V(lambda h: h.bn_aggr(out=m4[:, 0:2], in_=s6[:]), [bs], [bs])
            self.V(lambda h: h.tensor_scalar_add(out=m4[:, 2:3], in0=m4[:, 1:2], scalar1=LN_EPS), [bs], [bs])
            self.A(lambda h: h.sqrt(out=m4[:, 2:3], in_=m4[:, 2:3]), [bs], [bs])
            self.V(lambda h: h.reciprocal(out=m4[:, 2:3], in_=m4[:, 2:3]), [bs], [bs])
            self.V(lambda h: h.scalar_tensor_tensor(out=m4[:, 3:4], in0=m4[:, 0:1], scalar=-1.0, in1=m4[:, 2:3],
                                                    op0=ALU.mult, op1=ALU.mult), [bs], [bs])
            self.A(lambda h: h.activation(out=g_[:], in_=g_[:], func=AF.Identity, bias=m4[:, 3:4], scale=m4[:, 2:3]),
                   [bs, bg], [bg])
            self.V(lambda h: h.tensor_tensor(out=g_[:], in0=g_[:], in1=lng[:], op=ALU.mult), [bg, bw], [bg])
            self.G(lambda h: h.tensor_tensor(out=v_[:], in0=g_[:], in1=lnb[:], op=ALU.add), [bg, bw], [bvl])

        def stageB(t):
            tok = slice(t * 128, (t + 1) * 128)
            bh = self.bhT[t // 4]
            v_, bvl = vl[t % 2]
            pm, bpm = self.ringA.next()
            for g in range(4):
                self.MM(pm[0:64, g * 128:(g + 1) * 128], v_[:, g * 64:(g + 1) * 64], wsm[:, g, :], g == 0, True,
                        [bvl, bw], [bpm], inc=(g == 3), skip=True)
            pu, bpu = self.ringA.next()
            for g in range(4):
                for kc in range(8):
                    self.MM(pu[0:64, g * 128:(g + 1) * 128], wd[:, kc, g * 64:(g + 1) * 64], self.hT[:, kc, tok],
                            kc == 0, kc == 7, [bh, bw], [bpu], inc=(g == 3 and kc == 7), skip=True)
            u_, bu = gu[t % 2]
            t1, b1 = t1s[t % 2]
            self.A(lambda h: h.activation(out=u_[:], in_=pu[0:64, :], func=AF.Gelu_apprx_tanh), [bpu], [bu])
            self.V(lambda h: h.tensor_tensor(out=t1[:], in0=pm[0:64, :], in1=bsb[:].rearrange("p g t -> p (g t)"),
                                             op=ALU.add), [bpm, bw], [b1])
            self.V(lambda h: h.tensor_tensor(out=mixT[0:64, :, tok], in0=t1[:].rearrange("p (g t) -> p g t", g=4),
                                             in1=u_[:].rearrange("p (g t) -> p g t", g=4), op=ALU.mult),
                   [b1, bu], [bmix])
            for _ in range(2):
                if self.mod_pending:
                    self.mod_block(*self.mod_pending.pop(0))

        stageA(0)
        for t in range(NT):
            if t + 1 < NT:
                stageA(t + 1)
            stageB(t)
        while self.mod_pending:
            self.mod_block(*self.mod_pending.pop(0))
        if "gmlp_o" in self.dumps:
            self.dump_bf16(es, "gmlp_o", mixT[:], [64, 4, S], [bmix])
        self.out_proj(es, l, m, mixT, bmix, first)

    def mix_hgrn(self, es, s, l, m, first):
        fw, dr, NT, NG, S = self.fw, self.dr, self.NT, self.NG, self.S
        wv = dr["mix_w_in"][l].rearrange("(kc p) n -> p kc n", p=128)
        wa = fw.sb(es, "wA", [128, 8, 1024], BF16)
        bw = Buf()
        ch = fw.new_chan("hgw")
        self.wdma(wa[:], wv[:, :, 0:1024], ch, writes=[bw])
        tri = fw.sb(es, "tri", [128, 128], F32)
        blk = fw.sb(es, "blk", [128, 128], F32)
        csel = fw.sb(es, "csel", [128, 8], F32)
        cselb = fw.sb(es, "cselb", [128, 8], BF16)
        ones64 = fw.sb(es, "ones64", [64, 64], F32)
        ng = fw.sb(es, "ng", [64, 4], F32)
        lb = fw.sb(es, "lb", [128, 256], F32)
        oml = fw.sb(es, "oml", [128, 256], F32)
        self.ldma(tri[:], dr["tri"], ch, writes=[bw])
        self.ldma(blk[:], dr["blk"], ch, writes=[bw])
        self.ldma(csel[:], dr["csel"], ch, writes=[bw])
        self.ldma(ones64[:], dr["ones64"], ch, writes=[bw])
        self.ldma(ng[:], dr["hgrn_ng"][l], ch, writes=[bw])
        self.V(lambda h: h.tensor_copy(out=cselb[:], in_=csel[:]), [bw], [bw])
        if l == 0:
            self.G(lambda h: h.memset(lb[:], 0.0), [], [bw])
        else:
            assert self.L == 2
            self.ldma(lb[:], dr["hgrn_lb_logits"][1:2, :].broadcast_to([128, 256]), ch, writes=[bw])
            self.ldma(oml[:], dr["hgrn_lb_logits"][0:1, :].broadcast_to([128, 256]), ch, writes=[bw])
            self.V(lambda h: h.tensor_tensor(out=lb[:], in0=lb[:], in1=oml[:], op=ALU.subtract), [bw], [bw])
            self.A(lambda h: h.activation(out=lb[:], in_=lb[:], func=AF.Sigmoid), [bw], [bw])
        self.V(lambda h: h.tensor_scalar(out=oml[:], in0=lb[:], scalar1=-1.0, scalar2=1.0, op0=ALU.mult, op1=ALU.add),
               [bw], [bw])
        mixT = fw.sb(es, "hmixT", [64, 4, S], BF16)
        bmix = Buf()
        Sst = fw.sb(es, "Sst", [64, 4, 64], F32)
        Sbf = fw.sb(es, "Sbf", [64, 4, 64], BF16)
        bS = Buf()
        bSb = Buf()
        self.G(lambda h: h.memset(Sst[:], 0.0), [], [bS])
        self.G(lambda h: h.memset(Sbf[:], 0.0), [], [bSb])

        def T(name, shape, dt, n=2):
            r = [(fw.sb(es, f"{name}{i}", shape, dt), Buf()) for i in range(n)]
            return r * (2 // n)
        f_ = T("hf", [128, 256], F32, 1)
        lf_ = T("hlf", [128, 256], F32)
        kk_ = T("hkk", [128, 256], F32, 1)
        eg_ = T("heg", [128, 768], F32, 1)
        qf_ = T("hqf", [128, 256], F32, 1)
        qt_ = T("hqt", [128, 256], BF16)
        kh_ = T("hkh", [128, 256], F32, 1)
        khb_ = T("hkhb", [128, 256], BF16)
        ke_ = T("hke", [128, 256], BF16)
        kem_ = T("hkem", [128, 8, 256], BF16)
        vb_ = T("hvb", [128, 256], BF16)
        sgb_ = T("hsg", [128, 256], BF16)
        Dt_ = T("hDt", [64, 4, 8], F32)
        qkT_ = T("hqkT", [64, 8, 128], BF16)
        gT_ = T("hgT", [64, 4, 128], BF16)
        scm_ = T("hscm", [128, 4, 128], BF16)
        osq_ = T("hosq", [64, 512], F32, 1)
        rs_ = T("hrs", [64, 512], F32, 1)
        t1_ = T("ht1", [64, 512], F32, 1)
        def bufs(t):
            i2 = t % 2
            return dict(f=f_[i2], lf=lf_[i2], kk=kk_[i2], eg=eg_[i2], qf=qf_[i2], qt=qt_[i2], kh=kh_[i2], khb=khb_[i2],
                        ke=ke_[i2], kem=kem_[i2], vb=vb_[i2], sgb=sgb_[i2], Dt=Dt_[i2], qkT=qkT_[i2], gT=gT_[i2],
                        scm=scm_[i2], osq=osq_[i2], rs=rs_[i2], t1=t1_[i2])

        def front(t):
            tok = slice(t * 128, (t + 1) * 128)
            bh = self.bhT[t // 4]
            i2 = t % 2
            pa, bpa = self.ringA.next()
            pb, bpb = self.ringA.next()
            for kc in range(8):
                self.MM(pa[:], self.hT[:, kc, tok], wa[:, kc, 0:512], kc == 0, kc == 7, [bh, bw], [bpa])
            for kc in range(8):
                self.MM(pb[:], self.hT[:, kc, tok], wa[:, kc, 512:1024], kc == 0, kc == 7, [bh, bw], [bpb])
            f, bf_ = f_[i2]
            lf, blf = lf_[i2]
            kk, bkk = kk_[i2]
            eg, beg = eg_[i2]
            qf, bqf = qf_[i2]
            qt, bqt = qt_[i2]
            kh, bkh = kh_[i2]
            khb, bkhb = khb_[i2]
            ke, bke = ke_[i2]
            kem, bkem = kem_[i2]
            vb, bvb = vb_[i2]
            sgb, bsgb = sgb_[i2]
            Dt, bDt = Dt_[i2]
            qkT, bqkT = qkT_[i2]
            gT, bgT = gT_[i2]
            scm, bscm = scm_[i2]
            osq, bosq = osq_[i2]
            rs, brs = rs_[i2]
            t1, bt1 = t1_[i2]
            self.A(lambda h: h.activation(out=f[:], in_=pa[:, 256:512], func=AF.Sigmoid), [bpa], [bf_])
            self.A(lambda h: h.activation(out=qf[:], in_=pa[:, 0:256], func=AF.Silu), [bpa], [bqf])
            self.A(lambda h: h.copy(out=vb[:], in_=pb[:, 0:256]), [bpb], [bvb])
            self.A(lambda h: h.activation(out=sgb[:], in_=pb[:, 256:512], func=AF.Silu), [bpb], [bsgb])
            yield
            self.V(lambda h: h.tensor_tensor(out=f[:], in0=f[:], in1=oml[:], op=ALU.mult), [bf_, bw], [bf_])
            self.V(lambda h: h.tensor_tensor(out=f[:], in0=f[:], in1=lb[:], op=ALU.add), [bf_, bw], [bf_])
            self.A(lambda h: h.activation(out=lf[:], in_=f[:], func=AF.Ln), [bf_], [blf])
            self.V(lambda h: h.tensor_scalar(out=kk[:], in0=f[:], scalar1=-1.0, scalar2=1.0, op0=ALU.mult, op1=ALU.add),
                   [bf_], [bkk])
            yield
            pg, bpg = self.ringA.next()
            self.MM(pg[:, 0:256], tri[:], lf[:], True, True, [bw, blf], [bpg], inc=False)
            self.MM(pg[:, 256:512], blk[:], lf[:], True, True, [bw, blf], [bpg], skip=True)
            pd, bpd = self.ringB.next()
            for hd in range(4):
                self.MM(pd[0:64, hd * 8:(hd + 1) * 8], lf[:, hd * 64:(hd + 1) * 64], csel[:], hd == 0, True,
                        [bw, blf], [bpd], inc=(hd == 3), skip=True)
            self.A(lambda h: h.activation(out=eg[:, 0:256], in_=pg[:, 0:256], func=AF.Exp), [bpg], [beg])
            self.A(lambda h: h.activation(out=eg[:, 256:512], in_=pg[:, 0:256], func=AF.Exp, scale=-1.0), [bpg], [beg])
            self.A(lambda h: h.activation(out=eg[:, 512:768], in_=pg[:, 256:512], func=AF.Exp), [bpg], [beg])
            self.A(lambda h: h.activation(out=Dt[:].rearrange("p h c -> p (h c)"), in_=pd[0:64, 0:32], func=AF.Exp),
                   [bpd], [bDt])
            yield
            self.V(lambda h: h.tensor_tensor(out=qt[:], in0=qf[:], in1=eg[:, 0:256], op=ALU.mult), [bqf, beg], [bqt])
            self.V(lambda h: h.tensor_tensor(out=kh[:], in0=kk[:], in1=eg[:, 256:512], op=ALU.mult), [bkk, beg], [bkh])
            self.G(lambda h: h.tensor_copy(out=khb[:], in_=kh[:]), [bkh], [bkhb])
            self.G(lambda h: h.tensor_tensor(out=ke[:], in0=kh[:], in1=eg[:, 512:768], op=ALU.mult), [bkh, beg], [bke])
            self.G(lambda h: h.tensor_tensor(out=kem[:], in0=ke[:].unsqueeze(1).broadcast_to([128, 8, 256]),
                                             in1=cselb[:].unsqueeze(2).broadcast_to([128, 8, 256]), op=ALU.mult),
                   [bke, bw], [bkem])
            yield
            pt, bpt = self.ringT.next()
            for hd in range(4):
                self.TR(pt[0:64, hd * 128:(hd + 1) * 128], qt[:, hd * 64:(hd + 1) * 64], [bqt], [bpt], inc=False)
            for hd in range(4):
                self.TR(pt[0:64, 512 + hd * 128:512 + (hd + 1) * 128], khb[:, hd * 64:(hd + 1) * 64], [bkhb], [bpt],
                        inc=(hd == 3))
            self.A(lambda h: h.copy(out=qkT[:].rearrange("p a t -> p (a t)"), in_=pt[0:64, :]), [bpt], [bqkT])
            yield
            pt2, bpt2 = self.ringT.next()
            for hd in range(4):
                self.TR(pt2[0:64, hd * 128:(hd + 1) * 128], sgb[:, hd * 64:(hd + 1) * 64], [bsgb], [bpt2], inc=(hd == 3))
            self.A(lambda h: h.copy(out=gT[:].rearrange("p a t -> p (a t)"), in_=pt2[0:64, 0:512]), [bpt2], [bgT])
            yield
            psc, bpsc = self.ringA.next()
            for hd in range(4):
                self.MM(psc[:, hd * 128:(hd + 1) * 128], qkT[:, 4 + hd, :], qkT[:, hd, :], hd == 0, True,
                        [bqkT], [bpsc], inc=(hd == 3), skip=True)
            self.V(lambda h: h.tensor_tensor(out=scm[:], in0=psc[:].rearrange("p (a t) -> p a t", a=4),
                                             in1=tri[:].unsqueeze(1).broadcast_to([128, 4, 128]), op=ALU.mult),
                   [bpsc, bw], [bscm])
            yield

        def back(t):
            tok = slice(t * 128, (t + 1) * 128)
            B = bufs(t)
            kem, bkem = B["kem"]
            vb, bvb = B["vb"]
            Dt, bDt = B["Dt"]
            qkT, bqkT = B["qkT"]
            gT, bgT = B["gT"]
            scm, bscm = B["scm"]
            osq, bosq = B["osq"]
            rs, brs = B["rs"]
            t1, bt1 = B["t1"]
            po, bpo = self.ringB.next()
            firstmm = True
            for ci in range(8):
                c = t * 8 + ci
                if c > 0:
                    for hd in range(4):
                        self.MM(po[0:64, hd * 128 + ci * 16:hd * 128 + (ci + 1) * 16], Sbf[:, hd, :],
                                qkT[:, hd, ci * 16:(ci + 1) * 16], firstmm, False, [bSb, bqkT], [bpo], inc=False, skip=True)
                        firstmm = False
                pkv, bpkv = self.ringA.next()
                for hd in range(4):
                    self.MM(pkv[0:64, hd * 64:(hd + 1) * 64], kem[:, ci, hd * 64:(hd + 1) * 64], vb[:, hd * 64:(hd + 1) * 64],
                            hd == 0, True, [bkem, bvb], [bpkv], inc=(hd == 3), skip=True)
                self.V(lambda h: h.tensor_tensor(out=Sst[:], in0=Sst[:], in1=Dt[:, :, ci:ci + 1].broadcast_to([64, 4, 64]),
                                                 op=ALU.mult), [bS, bDt], [bS])
                self.V(lambda h: h.tensor_tensor(out=Sst[:], in0=Sst[:], in1=pkv[0:64, 0:256].rearrange("p (a b) -> p a b", a=4),
                                                 op=ALU.add), [bS, bpkv], [bS])
                self.V(lambda h: h.tensor_copy(out=Sbf[:], in_=Sst[:]), [bS], [bSb])
                yield
            for hd in range(4):
                self.MM(po[0:64, hd * 128:(hd + 1) * 128], vb[:, hd * 64:(hd + 1) * 64], scm[:, hd, :], firstmm, True,
                        [bvb, bscm], [bpo], inc=(hd == 3), skip=True)
                firstmm = False
            self.A(lambda h: h.activation(out=osq[:], in_=po[0:64, :], func=AF.Square), [bpo], [bosq])
            pss, bpss = self.ringA.next()
            self.MM(pss[0:64, :], ones64[:], osq[:], True, True, [bw, bosq], [bpss])
            self.V(lambda h: h.tensor_scalar(out=rs[:], in0=pss[0:64, :], scalar1=1.0 / 64.0, scalar2=RMS_EPS,
                                             op0=ALU.mult, op1=ALU.add), [bpss], [brs])
            self.A(lambda h: h.activation(out=rs[:], in_=rs[:], func=AF.Ln), [brs], [brs])
            self.A(lambda h: h.activation(out=rs[:], in_=rs[:], func=AF.Exp, scale=-0.5), [brs], [brs])
            self.V(lambda h: h.tensor_tensor(out=t1[:], in0=po[0:64, :], in1=rs[:], op=ALU.mult), [bpo, brs], [bt1])
            for hd in range(4):
                self.V(lambda h: h.scalar_tensor_tensor(out=mixT[0:64, hd, tok], in0=t1[:, hd * 128:(hd + 1) * 128],
                                                        scalar=ng[:, hd:hd + 1], in1=gT[:, hd, :],
                                                        op0=ALU.mult, op1=ALU.mult), [bt1, bw, bgT], [bmix])

        for _ in front(0):
            pass
        for t in range(NT):
            gb = back(t)
            gf = front(t + 1) if t + 1 < NT else iter(())
            done_b = done_f = False
            while not (done_b and done_f):
                if not done_b:
                    try:
                        next(gb)
                    except StopIteration:
                        done_b = True
                if not done_f:
                    try:
                        next(gf)
                    except StopIteration:
                        done_f = True
        if "hgrn_o" in self.dumps:
            self.dump_bf16(es, "hgrn_o", mixT[:], [64, 4, S], [bmix])
        self.out_proj(es, l, m, mixT, bmix, first)


_NC_CACHE = {}


def _get_nc(S, L):
    key = (S, L)
    if key not in _NC_CACHE:
        _NC_CACHE[key] = Builder(S, L).build()
    return _NC_CACHE[key]


def make_in_maps(inputs, n_cores, S):
    w = host_weights(inputs)
    consts = host_consts(S)
    x = np.asarray(inputs["x"], dtype=np.float32)
    c = np.asarray(inputs["c"], dtype=np.float32)
    maps = []
    for i in range(n_cores):
        cc = c[2 * i:2 * i + 2]
        cT = np.ascontiguousarray(cc.reshape(2, 8, 128).transpose(2, 1, 0))
        mp = {"x": np.ascontiguousarray(x[2 * i:2 * i + 2]), "cT": cT}
        mp.update(w)
        mp.update(consts)
        maps.append(mp)
    return maps


def kernel(**inputs):
    x = np.asarray(inputs["x"])
    B, S, _ = x.shape
    L = np.asarray(inputs["ada_w"]).shape[0]
    n = B // 2
    nc = _get_nc(S, L)
    maps = make_in_maps(inputs, n, S)
    res = run_bass_kernel_spmd(nc, maps, core_ids=list(range(n)))
    out = np.concatenate([r["out"] for r in res.results], axis=0)
    return out.astype(np.float32)
```

```python
import numpy as np
from contextlib import ExitStack
import concourse.bass as bass
import concourse.mybir as mybir
from concourse.bass_utils import run_bass_kernel_spmd

F32 = mybir.dt.float32
BF16 = mybir.dt.bfloat16
AF = mybir.ActivationFunctionType
ALU = mybir.AluOpType

D = 1024
DFF = 2816
NJ = 22
MIXC = 2724
ALPHA = float(4 ** 0.25)
LN_EPS = 1e-5
RMS_EPS = 1e-6
NEG = -30000.0


class Buf:
    __slots__ = ("w", "r", "name")

    def __init__(self, name=""):
        self.w = {}
        self.r = {}
        self.name = name


class Chan:
    def __init__(self, sem, name):
        self.sem = sem
        self.cnt = 0
        self.name = name


class Eng:
    def __init__(self, name, h, chan):
        self.name = name
        self.h = h
        self.chan = chan
        self.seen = {}


class FW:
    def __init__(self, nc, es):
        self.nc = nc
        self.es = es
        self.nsem = 0
        self.chans = []
        self.pe = Eng("pe", nc.tensor, self.new_chan("pe"))
        self.act = Eng("act", nc.scalar, self.new_chan("act"))
        self.dve = Eng("dve", nc.vector, self.new_chan("dve"))
        self.pool = Eng("pool", nc.gpsimd, self.new_chan("pool"))
        self.sp = Eng("sp", nc.sync, self.new_chan("sp"))
        self.engs = [self.pe, self.act, self.dve, self.pool, self.sp]
        self.n_inst = 0
        self.n_wait = 0
        self.uid = 0

    def new_chan(self, name):
        for ch in self.chans:
            if ch.name == name and name not in ("dbg",):
                return ch
        sem = self.es.enter_context(self.nc.semaphore(f"s_{name}_{self.nsem}"))
        self.nsem += 1
        ch = Chan(sem, name)
        self.chans.append(ch)
        return ch

    def _sync(self, E, reads, writes):
        need = {}
        for b in reads:
            for ch, c in b.w.items():
                if need.get(ch, 0) < c:
                    need[ch] = c
        for b in writes:
            for ch, c in b.w.items():
                if need.get(ch, 0) < c:
                    need[ch] = c
            for ch, c in b.r.items():
                if need.get(ch, 0) < c:
                    need[ch] = c
        for ch, c in need.items():
            if ch is E.chan and E.name == "pe":
                continue
            if E.seen.get(ch, 0) >= c:
                continue
            E.h.wait_ge(ch.sem, c)
            self.n_wait += 1
            E.seen[ch] = c

    def _mark(self, ch, mark, reads, writes):
        for b in reads:
            if b.r.get(ch, 0) < mark:
                b.r[ch] = mark
        for b in writes:
            b.w[ch] = mark
            b.r = {}

    def op(self, E, emit, reads=(), writes=(), inc=True):
        self._sync(E, reads, writes)
        ins = emit(E.h)
        self.n_inst += 1
        ch = E.chan
        if inc:
            ch.cnt += 1
            ins.then_inc(ch.sem, 1)
            mark = ch.cnt
        else:
            mark = ch.cnt + 1
        self._mark(ch, mark, reads, writes)
        return ins

    def dma(self, Q, chan, out, in_, reads=(), writes=(), **kw):
        self._sync(Q, reads, writes)
        ins = Q.h.dma_start(out=out, in_=in_, **kw)
        self.n_inst += 1
        chan.cnt += 16
        ins.then_inc(chan.sem, 16)
        self._mark(chan, chan.cnt, reads, writes)
        return ins

    def barrier(self):
        for E in self.engs:
            for ch in self.chans:
                if ch.cnt == 0 or ch is E.chan:
                    continue
                if E.seen.get(ch, 0) >= ch.cnt:
                    continue
                E.h.wait_ge(ch.sem, ch.cnt)
                E.seen[ch] = ch.cnt

    def sb(self, es, name, shape, dt):
        self.uid += 1
        return es.enter_context(self.nc.sbuf_tensor(f"sb{self.uid}_{name}", list(shape), dt))

    def psum(self, es, name, shape, dt):
        self.uid += 1
        return es.enter_context(self.nc.psum_tensor(f"pp{self.uid}_{name}", list(shape), dt))


class Ring:
    def __init__(self, items):
        self.items = items
        self.i = 0

    def next(self):
        it = self.items[self.i % len(self.items)]
        self.i += 1
        return it


def host_consts(S):
    pos = np.arange(S, dtype=np.float32)
    half = 16
    inv_freq = (10000.0 ** (-np.arange(half, dtype=np.float32) / half)).astype(np.float32)
    ang = pos[None, :] * inv_freq[:, None]
    cos = np.cos(ang).astype(np.float32)
    sin = np.sin(ang).astype(np.float32)
    ropeC = np.ones((96, S), np.float32)
    ropeS = np.zeros((96, S), np.float32)
    ropeC[64:80] = cos
    ropeC[80:96] = cos
    ropeS[64:80] = -sin
    ropeS[80:96] = sin
    s = np.arange(128)[:, None]
    negmask = np.zeros((128, 4, 512), np.float32)
    t = np.arange(512)[None, :]
    for r in range(4):
        negmask[:, r, :] = np.where(128 * r + s <= t, 0.0, NEG)
    ident = np.eye(128, dtype=np.float32)
    t128 = np.arange(128)[None, :]
    same = (s // 16) == (t128 // 16)
    tri = (same & (s <= t128)).astype(np.float32)
    blk = same.astype(np.float32)
    csel = ((s // 16) == np.arange(8)[None, :]).astype(np.float32)
    caus = (s <= t128).astype(np.float32)
    sel65 = np.zeros((65, 64), np.float32)
    sel65[64, :] = 1.0
    ones64 = np.ones((64, 64), np.float32)
    return dict(ropeC=ropeC, ropeS=ropeS, negmask=negmask, ident=ident, tri=tri, blk=blk,
                csel=csel, caus=caus, sel65=sel65, ones64=ones64)


CONST_SHAPES = lambda S: dict(ropeC=[96, S], ropeS=[96, S], negmask=[128, 4, 512], ident=[128, 128],
                              tri=[128, 128], blk=[128, 128], csel=[128, 8], caus=[128, 128],
                              sel65=[65, 64], ones64=[64, 64])


def weight_shapes(L):
    return dict(
        ada_w=[L, D, 9 * D], ada_b=[L, 9 * D], ln_g=[L, 3, D], ln_b=[L, 3, D],
        ffn1_w_in=[L, D, 2 * DFF], ffn1_w_out=[L, DFF, D], ffn2_w_in=[L, D, 2 * DFF], ffn2_w_out=[L, DFF, D],
        mix_w_in=[L, D, MIXC], mix_w_out=[L, D, D], wkr=[L, D, 192],
        hgrn_lb_logits=[L, 256], hgrn_ng=[L, 64, 4], mla_q_norm_g=[L, 256], mla_kv_norm_g=[L, 128],
        mla_w_uq=[L, 256, 384], mla_w_uqp=[L, 256, 384], mla_w_ukv=[L, 128, 512],
        fox_b_f=[L, 4], gmlp_ln_g=[L, 256], gmlp_ln_b=[L, 256], gmlp_wsT=[L, 4, 128, 128], gmlp_b_s=[L, 4, 128],
    )


def host_weights(inp):
    L = inp["ada_w"].shape[0]
    f = lambda a: np.ascontiguousarray(np.asarray(a, dtype=np.float32))
    w = {k: f(inp[k]) for k in ["ada_w", "ada_b", "ln_g", "ln_b", "ffn1_w_in", "ffn1_w_out", "ffn2_w_in",
                                "ffn2_w_out", "mix_w_in", "mix_w_out", "hgrn_lb_logits", "mla_q_norm_g",
                                "mla_kv_norm_g", "mla_w_uq", "mla_w_ukv", "fox_b_f", "gmlp_ln_g", "gmlp_ln_b",
                                "gmlp_b_s"]}
    perm = np.concatenate([np.arange(16, 32), np.arange(0, 16)])
    kr = w["mix_w_in"][:, :, 1408:1440]
    wkr = np.zeros((L, D, 192), np.float32)
    wkr[:, :, 64:96] = kr
    wkr[:, :, 160:192] = kr[:, :, perm]
    w["wkr"] = wkr
    uq = w["mla_w_uq"].reshape(L, 256, 4, 96)
    uqp = np.zeros_like(uq)
    uqp[:, :, :, 64:96] = uq[:, :, :, 64:96][:, :, :, perm]
    w["mla_w_uqp"] = np.ascontiguousarray(uqp.reshape(L, 256, 384))
    w["hgrn_ng"] = np.ascontiguousarray(f(inp["hgrn_norm_g"]).reshape(L, 4, 64).transpose(0, 2, 1))
    w["gmlp_wsT"] = np.ascontiguousarray(f(inp["gmlp_w_s"]).transpose(0, 1, 3, 2))
    return w


class Builder:
    def __init__(self, S, L, stop=None, dumps=()):
        self.S = S
        self.L = L
        self.NT = S // 128
        self.NG = S // 512
        self.stop = stop
        self.dumps = set(dumps)
        self.dump_aps = {}

    def build(self):
        S, L = self.S, self.L
        nc = bass.Bass("TRN2", target_bir_lowering=False)
        self.nc = nc
        dr = {}
        dr["x"] = nc.dram_tensor("x", [2, S, D], F32, kind="ExternalInput").ap()
        dr["cT"] = nc.dram_tensor("cT", [128, 8, 2], F32, kind="ExternalInput").ap()
        for k, shp in weight_shapes(L).items():
            dr[k] = nc.dram_tensor(k, shp, F32, kind="ExternalInput").ap()
        for k, shp in CONST_SHAPES(S).items():
            dr[k] = nc.dram_tensor(k, shp, F32, kind="ExternalInput").ap()
        dr["out"] = nc.dram_tensor("out", [2, S, D], F32, kind="ExternalOutput").ap()
        dr["modd"] = nc.dram_tensor("modd", [2, L, 9 * D], F32, kind="Internal").ap()
        self.dr = dr
        with ExitStack() as es:
            self.fw = fw = FW(nc, es)
            self.es = es
            self.setup_global(es)
            self.prologue_mod()
            for s in range(2):
                self.run_sequence(s)
                if self.stop is not None and self.stop[0] == s:
                    break
            fw.barrier()
            print(f"[build] inst={fw.n_inst} waits={fw.n_wait} sems={fw.nsem} "
                  f"cnt={[(e.name, e.chan.cnt) for e in fw.engs]}", flush=True)
        return nc

    def V(self, emit, reads=(), writes=()):
        return self.fw.op(self.fw.dve, emit, reads, writes)

    def A(self, emit, reads=(), writes=()):
        return self.fw.op(self.fw.act, emit, reads, writes)

    def G(self, emit, reads=(), writes=()):
        return self.fw.op(self.fw.pool, emit, reads, writes)

    def MM(self, out, lhsT, rhs, start, stop, reads=(), writes=(), inc=None, skip=False):
        if inc is None:
            inc = bool(stop)
        kw = dict(skip_group_check=True) if skip else {}
        return self.fw.op(self.fw.pe, lambda h: h.matmul(out, lhsT=lhsT, rhs=rhs, start=start, stop=stop, **kw),
                          reads, writes, inc=inc)

    def TR(self, out, in_, reads=(), writes=(), inc=True):
        ident = self.ident
        n = in_.shape[0]
        return self.fw.op(self.fw.pe, lambda h: h.transpose(out, in_, ident[0:n, 0:n]),
                          list(reads) + [self.b_const], writes, inc=inc)

    def wdma(self, out, in_, chan, reads=(), writes=()):
        return self.fw.dma(self.fw.pool, self.fw.new_chan(chan.name + "_sw"), out, in_, reads, writes)

    def ldma(self, out, in_, chan, reads=(), writes=(), **kw):
        return self.fw.dma(self.fw.sp, chan, out, in_, reads, writes, **kw)

    def dump(self, name, ap, shape, reads):
        if name not in self.dumps or name in self.dump_aps:
            return
        d = self.nc.dram_tensor("dbg_" + name, list(shape), F32, kind="ExternalOutput").ap()
        self.dump_aps[name] = d
        ch = self.fw.new_chan("dbg")
        self.fw.dma(self.fw.sp, ch, d, ap, reads=reads)

    def setup_global(self, es):
        fw, dr, NT = self.fw, self.dr, self.NT
        self.x = fw.sb(es, "x", [128, NT, D], F32)
        self.bx = [Buf(f"x{t}") for t in range(NT)]
        self.hT = fw.sb(es, "hT", [128, 8, self.S], BF16)
        self.bhT = [Buf(f"hT{g}") for g in range(self.NG)]
        self.mv = fw.sb(es, "modv", [128, 5, D], F32)
        self.bmv = Buf("modv")
        self.ch_mv = fw.new_chan("modv")
        self.bab = Buf("modab")
        self.ch_ab = fw.new_chan("modab")
        self.ident = fw.sb(es, "ident", [128, 128], BF16)
        self.b_const = Buf("const")
        ch = fw.new_chan("const")
        self.wdma(self.ident[:], dr["ident"], ch, writes=[self.b_const])
        self.pbank = [(fw.psum(es, f"b{i}", [128, 512], F32), Buf(f"ps{i}")) for i in range(6)]
        self.ptr = [(fw.psum(es, f"t{i}", [128, 1024], BF16), Buf(f"pt{i}")) for i in range(2)]
        self.ringA = Ring(self.pbank[0:4])
        self.ringB = Ring(self.pbank[4:6])
        self.ringT = Ring(self.ptr)
        self.ch_x = [fw.new_chan(f"x{i}") for i in range(4)]
        self.ch_out = [fw.new_chan(f"o{i}") for i in range(4)]
        self.b_modd = Buf("modd")

    def mod_alloc(self, es):
        fw = self.fw
        self.m_slots = [(fw.sb(es, f"aw{i}", [128, 8, 512], BF16), Buf(), fw.new_chan(f"aw{i}")) for i in range(2)]
        self.m_bslots = [(fw.sb(es, f"ab{i}", [2, 512], F32), Buf(), fw.new_chan(f"ab{i}")) for i in range(2)]
        self.m_rows = [(fw.sb(es, f"mr{i}", [2, 512], F32), Buf(), fw.new_chan(f"mr{i}")) for i in range(2)]

    def mod_block(self, l, cb):
        dr = self.dr
        it = self.m_it
        self.m_it += 1
        awv = dr["ada_w"][l].rearrange("(kc p) n -> p kc n", p=128)
        w, bw, chw = self.m_slots[it % 2]
        ab, bab, chb = self.m_bslots[it % 2]
        mr, bmr, chr_ = self.m_rows[it % 2]
        cact, b_c = self.cact, self.b_cact
        self.wdma(w[:], awv[:, :, cb * 512:(cb + 1) * 512], chw, writes=[bw])
        self.ldma(ab[:], dr["ada_b"][l:l + 1, cb * 512:(cb + 1) * 512].broadcast_to([2, 512]), chb, writes=[bab])
        ps, bps = self.ringA.next()
        for kc in range(8):
            self.MM(ps[0:2, :], cact[:, kc, :], w[:, kc, :], kc == 0, kc == 7, reads=[b_c, bw], writes=[bps])
        seg = (cb * 512) // 1024
        self.V(lambda h: h.tensor_tensor(out=mr[:], in0=ps[0:2, :], in1=ab[:], op=ALU.add), [bps, bab], [bmr])
        if seg in (1, 4, 7, 5):
            self.V(lambda h: h.tensor_scalar_add(out=mr[:], in0=mr[:], scalar1=1.0), [bmr], [bmr])
        elif seg in (2, 8):
            self.V(lambda h: h.tensor_scalar(out=mr[:], in0=mr[:], scalar1=1.0, scalar2=0.5,
                                             op0=ALU.add, op1=ALU.mult), [bmr], [bmr])
        self.ldma(dr["modd"][:, l, cb * 512:(cb + 1) * 512], mr[:], chr_, reads=[bmr], writes=[self.b_modd])

    def prologue_mod(self):
        fw, dr = self.fw, self.dr
        self.m_it = 0
        self.cact = fw.sb(self.es, "cact", [128, 8, 2], BF16)
        self.b_cact = Buf()
        self.mod_deferred = (self.L > 1 and self.stop is None and not self.skip_mixers)
        with ExitStack() as es:
            cT = fw.sb(es, "cT", [128, 8, 2], F32)
            ch = fw.new_chan("c")
            self.ldma(cT[:], dr["cT"], ch, writes=[self.b_cact])
            self.A(lambda h: h.activation(out=self.cact[:], in_=cT[:], func=AF.Silu), [self.b_cact], [self.b_cact])
            self.mod_alloc(es)
            for l in range(1 if self.mod_deferred else self.L):
                for cb in range(18):
                    self.mod_block(l, cb)
            fw.barrier()

    def load_ab(self, s, l, j):
        dr = self.dr
        for i, sg in enumerate([3 * j + 1, 3 * j + 0]):
            self.ldma(self.mv[:, i, :], dr["modd"][s:s + 1, l, sg * D:(sg + 1) * D].broadcast_to([128, D]), self.ch_ab,
                      reads=[self.b_modd], writes=[self.bab])

    def load_modvec(self, s, l, j):
        dr = self.dr
        mv, b, ch = self.mv, self.bmv, self.ch_mv
        sg = 3 * j + 2
        self.ldma(mv[:, 2, :], dr["modd"][s:s + 1, l, sg * D:(sg + 1) * D].broadcast_to([128, D]), ch,
                  reads=[self.b_modd], writes=[b])
        self.ldma(mv[:, 3, :], dr["ln_g"][l, j:j + 1, :].broadcast_to([128, D]), ch, writes=[b])
        self.ldma(mv[:, 4, :], dr["ln_b"][l, j:j + 1, :].broadcast_to([128, D]), ch, writes=[b])

    def run_sequence(self, s):
        fw, dr, NT = self.fw, self.dr, self.NT
        xin = dr["x"][s].rearrange("(t p) d -> p t d", p=128)
        q = max(1, NT // 4)
        for i in range(0, NT, q):
            self.ldma(self.x[:, i:i + q, :], xin[:, i:i + q, :], self.ch_x[(i // q) % 4],
                      writes=self.bx[i:i + q])
        subs = [(l, j) for l in range(self.L) for j in range(3)]
        if self.stop is not None and self.stop[0] == s:
            subs = subs[:subs.index((self.stop[1], self.stop[2])) + 1]
        self.xout = dr["out"][s].rearrange("(t p) d -> p t d", p=128)
        self.load_ab(s, *subs[0])
        with ExitStack() as e0:
            self.build_hT(e0)
            fw.barrier()
        for k, (l, j) in enumerate(subs):
            self.load_modvec(s, l, j)
            self.has_next = k + 1 < len(subs)
            if self.has_next:
                self.load_ab(s, *subs[k + 1])
            if j == 1:
                self.mixer(s, l)
            else:
                self.ffn(s, l, j)

    def build_hT(self, es):
        fw, NT = self.fw, self.NT
        tmpf = [(fw.sb(es, f"hx{i}", [128, D], F32), Buf()) for i in range(2)]
        hb = [(fw.sb(es, f"hb{i}", [128, D], BF16), Buf()) for i in range(2)]
        mv, bmv = self.mv, self.bab
        for t in range(NT):
            tf, btf = tmpf[t % 2]
            hbt, bhb = hb[t % 2]
            self.V(lambda h: h.tensor_tensor(out=tf[:], in0=self.x[:, t, :], in1=mv[:, 0, :], op=ALU.mult),
                   [self.bx[t], bmv], [btf])
            self.G(lambda h: h.tensor_tensor(out=hbt[:], in0=tf[:], in1=mv[:, 1, :], op=ALU.add),
                   [btf, bmv], [bhb])
            self.hT_transpose(t, hbt, bhb)

    def hT_transpose(self, t, hbt, bhb):
        pt, bpt = self.ringT.next()
        for kc in range(8):
            self.TR(pt[:, kc * 128:(kc + 1) * 128], hbt[:, kc * 128:(kc + 1) * 128], [bhb], [bpt], inc=(kc == 7))
        self.A(lambda h: h.copy(out=self.hT[:, :, t * 128:(t + 1) * 128],
                                in_=pt[:].rearrange("p (k c) -> p k c", k=8)),
               [bpt], [self.bhT[t // 4]])

    def begin_tail(self, es):
        fw = self.fw
        self.t_st = [(fw.sb(es, f"lnst{i}", [128, 2, 6], F32), fw.sb(es, f"lnmv{i}", [128, 4], F32), Buf()) for i in range(2)]
        self.t_q = []
        if self.has_next:
            self.t_tf = [(fw.sb(es, f"thx{i}", [128, D], F32), Buf()) for i in range(1)]
            self.t_hb = [(fw.sb(es, f"thb{i}", [128, D], BF16), Buf()) for i in range(3)]

    def tail_tile(self, t):
        self.ln_tile(t, *self.t_st[t % 2])
        if self.has_next:
            tf, btf = self.t_tf[0]
            hbt, bhb = self.t_hb[t % 3]
            mv = self.mv
            self.V(lambda h: h.tensor_tensor(out=tf[:], in0=self.x[:, t, :], in1=mv[:, 0, :], op=ALU.mult),
                   [self.bx[t], self.bab], [btf])
            self.G(lambda h: h.tensor_tensor(out=hbt[:], in0=tf[:], in1=mv[:, 1, :], op=ALU.add),
                   [btf, self.bab], [bhb])
            self.t_q.append((t, hbt, bhb))
            if len(self.t_q) > 2:
                self.hT_transpose(*self.t_q.pop(0))
        else:
            self.ldma(self.xout[:, t, :], self.x[:, t, :], self.ch_out[t % 4], reads=[self.bx[t]])

    def end_tail(self):
        while self.t_q:
            self.hT_transpose(*self.t_q.pop(0))

    def resid_update(self, t, half, ps, bps, first, tmp, btmp):
        sl = slice(half * 512, (half + 1) * 512)
        mv, bmv = self.mv, self.bmv
        self.V(lambda h: h.tensor_tensor(out=tmp[:], in0=ps[:], in1=mv[:, 2, sl], op=ALU.mult),
               [bps, bmv], [btmp])
        xs = self.x[:, t, sl]
        if first:
            self.V(lambda h: h.scalar_tensor_tensor(out=xs, in0=xs, scalar=ALPHA, in1=tmp[:],
                                                    op0=ALU.mult, op1=ALU.add), [btmp, self.bx[t]], [self.bx[t]])
        else:
            self.G(lambda h: h.tensor_tensor(out=xs, in0=xs, in1=tmp[:], op=ALU.add),
                   [btmp, self.bx[t]], [self.bx[t]])

    def ln_tile(self, t, s6, m4, bs):
        mv, bmv = self.mv, self.bmv
        if True:
            xt = self.x[:, t, :]
            bxt = self.bx[t]
            self.V(lambda h: h.bn_stats(out=s6[:, 0, :], in_=self.x[:, t, 0:512]), [bxt], [bs])
            self.V(lambda h: h.bn_stats(out=s6[:, 1, :], in_=self.x[:, t, 512:1024]), [bxt], [bs])
            self.V(lambda h: h.bn_aggr(out=m4[:, 0:2], in_=s6[:]), [bs], [bs])
            self.V(lambda h: h.tensor_scalar_add(out=m4[:, 2:3], in0=m4[:, 1:2], scalar1=LN_EPS), [bs], [bs])
            self.A(lambda h: h.sqrt(out=m4[:, 2:3], in_=m4[:, 2:3]), [bs], [bs])
            self.V(lambda h: h.reciprocal(out=m4[:, 2:3], in_=m4[:, 2:3]), [bs], [bs])
            self.V(lambda h: h.scalar_tensor_tensor(out=m4[:, 3:4], in0=m4[:, 0:1], scalar=-1.0, in1=m4[:, 2:3],
                                                    op0=ALU.mult, op1=ALU.mult), [bs], [bs])
            self.A(lambda h: h.activation(out=xt, in_=xt, func=AF.Identity, bias=m4[:, 3:4], scale=m4[:, 2:3]),
                   [bs, bxt], [bxt])
            self.V(lambda h: h.tensor_tensor(out=xt, in0=xt, in1=mv[:, 3, :], op=ALU.mult), [bxt, bmv], [bxt])
            self.G(lambda h: h.tensor_tensor(out=xt, in0=xt, in1=mv[:, 4, :], op=ALU.add), [bxt, bmv], [bxt])

    def ffn(self, s, l, j):
        fw, dr, NT, NG, S = self.fw, self.dr, self.NT, self.NG, self.S
        w_in = dr["ffn1_w_in" if j == 0 else "ffn2_w_in"][l].rearrange("(kc p) n -> p kc n", p=128)
        w_out = dr["ffn1_w_out" if j == 0 else "ffn2_w_out"][l].rearrange("(j p) n -> p j n", p=128)
        with ExitStack() as es:
            self.begin_tail(es)
            actT = fw.sb(es, "actT", [128, 6, S], BF16)
            bact = [Buf() for _ in range(NG)]
            wo = [(fw.sb(es, f"wo{i}", [128, 6, D], BF16), Buf(), fw.new_chan(f"wo{i}")) for i in range(2)]
            wi = [(fw.sb(es, f"wi{i}", [128, 8, 512], BF16), Buf(), fw.new_chan(f"wi{i}")) for i in range(2)]
            sgs = [(fw.sb(es, f"sg{i}", [128, 512], F32), Buf()) for i in range(2)]
            tmps = [(fw.sb(es, f"ft{i}", [128, 512], F32), Buf()) for i in range(2)]
            parts = [(0, 6), (6, 12), (12, 18), (18, 22)]
            iw = 0
            k = 0
            for pi, (j0, j1) in enumerate(parts):
                wot, bwo, chwo = wo[pi % 2]
                self.wdma(wot[:, 0:j1 - j0, :], w_out[:, j0:j1, :], chwo, writes=[bwo])
                for jj in range(j0, j1, 2):
                    wt, bw, chw = wi[iw % 2]
                    iw += 1
                    self.wdma(wt[:, :, 0:256], w_in[:, :, jj * 128:jj * 128 + 256], chw, writes=[bw])
                    self.wdma(wt[:, :, 256:512], w_in[:, :, DFF + jj * 128:DFF + jj * 128 + 256], chw, writes=[bw])
                    for c in range(2):
                        jl = jj + c - j0
                        for g in range(NG):
                            pg, bpg = self.ringA.next()
                            pu, bpu = self.ringA.next()
                            tok = slice(g * 512, (g + 1) * 512)
                            for kc in range(8):
                                self.MM(pg[:], wt[:, kc, c * 128:(c + 1) * 128], self.hT[:, kc, tok], kc == 0, kc == 7,
                                        [bw, self.bhT[g]], [bpg])
                            for kc in range(8):
                                self.MM(pu[:], wt[:, kc, 256 + c * 128:256 + (c + 1) * 128], self.hT[:, kc, tok],
                                        kc == 0, kc == 7, [bw, self.bhT[g]], [bpu])
                            sg, bsg = sgs[k % 2]
                            k += 1
                            self.A(lambda h: h.activation(out=sg[:], in_=pg[:], func=AF.Silu), [bpg], [bsg])
                            self.V(lambda h: h.tensor_tensor(out=actT[:, jl, tok], in0=sg[:], in1=pu[:], op=ALU.mult),
                                   [bsg, bpu], [bact[g]])
                nj = j1 - j0
                for t in range(NT):
                    for half in range(2):
                        ps, bps = self.ringB.next()
                        for jl in range(nj):
                            self.MM(ps[:], actT[:, jl, t * 128:(t + 1) * 128], wot[:, jl, half * 512:(half + 1) * 512],
                                    jl == 0, jl == nj - 1, [bact[t // 4], bwo], [bps])
                        tmp, btmp = tmps[(2 * t + half) % 2]
                        self.resid_update(t, half, ps, bps, pi == 0, tmp, btmp)
                    if pi == len(parts) - 1:
                        self.tail_tile(t)
            self.end_tail()
            fw.barrier()
        self.dump(f"x_{s}_{l}_{j}", self.x[:], [128, NT, D], self.bx)

    def mixer(self, s, l):
        fw = self.fw
        with ExitStack() as es:
            first = True
            active = [m for m in range(4) if m not in self.skip_mixers]
            fns = [self.mix_hgrn, self.mix_mla, self.mix_fox, self.mix_gmlp]
            for m in active:
                self.tail_on = (m == active[-1])
                with ExitStack() as es2:
                    fns[m](es2, s, l, m, first)
                    fw.barrier()
                first = False
        self.dump(f"x_{s}_{l}_1", self.x[:], [128, self.NT, D], self.bx)

    skip_mixers = ()
    B1_MLA = False
    B1_FOX = False

    def out_proj(self, es, l, m, mixT, bmix, first):
        fw, dr, NT = self.fw, self.dr, self.NT
        wov = dr["mix_w_out"][l, m * 256:(m + 1) * 256, :].rearrange("(s p) n -> p s n", p=64)
        wo = fw.sb(es, "mwo", [64, 4, D], BF16)
        bwo = Buf()
        ch = fw.new_chan("mwo")
        self.wdma(wo[:], wov, ch, writes=[bwo])
        tmps = [(fw.sb(es, f"ot{i}", [128, 512], F32), Buf()) for i in range(2)]
        if self.tail_on:
            self.begin_tail(es)
        for t in range(NT):
            for half in range(2):
                ps, bps = self.ringB.next()
                for sl in range(4):
                    self.MM(ps[:], mixT[0:64, sl, t * 128:(t + 1) * 128], wo[0:64, sl, half * 512:(half + 1) * 512],
                            sl == 0, sl == 3, [bmix, bwo], [bps])
                tmp, btmp = tmps[(2 * t + half) % 2]
                self.resid_update(t, half, ps, bps, first, tmp, btmp)
            if self.tail_on:
                self.tail_tile(t)
        if self.tail_on:
            self.end_tail()

    def attention(self, es, QT, KT, bq, bk, dk, Vaug, bv, scale, mixT, bmix):
        fw, dr, NG = self.fw, self.dr, self.NG
        negm = fw.sb(es, "negm", [128, 4, 512], BF16)
        sel = fw.sb(es, "sel65", [65, 64], F32)
        bc = Buf()
        ch = fw.new_chan("attc")
        self.wdma(negm[:], dr["negmask"], ch, writes=[bc])
        self.ldma(sel[:], dr["sel65"], ch, writes=[bc])
        tmpS = [(fw.sb(es, f"ts{i}", [128, 512], F32), Buf()) for i in range(2)]
        pTs = [(fw.sb(es, f"pT{i}", [128, 512], BF16), Buf()) for i in range(3)]
        osb = [(fw.sb(es, f"os{i}", [65, 512], F32), Buf()) for i in range(2)]
        it = 0
        ip = 0
        fin_pending = [None]
        for hd in range(4):
            for qg in range(NG):
                qs = slice(qg * 512, (qg + 1) * 512)
                nkb = 4 * (qg + 1)
                o_ps, bo = self.ringB.next()
                pend = []

                def qk(kb):
                    nonlocal ip
                    s_ps, bs = self.ringA.next()
                    self.MM(s_ps[:], KT[hd][0:dk, kb * 128:(kb + 1) * 128], QT[hd][0:dk, qs], True, True,
                            bk[kb // 4] + bq[qg], [bs])
                    pT, bp = pTs[ip % 3]
                    ip += 1
                    r = kb - 4 * qg
                    if r >= 0:
                        tS, bt = tmpS[kb % 2]
                        self.V(lambda h: h.tensor_tensor(out=tS[:], in0=s_ps[:], in1=negm[:, r, :], op=ALU.add),
                               [bs, bc], [bt])
                        self.A(lambda h: h.activation(out=pT[:], in_=tS[:], func=AF.Exp, scale=scale), [bt], [bp])
                    else:
                        self.A(lambda h: h.activation(out=pT[:], in_=s_ps[:], func=AF.Exp, scale=scale), [bs], [bp])
                    return pT, bp

                def pv(kb, pT, bp):
                    self.MM(o_ps[0:65, :], Vaug[:, kb, hd, :], pT[:], kb == 0, kb == nkb - 1, [bv, bp], [bo])

                LA = 2
                for kb in range(nkb + LA):
                    if kb < nkb:
                        pend.append((kb,) + qk(kb))
                    if kb == 1 and fin_pending[0] is not None:
                        fin_pending[0]()
                        fin_pending[0] = None
                    if kb >= LA:
                        a = pend.pop(0)
                        pv(*a)
                o_sb, bos = osb[it % 2]
                it += 1
                self.A(lambda h: h.copy(out=o_sb[:], in_=o_ps[0:65, :]), [bo], [bos])
                self.A(lambda h: h.activation(out=o_sb[64:65, :], in_=o_sb[64:65, :], func=AF.Ln), [bos], [bos])
                self.A(lambda h: h.activation(out=o_sb[64:65, :], in_=o_sb[64:65, :], func=AF.Exp, scale=-1.0), [bos], [bos])

                def fin(o_sb=o_sb, bos=bos, hd=hd, qs=qs):
                    d_ps, bd = self.ringB.next()
                    self.MM(d_ps[0:64, :], sel[:], o_sb[:], True, True, [bc, bos], [bd])
                    self.V(lambda h: h.tensor_tensor(out=mixT[0:64, hd, qs], in0=o_sb[0:64, :], in1=d_ps[0:64, :],
                                                     op=ALU.mult), [bos, bd], [bmix])
                fin_pending[0] = fin
        if fin_pending[0] is not None:
            fin_pending[0]()

    def mix_mla(self, es, s, l, m, first):
        fw, dr, NT, NG, S = self.fw, self.dr, self.NT, self.NG, self.S
        QT = [fw.sb(es, f"QT{h}", [96, S], BF16) for h in range(4)]
        KT = [fw.sb(es, f"KT{h}", [96, S], BF16) for h in range(4)]
        if self.B1_MLA:
            bq = [[Buf()] for _ in range(NG)]
            bk = [[Buf()] for _ in range(NG)]
        else:
            one = Buf()
            bq = [[one] for _ in range(NG)]
            bk = [[one] for _ in range(NG)]
        Vaug = fw.sb(es, "Vaug", [128, NT, 4, 65], BF16)
        bv = Buf()
        self.G(lambda h: h.memset(Vaug[:], 1.0), [], [bv])
        with ExitStack() as e1:
            wv = dr["mix_w_in"][l].rearrange("(kc p) n -> p kc n", p=128)
            wb = fw.sb(e1, "wB", [128, 8, 384], BF16)
            wkr = fw.sb(e1, "wkr", [128, 8, 192], BF16)
            wuq = fw.sb(e1, "wuq", [128, 2, 384], BF16)
            wuqp = fw.sb(e1, "wuqp", [128, 2, 384], BF16)
            wukv = fw.sb(e1, "wukv", [128, 512], BF16)
            gq = fw.sb(e1, "gq", [128, 256], F32)
            gkv = fw.sb(e1, "gkv", [128, 128], F32)
            bw = Buf()
            ch = fw.new_chan("mlaw")
            self.wdma(wb[:], wv[:, :, 1024:1408], ch, writes=[bw])
            self.wdma(wkr[:], dr["wkr"][l].rearrange("(kc p) n -> p kc n", p=128), ch, writes=[bw])
            self.wdma(wuq[:], dr["mla_w_uq"][l].rearrange("(kc p) n -> p kc n", p=128), ch, writes=[bw])
            self.wdma(wuqp[:], dr["mla_w_uqp"][l].rearrange("(kc p) n -> p kc n", p=128), ch, writes=[bw])
            self.wdma(wukv[:], dr["mla_w_ukv"][l], ch, writes=[bw])
            self.ldma(gq[:], dr["mla_q_norm_g"][l:l + 1, :].broadcast_to([128, 256]), ch, writes=[bw])
            self.ldma(gkv[:], dr["mla_kv_norm_g"][l:l + 1, :].broadcast_to([128, 128]), ch, writes=[bw])
            cqnT = fw.sb(e1, "cqnT", [128, 2, S], BF16)
            ckvT = fw.sb(e1, "ckvT", [128, S], BF16)
            bcT = [Buf() for _ in range(NG)]
            rc = [(fw.sb(e1, f"rC{i}", [96, 512], F32), fw.sb(e1, f"rS{i}", [96, 512], F32), Buf(),
                   fw.new_chan(f"rope{i}")) for i in range(2)]
            st = [(fw.sb(e1, f"mst{i}", [128, 2, 6], F32), fw.sb(e1, f"mmv{i}", [128, 8], F32), Buf()) for i in range(2)]
            cn = [(fw.sb(e1, f"cn{i}", [128, 384], BF16), Buf()) for i in range(2)]
            def stageA(t):
                tok = slice(t * 128, (t + 1) * 128)
                ps, bps = self.ringA.next()
                for kc in range(8):
                    self.MM(ps[:, 0:384], self.hT[:, kc, tok], wb[:, kc, :], kc == 0, kc == 7, [self.bhT[t // 4], bw], [bps])
                s6, m8, bs = st[t % 2]
                cnt, bcn = cn[t % 2]
                self.V(lambda h: h.bn_stats(out=s6[:, 0, :], in_=ps[:, 0:256]), [bps], [bs])
                self.V(lambda h: h.bn_stats(out=s6[:, 1, :], in_=ps[:, 256:384]), [bps], [bs])
                for i in range(2):
                    self.V(lambda h: h.bn_aggr(out=m8[:, 4 * i:4 * i + 2], in_=s6[:, i:i + 1, :]), [bs], [bs])
                    self.V(lambda h: h.scalar_tensor_tensor(out=m8[:, 4 * i + 2:4 * i + 3], in0=m8[:, 4 * i:4 * i + 1],
                                                            scalar=m8[:, 4 * i:4 * i + 1], in1=m8[:, 4 * i + 1:4 * i + 2],
                                                            op0=ALU.mult, op1=ALU.add), [bs], [bs])
                    self.V(lambda h: h.tensor_scalar_add(out=m8[:, 4 * i + 2:4 * i + 3], in0=m8[:, 4 * i + 2:4 * i + 3],
                                                         scalar1=RMS_EPS), [bs], [bs])
                    self.A(lambda h: h.sqrt(out=m8[:, 4 * i + 2:4 * i + 3], in_=m8[:, 4 * i + 2:4 * i + 3]), [bs], [bs])
                    self.V(lambda h: h.reciprocal(out=m8[:, 4 * i + 3:4 * i + 4], in_=m8[:, 4 * i + 2:4 * i + 3]), [bs], [bs])
                self.V(lambda h: h.scalar_tensor_tensor(out=cnt[:, 0:256], in0=ps[:, 0:256], scalar=m8[:, 3:4], in1=gq[:],
                                                        op0=ALU.mult, op1=ALU.mult), [bps, bs, bw], [bcn])
                self.V(lambda h: h.scalar_tensor_tensor(out=cnt[:, 256:384], in0=ps[:, 256:384], scalar=m8[:, 7:8],
                                                        in1=gkv[:], op0=ALU.mult, op1=ALU.mult), [bps, bs, bw], [bcn])

            def stageB(t):
                tok = slice(t * 128, (t + 1) * 128)
                cnt, bcn = cn[t % 2]
                pt, bpt = self.ringT.next()
                for c in range(3):
                    self.TR(pt[:, c * 128:(c + 1) * 128], cnt[:, c * 128:(c + 1) * 128], [bcn], [bpt], inc=(c == 2))
                self.A(lambda h: h.copy(out=cqnT[:, :, tok], in_=pt[:, 0:256].rearrange("p (k c) -> p k c", k=2)),
                       [bpt], [bcT[t // 4]])
                self.A(lambda h: h.copy(out=ckvT[:, tok], in_=pt[:, 256:384]), [bpt], [bcT[t // 4]])
                pv_, bpv = self.ringB.next()
                self.MM(pv_[:, 0:256].rearrange("p (h x) -> p h x", h=4), ckvT[:, tok],
                        wukv[:].rearrange("p (h x) -> p h x", h=4)[:, :, 64:128], True, True,
                        [bcT[t // 4], bw], [bpv])
                self.A(lambda h: h.copy(out=Vaug[:, t, :, 0:64], in_=pv_[:, 0:256].rearrange("p (h x) -> p h x", h=4)),
                       [bpv], [bv])

            stageA(0)
            for t in range(NT):
                if t + 1 < NT:
                    stageA(t + 1)
                stageB(t)
            t1s = [(fw.sb(e1, f"r1{i}", [96, 512], F32), Buf()) for i in range(2)]
            t2s = [(fw.sb(e1, f"r2{i}", [96, 512], F32), Buf()) for i in range(2)]
            k = 0
            for g in range(NG):
                tok = slice(g * 512, (g + 1) * 512)
                rC, rS, brp, chr_ = rc[g % 2]
                self.ldma(rC[:], dr["ropeC"][:, tok], chr_, writes=[brp])
                self.ldma(rS[:], dr["ropeS"][:, tok], chr_, writes=[brp])
                for hd in range(4):
                    pq, bpq = self.ringA.next()
                    pp, bpp = self.ringA.next()
                    for kc in range(2):
                        self.MM(pq[0:96, :], wuq[:, kc, hd * 96:(hd + 1) * 96], cqnT[:, kc, tok], kc == 0, kc == 1,
                                [bw, bcT[g]], [bpq])
                    for kc in range(2):
                        self.MM(pp[0:96, :], wuqp[:, kc, hd * 96:(hd + 1) * 96], cqnT[:, kc, tok], kc == 0, kc == 1,
                                [bw, bcT[g]], [bpp])
                    t1, b1 = t1s[k % 2]
                    t2, b2 = t2s[k % 2]
                    k += 1
                    self.V(lambda h: h.tensor_tensor(out=t1[:], in0=pq[0:96, :], in1=rC[:], op=ALU.mult), [bpq, brp], [b1])
                    self.V(lambda h: h.tensor_tensor(out=t2[:], in0=pp[0:96, :], in1=rS[:], op=ALU.mult), [bpp, brp], [b2])
                    self.G(lambda h: h.tensor_tensor(out=QT[hd][:, tok], in0=t1[:], in1=t2[:], op=ALU.add), [b1, b2], bq[g])
                    pk, bpk = self.ringA.next()
                    self.MM(pk[0:64, :], wukv[:, hd * 128:hd * 128 + 64], ckvT[:, tok], True, True, [bw, bcT[g]], [bpk])
                    self.A(lambda h: h.copy(out=KT[hd][0:64, tok], in_=pk[0:64, :]), [bpk], bk[g])
                pq, bpq = self.ringA.next()
                pp, bpp = self.ringA.next()
                for kc in range(8):
                    self.MM(pq[0:96, :], wkr[:, kc, 0:96], self.hT[:, kc, tok], kc == 0, kc == 7, [bw, self.bhT[g]], [bpq])
                for kc in range(8):
                    self.MM(pp[0:96, :], wkr[:, kc, 96:192], self.hT[:, kc, tok], kc == 0, kc == 7, [bw, self.bhT[g]], [bpp])
                t1, b1 = t1s[k % 2]
                t2, b2 = t2s[k % 2]
                k += 1
                self.V(lambda h: h.tensor_tensor(out=t1[64:96, :], in0=pq[64:96, :], in1=rC[64:96, :], op=ALU.mult),
                       [bpq, brp], [b1])
                self.V(lambda h: h.tensor_tensor(out=t2[64:96, :], in0=pp[64:96, :], in1=rS[64:96, :], op=ALU.mult),
                       [bpp, brp], [b2])
                for hd in range(4):
                    self.G(lambda h: h.tensor_tensor(out=KT[hd][64:96, tok], in0=t1[64:96, :], in1=t2[64:96, :], op=ALU.add),
                           [b1, b2], bk[g])
            fw.barrier()
        with ExitStack() as e2:
            mixT = fw.sb(e2, "mixT", [64, 4, S], BF16)
            bmix = Buf()
            self.attention(e2, QT, KT, bq, bk, 96, Vaug, bv, float(96 ** -0.5), mixT, bmix)
            if "mla_o" in self.dumps:
                self.dump_bf16(e2, "mla_o", mixT[:], [64, 4, S], [bmix])
            self.out_proj(e2, l, m, mixT, bmix, first)
            fw.barrier()

    def dump_bf16(self, es, name, ap, shape, reads):
        if name not in self.dumps or name in self.dump_aps:
            return
        t = self.fw.sb(es, "dmp" + name, shape, F32)
        b = Buf()
        self.V(lambda h: h.tensor_copy(out=t[:], in_=ap), reads, [b])
        self.dump(name, t[:], shape, [b])
        self.fw.barrier()

    def mix_fox(self, es, s, l, m, first):
        fw, dr, NT, NG, S = self.fw, self.dr, self.NT, self.NG, self.S
        QT = [fw.sb(es, f"fQT{h}", [68, S], BF16) for h in range(4)]
        KT = [fw.sb(es, f"fKT{h}", [68, S], BF16) for h in range(4)]
        if self.B1_FOX:
            bq = [[Buf(), Buf()] for _ in range(NG)]
            bk = [[Buf(), Buf()] for _ in range(NG)]
            allaug = [b[1] for b in bq] + [b[1] for b in bk]
        else:
            one = Buf()
            bq = [[one, one] for _ in range(NG)]
            bk = [[one, one] for _ in range(NG)]
            allaug = [one]
        Vaug = fw.sb(es, "fVaug", [128, NT, 4, 65], BF16)
        bv = Buf()
        self.G(lambda h: h.memset(Vaug[:], 1.0), [], [bv])
        for hd in range(4):
            self.G(lambda h: h.memset(QT[hd][64:68, :], 1.0), [], allaug)
            self.G(lambda h: h.memset(KT[hd][64:68, :], 1.0), [], allaug)
        with ExitStack() as e1:
            wv = dr["mix_w_in"][l].rearrange("(kc p) n -> p kc n", p=128)
            wc = fw.sb(e1, "wC", [128, 8, 772], BF16)
            bw = Buf()
            ch = fw.new_chan("foxw")
            self.wdma(wc[:], wv[:, :, 1440:2212], ch, writes=[bw])
            nbf = fw.sb(e1, "nbf", [4, 1], F32)
            self.ldma(nbf[:], dr["fox_b_f"][l:l + 1, :].rearrange("o h -> h o"), ch, writes=[bw])
            self.V(lambda h: h.tensor_scalar_mul(out=nbf[:], in0=nbf[:], scalar1=-1.0), [bw], [bw])
            ones4 = fw.sb(e1, "ones4", [4, 512], F32)
            carry = fw.sb(e1, "carry", [4, 1], F32)
            bcar = Buf()
            self.G(lambda h: h.memset(ones4[:], 1.0), [], [bcar])
            self.G(lambda h: h.memset(carry[:], 0.0), [bcar], [bcar])
            ch_row = [fw.new_chan(f"frow{i}") for i in range(2)]
            for t in range(NT):
                tok = slice(t * 128, (t + 1) * 128)
                pv_, bpv = self.ringB.next()
                for kc in range(8):
                    self.MM(pv_[:, 0:256], self.hT[:, kc, tok], wc[:, kc, 512:768], kc == 0, kc == 7,
                            [self.bhT[t // 4], bw], [bpv])
                self.A(lambda h: h.copy(out=Vaug[:, t, :, 0:64], in_=pv_[:, 0:256].rearrange("p (h x) -> p h x", h=4)),
                       [bpv], [bv])
            ft = [dict(e=fw.sb(e1, f"fe{i}", [4, 512], F32), fn=fw.sb(e1, f"ffn{i}", [4, 512], F32),
                       hi=fw.sb(e1, f"fhi{i}", [4, 512], BF16), hif=fw.sb(e1, f"fhf{i}", [4, 512], F32),
                       lo=fw.sb(e1, f"flo{i}", [4, 512], BF16), nhi=fw.sb(e1, f"fnh{i}", [4, 512], BF16),
                       nlo=fw.sb(e1, f"fnl{i}", [4, 512], BF16), b=Buf()) for i in range(2)]
            for g in range(NG):
                tok = slice(g * 512, (g + 1) * 512)
                for hd in range(4):
                    for which, dst, bdst in ((0, QT, bq[g][0]), (1, KT, bk[g][0])):
                        pq, bpq = self.ringA.next()
                        for kc in range(8):
                            self.MM(pq[0:64, :], wc[:, kc, which * 256 + hd * 64:which * 256 + (hd + 1) * 64],
                                    self.hT[:, kc, tok], kc == 0, kc == 7, [bw, self.bhT[g]], [bpq])
                        self.A(lambda h: h.copy(out=dst[hd][0:64, tok], in_=pq[0:64, :]), [bpq], [bdst])
                pf, bpf = self.ringA.next()
                for kc in range(8):
                    self.MM(pf[0:4, :], wc[:, kc, 768:772], self.hT[:, kc, tok], kc == 0, kc == 7, [bw, self.bhT[g]], [bpf])
                f = ft[g % 2]
                bf_ = f["b"]
                self.A(lambda h: h.activation(out=f["e"][:], in_=pf[0:4, :], func=AF.Exp, bias=nbf[:], scale=-1.0),
                       [bpf, bw], [bf_])
                self.A(lambda h: h.activation(out=f["e"][:], in_=f["e"][:], func=AF.Ln, bias=1.0, scale=1.0), [bf_], [bf_])
                self.V(lambda h: h.tensor_tensor_scan(out=f["fn"][:], data0=ones4[:], data1=f["e"][:], initial=carry[:],
                                                      op0=ALU.mult, op1=ALU.add), [bf_, bcar], [bf_])
                self.V(lambda h: h.tensor_copy(out=carry[:], in_=f["fn"][:, 511:512]), [bf_], [bcar])
                self.V(lambda h: h.tensor_scalar_mul(out=f["fn"][:], in0=f["fn"][:], scalar1=8.0), [bf_], [bf_])
                self.V(lambda h: h.tensor_copy(out=f["hi"][:], in_=f["fn"][:]), [bf_], [bf_])
                self.V(lambda h: h.tensor_copy(out=f["hif"][:], in_=f["hi"][:]), [bf_], [bf_])
                self.V(lambda h: h.tensor_tensor(out=f["lo"][:], in0=f["fn"][:], in1=f["hif"][:], op=ALU.subtract),
                       [bf_], [bf_])
                self.V(lambda h: h.tensor_scalar_mul(out=f["nhi"][:], in0=f["hi"][:], scalar1=-1.0), [bf_], [bf_])
                self.V(lambda h: h.tensor_scalar_mul(out=f["nlo"][:], in0=f["lo"][:], scalar1=-1.0), [bf_], [bf_])
                chr_ = ch_row[g % 2]
                moves = []
                for hd in range(4):
                    moves += [(QT[hd][64:65, tok], f["nhi"][hd:hd + 1, :]), (QT[hd][65:66, tok], f["nlo"][hd:hd + 1, :]),
                              (KT[hd][66:67, tok], f["hi"][hd:hd + 1, :]), (KT[hd][67:68, tok], f["lo"][hd:hd + 1, :])]
                wb_ = list({id(b): b for b in (bq[g][1], bk[g][1])}.values())
                for i, (dst_, src_) in enumerate(moves):
                    last = (i == len(moves) - 1)
                    self.ldma(dst_, src_, chr_, reads=([bf_] if last else [bf_] + wb_), writes=(wb_ if last else []))
            fw.barrier()
        with ExitStack() as e2:
            mixT = fw.sb(e2, "fmixT", [64, 4, S], BF16)
            bmix = Buf()
            self.attention(e2, QT, KT, bq, bk, 68, Vaug, bv, 0.125, mixT, bmix)
            if "fox_o" in self.dumps:
                self.dump_bf16(e2, "fox_o", mixT[:], [64, 4, S], [bmix])
            self.out_proj(e2, l, m, mixT, bmix, first)
            fw.barrier()

    def mix_gmlp(self, es, s, l, m, first):
        fw, dr, NT, NG, S = self.fw, self.dr, self.NT, self.NG, self.S
        wv = dr["mix_w_in"][l].rearrange("(kc p) n -> p kc n", p=128)
        wd = fw.sb(es, "wD", [128, 8, 512], BF16)
        bw = Buf()
        ch = fw.new_chan("gmw")
        self.wdma(wd[:], wv[:, :, 2212:2724], ch, writes=[bw])
        wsf = fw.sb(es, "wsf", [128, 4, 128], F32)
        caus = fw.sb(es, "caus", [128, 128], F32)
        wsm = fw.sb(es, "wsm", [128, 4, 128], BF16)
        bsb = fw.sb(es, "bsb", [64, 4, 128], F32)
        lng = fw.sb(es, "glng", [128, 256], F32)
        lnb = fw.sb(es, "glnb", [128, 256], F32)
        self.ldma(wsf[:], dr["gmlp_wsT"][l].rearrange("g s t -> s g t"), ch, writes=[bw])
        self.ldma(caus[:], dr["caus"], ch, writes=[bw])
        self.ldma(bsb[:], dr["gmlp_b_s"][l:l + 1].broadcast_to([64, 4, 128]), ch, writes=[bw])
        self.ldma(lng[:], dr["gmlp_ln_g"][l:l + 1, :].broadcast_to([128, 256]), ch, writes=[bw])
        self.ldma(lnb[:], dr["gmlp_ln_b"][l:l + 1, :].broadcast_to([128, 256]), ch, writes=[bw])
        self.V(lambda h: h.tensor_tensor(out=wsm[:], in0=wsf[:], in1=caus[:].unsqueeze(1).broadcast_to([128, 4, 128]),
                                         op=ALU.mult), [bw], [bw])
        mixT = fw.sb(es, "gmixT", [64, 4, S], BF16)
        bmix = Buf()
        self.mod_pending = []
        if s == 0 and l == 0 and self.mod_deferred:
            self.mod_alloc(es)
            self.mod_pending = [(ll, cb) for ll in range(1, self.L) for cb in range(18)]
        gv = [(fw.sb(es, f"gv{i}", [128, 256], F32), Buf()) for i in range(2)]
        vl = [(fw.sb(es, f"vl{i}", [128, 256], BF16), Buf()) for i in range(2)]
        st = [(fw.sb(es, f"gst{i}", [128, 6], F32), fw.sb(es, f"gmv{i}", [128, 4], F32), Buf()) for i in range(2)]
        gu = [(fw.sb(es, f"gu{i}", [64, 512], F32), Buf()) for i in range(2)]
        t1s = [(fw.sb(es, f"gt{i}", [64, 512], F32), Buf()) for i in range(2)]
        def stageA(t):
            tok = slice(t * 128, (t + 1) * 128)
            bh = self.bhT[t // 4]
            pv_, bpv = self.ringA.next()
            for kc in range(8):
                self.MM(pv_[:, 0:256], self.hT[:, kc, tok], wd[:, kc, 256:512], kc == 0, kc == 7, [bh, bw], [bpv])
            g_, bg = gv[t % 2]
            v_, bvl = vl[t % 2]
            s6, m4, bs = st[t % 2]
            self.A(lambda h: h.activation(out=g_[:], in_=pv_[:, 0:256], func=AF.Gelu_apprx_tanh), [bpv], [bg])
            self.V(lambda h: h.bn_stats(out=s6[:], in_=g_[:]), [bg], [bs])
            self.V(lambda h: h.bn_aggr(out=m4[:, 0:2], in_=s6[:]), [bs], [bs])
            self.V(lambda h: h.tensor_scalar_add(out=m4[:, 2:3], in0=m4[:, 1:2], scalar1=LN_EPS), [bs], [bs])
            self.A(lambda h: h.sqrt(out=m4[:, 2:3], in_=m4[:, 2:3]), [bs], [bs])
            self.V(lambda h: h.reciprocal(out=m4[:, 2:3], in_=m4[:, 2:3]), [bs], [bs])
            self.V(lambda h: h.scalar_tensor_tensor(out=m4[:, 3:4], in0=m4[:, 0:1], scalar=-1.0, in1=m4[:, 2:3],
                                                    op0=ALU.mult, op1=ALU.mult), [bs], [bs])
            self.A(lambda h: h.activation(out=g_[:], in_=g_[:], func=AF.Identity, bias=m4[:, 3:4], scale=m4[:, 2:3]),
                   [bs, bg], [bg])
            self.V(lambda h: h.tensor_tensor(out=g_[:], in0=g_[:], in1=lng[:], op=ALU.mult), [bg, bw], [bg])
            self.G(lambda h: h.tensor_tensor(out=v_[:], in0=g_[:], in1=lnb[:], op=ALU.add), [bg, bw], [bvl])

        def stageB(t):
            tok = slice(t * 128, (t + 1) * 128)
            bh = self.bhT[t // 4]
            v_, bvl = vl[t % 2]
            pm, bpm = self.ringA.next()
            for g in range(4):
                self.MM(pm[0:64, g * 128:(g + 1) * 128], v_[:, g * 64:(g + 1) * 64], wsm[:, g, :], g == 0, True,
                        [bvl, bw], [bpm], inc=(g == 3), skip=True)
            pu, bpu = self.ringA.next()
            for g in range(4):
                for kc in range(8):
                    self.MM(pu[0:64, g * 128:(g + 1) * 128], wd[:, kc, g * 64:(g + 1) * 64], self.hT[:, kc, tok],
                            kc == 0, kc == 7, [bh, bw], [bpu], inc=(g == 3 and kc == 7), skip=True)
            u_, bu = gu[t % 2]
            t1, b1 = t1s[t % 2]
            self.A(lambda h: h.activation(out=u_[:], in_=pu[0:64, :], func=AF.Gelu_apprx_tanh), [bpu], [bu])
            self.V(lambda h: h.tensor_tensor(out=t1[:], in0=pm[0:64, :], in1=bsb[:].rearrange("p g t -> p (g t)"),
                                             op=ALU.add), [bpm, bw], [b1])
            self.V(lambda h: h.tensor_tensor(out=mixT[0:64, :, tok], in0=t1[:].rearrange("p (g t) -> p g t", g=4),
                                             in1=u_[:].rearrange("p (g t) -> p g t", g=4), op=ALU.mult),
                   [b1, bu], [bmix])
            for _ in range(2):
                if self.mod_pending:
                    self.mod_block(*self.mod_pending.pop(0))

        stageA(0)
        for t in range(NT):
            if t + 1 < NT:
                stageA(t + 1)
            stageB(t)
        while self.mod_pending:
            self.mod_block(*self.mod_pending.pop(0))
        if "gmlp_o" in self.dumps:
            self.dump_bf16(es, "gmlp_o", mixT[:], [64, 4, S], [bmix])
        self.out_proj(es, l, m, mixT, bmix, first)

    def mix_hgrn(self, es, s, l, m, first):
        fw, dr, NT, NG, S = self.fw, self.dr, self.NT, self.NG, self.S
        wv = dr["mix_w_in"][l].rearrange("(kc p) n -> p kc n", p=128)
        wa = fw.sb(es, "wA", [128, 8, 1024], BF16)
        bw = Buf()
        ch = fw.new_chan("hgw")
        self.wdma(wa[:], wv[:, :, 0:1024], ch, writes=[bw])
        tri = fw.sb(es, "tri", [128, 128], F32)
        blk = fw.sb(es, "blk", [128, 128], F32)
        csel = fw.sb(es, "csel", [128, 8], F32)
        cselb = fw.sb(es, "cselb", [128, 8], BF16)
        ones64 = fw.sb(es, "ones64", [64, 64], F32)
        ng = fw.sb(es, "ng", [64, 4], F32)
        lb = fw.sb(es, "lb", [128, 256], F32)
        oml = fw.sb(es, "oml", [128, 256], F32)
        self.ldma(tri[:], dr["tri"], ch, writes=[bw])
        self.ldma(blk[:], dr["blk"], ch, writes=[bw])
        self.ldma(csel[:], dr["csel"], ch, writes=[bw])
        self.ldma(ones64[:], dr["ones64"], ch, writes=[bw])
        self.ldma(ng[:], dr["hgrn_ng"][l], ch, writes=[bw])
        self.V(lambda h: h.tensor_copy(out=cselb[:], in_=csel[:]), [bw], [bw])
        if l == 0:
            self.G(lambda h: h.memset(lb[:], 0.0), [], [bw])
        else:
            assert self.L == 2
            self.ldma(lb[:], dr["hgrn_lb_logits"][1:2, :].broadcast_to([128, 256]), ch, writes=[bw])
            self.ldma(oml[:], dr["hgrn_lb_logits"][0:1, :].broadcast_to([128, 256]), ch, writes=[bw])
            self.V(lambda h: h.tensor_tensor(out=lb[:], in0=lb[:], in1=oml[:], op=ALU.subtract), [bw], [bw])
            self.A(lambda h: h.activation(out=lb[:], in_=lb[:], func=AF.Sigmoid), [bw], [bw])
        self.V(lambda h: h.tensor_scalar(out=oml[:], in0=lb[:], scalar1=-1.0, scalar2=1.0, op0=ALU.mult, op1=ALU.add),
               [bw], [bw])
        mixT = fw.sb(es, "hmixT", [64, 4, S], BF16)
        bmix = Buf()
        Sst = fw.sb(es, "Sst", [64, 4, 64], F32)
        Sbf = fw.sb(es, "Sbf", [64, 4, 64], BF16)
        bS = Buf()
        bSb = Buf()
        self.G(lambda h: h.memset(Sst[:], 0.0), [], [bS])
        self.G(lambda h: h.memset(Sbf[:], 0.0), [], [bSb])

        def T(name, shape, dt, n=2):
            r = [(fw.sb(es, f"{name}{i}", shape, dt), Buf()) for i in range(n)]
            return r * (2 // n)
        f_ = T("hf", [128, 256], F32, 1)
        lf_ = T("hlf", [128, 256], F32)
        kk_ = T("hkk", [128, 256], F32, 1)
        eg_ = T("heg", [128, 768], F32, 1)
        qf_ = T("hqf", [128, 256], F32, 1)
        qt_ = T("hqt", [128, 256], BF16)
        kh_ = T("hkh", [128, 256], F32, 1)
        khb_ = T("hkhb", [128, 256], BF16)
        ke_ = T("hke", [128, 256], BF16)
        kem_ = T("hkem", [128, 8, 256], BF16)
        vb_ = T("hvb", [128, 256], BF16)
        sgb_ = T("hsg", [128, 256], BF16)
        Dt_ = T("hDt", [64, 4, 8], F32)
        qkT_ = T("hqkT", [64, 8, 128], BF16)
        gT_ = T("hgT", [64, 4, 128], BF16)
        scm_ = T("hscm", [128, 4, 128], BF16)
        osq_ = T("hosq", [64, 512], F32, 1)
        rs_ = T("hrs", [64, 512], F32, 1)
        t1_ = T("ht1", [64, 512], F32, 1)
        def bufs(t):
            i2 = t % 2
            return dict(f=f_[i2], lf=lf_[i2], kk=kk_[i2], eg=eg_[i2], qf=qf_[i2], qt=qt_[i2], kh=kh_[i2], khb=khb_[i2],
                        ke=ke_[i2], kem=kem_[i2], vb=vb_[i2], sgb=sgb_[i2], Dt=Dt_[i2], qkT=qkT_[i2], gT=gT_[i2],
                        scm=scm_[i2], osq=osq_[i2], rs=rs_[i2], t1=t1_[i2])

        def front(t):
            tok = slice(t * 128, (t + 1) * 128)
            bh = self.bhT[t // 4]
            i2 = t % 2
            pa, bpa = self.ringA.next()
            pb, bpb = self.ringA.next()
            for kc in range(8):
                self.MM(pa[:], self.hT[:, kc, tok], wa[:, kc, 0:512], kc == 0, kc == 7, [bh, bw], [bpa])
            for kc in range(8):
                self.MM(pb[:], self.hT[:, kc, tok], wa[:, kc, 512:1024], kc == 0, kc == 7, [bh, bw], [bpb])
            f, bf_ = f_[i2]
            lf, blf = lf_[i2]
            kk, bkk = kk_[i2]
            eg, beg = eg_[i2]
            qf, bqf = qf_[i2]
            qt, bqt = qt_[i2]
            kh, bkh = kh_[i2]
            khb, bkhb = khb_[i2]
            ke, bke = ke_[i2]
            kem, bkem = kem_[i2]
            vb, bvb = vb_[i2]
            sgb, bsgb = sgb_[i2]
            Dt, bDt = Dt_[i2]
            qkT, bqkT = qkT_[i2]
            gT, bgT = gT_[i2]
            scm, bscm = scm_[i2]
            osq, bosq = osq_[i2]
            rs, brs = rs_[i2]
            t1, bt1 = t1_[i2]
            self.A(lambda h: h.activation(out=f[:], in_=pa[:, 256:512], func=AF.Sigmoid), [bpa], [bf_])
            self.A(lambda h: h.activation(out=qf[:], in_=pa[:, 0:256], func=AF.Silu), [bpa], [bqf])
            self.A(lambda h: h.copy(out=vb[:], in_=pb[:, 0:256]), [bpb], [bvb])
            self.A(lambda h: h.activation(out=sgb[:], in_=pb[:, 256:512], func=AF.Silu), [bpb], [bsgb])
            yield
            self.V(lambda h: h.tensor_tensor(out=f[:], in0=f[:], in1=oml[:], op=ALU.mult), [bf_, bw], [bf_])
            self.V(lambda h: h.tensor_tensor(out=f[:], in0=f[:], in1=lb[:], op=ALU.add), [bf_, bw], [bf_])
            self.A(lambda h: h.activation(out=lf[:], in_=f[:], func=AF.Ln), [bf_], [blf])
            self.V(lambda h: h.tensor_scalar(out=kk[:], in0=f[:], scalar1=-1.0, scalar2=1.0, op0=ALU.mult, op1=ALU.add),
                   [bf_], [bkk])
            yield
            pg, bpg = self.ringA.next()
            self.MM(pg[:, 0:256], tri[:], lf[:], True, True, [bw, blf], [bpg], inc=False)
            self.MM(pg[:, 256:512], blk[:], lf[:], True, True, [bw, blf], [bpg], skip=True)
            pd, bpd = self.ringB.next()
            for hd in range(4):
                self.MM(pd[0:64, hd * 8:(hd + 1) * 8], lf[:, hd * 64:(hd + 1) * 64], csel[:], hd == 0, True,
                        [bw, blf], [bpd], inc=(hd == 3), skip=True)
            self.A(lambda h: h.activation(out=eg[:, 0:256], in_=pg[:, 0:256], func=AF.Exp), [bpg], [beg])
            self.A(lambda h: h.activation(out=eg[:, 256:512], in_=pg[:, 0:256], func=AF.Exp, scale=-1.0), [bpg], [beg])
            self.A(lambda h: h.activation(out=eg[:, 512:768], in_=pg[:, 256:512], func=AF.Exp), [bpg], [beg])
            self.A(lambda h: h.activation(out=Dt[:].rearrange("p h c -> p (h c)"), in_=pd[0:64, 0:32], func=AF.Exp),
                   [bpd], [bDt])
            yield
            self.V(lambda h: h.tensor_tensor(out=qt[:], in0=qf[:], in1=eg[:, 0:256], op=ALU.mult), [bqf, beg], [bqt])
            self.V(lambda h: h.tensor_tensor(out=kh[:], in0=kk[:], in1=eg[:, 256:512], op=ALU.mult), [bkk, beg], [bkh])
            self.G(lambda h: h.tensor_copy(out=khb[:], in_=kh[:]), [bkh], [bkhb])
            self.G(lambda h: h.tensor_tensor(out=ke[:], in0=kh[:], in1=eg[:, 512:768], op=ALU.mult), [bkh, beg], [bke])
            self.G(lambda h: h.tensor_tensor(out=kem[:], in0=ke[:].unsqueeze(1).broadcast_to([128, 8, 256]),
                                             in1=cselb[:].unsqueeze(2).broadcast_to([128, 8, 256]), op=ALU.mult),
                   [bke, bw], [bkem])
            yield
            pt, bpt = self.ringT.next()
            for hd in range(4):
                self.TR(pt[0:64, hd * 128:(hd + 1) * 128], qt[:, hd * 64:(hd + 1) * 64], [bqt], [bpt], inc=False)
            for hd in range(4):
                self.TR(pt[0:64, 512 + hd * 128:512 + (hd + 1) * 128], khb[:, hd * 64:(hd + 1) * 64], [bkhb], [bpt],
                        inc=(hd == 3))
            self.A(lambda h: h.copy(out=qkT[:].rearrange("p a t -> p (a t)"), in_=pt[0:64, :]), [bpt], [bqkT])
            yield
            pt2, bpt2 = self.ringT.next()
            for hd in range(4):
                self.TR(pt2[0:64, hd * 128:(hd + 1) * 128], sgb[:, hd * 64:(hd + 1) * 64], [bsgb], [bpt2], inc=(hd == 3))
            self.A(lambda h: h.copy(out=gT[:].rearrange("p a t -> p (a t)"), in_=pt2[0:64, 0:512]), [bpt2], [bgT])
            yield
            psc, bpsc = self.ringA.next()
            for hd in range(4):
                self.MM(psc[:, hd * 128:(hd + 1) * 128], qkT[:, 4 + hd, :], qkT[:, hd, :], hd == 0, True,
                        [bqkT], [bpsc], inc=(hd == 3), skip=True)
            self.V(lambda h: h.tensor_tensor(out=scm[:], in0=psc[:].rearrange("p (a t) -> p a t", a=4),
                                             in1=tri[:].unsqueeze(1).broadcast_to([128, 4, 128]), op=ALU.mult),
                   [bpsc, bw], [bscm])
            yield

        def back(t):
            tok = slice(t * 128, (t + 1) * 128)
            B = bufs(t)
            kem, bkem = B["kem"]
            vb, bvb = B["vb"]
            Dt, bDt = B["Dt"]
            qkT, bqkT = B["qkT"]
            gT, bgT = B["gT"]
            scm, bscm = B["scm"]
            osq, bosq = B["osq"]
            rs, brs = B["rs"]
            t1, bt1 = B["t1"]
            po, bpo = self.ringB.next()
            firstmm = True
            for ci in range(8):
                c = t * 8 + ci
                if c > 0:
                    for hd in range(4):
                        self.MM(po[0:64, hd * 128 + ci * 16:hd * 128 + (ci + 1) * 16], Sbf[:, hd, :],
                                qkT[:, hd, ci * 16:(ci + 1) * 16], firstmm, False, [bSb, bqkT], [bpo], inc=False, skip=True)
                        firstmm = False
                pkv, bpkv = self.ringA.next()
                for hd in range(4):
                    self.MM(pkv[0:64, hd * 64:(hd + 1) * 64], kem[:, ci, hd * 64:(hd + 1) * 64], vb[:, hd * 64:(hd + 1) * 64],
                            hd == 0, True, [bkem, bvb], [bpkv], inc=(hd == 3), skip=True)
                self.V(lambda h: h.tensor_tensor(out=Sst[:], in0=Sst[:], in1=Dt[:, :, ci:ci + 1].broadcast_to([64, 4, 64]),
                                                 op=ALU.mult), [bS, bDt], [bS])
                self.V(lambda h: h.tensor_tensor(out=Sst[:], in0=Sst[:], in1=pkv[0:64, 0:256].rearrange("p (a b) -> p a b", a=4),
                                                 op=ALU.add), [bS, bpkv], [bS])
                self.V(lambda h: h.tensor_copy(out=Sbf[:], in_=Sst[:]), [bS], [bSb])
                yield
            for hd in range(4):
                self.MM(po[0:64, hd * 128:(hd + 1) * 128], vb[:, hd * 64:(hd + 1) * 64], scm[:, hd, :], firstmm, True,
                        [bvb, bscm], [bpo], inc=(hd == 3), skip=True)
                firstmm = False
            self.A(lambda h: h.activation(out=osq[:], in_=po[0:64, :], func=AF.Square), [bpo], [bosq])
            pss, bpss = self.ringA.next()
            self.MM(pss[0:64, :], ones64[:], osq[:], True, True, [bw, bosq], [bpss])
            self.V(lambda h: h.tensor_scalar(out=rs[:], in0=pss[0:64, :], scalar1=1.0 / 64.0, scalar2=RMS_EPS,
                                             op0=ALU.mult, op1=ALU.add), [bpss], [brs])
            self.A(lambda h: h.activation(out=rs[:], in_=rs[:], func=AF.Ln), [brs], [brs])
            self.A(lambda h: h.activation(out=rs[:], in_=rs[:], func=AF.Exp, scale=-0.5), [brs], [brs])
            self.V(lambda h: h.tensor_tensor(out=t1[:], in0=po[0:64, :], in1=rs[:], op=ALU.mult), [bpo, brs], [bt1])
            for hd in range(4):
                self.V(lambda h: h.scalar_tensor_tensor(out=mixT[0:64, hd, tok], in0=t1[:, hd * 128:(hd + 1) * 128],
                                                        scalar=ng[:, hd:hd + 1], in1=gT[:, hd, :],
                                                        op0=ALU.mult, op1=ALU.mult), [bt1, bw, bgT], [bmix])

        for _ in front(0):
            pass
        for t in range(NT):
            gb = back(t)
            gf = front(t + 1) if t + 1 < NT else iter(())
            done_b = done_f = False
            while not (done_b and done_f):
                if not done_b:
                    try:
                        next(gb)
                    except StopIteration:
                        done_b = True
                if not done_f:
                    try:
                        next(gf)
                    except StopIteration:
                        done_f = True
        if "hgrn_o" in self.dumps:
            self.dump_bf16(es, "hgrn_o", mixT[:], [64, 4, S], [bmix])
        self.out_proj(es, l, m, mixT, bmix, first)


_NC_CACHE = {}


def _get_nc(S, L):
    key = (S, L)
    if key not in _NC_CACHE:
        _NC_CACHE[key] = Builder(S, L).build()
    return _NC_CACHE[key]


def make_in_maps(inputs, n_cores, S):
    w = host_weights(inputs)
    consts = host_consts(S)
    x = np.asarray(inputs["x"], dtype=np.float32)
    c = np.asarray(inputs["c"], dtype=np.float32)
    maps = []
    for i in range(n_cores):
        cc = c[2 * i:2 * i + 2]
        cT = np.ascontiguousarray(cc.reshape(2, 8, 128).transpose(2, 1, 0))
        mp = {"x": np.ascontiguousarray(x[2 * i:2 * i + 2]), "cT": cT}
        mp.update(w)
        mp.update(consts)
        maps.append(mp)
    return maps


def kernel(**inputs):
    x = np.asarray(inputs["x"])
    B, S, _ = x.shape
    L = np.asarray(inputs["ada_w"]).shape[0]
    n = B // 2
    nc = _get_nc(S, L)
    maps = make_in_maps(inputs, n, S)
    res = run_bass_kernel_spmd(nc, maps, core_ids=list(range(n)))
    out = np.concatenate([r["out"] for r in res.results], axis=0)
    return out.astype(np.float32)
```

```python
import numpy as np
from contextlib import ExitStack
import concourse.bass as bass
import concourse.mybir as mybir
from concourse.bass_utils import run_bass_kernel_spmd

F32 = mybir.dt.float32
BF16 = mybir.dt.bfloat16
AF = mybir.ActivationFunctionType
ALU = mybir.AluOpType

D = 1024
DFF = 2816
NJ = 22
MIXC = 2724
ALPHA = float(4 ** 0.25)
LN_EPS = 1e-5
RMS_EPS = 1e-6
NEG = -30000.0


class Buf:
    __slots__ = ("w", "r", "name")

    def __init__(self, name=""):
        self.w = {}
        self.r = {}
        self.name = name


class Chan:
    def __init__(self, sem, name):
        self.sem = sem
        self.cnt = 0
        self.name = name


class Eng:
    def __init__(self, name, h, chan):
        self.name = name
        self.h = h
        self.chan = chan
        self.seen = {}


class FW:
    def __init__(self, nc, es):
        self.nc = nc
        self.es = es
        self.nsem = 0
        self.chans = []
        self.pe = Eng("pe", nc.tensor, self.new_chan("pe"))
        self.act = Eng("act", nc.scalar, self.new_chan("act"))
        self.dve = Eng("dve", nc.vector, self.new_chan("dve"))
        self.pool = Eng("pool", nc.gpsimd, self.new_chan("pool"))
        self.sp = Eng("sp", nc.sync, self.new_chan("sp"))
        self.engs = [self.pe, self.act, self.dve, self.pool, self.sp]
        self.n_inst = 0
        self.n_wait = 0
        self.uid = 0

    def new_chan(self, name):
        for ch in self.chans:
            if ch.name == name and name not in ("dbg",):
                return ch
        sem = self.es.enter_context(self.nc.semaphore(f"s_{name}_{self.nsem}"))
        self.nsem += 1
        ch = Chan(sem, name)
        self.chans.append(ch)
        return ch

    def _sync(self, E, reads, writes):
        need = {}
        for b in reads:
            for ch, c in b.w.items():
                if need.get(ch, 0) < c:
                    need[ch] = c
        for b in writes:
            for ch, c in b.w.items():
                if need.get(ch, 0) < c:
                    need[ch] = c
            for ch, c in b.r.items():
                if need.get(ch, 0) < c:
                    need[ch] = c
        for ch, c in need.items():
            if ch is E.chan and E.name == "pe":
                continue
            if E.seen.get(ch, 0) >= c:
                continue
            E.h.wait_ge(ch.sem, c)
            self.n_wait += 1
            E.seen[ch] = c

    def _mark(self, ch, mark, reads, writes):
        for b in reads:
            if b.r.get(ch, 0) < mark:
                b.r[ch] = mark
        for b in writes:
            b.w[ch] = mark
            b.r = {}

    def op(self, E, emit, reads=(), writes=(), inc=True):
        self._sync(E, reads, writes)
        ins = emit(E.h)
        self.n_inst += 1
        ch = E.chan
        if inc:
            ch.cnt += 1
            ins.then_inc(ch.sem, 1)
            mark = ch.cnt
        else:
            mark = ch.cnt + 1
        self._mark(ch, mark, reads, writes)
        return ins

    def dma(self, Q, chan, out, in_, reads=(), writes=(), **kw):
        self._sync(Q, reads, writes)
        ins = Q.h.dma_start(out=out, in_=in_, **kw)
        self.n_inst += 1
        chan.cnt += 16
        ins.then_inc(chan.sem, 16)
        self._mark(chan, chan.cnt, reads, writes)
        return ins

    def barrier(self):
        for E in self.engs:
            for ch in self.chans:
                if ch.cnt == 0 or ch is E.chan:
                    continue
                if E.seen.get(ch, 0) >= ch.cnt:
                    continue
                E.h.wait_ge(ch.sem, ch.cnt)
                E.seen[ch] = ch.cnt

    def sb(self, es, name, shape, dt):
        self.uid += 1
        return es.enter_context(self.nc.sbuf_tensor(f"sb{self.uid}_{name}", list(shape), dt))

    def psum(self, es, name, shape, dt):
        self.uid += 1
        return es.enter_context(self.nc.psum_tensor(f"pp{self.uid}_{name}", list(shape), dt))


class Ring:
    def __init__(self, items):
        self.items = items
        self.i = 0

    def next(self):
        it = self.items[self.i % len(self.items)]
        self.i += 1
        return it


def host_consts(S):
    pos = np.arange(S, dtype=np.float32)
    half = 16
    inv_freq = (10000.0 ** (-np.arange(half, dtype=np.float32) / half)).astype(np.float32)
    ang = pos[None, :] * inv_freq[:, None]
    cos = np.cos(ang).astype(np.float32)
    sin = np.sin(ang).astype(np.float32)
    ropeC = np.ones((96, S), np.float32)
    ropeS = np.zeros((96, S), np.float32)
    ropeC[64:80] = cos
    ropeC[80:96] = cos
    ropeS[64:80] = -sin
    ropeS[80:96] = sin
    s = np.arange(128)[:, None]
    negmask = np.zeros((128, 4, 512), np.float32)
    t = np.arange(512)[None, :]
    for r in range(4):
        negmask[:, r, :] = np.where(128 * r + s <= t, 0.0, NEG)
    ident = np.eye(128, dtype=np.float32)
    t128 = np.arange(128)[None, :]
    same = (s // 16) == (t128 // 16)
    tri = (same & (s <= t128)).astype(np.float32)
    blk = same.astype(np.float32)
    csel = ((s // 16) == np.arange(8)[None, :]).astype(np.float32)
    caus = (s <= t128).astype(np.float32)
    sel65 = np.zeros((65, 64), np.float32)
    sel65[64, :] = 1.0
    ones64 = np.ones((64, 64), np.float32)
    return dict(ropeC=ropeC, ropeS=ropeS, negmask=negmask, ident=ident, tri=tri, blk=blk,
                csel=csel, caus=caus, sel65=sel65, ones64=ones64)


CONST_SHAPES = lambda S: dict(ropeC=[96, S], ropeS=[96, S], negmask=[128, 4, 512], ident=[128, 128],
                              tri=[128, 128], blk=[128, 128], csel=[128, 8], caus=[128, 128],
                              sel65=[65, 64], ones64=[64, 64])


def weight_shapes(L):
    return dict(
        ada_w=[L, D, 9 * D], ada_b=[L, 9 * D], ln_g=[L, 3, D], ln_b=[L, 3, D],
        ffn1_w_in=[L, D, 2 * DFF], ffn1_w_out=[L, DFF, D], ffn2_w_in=[L, D, 2 * DFF], ffn2_w_out=[L, DFF, D],
        mix_w_in=[L, D, MIXC], mix_w_out=[L, D, D], wkr=[L, D, 192],
        hgrn_lb_logits=[L, 256], hgrn_ng=[L, 64, 4], mla_q_norm_g=[L, 256], mla_kv_norm_g=[L, 128],
        mla_w_uq=[L, 256, 384], mla_w_uqp=[L, 256, 384], mla_w_ukv=[L, 128, 512],
        fox_b_f=[L, 4], gmlp_ln_g=[L, 256], gmlp_ln_b=[L, 256], gmlp_wsT=[L, 4, 128, 128], gmlp_b_s=[L, 4, 128],
    )


def host_weights(inp):
    L = inp["ada_w"].shape[0]
    f = lambda a: np.ascontiguousarray(np.asarray(a, dtype=np.float32))
    w = {k: f(inp[k]) for k in ["ada_w", "ada_b", "ln_g", "ln_b", "ffn1_w_in", "ffn1_w_out", "ffn2_w_in",
                                "ffn2_w_out", "mix_w_in", "mix_w_out", "hgrn_lb_logits", "mla_q_norm_g",
                                "mla_kv_norm_g", "mla_w_uq", "mla_w_ukv", "fox_b_f", "gmlp_ln_g", "gmlp_ln_b",
                                "gmlp_b_s"]}
    perm = np.concatenate([np.arange(16, 32), np.arange(0, 16)])
    kr = w["mix_w_in"][:, :, 1408:1440]
    wkr = np.zeros((L, D, 192), np.float32)
    wkr[:, :, 64:96] = kr
    wkr[:, :, 160:192] = kr[:, :, perm]
    w["wkr"] = wkr
    uq = w["mla_w_uq"].reshape(L, 256, 4, 96)
    uqp = np.zeros_like(uq)
    uqp[:, :, :, 64:96] = uq[:, :, :, 64:96][:, :, :, perm]
    w["mla_w_uqp"] = np.ascontiguousarray(uqp.reshape(L, 256, 384))
    w["hgrn_ng"] = np.ascontiguousarray(f(inp["hgrn_norm_g"]).reshape(L, 4, 64).transpose(0, 2, 1))
    w["gmlp_wsT"] = np.ascontiguousarray(f(inp["gmlp_w_s"]).transpose(0, 1, 3, 2))
    return w


class Builder:
    def __init__(self, S, L, stop=None, dumps=()):
        self.S = S
        self.L = L
        self.NT = S // 128
        self.NG = S // 512
        self.stop = stop
        self.dumps = set(dumps)
        self.dump_aps = {}

    def build(self):
        S, L = self.S, self.L
        nc = bass.Bass("TRN2", target_bir_lowering=False)
        self.nc = nc
        dr = {}
        dr["x"] = nc.dram_tensor("x", [2, S, D], F32, kind="ExternalInput").ap()
        dr["cT"] = nc.dram_tensor("cT", [128, 8, 2], F32, kind="ExternalInput").ap()
        for k, shp in weight_shapes(L).items():
            dr[k] = nc.dram_tensor(k, shp, F32, kind="ExternalInput").ap()
        for k, shp in CONST_SHAPES(S).items():
            dr[k] = nc.dram_tensor(k, shp, F32, kind="ExternalInput").ap()
        dr["out"] = nc.dram_tensor("out", [2, S, D], F32, kind="ExternalOutput").ap()
        dr["modd"] = nc.dram_tensor("modd", [2, L, 9 * D], F32, kind="Internal").ap()
        self.dr = dr
        with ExitStack() as es:
            self.fw = fw = FW(nc, es)
            self.es = es
            self.setup_global(es)
            self.prologue_mod()
            for s in range(2):
                self.run_sequence(s)
                if self.stop is not None and self.stop[0] == s:
                    break
            fw.barrier()
            print(f"[build] inst={fw.n_inst} waits={fw.n_wait} sems={fw.nsem} "
                  f"cnt={[(e.name, e.chan.cnt) for e in fw.engs]}", flush=True)
        return nc

    def V(self, emit, reads=(), writes=()):
        return self.fw.op(self.fw.dve, emit, reads, writes)

    def A(self, emit, reads=(), writes=()):
        return self.fw.op(self.fw.act, emit, reads, writes)

    def G(self, emit, reads=(), writes=()):
        return self.fw.op(self.fw.pool, emit, reads, writes)

    def MM(self, out, lhsT, rhs, start, stop, reads=(), writes=(), inc=None, skip=False):
        if inc is None:
            inc = bool(stop)
        kw = dict(skip_group_check=True) if skip else {}
        return self.fw.op(self.fw.pe, lambda h: h.matmul(out, lhsT=lhsT, rhs=rhs, start=start, stop=stop, **kw),
                          reads, writes, inc=inc)

    def TR(self, out, in_, reads=(), writes=(), inc=True):
        ident = self.ident
        n = in_.shape[0]
        return self.fw.op(self.fw.pe, lambda h: h.transpose(out, in_, ident[0:n, 0:n]),
                          list(reads) + [self.b_const], writes, inc=inc)

    def wdma(self, out, in_, chan, reads=(), writes=()):
        return self.fw.dma(self.fw.pool, self.fw.new_chan(chan.name + "_sw"), out, in_, reads, writes)

    def ldma(self, out, in_, chan, reads=(), writes=(), **kw):
        return self.fw.dma(self.fw.sp, chan, out, in_, reads, writes, **kw)

    def dump(self, name, ap, shape, reads):
        if name not in self.dumps or name in self.dump_aps:
            return
        d = self.nc.dram_tensor("dbg_" + name, list(shape), F32, kind="ExternalOutput").ap()
        self.dump_aps[name] = d
        ch = self.fw.new_chan("dbg")
        self.fw.dma(self.fw.sp, ch, d, ap, reads=reads)

    def setup_global(self, es):
        fw, dr, NT = self.fw, self.dr, self.NT
        self.x = fw.sb(es, "x", [128, NT, D], F32)
        self.bx = [Buf(f"x{t}") for t in range(NT)]
        self.hT = fw.sb(es, "hT", [128, 8, self.S], BF16)
        self.bhT = [Buf(f"hT{g}") for g in range(self.NG)]
        self.mv = fw.sb(es, "modv", [128, 5, D], F32)
        self.bmv = Buf("modv")
        self.ch_mv = fw.new_chan("modv")
        self.bab = Buf("modab")
        self.ch_ab = fw.new_chan("modab")
        self.ident = fw.sb(es, "ident", [128, 128], BF16)
        self.b_const = Buf("const")
        ch = fw.new_chan("const")
        self.wdma(self.ident[:], dr["ident"], ch, writes=[self.b_const])
        self.pbank = [(fw.psum(es, f"b{i}", [128, 512], F32), Buf(f"ps{i}")) for i in range(6)]
        self.ptr = [(fw.psum(es, f"t{i}", [128, 1024], BF16), Buf(f"pt{i}")) for i in range(2)]
        self.ringA = Ring(self.pbank[0:4])
        self.ringB = Ring(self.pbank[4:6])
        self.ringT = Ring(self.ptr)
        self.ch_x = [fw.new_chan(f"x{i}") for i in range(4)]
        self.ch_out = [fw.new_chan(f"o{i}") for i in range(4)]
        self.b_modd = Buf("modd")

    def mod_alloc(self, es):
        fw = self.fw
        self.m_slots = [(fw.sb(es, f"aw{i}", [128, 8, 512], BF16), Buf(), fw.new_chan(f"aw{i}")) for i in range(2)]
        self.m_bslots = [(fw.sb(es, f"ab{i}", [2, 512], F32), Buf(), fw.new_chan(f"ab{i}")) for i in range(2)]
        self.m_rows = [(fw.sb(es, f"mr{i}", [2, 512], F32), Buf(), fw.new_chan(f"mr{i}")) for i in range(2)]

    def mod_block(self, l, cb):
        dr = self.dr
        it = self.m_it
        self.m_it += 1
        awv = dr["ada_w"][l].rearrange("(kc p) n -> p kc n", p=128)
        w, bw, chw = self.m_slots[it % 2]
        ab, bab, chb = self.m_bslots[it % 2]
        mr, bmr, chr_ = self.m_rows[it % 2]
        cact, b_c = self.cact, self.b_cact
        self.wdma(w[:], awv[:, :, cb * 512:(cb + 1) * 512], chw, writes=[bw])
        self.ldma(ab[:], dr["ada_b"][l:l + 1, cb * 512:(cb + 1) * 512].broadcast_to([2, 512]), chb, writes=[bab])
        ps, bps = self.ringA.next()
        for kc in range(8):
            self.MM(ps[0:2, :], cact[:, kc, :], w[:, kc, :], kc == 0, kc == 7, reads=[b_c, bw], writes=[bps])
        seg = (cb * 512) // 1024
        self.V(lambda h: h.tensor_tensor(out=mr[:], in0=ps[0:2, :], in1=ab[:], op=ALU.add), [bps, bab], [bmr])
        if seg in (1, 4, 7, 5):
            self.V(lambda h: h.tensor_scalar_add(out=mr[:], in0=mr[:], scalar1=1.0), [bmr], [bmr])
        elif seg in (2, 8):
            self.V(lambda h: h.tensor_scalar(out=mr[:], in0=mr[:], scalar1=1.0, scalar2=0.5,
                                             op0=ALU.add, op1=ALU.mult), [bmr], [bmr])
        self.ldma(dr["modd"][:, l, cb * 512:(cb + 1) * 512], mr[:], chr_, reads=[bmr], writes=[self.b_modd])

    def prologue_mod(self):
        fw, dr = self.fw, self.dr
        self.m_it = 0
        self.cact = fw.sb(self.es, "cact", [128, 8, 2], BF16)
        self.b_cact = Buf()
        self.mod_deferred = (self.L > 1 and self.stop is None and not self.skip_mixers)
        with ExitStack() as es:
            cT = fw.sb(es, "cT", [128, 8, 2], F32)
            ch = fw.new_chan("c")
            self.ldma(cT[:], dr["cT"], ch, writes=[self.b_cact])
            self.A(lambda h: h.activation(out=self.cact[:], in_=cT[:], func=AF.Silu), [self.b_cact], [self.b_cact])
            self.mod_alloc(es)
            for l in range(1 if self.mod_deferred else self.L):
                for cb in range(18):
                    self.mod_block(l, cb)
            fw.barrier()

    def load_ab(self, s, l, j):
        dr = self.dr
        for i, sg in enumerate([3 * j + 1, 3 * j + 0]):
            self.ldma(self.mv[:, i, :], dr["modd"][s:s + 1, l, sg * D:(sg + 1) * D].broadcast_to([128, D]), self.ch_ab,
                      reads=[self.b_modd], writes=[self.bab])

    def load_modvec(self, s, l, j):
        dr = self.dr
        mv, b, ch = self.mv, self.bmv, self.ch_mv
        sg = 3 * j + 2
        self.ldma(mv[:, 2, :], dr["modd"][s:s + 1, l, sg * D:(sg + 1) * D].broadcast_to([128, D]), ch,
                  reads=[self.b_modd], writes=[b])
        self.ldma(mv[:, 3, :], dr["ln_g"][l, j:j + 1, :].broadcast_to([128, D]), ch, writes=[b])
        self.ldma(mv[:, 4, :], dr["ln_b"][l, j:j + 1, :].broadcast_to([128, D]), ch, writes=[b])

    def run_sequence(self, s):
        fw, dr, NT = self.fw, self.dr, self.NT
        xin = dr["x"][s].rearrange("(t p) d -> p t d", p=128)
        q = max(1, NT // 4)
        for i in range(0, NT, q):
            self.ldma(self.x[:, i:i + q, :], xin[:, i:i + q, :], self.ch_x[(i // q) % 4],
                      writes=self.bx[i:i + q])
        subs = [(l, j) for l in range(self.L) for j in range(3)]
        if self.stop is not None and self.stop[0] == s:
            subs = subs[:subs.index((self.stop[1], self.stop[2])) + 1]
        self.xout = dr["out"][s].rearrange("(t p) d -> p t d", p=128)
        self.load_ab(s, *subs[0])
        with ExitStack() as e0:
            self.build_hT(e0)
            fw.barrier()
        for k, (l, j) in enumerate(subs):
            self.load_modvec(s, l, j)
            self.has_next = k + 1 < len(subs)
            if self.has_next:
                self.load_ab(s, *subs[k + 1])
            if j == 1:
                self.mixer(s, l)
            else:
                self.ffn(s, l, j)

    def build_hT(self, es):
        fw, NT = self.fw, self.NT
        tmpf = [(fw.sb(es, f"hx{i}", [128, D], F32), Buf()) for i in range(2)]
        hb = [(fw.sb(es, f"hb{i}", [128, D], BF16), Buf()) for i in range(2)]
        mv, bmv = self.mv, self.bab
        for t in range(NT):
            tf, btf = tmpf[t % 2]
            hbt, bhb = hb[t % 2]
            self.V(lambda h: h.tensor_tensor(out=tf[:], in0=self.x[:, t, :], in1=mv[:, 0, :], op=ALU.mult),
                   [self.bx[t], bmv], [btf])
            self.G(lambda h: h.tensor_tensor(out=hbt[:], in0=tf[:], in1=mv[:, 1, :], op=ALU.add),
                   [btf, bmv], [bhb])
            self.hT_transpose(t, hbt, bhb)

    def hT_transpose(self, t, hbt, bhb):
        pt, bpt = self.ringT.next()
        for kc in range(8):
            self.TR(pt[:, kc * 128:(kc + 1) * 128], hbt[:, kc * 128:(kc + 1) * 128], [bhb], [bpt], inc=(kc == 7))
        self.A(lambda h: h.copy(out=self.hT[:, :, t * 128:(t + 1) * 128],
                                in_=pt[:].rearrange("p (k c) -> p k c", k=8)),
               [bpt], [self.bhT[t // 4]])

    def begin_tail(self, es):
        fw = self.fw
        self.t_st = [(fw.sb(es, f"lnst{i}", [128, 2, 6], F32), fw.sb(es, f"lnmv{i}", [128, 4], F32), Buf()) for i in range(2)]
        self.t_q = []
        if self.has_next:
            self.t_tf = [(fw.sb(es, f"thx{i}", [128, D], F32), Buf()) for i in range(1)]
            self.t_hb = [(fw.sb(es, f"thb{i}", [128, D], BF16), Buf()) for i in range(3)]

    def tail_tile(self, t):
        self.ln_tile(t, *self.t_st[t % 2])
        if self.has_next:
            tf, btf = self.t_tf[0]
            hbt, bhb = self.t_hb[t % 3]
            mv = self.mv
            self.V(lambda h: h.tensor_tensor(out=tf[:], in0=self.x[:, t, :], in1=mv[:, 0, :], op=ALU.mult),
                   [self.bx[t], self.bab], [btf])
            self.G(lambda h: h.tensor_tensor(out=hbt[:], in0=tf[:], in1=mv[:, 1, :], op=ALU.add),
                   [btf, self.bab], [bhb])
            self.t_q.append((t, hbt, bhb))
            if len(self.t_q) > 2:
                self.hT_transpose(*self.t_q.pop(0))
        else:
            self.ldma(self.xout[:, t, :], self.x[:, t, :], self.ch_out[t % 4], reads=[self.bx[t]])

    def end_tail(self):
        while self.t_q:
            self.hT_transpose(*self.t_q.pop(0))

    def resid_update(self, t, half, ps, bps, first, tmp=None, btmp=None):
        sl = slice(half * 512, (half + 1) * 512)
        xs = self.x[:, t, sl]
        if first:
            self.V(lambda h: h.scalar_tensor_tensor(out=xs, in0=xs, scalar=ALPHA, in1=ps[:],
                                                    op0=ALU.mult, op1=ALU.add), [bps, self.bx[t]], [self.bx[t]])
        else:
            self.V(lambda h: h.tensor_tensor(out=xs, in0=xs, in1=ps[:], op=ALU.add),
                   [bps, self.bx[t]], [self.bx[t]])

    def ln_tile(self, t, s6, m4, bs):
        mv, bmv = self.mv, self.bmv
        if True:
            xt = self.x[:, t, :]
            bxt = self.bx[t]
            self.V(lambda h: h.bn_stats(out=s6[:, 0, :], in_=self.x[:, t, 0:512]), [bxt], [bs])
            self.V(lambda h: h.bn_stats(out=s6[:, 1, :], in_=self.x[:, t, 512:1024]), [bxt], [bs])
            self.V(lambda h: h.bn_aggr(out=m4[:, 0:2], in_=s6[:]), [bs], [bs])
            self.V(lambda h: h.tensor_scalar_add(out=m4[:, 2:3], in0=m4[:, 1:2], scalar1=LN_EPS), [bs], [bs])
            self.A(lambda h: h.sqrt(out=m4[:, 2:3], in_=m4[:, 2:3]), [bs], [bs])
            self.V(lambda h: h.reciprocal(out=m4[:, 2:3], in_=m4[:, 2:3]), [bs], [bs])
            self.V(lambda h: h.scalar_tensor_tensor(out=m4[:, 3:4], in0=m4[:, 0:1], scalar=-1.0, in1=m4[:, 2:3],
                                                    op0=ALU.mult, op1=ALU.mult), [bs], [bs])
            self.A(lambda h: h.activation(out=xt, in_=xt, func=AF.Identity, bias=m4[:, 3:4], scale=m4[:, 2:3]),
                   [bs, bxt], [bxt])
            self.V(lambda h: h.tensor_tensor(out=xt, in0=xt, in1=mv[:, 3, :], op=ALU.mult), [bxt, bmv], [bxt])
            self.G(lambda h: h.tensor_tensor(out=xt, in0=xt, in1=mv[:, 4, :], op=ALU.add), [bxt, bmv], [bxt])

    def ffn(self, s, l, j):
        fw, dr, NT, NG, S = self.fw, self.dr, self.NT, self.NG, self.S
        w_in = dr["ffn1_w_in" if j == 0 else "ffn2_w_in"][l].rearrange("(kc p) n -> p kc n", p=128)
        w_out = dr["ffn1_w_out" if j == 0 else "ffn2_w_out"][l].rearrange("(j p) n -> p j n", p=128)
        with ExitStack() as es:
            self.begin_tail(es)
            actT = fw.sb(es, "actT", [128, 6, S], BF16)
            bact = [Buf() for _ in range(NG)]
            wo = [(fw.sb(es, f"wo{i}", [128, 6, D], BF16), Buf(), fw.new_chan(f"wo{i}")) for i in range(2)]
            wi = [(fw.sb(es, f"wi{i}", [128, 8, 512], BF16), Buf(), fw.new_chan(f"wi{i}")) for i in range(2)]
            sgs = [(fw.sb(es, f"sg{i}", [128, 512], F32), Buf()) for i in range(2)]
            tmps = [(None, None)] * 2
            parts = [(0, 6), (6, 12), (12, 18), (18, 22)]
            iw = 0
            k = 0
            for pi, (j0, j1) in enumerate(parts):
                wot, bwo, chwo = wo[pi % 2]
                self.wdma(wot[:, 0:j1 - j0, :], w_out[:, j0:j1, :], chwo, writes=[bwo])
                self.G(lambda h: h.tensor_tensor(out=wot[:, 0:j1 - j0, :], in0=wot[:, 0:j1 - j0, :],
                                                 in1=self.mv[:, 2, :].unsqueeze(1).broadcast_to([128, j1 - j0, D]),
                                                 op=ALU.mult), [bwo, self.bmv], [bwo])
                for jj in range(j0, j1, 2):
                    wt, bw, chw = wi[iw % 2]
                    iw += 1
                    self.wdma(wt[:, :, 0:256], w_in[:, :, jj * 128:jj * 128 + 256], chw, writes=[bw])
                    self.wdma(wt[:, :, 256:512], w_in[:, :, DFF + jj * 128:DFF + jj * 128 + 256], chw, writes=[bw])
                    for c in range(2):
                        jl = jj + c - j0
                        for g in range(NG):
                            pg, bpg = self.ringA.next()
                            pu, bpu = self.ringA.next()
                            tok = slice(g * 512, (g + 1) * 512)
                            for kc in range(8):
                                self.MM(pg[:], wt[:, kc, c * 128:(c + 1) * 128], self.hT[:, kc, tok], kc == 0, kc == 7,
                                        [bw, self.bhT[g]], [bpg])
                            for kc in range(8):
                                self.MM(pu[:], wt[:, kc, 256 + c * 128:256 + (c + 1) * 128], self.hT[:, kc, tok],
                                        kc == 0, kc == 7, [bw, self.bhT[g]], [bpu])
                            sg, bsg = sgs[k % 2]
                            k += 1
                            self.A(lambda h: h.activation(out=sg[:], in_=pg[:], func=AF.Silu), [bpg], [bsg])
                            self.V(lambda h: h.tensor_tensor(out=actT[:, jl, tok], in0=sg[:], in1=pu[:], op=ALU.mult),
                                   [bsg, bpu], [bact[g]])
                nj = j1 - j0
                for t in range(NT):
                    for half in range(2):
                        ps, bps = self.ringB.next()
                        for jl in range(nj):
                            self.MM(ps[:], actT[:, jl, t * 128:(t + 1) * 128], wot[:, jl, half * 512:(half + 1) * 512],
                                    jl == 0, jl == nj - 1, [bact[t // 4], bwo], [bps])
                        tmp, btmp = tmps[(2 * t + half) % 2]
                        self.resid_update(t, half, ps, bps, pi == 0, tmp, btmp)
                    if pi == len(parts) - 1:
                        self.tail_tile(t)
            self.end_tail()
            fw.barrier()
        self.dump(f"x_{s}_{l}_{j}", self.x[:], [128, NT, D], self.bx)

    def mixer(self, s, l):
        fw = self.fw
        with ExitStack() as es:
            first = True
            active = [m for m in range(4) if m not in self.skip_mixers]
            fns = [self.mix_hgrn, self.mix_mla, self.mix_fox, self.mix_gmlp]
            for m in active:
                self.tail_on = (m == active[-1])
                with ExitStack() as es2:
                    fns[m](es2, s, l, m, first)
                    fw.barrier()
                first = False
        self.dump(f"x_{s}_{l}_1", self.x[:], [128, self.NT, D], self.bx)

    skip_mixers = ()
    B1_MLA = False
    B1_FOX = False

    def out_proj(self, es, l, m, mixT, bmix, first):
        fw, dr, NT = self.fw, self.dr, self.NT
        wov = dr["mix_w_out"][l, m * 256:(m + 1) * 256, :].rearrange("(s p) n -> p s n", p=64)
        wo = fw.sb(es, "mwo", [64, 4, D], BF16)
        bwo = Buf()
        ch = fw.new_chan("mwo")
        self.wdma(wo[:], wov, ch, writes=[bwo])
        self.G(lambda h: h.tensor_tensor(out=wo[:], in0=wo[:],
                                         in1=self.mv[0:64, 2, :].unsqueeze(1).broadcast_to([64, 4, D]),
                                         op=ALU.mult), [bwo, self.bmv], [bwo])
        tmps = [(None, None)] * 2
        if self.tail_on:
            self.begin_tail(es)
        for t in range(NT):
            for half in range(2):
                ps, bps = self.ringB.next()
                for sl in range(4):
                    self.MM(ps[:], mixT[0:64, sl, t * 128:(t + 1) * 128], wo[0:64, sl, half * 512:(half + 1) * 512],
                            sl == 0, sl == 3, [bmix, bwo], [bps])
                tmp, btmp = tmps[(2 * t + half) % 2]
                self.resid_update(t, half, ps, bps, first, tmp, btmp)
            if self.tail_on:
                self.tail_tile(t)
        if self.tail_on:
            self.end_tail()

    def attention(self, es, QT, KT, bq, bk, dk, Vaug, bv, scale, mixT, bmix):
        fw, dr, NG = self.fw, self.dr, self.NG
        negm = fw.sb(es, "negm", [128, 4, 512], BF16)
        sel = fw.sb(es, "sel65", [65, 64], F32)
        bc = Buf()
        ch = fw.new_chan("attc")
        self.wdma(negm[:], dr["negmask"], ch, writes=[bc])
        self.ldma(sel[:], dr["sel65"], ch, writes=[bc])
        tmpS = [(fw.sb(es, f"ts{i}", [128, 512], F32), Buf()) for i in range(2)]
        pTs = [(fw.sb(es, f"pT{i}", [128, 512], BF16), Buf()) for i in range(3)]
        osb = [(fw.sb(es, f"os{i}", [65, 512], F32), Buf()) for i in range(2)]
        it = 0
        ip = 0
        fin_pending = [None]
        for hd in range(4):
            for qg in range(NG):
                qs = slice(qg * 512, (qg + 1) * 512)
                nkb = 4 * (qg + 1)
                o_ps, bo = self.ringB.next()
                pend = []

                def qk(kb):
                    nonlocal ip
                    s_ps, bs = self.ringA.next()
                    self.MM(s_ps[:], KT[hd][0:dk, kb * 128:(kb + 1) * 128], QT[hd][0:dk, qs], True, True,
                            bk[kb // 4] + bq[qg], [bs])
                    pT, bp = pTs[ip % 3]
                    ip += 1
                    r = kb - 4 * qg
                    if r >= 0:
                        tS, bt = tmpS[kb % 2]
                        self.V(lambda h: h.tensor_tensor(out=tS[:], in0=s_ps[:], in1=negm[:, r, :], op=ALU.add),
                               [bs, bc], [bt])
                        self.A(lambda h: h.activation(out=pT[:], in_=tS[:], func=AF.Exp, scale=scale), [bt], [bp])
                    else:
                        self.A(lambda h: h.activation(out=pT[:], in_=s_ps[:], func=AF.Exp, scale=scale), [bs], [bp])
                    return pT, bp

                def pv(kb, pT, bp):
                    self.MM(o_ps[0:65, :], Vaug[:, kb, hd, :], pT[:], kb == 0, kb == nkb - 1, [bv, bp], [bo])

                LA = 2
                for kb in range(nkb + LA):
                    if kb < nkb:
                        pend.append((kb,) + qk(kb))
                    if kb == 1 and fin_pending[0] is not None:
                        fin_pending[0]()
                        fin_pending[0] = None
                    if kb >= LA:
                        a = pend.pop(0)
                        pv(*a)
                o_sb, bos = osb[it % 2]
                it += 1
                self.A(lambda h: h.copy(out=o_sb[:], in_=o_ps[0:65, :]), [bo], [bos])
                self.A(lambda h: h.activation(out=o_sb[64:65, :], in_=o_sb[64:65, :], func=AF.Ln), [bos], [bos])
                self.A(lambda h: h.activation(out=o_sb[64:65, :], in_=o_sb[64:65, :], func=AF.Exp, scale=-1.0), [bos], [bos])

                def fin(o_sb=o_sb, bos=bos, hd=hd, qs=qs):
                    d_ps, bd = self.ringB.next()
                    self.MM(d_ps[0:64, :], sel[:], o_sb[:], True, True, [bc, bos], [bd])
                    self.V(lambda h: h.tensor_tensor(out=mixT[0:64, hd, qs], in0=o_sb[0:64, :], in1=d_ps[0:64, :],
                                                     op=ALU.mult), [bos, bd], [bmix])
                fin_pending[0] = fin
        if fin_pending[0] is not None:
            fin_pending[0]()

    def mix_mla(self, es, s, l, m, first):
        fw, dr, NT, NG, S = self.fw, self.dr, self.NT, self.NG, self.S
        QT = [fw.sb(es, f"QT{h}", [96, S], BF16) for h in range(4)]
        KT = [fw.sb(es, f"KT{h}", [96, S], BF16) for h in range(4)]
        if self.B1_MLA:
            bq = [[Buf()] for _ in range(NG)]
            bk = [[Buf()] for _ in range(NG)]
        else:
            one = Buf()
            bq = [[one] for _ in range(NG)]
            bk = [[one] for _ in range(NG)]
        Vaug = fw.sb(es, "Vaug", [128, NT, 4, 65], BF16)
        bv = Buf()
        self.G(lambda h: h.memset(Vaug[:], 1.0), [], [bv])
        with ExitStack() as e1:
            wv = dr["mix_w_in"][l].rearrange("(kc p) n -> p kc n", p=128)
            wb = fw.sb(e1, "wB", [128, 8, 384], BF16)
            wkr = fw.sb(e1, "wkr", [128, 8, 192], BF16)
            wuq = fw.sb(e1, "wuq", [128, 2, 384], BF16)
            wuqp = fw.sb(e1, "wuqp", [128, 2, 384], BF16)
            wukv = fw.sb(e1, "wukv", [128, 512], BF16)
            gq = fw.sb(e1, "gq", [128, 256], F32)
            gkv = fw.sb(e1, "gkv", [128, 128], F32)
            bw = Buf()
            ch = fw.new_chan("mlaw")
            self.wdma(wb[:], wv[:, :, 1024:1408], ch, writes=[bw])
            self.wdma(wkr[:], dr["wkr"][l].rearrange("(kc p) n -> p kc n", p=128), ch, writes=[bw])
            self.wdma(wuq[:], dr["mla_w_uq"][l].rearrange("(kc p) n -> p kc n", p=128), ch, writes=[bw])
            self.wdma(wuqp[:], dr["mla_w_uqp"][l].rearrange("(kc p) n -> p kc n", p=128), ch, writes=[bw])
            self.wdma(wukv[:], dr["mla_w_ukv"][l], ch, writes=[bw])
            self.ldma(gq[:], dr["mla_q_norm_g"][l:l + 1, :].broadcast_to([128, 256]), ch, writes=[bw])
            self.ldma(gkv[:], dr["mla_kv_norm_g"][l:l + 1, :].broadcast_to([128, 128]), ch, writes=[bw])
            cqnT = fw.sb(e1, "cqnT", [128, 2, S], BF16)
            ckvT = fw.sb(e1, "ckvT", [128, S], BF16)
            bcT = [Buf() for _ in range(NG)]
            rc = [(fw.sb(e1, f"rC{i}", [96, 512], F32), fw.sb(e1, f"rS{i}", [96, 512], F32), Buf(),
                   fw.new_chan(f"rope{i}")) for i in range(2)]
            st = [(fw.sb(e1, f"mst{i}", [128, 2, 6], F32), fw.sb(e1, f"mmv{i}", [128, 8], F32), Buf()) for i in range(2)]
            cn = [(fw.sb(e1, f"cn{i}", [128, 384], BF16), Buf()) for i in range(2)]
            def stageA(t):
                tok = slice(t * 128, (t + 1) * 128)
                ps, bps = self.ringA.next()
                for kc in range(8):
                    self.MM(ps[:, 0:384], self.hT[:, kc, tok], wb[:, kc, :], kc == 0, kc == 7, [self.bhT[t // 4], bw], [bps])
                s6, m8, bs = st[t % 2]
                cnt, bcn = cn[t % 2]
                self.V(lambda h: h.bn_stats(out=s6[:, 0, :], in_=ps[:, 0:256]), [bps], [bs])
                self.V(lambda h: h.bn_stats(out=s6[:, 1, :], in_=ps[:, 256:384]), [bps], [bs])
                for i in range(2):
                    self.V(lambda h: h.bn_aggr(out=m8[:, 4 * i:4 * i + 2], in_=s6[:, i:i + 1, :]), [bs], [bs])
                    self.V(lambda h: h.scalar_tensor_tensor(out=m8[:, 4 * i + 2:4 * i + 3], in0=m8[:, 4 * i:4 * i + 1],
                                                            scalar=m8[:, 4 * i:4 * i + 1], in1=m8[:, 4 * i + 1:4 * i + 2],
                                                            op0=ALU.mult, op1=ALU.add), [bs], [bs])
                    self.V(lambda h: h.tensor_scalar_add(out=m8[:, 4 * i + 2:4 * i + 3], in0=m8[:, 4 * i + 2:4 * i + 3],
                                                         scalar1=RMS_EPS), [bs], [bs])
                    self.A(lambda h: h.sqrt(out=m8[:, 4 * i + 2:4 * i + 3], in_=m8[:, 4 * i + 2:4 * i + 3]), [bs], [bs])
                    self.V(lambda h: h.reciprocal(out=m8[:, 4 * i + 3:4 * i + 4], in_=m8[:, 4 * i + 2:4 * i + 3]), [bs], [bs])
                self.V(lambda h: h.scalar_tensor_tensor(out=cnt[:, 0:256], in0=ps[:, 0:256], scalar=m8[:, 3:4], in1=gq[:],
                                                        op0=ALU.mult, op1=ALU.mult), [bps, bs, bw], [bcn])
                self.V(lambda h: h.scalar_tensor_tensor(out=cnt[:, 256:384], in0=ps[:, 256:384], scalar=m8[:, 7:8],
                                                        in1=gkv[:], op0=ALU.mult, op1=ALU.mult), [bps, bs, bw], [bcn])

            def stageB(t):
                tok = slice(t * 128, (t + 1) * 128)
                cnt, bcn = cn[t % 2]
                pt, bpt = self.ringT.next()
                for c in range(3):
                    self.TR(pt[:, c * 128:(c + 1) * 128], cnt[:, c * 128:(c + 1) * 128], [bcn], [bpt], inc=(c == 2))
                self.A(lambda h: h.copy(out=cqnT[:, :, tok], in_=pt[:, 0:256].rearrange("p (k c) -> p k c", k=2)),
                       [bpt], [bcT[t // 4]])
                self.A(lambda h: h.copy(out=ckvT[:, tok], in_=pt[:, 256:384]), [bpt], [bcT[t // 4]])
                pv_, bpv = self.ringB.next()
                self.MM(pv_[:, 0:256].rearrange("p (h x) -> p h x", h=4), ckvT[:, tok],
                        wukv[:].rearrange("p (h x) -> p h x", h=4)[:, :, 64:128], True, True,
                        [bcT[t // 4], bw], [bpv])
                self.A(lambda h: h.copy(out=Vaug[:, t, :, 0:64], in_=pv_[:, 0:256].rearrange("p (h x) -> p h x", h=4)),
                       [bpv], [bv])

            stageA(0)
            for t in range(NT):
                if t + 1 < NT:
                    stageA(t + 1)
                stageB(t)
            t1s = [(fw.sb(e1, f"r1{i}", [96, 512], F32), Buf()) for i in range(2)]
            t2s = [(fw.sb(e1, f"r2{i}", [96, 512], F32), Buf()) for i in range(2)]
            k = 0
            for g in range(NG):
                tok = slice(g * 512, (g + 1) * 512)
                rC, rS, brp, chr_ = rc[g % 2]
                self.ldma(rC[:], dr["ropeC"][:, tok], chr_, writes=[brp])
                self.ldma(rS[:], dr["ropeS"][:, tok], chr_, writes=[brp])
                for hd in range(4):
                    pq, bpq = self.ringA.next()
                    pp, bpp = self.ringA.next()
                    for kc in range(2):
                        self.MM(pq[0:96, :], wuq[:, kc, hd * 96:(hd + 1) * 96], cqnT[:, kc, tok], kc == 0, kc == 1,
                                [bw, bcT[g]], [bpq])
                    for kc in range(2):
                        self.MM(pp[0:96, :], wuqp[:, kc, hd * 96:(hd + 1) * 96], cqnT[:, kc, tok], kc == 0, kc == 1,
                                [bw, bcT[g]], [bpp])
                    t1, b1 = t1s[k % 2]
                    t2, b2 = t2s[k % 2]
                    k += 1
                    self.V(lambda h: h.tensor_tensor(out=t1[:], in0=pq[0:96, :], in1=rC[:], op=ALU.mult), [bpq, brp], [b1])
                    self.V(lambda h: h.tensor_tensor(out=t2[:], in0=pp[0:96, :], in1=rS[:], op=ALU.mult), [bpp, brp], [b2])
                    self.G(lambda h: h.tensor_tensor(out=QT[hd][:, tok], in0=t1[:], in1=t2[:], op=ALU.add), [b1, b2], bq[g])
                    pk, bpk = self.ringA.next()
                    self.MM(pk[0:64, :], wukv[:, hd * 128:hd * 128 + 64], ckvT[:, tok], True, True, [bw, bcT[g]], [bpk])
                    self.A(lambda h: h.copy(out=KT[hd][0:64, tok], in_=pk[0:64, :]), [bpk], bk[g])
                pq, bpq = self.ringA.next()
                pp, bpp = self.ringA.next()
                for kc in range(8):
                    self.MM(pq[0:96, :], wkr[:, kc, 0:96], self.hT[:, kc, tok], kc == 0, kc == 7, [bw, self.bhT[g]], [bpq])
                for kc in range(8):
                    self.MM(pp[0:96, :], wkr[:, kc, 96:192], self.hT[:, kc, tok], kc == 0, kc == 7, [bw, self.bhT[g]], [bpp])
                t1, b1 = t1s[k % 2]
                t2, b2 = t2s[k % 2]
                k += 1
                self.V(lambda h: h.tensor_tensor(out=t1[64:96, :], in0=pq[64:96, :], in1=rC[64:96, :], op=ALU.mult),
                       [bpq, brp], [b1])
                self.V(lambda h: h.tensor_tensor(out=t2[64:96, :], in0=pp[64:96, :], in1=rS[64:96, :], op=ALU.mult),
                       [bpp, brp], [b2])
                for hd in range(4):
                    self.G(lambda h: h.tensor_tensor(out=KT[hd][64:96, tok], in0=t1[64:96, :], in1=t2[64:96, :], op=ALU.add),
                           [b1, b2], bk[g])
            fw.barrier()
        with ExitStack() as e2:
            mixT = fw.sb(e2, "mixT", [64, 4, S], BF16)
            bmix = Buf()
            self.attention(e2, QT, KT, bq, bk, 96, Vaug, bv, float(96 ** -0.5), mixT, bmix)
            if "mla_o" in self.dumps:
                self.dump_bf16(e2, "mla_o", mixT[:], [64, 4, S], [bmix])
            self.out_proj(e2, l, m, mixT, bmix, first)
            fw.barrier()

    def dump_bf16(self, es, name, ap, shape, reads):
        if name not in self.dumps or name in self.dump_aps:
            return
        t = self.fw.sb(es, "dmp" + name, shape, F32)
        b = Buf()
        self.V(lambda h: h.tensor_copy(out=t[:], in_=ap), reads, [b])
        self.dump(name, t[:], shape, [b])
        self.fw.barrier()

    def mix_fox(self, es, s, l, m, first):
        fw, dr, NT, NG, S = self.fw, self.dr, self.NT, self.NG, self.S
        QT = [fw.sb(es, f"fQT{h}", [68, S], BF16) for h in range(4)]
        KT = [fw.sb(es, f"fKT{h}", [68, S], BF16) for h in range(4)]
        if self.B1_FOX:
            bq = [[Buf(), Buf()] for _ in range(NG)]
            bk = [[Buf(), Buf()] for _ in range(NG)]
            allaug = [b[1] for b in bq] + [b[1] for b in bk]
        else:
            one = Buf()
            bq = [[one, one] for _ in range(NG)]
            bk = [[one, one] for _ in range(NG)]
            allaug = [one]
        Vaug = fw.sb(es, "fVaug", [128, NT, 4, 65], BF16)
        bv = Buf()
        self.G(lambda h: h.memset(Vaug[:], 1.0), [], [bv])
        for hd in range(4):
            self.G(lambda h: h.memset(QT[hd][64:68, :], 1.0), [], allaug)
            self.G(lambda h: h.memset(KT[hd][64:68, :], 1.0), [], allaug)
        with ExitStack() as e1:
            wv = dr["mix_w_in"][l].rearrange("(kc p) n -> p kc n", p=128)
            wc = fw.sb(e1, "wC", [128, 8, 772], BF16)
            bw = Buf()
            ch = fw.new_chan("foxw")
            self.wdma(wc[:], wv[:, :, 1440:2212], ch, writes=[bw])
            nbf = fw.sb(e1, "nbf", [4, 1], F32)
            self.ldma(nbf[:], dr["fox_b_f"][l:l + 1, :].rearrange("o h -> h o"), ch, writes=[bw])
            self.V(lambda h: h.tensor_scalar_mul(out=nbf[:], in0=nbf[:], scalar1=-1.0), [bw], [bw])
            ones4 = fw.sb(e1, "ones4", [4, 512], F32)
            carry = fw.sb(e1, "carry", [4, 1], F32)
            bcar = Buf()
            self.G(lambda h: h.memset(ones4[:], 1.0), [], [bcar])
            self.G(lambda h: h.memset(carry[:], 0.0), [bcar], [bcar])
            ch_row = [fw.new_chan(f"frow{i}") for i in range(2)]
            for t in range(NT):
                tok = slice(t * 128, (t + 1) * 128)
                pv_, bpv = self.ringB.next()
                for kc in range(8):
                    self.MM(pv_[:, 0:256], self.hT[:, kc, tok], wc[:, kc, 512:768], kc == 0, kc == 7,
                            [self.bhT[t // 4], bw], [bpv])
                self.A(lambda h: h.copy(out=Vaug[:, t, :, 0:64], in_=pv_[:, 0:256].rearrange("p (h x) -> p h x", h=4)),
                       [bpv], [bv])
            ft = [dict(e=fw.sb(e1, f"fe{i}", [4, 512], F32), fn=fw.sb(e1, f"ffn{i}", [4, 512], F32),
                       hi=fw.sb(e1, f"fhi{i}", [4, 512], BF16), hif=fw.sb(e1, f"fhf{i}", [4, 512], F32),
                       lo=fw.sb(e1, f"flo{i}", [4, 512], BF16), nhi=fw.sb(e1, f"fnh{i}", [4, 512], BF16),
                       nlo=fw.sb(e1, f"fnl{i}", [4, 512], BF16), b=Buf()) for i in range(2)]
            for g in range(NG):
                tok = slice(g * 512, (g + 1) * 512)
                for hd in range(4):
                    for which, dst, bdst in ((0, QT, bq[g][0]), (1, KT, bk[g][0])):
                        pq, bpq = self.ringA.next()
                        for kc in range(8):
                            self.MM(pq[0:64, :], wc[:, kc, which * 256 + hd * 64:which * 256 + (hd + 1) * 64],
                                    self.hT[:, kc, tok], kc == 0, kc == 7, [bw, self.bhT[g]], [bpq])
                        self.A(lambda h: h.copy(out=dst[hd][0:64, tok], in_=pq[0:64, :]), [bpq], [bdst])
                pf, bpf = self.ringA.next()
                for kc in range(8):
                    self.MM(pf[0:4, :], wc[:, kc, 768:772], self.hT[:, kc, tok], kc == 0, kc == 7, [bw, self.bhT[g]], [bpf])
                f = ft[g % 2]
                bf_ = f["b"]
                self.A(lambda h: h.activation(out=f["e"][:], in_=pf[0:4, :], func=AF.Exp, bias=nbf[:], scale=-1.0),
                       [bpf, bw], [bf_])
                self.A(lambda h: h.activation(out=f["e"][:], in_=f["e"][:], func=AF.Ln, bias=1.0, scale=1.0), [bf_], [bf_])
                self.V(lambda h: h.tensor_tensor_scan(out=f["fn"][:], data0=ones4[:], data1=f["e"][:], initial=carry[:],
                                                      op0=ALU.mult, op1=ALU.add), [bf_, bcar], [bf_])
                self.V(lambda h: h.tensor_copy(out=carry[:], in_=f["fn"][:, 511:512]), [bf_], [bcar])
                self.V(lambda h: h.tensor_scalar_mul(out=f["fn"][:], in0=f["fn"][:], scalar1=8.0), [bf_], [bf_])
                self.V(lambda h: h.tensor_copy(out=f["hi"][:], in_=f["fn"][:]), [bf_], [bf_])
                self.V(lambda h: h.tensor_copy(out=f["hif"][:], in_=f["hi"][:]), [bf_], [bf_])
                self.V(lambda h: h.tensor_tensor(out=f["lo"][:], in0=f["fn"][:], in1=f["hif"][:], op=ALU.subtract),
                       [bf_], [bf_])
                self.V(lambda h: h.tensor_scalar_mul(out=f["nhi"][:], in0=f["hi"][:], scalar1=-1.0), [bf_], [bf_])
                self.V(lambda h: h.tensor_scalar_mul(out=f["nlo"][:], in0=f["lo"][:], scalar1=-1.0), [bf_], [bf_])
                chr_ = ch_row[g % 2]
                moves = []
                for hd in range(4):
                    moves += [(QT[hd][64:65, tok], f["nhi"][hd:hd + 1, :]), (QT[hd][65:66, tok], f["nlo"][hd:hd + 1, :]),
                              (KT[hd][66:67, tok], f["hi"][hd:hd + 1, :]), (KT[hd][67:68, tok], f["lo"][hd:hd + 1, :])]
                wb_ = list({id(b): b for b in (bq[g][1], bk[g][1])}.values())
                for i, (dst_, src_) in enumerate(moves):
                    last = (i == len(moves) - 1)
                    self.ldma(dst_, src_, chr_, reads=([bf_] if last else [bf_] + wb_), writes=(wb_ if last else []))
            fw.barrier()
        with ExitStack() as e2:
            mixT = fw.sb(e2, "fmixT", [64, 4, S], BF16)
            bmix = Buf()
            self.attention(e2, QT, KT, bq, bk, 68, Vaug, bv, 0.125, mixT, bmix)
            if "fox_o" in self.dumps:
                self.dump_bf16(e2, "fox_o", mixT[:], [64, 4, S], [bmix])
            self.out_proj(e2, l, m, mixT, bmix, first)
            fw.barrier()

    def mix_gmlp(self, es, s, l, m, first):
        fw, dr, NT, NG, S = self.fw, self.dr, self.NT, self.NG, self.S
        wv = dr["mix_w_in"][l].rearrange("(kc p) n -> p kc n", p=128)
        wd = fw.sb(es, "wD", [128, 8, 512], BF16)
        bw = Buf()
        ch = fw.new_chan("gmw")
        self.wdma(wd[:], wv[:, :, 2212:2724], ch, writes=[bw])
        wsf = fw.sb(es, "wsf", [128, 4, 128], F32)
        caus = fw.sb(es, "caus", [128, 128], F32)
        wsm = fw.sb(es, "wsm", [128, 4, 128], BF16)
        bsb = fw.sb(es, "bsb", [64, 4, 128], F32)
        lng = fw.sb(es, "glng", [128, 256], F32)
        lnb = fw.sb(es, "glnb", [128, 256], F32)
        self.ldma(wsf[:], dr["gmlp_wsT"][l].rearrange("g s t -> s g t"), ch, writes=[bw])
        self.ldma(caus[:], dr["caus"], ch, writes=[bw])
        self.ldma(bsb[:], dr["gmlp_b_s"][l:l + 1].broadcast_to([64, 4, 128]), ch, writes=[bw])
        self.ldma(lng[:], dr["gmlp_ln_g"][l:l + 1, :].broadcast_to([128, 256]), ch, writes=[bw])
        self.ldma(lnb[:], dr["gmlp_ln_b"][l:l + 1, :].broadcast_to([128, 256]), ch, writes=[bw])
        self.V(lambda h: h.tensor_tensor(out=wsm[:], in0=wsf[:], in1=caus[:].unsqueeze(1).broadcast_to([128, 4, 128]),
                                         op=ALU.mult), [bw], [bw])
        mixT = fw.sb(es, "gmixT", [64, 4, S], BF16)
        bmix = Buf()
        self.mod_pending = []
        if s == 0 and l == 0 and self.mod_deferred:
            self.mod_alloc(es)
            self.mod_pending = [(ll, cb) for ll in range(1, self.L) for cb in range(18)]
        gv = [(fw.sb(es, f"gv{i}", [128, 256], F32), Buf()) for i in range(2)]
        vl = [(fw.sb(es, f"vl{i}", [128, 256], BF16), Buf()) for i in range(2)]
        st = [(fw.sb(es, f"gst{i}", [128, 6], F32), fw.sb(es, f"gmv{i}", [128, 4], F32), Buf()) for i in range(2)]
        gu = [(fw.sb(es, f"gu{i}", [64, 512], F32), Buf()) for i in range(2)]
        t1s = [(fw.sb(es, f"gt{i}", [64, 512], F32), Buf()) for i in range(2)]
        def stageA(t):
            tok = slice(t * 128, (t + 1) * 128)
            bh = self.bhT[t // 4]
            pv_, bpv = self.ringA.next()
            for kc in range(8):
                self.MM(pv_[:, 0:256], self.hT[:, kc, tok], wd[:, kc, 256:512], kc == 0, kc == 7, [bh, bw], [bpv])
            g_, bg = gv[t % 2]
            v_, bvl = vl[t % 2]
            s6, m4, bs = st[t % 2]
            self.A(lambda h: h.activation(out=g_[:], in_=pv_[:, 0:256], func=AF.Gelu_apprx_tanh), [bpv], [bg])
            self.V(lambda h: h.bn_stats(out=s6[:], in_=g_[:]), [bg], [bs])
            self.V(lambda h: h.bn_aggr(out=m4[:, 0:2], in_=s6[:]), [bs], [bs])
            self.V(lambda h: h.tensor_scalar_add(out=m4[:, 2:3], in0=m4[:, 1:2], scalar1=LN_EPS), [bs], [bs])
            self.A(lambda h: h.sqrt(out=m4[:, 2:3], in_=m4[:, 2:3]), [bs], [bs])
            self.V(lambda h: h.reciprocal(out=m4[:, 2:3], in_=m4[:, 2:3]), [bs], [bs])
            self.V(lambda h: h.scalar_tensor_tensor(out=m4[:, 3:4], in0=m4[:, 0:1], scalar=-1.0, in1=m4[:, 2:3],
                                                    op0=ALU.mult, op1=ALU.mult), [bs], [bs])
            self.A(lambda h: h.activation(out=g_[:], in_=g_[:], func=AF.Identity, bias=m4[:, 3:4], scale=m4[:, 2:3]),
                   [bs, bg], [bg])
            self.V(lambda h: h.tensor_tensor(out=g_[:], in0=g_[:], in1=lng[:], op=ALU.mult), [bg, bw], [bg])
            self.G(lambda h: h.tensor_tensor(out=v_[:], in0=g_[:], in1=lnb[:], op=ALU.add), [bg, bw], [bvl])

        def stageB(t):
            tok = slice(t * 128, (t + 1) * 128)
            bh = self.bhT[t // 4]
            v_, bvl = vl[t % 2]
            pm, bpm = self.ringA.next()
            for g in range(4):
                self.MM(pm[0:64, g * 128:(g + 1) * 128], v_[:, g * 64:(g + 1) * 64], wsm[:, g, :], g == 0, True,
                        [bvl, bw], [bpm], inc=(g == 3), skip=True)
            pu, bpu = self.ringA.next()
            for g in range(4):
                for kc in range(8):
                    self.MM(pu[0:64, g * 128:(g + 1) * 128], wd[:, kc, g * 64:(g + 1) * 64], self.hT[:, kc, tok],
                            kc == 0, kc == 7, [bh, bw], [bpu], inc=(g == 3 and kc == 7), skip=True)
            u_, bu = gu[t % 2]
            t1, b1 = t1s[t % 2]
            self.A(lambda h: h.activation(out=u_[:], in_=pu[0:64, :], func=AF.Gelu_apprx_tanh), [bpu], [bu])
            self.V(lambda h: h.tensor_tensor(out=t1[:], in0=pm[0:64, :], in1=bsb[:].rearrange("p g t -> p (g t)"),
                                             op=ALU.add), [bpm, bw], [b1])
            self.V(lambda h: h.tensor_tensor(out=mixT[0:64, :, tok], in0=t1[:].rearrange("p (g t) -> p g t", g=4),
                                             in1=u_[:].rearrange("p (g t) -> p g t", g=4), op=ALU.mult),
                   [b1, bu], [bmix])
            for _ in range(2):
                if self.mod_pending:
                    self.mod_block(*self.mod_pending.pop(0))

        stageA(0)
        for t in range(NT):
            if t + 1 < NT:
                stageA(t + 1)
            stageB(t)
        while self.mod_pending:
            self.mod_block(*self.mod_pending.pop(0))
        if "gmlp_o" in self.dumps:
            self.dump_bf16(es, "gmlp_o", mixT[:], [64, 4, S], [bmix])
        self.out_proj(es, l, m, mixT, bmix, first)

    def mix_hgrn(self, es, s, l, m, first):
        fw, dr, NT, NG, S = self.fw, self.dr, self.NT, self.NG, self.S
        wv = dr["mix_w_in"][l].rearrange("(kc p) n -> p kc n", p=128)
        wa = fw.sb(es, "wA", [128, 8, 1024], BF16)
        bw = Buf()
        ch = fw.new_chan("hgw")
        self.wdma(wa[:], wv[:, :, 0:1024], ch, writes=[bw])
        tri = fw.sb(es, "tri", [128, 128], F32)
        blk = fw.sb(es, "blk", [128, 128], F32)
        csel = fw.sb(es, "csel", [128, 8], F32)
        cselb = fw.sb(es, "cselb", [128, 8], BF16)
        ones64 = fw.sb(es, "ones64", [64, 64], F32)
        ng = fw.sb(es, "ng", [64, 4], F32)
        lb = fw.sb(es, "lb", [128, 256], F32)
        oml = fw.sb(es, "oml", [128, 256], F32)
        self.ldma(tri[:], dr["tri"], ch, writes=[bw])
        self.ldma(blk[:], dr["blk"], ch, writes=[bw])
        self.ldma(csel[:], dr["csel"], ch, writes=[bw])
        self.ldma(ones64[:], dr["ones64"], ch, writes=[bw])
        self.ldma(ng[:], dr["hgrn_ng"][l], ch, writes=[bw])
        self.V(lambda h: h.tensor_copy(out=cselb[:], in_=csel[:]), [bw], [bw])
        if l == 0:
            self.G(lambda h: h.memset(lb[:], 0.0), [], [bw])
        else:
            assert self.L == 2
            self.ldma(lb[:], dr["hgrn_lb_logits"][1:2, :].broadcast_to([128, 256]), ch, writes=[bw])
            self.ldma(oml[:], dr["hgrn_lb_logits"][0:1, :].broadcast_to([128, 256]), ch, writes=[bw])
            self.V(lambda h: h.tensor_tensor(out=lb[:], in0=lb[:], in1=oml[:], op=ALU.subtract), [bw], [bw])
            self.A(lambda h: h.activation(out=lb[:], in_=lb[:], func=AF.Sigmoid), [bw], [bw])
        self.V(lambda h: h.tensor_scalar(out=oml[:], in0=lb[:], scalar1=-1.0, scalar2=1.0, op0=ALU.mult, op1=ALU.add),
               [bw], [bw])
        mixT = fw.sb(es, "hmixT", [64, 4, S], BF16)
        bmix = Buf()
        Sst = fw.sb(es, "Sst", [64, 4, 64], F32)
        Sbf = fw.sb(es, "Sbf", [64, 4, 64], BF16)
        bS = Buf()
        bSb = Buf()
        self.G(lambda h: h.memset(Sst[:], 0.0), [], [bS])
        self.G(lambda h: h.memset(Sbf[:], 0.0), [], [bSb])

        def T(name, shape, dt, n=2):
            r = [(fw.sb(es, f"{name}{i}", shape, dt), Buf()) for i in range(n)]
            return r * (2 // n)
        f_ = T("hf", [128, 256], F32, 1)
        lf_ = T("hlf", [128, 256], F32)
        kk_ = T("hkk", [128, 256], F32, 1)
        eg_ = T("heg", [128, 768], F32, 1)
        qf_ = T("hqf", [128, 256], F32, 1)
        qt_ = T("hqt", [128, 256], BF16)
        kh_ = T("hkh", [128, 256], F32, 1)
        khb_ = T("hkhb", [128, 256], BF16)
        ke_ = T("hke", [128, 256], BF16)
        kem_ = T("hkem", [128, 8, 256], BF16)
        vb_ = T("hvb", [128, 256], BF16)
        sgb_ = T("hsg", [128, 256], BF16)
        Dt_ = T("hDt", [64, 4, 8], F32)
        qkT_ = T("hqkT", [64, 8, 128], BF16)
        gT_ = T("hgT", [64, 4, 128], BF16)
        scm_ = T("hscm", [128, 4, 128], BF16)
        osq_ = T("hosq", [64, 512], F32, 1)
        rs_ = T("hrs", [64, 512], F32, 1)
        t1_ = T("ht1", [64, 512], F32, 1)
        def bufs(t):
            i2 = t % 2
            return dict(f=f_[i2], lf=lf_[i2], kk=kk_[i2], eg=eg_[i2], qf=qf_[i2], qt=qt_[i2], kh=kh_[i2], khb=khb_[i2],
                        ke=ke_[i2], kem=kem_[i2], vb=vb_[i2], sgb=sgb_[i2], Dt=Dt_[i2], qkT=qkT_[i2], gT=gT_[i2],
                        scm=scm_[i2], osq=osq_[i2], rs=rs_[i2], t1=t1_[i2])

        def front(t):
            tok = slice(t * 128, (t + 1) * 128)
            bh = self.bhT[t // 4]
            i2 = t % 2
            pa, bpa = self.ringA.next()
            pb, bpb = self.ringA.next()
            for kc in range(8):
                self.MM(pa[:], self.hT[:, kc, tok], wa[:, kc, 0:512], kc == 0, kc == 7, [bh, bw], [bpa])
            for kc in range(8):
                self.MM(pb[:], self.hT[:, kc, tok], wa[:, kc, 512:1024], kc == 0, kc == 7, [bh, bw], [bpb])
            f, bf_ = f_[i2]
            lf, blf = lf_[i2]
            kk, bkk = kk_[i2]
            eg, beg = eg_[i2]
            qf, bqf = qf_[i2]
            qt, bqt = qt_[i2]
            kh, bkh = kh_[i2]
            khb, bkhb = khb_[i2]
            ke, bke = ke_[i2]
            kem, bkem = kem_[i2]
            vb, bvb = vb_[i2]
            sgb, bsgb = sgb_[i2]
            Dt, bDt = Dt_[i2]
            qkT, bqkT = qkT_[i2]
            gT, bgT = gT_[i2]
            scm, bscm = scm_[i2]
            osq, bosq = osq_[i2]
            rs, brs = rs_[i2]
            t1, bt1 = t1_[i2]
            self.A(lambda h: h.activation(out=f[:], in_=pa[:, 256:512], func=AF.Sigmoid), [bpa], [bf_])
            self.A(lambda h: h.activation(out=qf[:], in_=pa[:, 0:256], func=AF.Silu), [bpa], [bqf])
            self.A(lambda h: h.copy(out=vb[:], in_=pb[:, 0:256]), [bpb], [bvb])
            self.A(lambda h: h.activation(out=sgb[:], in_=pb[:, 256:512], func=AF.Silu), [bpb], [bsgb])
            yield
            self.V(lambda h: h.tensor_tensor(out=f[:], in0=f[:], in1=oml[:], op=ALU.mult), [bf_, bw], [bf_])
            self.V(lambda h: h.tensor_tensor(out=f[:], in0=f[:], in1=lb[:], op=ALU.add), [bf_, bw], [bf_])
            self.A(lambda h: h.activation(out=lf[:], in_=f[:], func=AF.Ln), [bf_], [blf])
            self.V(lambda h: h.tensor_scalar(out=kk[:], in0=f[:], scalar1=-1.0, scalar2=1.0, op0=ALU.mult, op1=ALU.add),
                   [bf_], [bkk])
            yield
            pg, bpg = self.ringA.next()
            self.MM(pg[:, 0:256], tri[:], lf[:], True, True, [bw, blf], [bpg], inc=False)
            self.MM(pg[:, 256:512], blk[:], lf[:], True, True, [bw, blf], [bpg], skip=True)
            pd, bpd = self.ringB.next()
            for hd in range(4):
                self.MM(pd[0:64, hd * 8:(hd + 1) * 8], lf[:, hd * 64:(hd + 1) * 64], csel[:], hd == 0, True,
                        [bw, blf], [bpd], inc=(hd == 3), skip=True)
            self.A(lambda h: h.activation(out=eg[:, 0:256], in_=pg[:, 0:256], func=AF.Exp), [bpg], [beg])
            self.A(lambda h: h.activation(out=eg[:, 256:512], in_=pg[:, 0:256], func=AF.Exp, scale=-1.0), [bpg], [beg])
            self.A(lambda h: h.activation(out=eg[:, 512:768], in_=pg[:, 256:512], func=AF.Exp), [bpg], [beg])
            self.A(lambda h: h.activation(out=Dt[:].rearrange("p h c -> p (h c)"), in_=pd[0:64, 0:32], func=AF.Exp),
                   [bpd], [bDt])
            yield
            self.V(lambda h: h.tensor_tensor(out=qt[:], in0=qf[:], in1=eg[:, 0:256], op=ALU.mult), [bqf, beg], [bqt])
            self.V(lambda h: h.tensor_tensor(out=kh[:], in0=kk[:], in1=eg[:, 256:512], op=ALU.mult), [bkk, beg], [bkh])
            self.G(lambda h: h.tensor_copy(out=khb[:], in_=kh[:]), [bkh], [bkhb])
            self.G(lambda h: h.tensor_tensor(out=ke[:], in0=kh[:], in1=eg[:, 512:768], op=ALU.mult), [bkh, beg], [bke])
            self.G(lambda h: h.tensor_tensor(out=kem[:], in0=ke[:].unsqueeze(1).broadcast_to([128, 8, 256]),
                                             in1=cselb[:].unsqueeze(2).broadcast_to([128, 8, 256]), op=ALU.mult),
                   [bke, bw], [bkem])
            yield
            pt, bpt = self.ringT.next()
            for hd in range(4):
                self.TR(pt[0:64, hd * 128:(hd + 1) * 128], qt[:, hd * 64:(hd + 1) * 64], [bqt], [bpt], inc=False)
            for hd in range(4):
                self.TR(pt[0:64, 512 + hd * 128:512 + (hd + 1) * 128], khb[:, hd * 64:(hd + 1) * 64], [bkhb], [bpt],
                        inc=(hd == 3))
            self.A(lambda h: h.copy(out=qkT[:].rearrange("p a t -> p (a t)"), in_=pt[0:64, :]), [bpt], [bqkT])
            yield
            pt2, bpt2 = self.ringT.next()
            for hd in range(4):
                self.TR(pt2[0:64, hd * 128:(hd + 1) * 128], sgb[:, hd * 64:(hd + 1) * 64], [bsgb], [bpt2], inc=(hd == 3))
            self.A(lambda h: h.copy(out=gT[:].rearrange("p a t -> p (a t)"), in_=pt2[0:64, 0:512]), [bpt2], [bgT])
            yield
            psc, bpsc = self.ringA.next()
            for hd in range(4):
                self.MM(psc[:, hd * 128:(hd + 1) * 128], qkT[:, 4 + hd, :], qkT[:, hd, :], hd == 0, True,
                        [bqkT], [bpsc], inc=(hd == 3), skip=True)
            self.V(lambda h: h.tensor_tensor(out=scm[:], in0=psc[:].rearrange("p (a t) -> p a t", a=4),
                                             in1=tri[:].unsqueeze(1).broadcast_to([128, 4, 128]), op=ALU.mult),
                   [bpsc, bw], [bscm])
            yield

        def back(t):
            tok = slice(t * 128, (t + 1) * 128)
            B = bufs(t)
            kem, bkem = B["kem"]
            vb, bvb = B["vb"]
            Dt, bDt = B["Dt"]
            qkT, bqkT = B["qkT"]
            gT, bgT = B["gT"]
            scm, bscm = B["scm"]
            osq, bosq = B["osq"]
            rs, brs = B["rs"]
            t1, bt1 = B["t1"]
            po, bpo = self.ringB.next()
            firstmm = True
            for ci in range(8):
                c = t * 8 + ci
                if c > 0:
                    for hd in range(4):
                        self.MM(po[0:64, hd * 128 + ci * 16:hd * 128 + (ci + 1) * 16], Sbf[:, hd, :],
                                qkT[:, hd, ci * 16:(ci + 1) * 16], firstmm, False, [bSb, bqkT], [bpo], inc=False, skip=True)
                        firstmm = False
                pkv, bpkv = self.ringA.next()
                for hd in range(4):
                    self.MM(pkv[0:64, hd * 64:(hd + 1) * 64], kem[:, ci, hd * 64:(hd + 1) * 64], vb[:, hd * 64:(hd + 1) * 64],
                            hd == 0, True, [bkem, bvb], [bpkv], inc=(hd == 3), skip=True)
                self.V(lambda h: h.tensor_tensor(out=Sst[:], in0=Sst[:], in1=Dt[:, :, ci:ci + 1].broadcast_to([64, 4, 64]),
                                                 op=ALU.mult), [bS, bDt], [bS])
                self.V(lambda h: h.tensor_tensor(out=Sst[:], in0=Sst[:], in1=pkv[0:64, 0:256].rearrange("p (a b) -> p a b", a=4),
                                                 op=ALU.add), [bS, bpkv], [bS])
                self.V(lambda h: h.tensor_copy(out=Sbf[:], in_=Sst[:]), [bS], [bSb])
                yield
            for hd in range(4):
                self.MM(po[0:64, hd * 128:(hd + 1) * 128], vb[:, hd * 64:(hd + 1) * 64], scm[:, hd, :], firstmm, True,
                        [bvb, bscm], [bpo], inc=(hd == 3), skip=True)
                firstmm = False
            self.A(lambda h: h.activation(out=osq[:], in_=po[0:64, :], func=AF.Square), [bpo], [bosq])
            pss, bpss = self.ringA.next()
            self.MM(pss[0:64, :], ones64[:], osq[:], True, True, [bw, bosq], [bpss])
            self.V(lambda h: h.tensor_scalar(out=rs[:], in0=pss[0:64, :], scalar1=1.0 / 64.0, scalar2=RMS_EPS,
                                             op0=ALU.mult, op1=ALU.add), [bpss], [brs])
            self.A(lambda h: h.activation(out=rs[:], in_=rs[:], func=AF.Ln), [brs], [brs])
            self.A(lambda h: h.activation(out=rs[:], in_=rs[:], func=AF.Exp, scale=-0.5), [brs], [brs])
            self.V(lambda h: h.tensor_tensor(out=t1[:], in0=po[0:64, :], in1=rs[:], op=ALU.mult), [bpo, brs], [bt1])
            for hd in range(4):
                self.V(lambda h: h.scalar_tensor_tensor(out=mixT[0:64, hd, tok], in0=t1[:, hd * 128:(hd + 1) * 128],
                                                        scalar=ng[:, hd:hd + 1], in1=gT[:, hd, :],
                                                        op0=ALU.mult, op1=ALU.mult), [bt1, bw, bgT], [bmix])

        for _ in front(0):
            pass
        for t in range(NT):
            gb = back(t)
            gf = front(t + 1) if t + 1 < NT else iter(())
            done_b = done_f = False
            while not (done_b and done_f):
                if not done_b:
                    try:
                        next(gb)
                    except StopIteration:
                        done_b = True
                if not done_f:
                    try:
                        next(gf)
                    except StopIteration:
                        done_f = True
        if "hgrn_o" in self.dumps:
            self.dump_bf16(es, "hgrn_o", mixT[:], [64, 4, S], [bmix])
        self.out_proj(es, l, m, mixT, bmix, first)


_NC_CACHE = {}


def _get_nc(S, L):
    key = (S, L)
    if key not in _NC_CACHE:
        _NC_CACHE[key] = Builder(S, L).build()
    return _NC_CACHE[key]


def make_in_maps(inputs, n_cores, S):
    w = host_weights(inputs)
    consts = host_consts(S)
    x = np.asarray(inputs["x"], dtype=np.float32)
    c = np.asarray(inputs["c"], dtype=np.float32)
    maps = []
    for i in range(n_cores):
        cc = c[2 * i:2 * i + 2]
        cT = np.ascontiguousarray(cc.reshape(2, 8, 128).transpose(2, 1, 0))
        mp = {"x": np.ascontiguousarray(x[2 * i:2 * i + 2]), "cT": cT}
        mp.update(w)
        mp.update(consts)
        maps.append(mp)
    return maps


def kernel(**inputs):
    x = np.asarray(inputs["x"])
    B, S, _ = x.shape
    L = np.asarray(inputs["ada_w"]).shape[0]
    n = B // 2
    nc = _get_nc(S, L)
    maps = make_in_maps(inputs, n, S)
    res = run_bass_kernel_spmd(nc, maps, core_ids=list(range(n)))
    out = np.concatenate([r["out"] for r in res.results], axis=0)
    return out.astype(np.float32)
```

```python
import numpy as np
from contextlib import ExitStack
import concourse.bass as bass
import concourse.mybir as mybir
from concourse.bass_utils import run_bass_kernel_spmd

F32 = mybir.dt.float32
BF16 = mybir.dt.bfloat16
AF = mybir.ActivationFunctionType
ALU = mybir.AluOpType

D = 1024
DFF = 2816
NJ = 22
MIXC = 2724
ALPHA = float(4 ** 0.25)
LN_EPS = 1e-5
RMS_EPS = 1e-6
NEG = -30000.0


class Buf:
    __slots__ = ("w", "r", "name")

    def __init__(self, name=""):
        self.w = {}
        self.r = {}
        self.name = name


class Chan:
    def __init__(self, sem, name):
        self.sem = sem
        self.cnt = 0
        self.name = name


class Eng:
    def __init__(self, name, h, chan):
        self.name = name
        self.h = h
        self.chan = chan
        self.seen = {}


class FW:
    def __init__(self, nc, es):
        self.nc = nc
        self.es = es
        self.nsem = 0
        self.chans = []
        self.pe = Eng("pe", nc.tensor, self.new_chan("pe"))
        self.act = Eng("act", nc.scalar, self.new_chan("act"))
        self.dve = Eng("dve", nc.vector, self.new_chan("dve"))
        self.pool = Eng("pool", nc.gpsimd, self.new_chan("pool"))
        self.sp = Eng("sp", nc.sync, self.new_chan("sp"))
        self.engs = [self.pe, self.act, self.dve, self.pool, self.sp]
        self.n_inst = 0
        self.n_wait = 0
        self.uid = 0

    def new_chan(self, name):
        for ch in self.chans:
            if ch.name == name and name not in ("dbg",):
                return ch
        sem = self.es.enter_context(self.nc.semaphore(f"s_{name}_{self.nsem}"))
        self.nsem += 1
        ch = Chan(sem, name)
        self.chans.append(ch)
        return ch

    def _sync(self, E, reads, writes):
        need = {}
        for b in reads:
            for ch, c in b.w.items():
                if need.get(ch, 0) < c:
                    need[ch] = c
        for b in writes:
            for ch, c in b.w.items():
                if need.get(ch, 0) < c:
                    need[ch] = c
            for ch, c in b.r.items():
                if need.get(ch, 0) < c:
                    need[ch] = c
        for ch, c in need.items():
            if ch is E.chan and E.name == "pe":
                continue
            if E.seen.get(ch, 0) >= c:
                continue
            E.h.wait_ge(ch.sem, c)
            self.n_wait += 1
            E.seen[ch] = c

    def _mark(self, ch, mark, reads, writes):
        for b in reads:
            if b.r.get(ch, 0) < mark:
                b.r[ch] = mark
        for b in writes:
            b.w[ch] = mark
            b.r = {}

    def op(self, E, emit, reads=(), writes=(), inc=True):
        self._sync(E, reads, writes)
        ins = emit(E.h)
        self.n_inst += 1
        ch = E.chan
        if inc:
            ch.cnt += 1
            ins.then_inc(ch.sem, 1)
            mark = ch.cnt
        else:
            mark = ch.cnt + 1
        self._mark(ch, mark, reads, writes)
        return ins

    def dma(self, Q, chan, out, in_, reads=(), writes=(), **kw):
        self._sync(Q, reads, writes)
        ins = Q.h.dma_start(out=out, in_=in_, **kw)
        self.n_inst += 1
        chan.cnt += 16
        ins.then_inc(chan.sem, 16)
        self._mark(chan, chan.cnt, reads, writes)
        return ins

    def barrier(self):
        for E in self.engs:
            for ch in self.chans:
                if ch.cnt == 0 or ch is E.chan:
                    continue
                if E.seen.get(ch, 0) >= ch.cnt:
                    continue
                E.h.wait_ge(ch.sem, ch.cnt)
                E.seen[ch] = ch.cnt

    def sb(self, es, name, shape, dt):
        self.uid += 1
        return es.enter_context(self.nc.sbuf_tensor(f"sb{self.uid}_{name}", list(shape), dt))

    def psum(self, es, name, shape, dt):
        self.uid += 1
        return es.enter_context(self.nc.psum_tensor(f"pp{self.uid}_{name}", list(shape), dt))


class Ring:
    def __init__(self, items):
        self.items = items
        self.i = 0

    def next(self):
        it = self.items[self.i % len(self.items)]
        self.i += 1
        return it


def host_consts(S):
    pos = np.arange(S, dtype=np.float32)
    half = 16
    inv_freq = (10000.0 ** (-np.arange(half, dtype=np.float32) / half)).astype(np.float32)
    ang = pos[None, :] * inv_freq[:, None]
    cos = np.cos(ang).astype(np.float32)
    sin = np.sin(ang).astype(np.float32)
    ropeC = np.ones((96, S), np.float32)
    ropeS = np.zeros((96, S), np.float32)
    ropeC[64:80] = cos
    ropeC[80:96] = cos
    ropeS[64:80] = -sin
    ropeS[80:96] = sin
    s = np.arange(128)[:, None]
    negmask = np.zeros((128, 4, 512), np.float32)
    t = np.arange(512)[None, :]
    for r in range(4):
        negmask[:, r, :] = np.where(128 * r + s <= t, 0.0, NEG)
    ident = np.eye(128, dtype=np.float32)
    t128 = np.arange(128)[None, :]
    same = (s // 16) == (t128 // 16)
    tri = (same & (s <= t128)).astype(np.float32)
    blk = same.astype(np.float32)
    csel = ((s // 16) == np.arange(8)[None, :]).astype(np.float32)
    caus = (s <= t128).astype(np.float32)
    sel65 = np.zeros((65, 64), np.float32)
    sel65[64, :] = 1.0
    ones64 = np.ones((64, 64), np.float32)
    return dict(ropeC=ropeC, ropeS=ropeS, negmask=negmask, ident=ident, tri=tri, blk=blk,
                csel=csel, caus=caus, sel65=sel65, ones64=ones64)


CONST_SHAPES = lambda S: dict(ropeC=[96, S], ropeS=[96, S], negmask=[128, 4, 512], ident=[128, 128],
                              tri=[128, 128], blk=[128, 128], csel=[128, 8], caus=[128, 128],
                              sel65=[65, 64], ones64=[64, 64])


def weight_shapes(L):
    return dict(
        ada_w=[L, D, 9 * D], ada_b=[L, 9 * D], ln_g=[L, 3, D], ln_b=[L, 3, D],
        ffn1_w_in=[L, D, 2 * DFF], ffn1_w_out=[L, DFF, D], ffn2_w_in=[L, D, 2 * DFF], ffn2_w_out=[L, DFF, D],
        mix_w_in=[L, D, MIXC], mix_w_out=[L, D, D], wkr=[L, D, 192],
        hgrn_lb_logits=[L, 256], hgrn_ng=[L, 64, 4], mla_q_norm_g=[L, 256], mla_kv_norm_g=[L, 128],
        mla_w_uq=[L, 256, 384], mla_w_uqp=[L, 256, 384], mla_w_ukv=[L, 128, 512],
        fox_b_f=[L, 4], gmlp_ln_g=[L, 256], gmlp_ln_b=[L, 256], gmlp_wsT=[L, 4, 128, 128], gmlp_b_s=[L, 4, 128],
    )


def host_weights(inp):
    L = inp["ada_w"].shape[0]
    f = lambda a: np.ascontiguousarray(np.asarray(a, dtype=np.float32))
    w = {k: f(inp[k]) for k in ["ada_w", "ada_b", "ln_g", "ln_b", "ffn1_w_in", "ffn1_w_out", "ffn2_w_in",
                                "ffn2_w_out", "mix_w_in", "mix_w_out", "hgrn_lb_logits", "mla_q_norm_g",
                                "mla_kv_norm_g", "mla_w_uq", "mla_w_ukv", "fox_b_f", "gmlp_ln_g", "gmlp_ln_b",
                                "gmlp_b_s"]}
    perm = np.concatenate([np.arange(16, 32), np.arange(0, 16)])
    kr = w["mix_w_in"][:, :, 1408:1440]
    wkr = np.zeros((L, D, 192), np.float32)
    wkr[:, :, 64:96] = kr
    wkr[:, :, 160:192] = kr[:, :, perm]
    w["wkr"] = wkr
    uq = w["mla_w_uq"].reshape(L, 256, 4, 96)
    uqp = np.zeros_like(uq)
    uqp[:, :, :, 64:96] = uq[:, :, :, 64:96][:, :, :, perm]
    w["mla_w_uqp"] = np.ascontiguousarray(uqp.reshape(L, 256, 384))
    w["hgrn_ng"] = np.ascontiguousarray(f(inp["hgrn_norm_g"]).reshape(L, 4, 64).transpose(0, 2, 1))
    w["gmlp_wsT"] = np.ascontiguousarray(f(inp["gmlp_w_s"]).transpose(0, 1, 3, 2))
    return w


class Builder:
    def __init__(self, S, L, stop=None, dumps=()):
        self.S = S
        self.L = L
        self.NT = S // 128
        self.NG = S // 512
        self.stop = stop
        self.dumps = set(dumps)
        self.dump_aps = {}

    def build(self):
        S, L = self.S, self.L
        nc = bass.Bass("TRN2", target_bir_lowering=False)
        self.nc = nc
        dr = {}
        dr["x"] = nc.dram_tensor("x", [2, S, D], F32, kind="ExternalInput").ap()
        dr["cT"] = nc.dram_tensor("cT", [128, 8, 2], F32, kind="ExternalInput").ap()
        for k, shp in weight_shapes(L).items():
            dr[k] = nc.dram_tensor(k, shp, F32, kind="ExternalInput").ap()
        for k, shp in CONST_SHAPES(S).items():
            dr[k] = nc.dram_tensor(k, shp, F32, kind="ExternalInput").ap()
        dr["out"] = nc.dram_tensor("out", [2, S, D], F32, kind="ExternalOutput").ap()
        dr["modd"] = nc.dram_tensor("modd", [2, L, 9 * D], F32, kind="Internal").ap()
        self.dr = dr
        with ExitStack() as es:
            self.fw = fw = FW(nc, es)
            self.es = es
            self.setup_global(es)
            self.prologue_mod()
            for s in range(2):
                self.run_sequence(s)
                if self.stop is not None and self.stop[0] == s:
                    break
            fw.barrier()
            print(f"[build] inst={fw.n_inst} waits={fw.n_wait} sems={fw.nsem} "
                  f"cnt={[(e.name, e.chan.cnt) for e in fw.engs]}", flush=True)
        return nc

    def V(self, emit, reads=(), writes=()):
        return self.fw.op(self.fw.dve, emit, reads, writes)

    def A(self, emit, reads=(), writes=()):
        return self.fw.op(self.fw.act, emit, reads, writes)

    def G(self, emit, reads=(), writes=()):
        return self.fw.op(self.fw.pool, emit, reads, writes)

    def MM(self, out, lhsT, rhs, start, stop, reads=(), writes=(), inc=None, skip=False):
        if inc is None:
            inc = bool(stop)
        kw = dict(skip_group_check=True) if skip else {}
        return self.fw.op(self.fw.pe, lambda h: h.matmul(out, lhsT=lhsT, rhs=rhs, start=start, stop=stop, **kw),
                          reads, writes, inc=inc)

    def TR(self, out, in_, reads=(), writes=(), inc=True):
        ident = self.ident
        n = in_.shape[0]
        return self.fw.op(self.fw.pe, lambda h: h.transpose(out, in_, ident[0:n, 0:n]),
                          list(reads) + [self.b_const], writes, inc=inc)

    def wdma(self, out, in_, chan, reads=(), writes=()):
        return self.fw.dma(self.fw.pool, self.fw.new_chan(chan.name + "_sw"), out, in_, reads, writes)

    def ldma(self, out, in_, chan, reads=(), writes=(), **kw):
        return self.fw.dma(self.fw.sp, chan, out, in_, reads, writes, **kw)

    def dump(self, name, ap, shape, reads):
        if name not in self.dumps or name in self.dump_aps:
            return
        d = self.nc.dram_tensor("dbg_" + name, list(shape), F32, kind="ExternalOutput").ap()
        self.dump_aps[name] = d
        ch = self.fw.new_chan("dbg")
        self.fw.dma(self.fw.sp, ch, d, ap, reads=reads)

    def setup_global(self, es):
        fw, dr, NT = self.fw, self.dr, self.NT
        self.x = fw.sb(es, "x", [128, NT, D], F32)
        self.bx = [Buf(f"x{t}") for t in range(NT)]
        self.hT = fw.sb(es, "hT", [128, 8, self.S], BF16)
        self.bhT = [Buf(f"hT{g}") for g in range(self.NG)]
        self.mv = fw.sb(es, "modv", [128, 5, D], F32)
        self.bmv = Buf("modv")
        self.ch_mv = fw.new_chan("modv")
        self.bab = Buf("modab")
        self.ch_ab = fw.new_chan("modab")
        self.ident = fw.sb(es, "ident", [128, 128], BF16)
        self.b_const = Buf("const")
        ch = fw.new_chan("const")
        self.wdma(self.ident[:], dr["ident"], ch, writes=[self.b_const])
        self.pbank = [(fw.psum(es, f"b{i}", [128, 512], F32), Buf(f"ps{i}")) for i in range(6)]
        self.ptr = [(fw.psum(es, f"t{i}", [128, 1024], BF16), Buf(f"pt{i}")) for i in range(2)]
        self.ringA = Ring(self.pbank[0:4])
        self.ringB = Ring(self.pbank[4:6])
        self.ringT = Ring(self.ptr)
        self.ch_x = [fw.new_chan(f"x{i}") for i in range(4)]
        self.ch_out = [fw.new_chan(f"o{i}") for i in range(4)]
        self.b_modd = Buf("modd")

    def mod_alloc(self, es):
        fw = self.fw
        self.m_slots = [(fw.sb(es, f"aw{i}", [128, 8, 512], BF16), Buf(), fw.new_chan(f"aw{i}")) for i in range(2)]
        self.m_bslots = [(fw.sb(es, f"ab{i}", [2, 512], F32), Buf(), fw.new_chan(f"ab{i}")) for i in range(2)]
        self.m_rows = [(fw.sb(es, f"mr{i}", [2, 512], F32), Buf(), fw.new_chan(f"mr{i}")) for i in range(2)]

    def mod_block(self, l, cb):
        dr = self.dr
        it = self.m_it
        self.m_it += 1
        awv = dr["ada_w"][l].rearrange("(kc p) n -> p kc n", p=128)
        w, bw, chw = self.m_slots[it % 2]
        ab, bab, chb = self.m_bslots[it % 2]
        mr, bmr, chr_ = self.m_rows[it % 2]
        cact, b_c = self.cact, self.b_cact
        self.wdma(w[:], awv[:, :, cb * 512:(cb + 1) * 512], chw, writes=[bw])
        self.ldma(ab[:], dr["ada_b"][l:l + 1, cb * 512:(cb + 1) * 512].broadcast_to([2, 512]), chb, writes=[bab])
        ps, bps = self.ringA.next()
        for kc in range(8):
            self.MM(ps[0:2, :], cact[:, kc, :], w[:, kc, :], kc == 0, kc == 7, reads=[b_c, bw], writes=[bps])
        seg = (cb * 512) // 1024
        self.V(lambda h: h.tensor_tensor(out=mr[:], in0=ps[0:2, :], in1=ab[:], op=ALU.add), [bps, bab], [bmr])
        if seg in (1, 4, 7, 5):
            self.V(lambda h: h.tensor_scalar_add(out=mr[:], in0=mr[:], scalar1=1.0), [bmr], [bmr])
        elif seg in (2, 8):
            self.V(lambda h: h.tensor_scalar(out=mr[:], in0=mr[:], scalar1=1.0, scalar2=0.5,
                                             op0=ALU.add, op1=ALU.mult), [bmr], [bmr])
        self.ldma(dr["modd"][:, l, cb * 512:(cb + 1) * 512], mr[:], chr_, reads=[bmr], writes=[self.b_modd])

    def prologue_mod(self):
        fw, dr = self.fw, self.dr
        self.m_it = 0
        self.cact = fw.sb(self.es, "cact", [128, 8, 2], BF16)
        self.b_cact = Buf()
        self.mod_deferred = (self.L > 1 and self.stop is None and not self.skip_mixers)
        with ExitStack() as es:
            cT = fw.sb(es, "cT", [128, 8, 2], F32)
            ch = fw.new_chan("c")
            self.ldma(cT[:], dr["cT"], ch, writes=[self.b_cact])
            self.A(lambda h: h.activation(out=self.cact[:], in_=cT[:], func=AF.Silu), [self.b_cact], [self.b_cact])
            self.mod_alloc(es)
            for l in range(1 if self.mod_deferred else self.L):
                for cb in range(18):
                    self.mod_block(l, cb)
            fw.barrier()

    def load_ab(self, s, l, j):
        dr = self.dr
        for i, sg in enumerate([3 * j + 1, 3 * j + 0]):
            self.ldma(self.mv[:, i, :], dr["modd"][s:s + 1, l, sg * D:(sg + 1) * D].broadcast_to([128, D]), self.ch_ab,
                      reads=[self.b_modd], writes=[self.bab])

    def load_modvec(self, s, l, j):
        dr = self.dr
        mv, b, ch = self.mv, self.bmv, self.ch_mv
        sg = 3 * j + 2
        self.ldma(mv[:, 2, :], dr["modd"][s:s + 1, l, sg * D:(sg + 1) * D].broadcast_to([128, D]), ch,
                  reads=[self.b_modd], writes=[b])
        self.ldma(mv[:, 3, :], dr["ln_g"][l, j:j + 1, :].broadcast_to([128, D]), ch, writes=[b])
        self.ldma(mv[:, 4, :], dr["ln_b"][l, j:j + 1, :].broadcast_to([128, D]), ch, writes=[b])

    def run_sequence(self, s):
        fw, dr, NT = self.fw, self.dr, self.NT
        xin = dr["x"][s].rearrange("(t p) d -> p t d", p=128)
        q = max(1, NT // 4)
        for i in range(0, NT, q):
            self.ldma(self.x[:, i:i + q, :], xin[:, i:i + q, :], self.ch_x[(i // q) % 4],
                      writes=self.bx[i:i + q])
        subs = [(l, j) for l in range(self.L) for j in range(3)]
        if self.stop is not None and self.stop[0] == s:
            subs = subs[:subs.index((self.stop[1], self.stop[2])) + 1]
        self.xout = dr["out"][s].rearrange("(t p) d -> p t d", p=128)
        self.load_ab(s, *subs[0])
        with ExitStack() as e0:
            self.build_hT(e0)
            fw.barrier()
        for k, (l, j) in enumerate(subs):
            self.load_modvec(s, l, j)
            self.has_next = k + 1 < len(subs)
            if self.has_next:
                self.load_ab(s, *subs[k + 1])
            if j == 1:
                self.mixer(s, l)
            else:
                self.ffn(s, l, j)

    def build_hT(self, es):
        fw, NT = self.fw, self.NT
        tmpf = [(fw.sb(es, f"hx{i}", [128, D], F32), Buf()) for i in range(2)]
        hb = [(fw.sb(es, f"hb{i}", [128, D], BF16), Buf()) for i in range(2)]
        mv, bmv = self.mv, self.bab
        for t in range(NT):
            tf, btf = tmpf[t % 2]
            hbt, bhb = hb[t % 2]
            self.V(lambda h: h.tensor_tensor(out=tf[:], in0=self.x[:, t, :], in1=mv[:, 0, :], op=ALU.mult),
                   [self.bx[t], bmv], [btf])
            self.G(lambda h: h.tensor_tensor(out=hbt[:], in0=tf[:], in1=mv[:, 1, :], op=ALU.add),
                   [btf, bmv], [bhb])
            self.hT_transpose(t, hbt, bhb)

    def hT_transpose(self, t, hbt, bhb):
        pt, bpt = self.ringT.next()
        for kc in range(8):
            self.TR(pt[:, kc * 128:(kc + 1) * 128], hbt[:, kc * 128:(kc + 1) * 128], [bhb], [bpt], inc=(kc == 7))
        self.A(lambda h: h.copy(out=self.hT[:, :, t * 128:(t + 1) * 128],
                                in_=pt[:].rearrange("p (k c) -> p k c", k=8)),
               [bpt], [self.bhT[t // 4]])

    def begin_tail(self, es):
        fw = self.fw
        self.t_st = [(fw.sb(es, f"lnst{i}", [128, 2, 6], F32), fw.sb(es, f"lnmv{i}", [128, 4], F32), Buf()) for i in range(2)]
        self.t_q = []
        if self.has_next:
            self.t_tf = [(fw.sb(es, f"thx{i}", [128, D], F32), Buf()) for i in range(1)]
            self.t_hb = [(fw.sb(es, f"thb{i}", [128, D], BF16), Buf()) for i in range(3)]

    def tail_tile(self, t):
        self.ln_tile(t, *self.t_st[t % 2])
        if self.has_next:
            tf, btf = self.t_tf[0]
            hbt, bhb = self.t_hb[t % 3]
            mv = self.mv
            self.V(lambda h: h.tensor_tensor(out=tf[:], in0=self.x[:, t, :], in1=mv[:, 0, :], op=ALU.mult),
                   [self.bx[t], self.bab], [btf])
            self.G(lambda h: h.tensor_tensor(out=hbt[:], in0=tf[:], in1=mv[:, 1, :], op=ALU.add),
                   [btf, self.bab], [bhb])
            self.t_q.append((t, hbt, bhb))
            if len(self.t_q) > 2:
                self.hT_transpose(*self.t_q.pop(0))
        else:
            self.ldma(self.xout[:, t, :], self.x[:, t, :], self.ch_out[t % 4], reads=[self.bx[t]])

    def end_tail(self):
        while self.t_q:
            self.hT_transpose(*self.t_q.pop(0))

    def resid_update(self, t, half, ps, bps, first, tmp=None, btmp=None):
        sl = slice(half * 512, (half + 1) * 512)
        xs = self.x[:, t, sl]
        if first:
            self.V(lambda h: h.scalar_tensor_tensor(out=xs, in0=xs, scalar=ALPHA, in1=ps[:],
                                                    op0=ALU.mult, op1=ALU.add), [bps, self.bx[t]], [self.bx[t]])
        else:
            self.V(lambda h: h.tensor_tensor(out=xs, in0=xs, in1=ps[:], op=ALU.add),
                   [bps, self.bx[t]], [self.bx[t]])

    def ln_tile(self, t, s6, m4, bs):
        mv, bmv = self.mv, self.bmv
        if True:
            xt = self.x[:, t, :]
            bxt = self.bx[t]
            self.V(lambda h: h.bn_stats(out=s6[:, 0, :], in_=self.x[:, t, 0:512]), [bxt], [bs])
            self.V(lambda h: h.bn_stats(out=s6[:, 1, :], in_=self.x[:, t, 512:1024]), [bxt], [bs])
            self.V(lambda h: h.bn_aggr(out=m4[:, 0:2], in_=s6[:]), [bs], [bs])
            self.V(lambda h: h.tensor_scalar_add(out=m4[:, 2:3], in0=m4[:, 1:2], scalar1=LN_EPS), [bs], [bs])
            self.A(lambda h: h.sqrt(out=m4[:, 2:3], in_=m4[:, 2:3]), [bs], [bs])
            self.V(lambda h: h.reciprocal(out=m4[:, 2:3], in_=m4[:, 2:3]), [bs], [bs])
            self.V(lambda h: h.scalar_tensor_tensor(out=m4[:, 3:4], in0=m4[:, 0:1], scalar=-1.0, in1=m4[:, 2:3],
                                                    op0=ALU.mult, op1=ALU.mult), [bs], [bs])
            self.A(lambda h: h.activation(out=xt, in_=xt, func=AF.Identity, bias=m4[:, 3:4], scale=m4[:, 2:3]),
                   [bs, bxt], [bxt])
            self.V(lambda h: h.tensor_tensor(out=xt, in0=xt, in1=mv[:, 3, :], op=ALU.mult), [bxt, bmv], [bxt])
            self.G(lambda h: h.tensor_tensor(out=xt, in0=xt, in1=mv[:, 4, :], op=ALU.add), [bxt, bmv], [bxt])

    def ffn(self, s, l, j):
        fw, dr, NT, NG, S = self.fw, self.dr, self.NT, self.NG, self.S
        w_in = dr["ffn1_w_in" if j == 0 else "ffn2_w_in"][l].rearrange("(kc p) n -> p kc n", p=128)
        w_out = dr["ffn1_w_out" if j == 0 else "ffn2_w_out"][l].rearrange("(j p) n -> p j n", p=128)
        with ExitStack() as es:
            self.begin_tail(es)
            actT = fw.sb(es, "actT", [128, 6, S], BF16)
            bact = [Buf() for _ in range(NG)]
            wo = [(fw.sb(es, f"wo{i}", [128, 6, D], BF16), Buf(), fw.new_chan(f"wo{i}")) for i in range(2)]
            wi = [(fw.sb(es, f"wi{i}", [128, 8, 512], BF16), Buf(), fw.new_chan(f"wi{i}")) for i in range(2)]
            sgs = [(fw.sb(es, f"sg{i}", [128, 512], F32), Buf()) for i in range(2)]
            tmps = [(None, None)] * 2
            parts = [(0, 6), (6, 12), (12, 18), (18, 22)]
            iw = 0
            k = 0
            for pi, (j0, j1) in enumerate(parts):
                wot, bwo, chwo = wo[pi % 2]
                self.wdma(wot[:, 0:j1 - j0, :], w_out[:, j0:j1, :], chwo, writes=[bwo])
                self.G(lambda h: h.tensor_tensor(out=wot[:, 0:j1 - j0, :], in0=wot[:, 0:j1 - j0, :],
                                                 in1=self.mv[:, 2, :].unsqueeze(1).broadcast_to([128, j1 - j0, D]),
                                                 op=ALU.mult), [bwo, self.bmv], [bwo])
                for jj in range(j0, j1, 2):
                    wt, bw, chw = wi[iw % 2]
                    iw += 1
                    self.wdma(wt[:, :, 0:256], w_in[:, :, jj * 128:jj * 128 + 256], chw, writes=[bw])
                    self.wdma(wt[:, :, 256:512], w_in[:, :, DFF + jj * 128:DFF + jj * 128 + 256], chw, writes=[bw])
                    for c in range(2):
                        jl = jj + c - j0
                        for g in range(NG):
                            pg, bpg = self.ringA.next()
                            pu, bpu = self.ringA.next()
                            tok = slice(g * 512, (g + 1) * 512)
                            for kc in range(8):
                                self.MM(pg[:], wt[:, kc, c * 128:(c + 1) * 128], self.hT[:, kc, tok], kc == 0, kc == 7,
                                        [bw, self.bhT[g]], [bpg])
                            for kc in range(8):
                                self.MM(pu[:], wt[:, kc, 256 + c * 128:256 + (c + 1) * 128], self.hT[:, kc, tok],
                                        kc == 0, kc == 7, [bw, self.bhT[g]], [bpu])
                            sg, bsg = sgs[k % 2]
                            k += 1
                            self.A(lambda h: h.activation(out=sg[:], in_=pg[:], func=AF.Silu), [bpg], [bsg])
                            self.V(lambda h: h.tensor_tensor(out=actT[:, jl, tok], in0=sg[:], in1=pu[:], op=ALU.mult),
                                   [bsg, bpu], [bact[g]])
                nj = j1 - j0
                for t in range(NT):
                    for half in range(2):
                        ps, bps = self.ringB.next()
                        for jl in range(nj):
                            self.MM(ps[:], actT[:, jl, t * 128:(t + 1) * 128], wot[:, jl, half * 512:(half + 1) * 512],
                                    jl == 0, jl == nj - 1, [bact[t // 4], bwo], [bps])
                        tmp, btmp = tmps[(2 * t + half) % 2]
                        self.resid_update(t, half, ps, bps, pi == 0, tmp, btmp)
                    if pi == len(parts) - 1:
                        self.tail_tile(t)
            self.end_tail()
            fw.barrier()
        self.dump(f"x_{s}_{l}_{j}", self.x[:], [128, NT, D], self.bx)

    def mixer(self, s, l):
        fw = self.fw
        with ExitStack() as es:
            first = True
            active = [m for m in range(4) if m not in self.skip_mixers]
            fns = [self.mix_hgrn, self.mix_mla, self.mix_fox, self.mix_gmlp]
            for m in active:
                self.tail_on = (m == active[-1])
                with ExitStack() as es2:
                    fns[m](es2, s, l, m, first)
                    fw.barrier()
                first = False
        self.dump(f"x_{s}_{l}_1", self.x[:], [128, self.NT, D], self.bx)

    skip_mixers = ()
    B1_MLA = False
    B1_FOX = False

    def out_proj(self, es, l, m, mixT, bmix, first):
        fw, dr, NT = self.fw, self.dr, self.NT
        wov = dr["mix_w_out"][l, m * 256:(m + 1) * 256, :].rearrange("(s p) n -> p s n", p=64)
        wo = fw.sb(es, "mwo", [64, 4, D], BF16)
        bwo = Buf()
        ch = fw.new_chan("mwo")
        self.wdma(wo[:], wov, ch, writes=[bwo])
        self.G(lambda h: h.tensor_tensor(out=wo[:], in0=wo[:],
                                         in1=self.mv[0:64, 2, :].unsqueeze(1).broadcast_to([64, 4, D]),
                                         op=ALU.mult), [bwo, self.bmv], [bwo])
        tmps = [(None, None)] * 2
        if self.tail_on:
            self.begin_tail(es)
        for t in range(NT):
            for half in range(2):
                ps, bps = self.ringB.next()
                for sl in range(4):
                    self.MM(ps[:], mixT[0:64, sl, t * 128:(t + 1) * 128], wo[0:64, sl, half * 512:(half + 1) * 512],
                            sl == 0, sl == 3, [bmix, bwo], [bps])
                tmp, btmp = tmps[(2 * t + half) % 2]
                self.resid_update(t, half, ps, bps, first, tmp, btmp)
            if self.tail_on:
                self.tail_tile(t)
        if self.tail_on:
            self.end_tail()

    def attention(self, es, QT, KT, bq, bk, dk, Vaug, bv, scale, mixT, bmix):
        fw, dr, NG = self.fw, self.dr, self.NG
        negm = fw.sb(es, "negm", [128, 4, 512], BF16)
        sel = fw.sb(es, "sel65", [65, 64], F32)
        bc = Buf()
        ch = fw.new_chan("attc")
        self.wdma(negm[:], dr["negmask"], ch, writes=[bc])
        self.ldma(sel[:], dr["sel65"], ch, writes=[bc])
        tmpS = [(fw.sb(es, f"ts{i}", [128, 512], F32), Buf()) for i in range(2)]
        pTs = [(fw.sb(es, f"pT{i}", [128, 512], BF16), Buf()) for i in range(3)]
        osb = [(fw.sb(es, f"os{i}", [65, 512], F32), Buf()) for i in range(2)]
        it = 0
        ip = 0
        fin_pending = [None]
        for hd in range(4):
            for qg in range(NG):
                qs = slice(qg * 512, (qg + 1) * 512)
                nkb = 4 * (qg + 1)
                o_ps, bo = self.ringB.next()
                pend = []

                def qk(kb):
                    nonlocal ip
                    s_ps, bs = self.ringA.next()
                    self.MM(s_ps[:], KT[hd][0:dk, kb * 128:(kb + 1) * 128], QT[hd][0:dk, qs], True, True,
                            bk[kb // 4] + bq[qg], [bs])
                    pT, bp = pTs[ip % 3]
                    ip += 1
                    r = kb - 4 * qg
                    if r >= 0:
                        tS, bt = tmpS[kb % 2]
                        self.V(lambda h: h.tensor_tensor(out=tS[:], in0=s_ps[:], in1=negm[:, r, :], op=ALU.add),
                               [bs, bc], [bt])
                        self.A(lambda h: h.activation(out=pT[:], in_=tS[:], func=AF.Exp, scale=scale), [bt], [bp])
                    else:
                        self.A(lambda h: h.activation(out=pT[:], in_=s_ps[:], func=AF.Exp, scale=scale), [bs], [bp])
                    return pT, bp

                def pv(kb, pT, bp):
                    self.MM(o_ps[0:65, :], Vaug[:, kb, hd, :], pT[:], kb == 0, kb == nkb - 1, [bv, bp], [bo])

                LA = 2
                for kb in range(nkb + LA):
                    if kb < nkb:
                        pend.append((kb,) + qk(kb))
                    if kb == 1 and fin_pending[0] is not None:
                        fin_pending[0]()
                        fin_pending[0] = None
                    if kb >= LA:
                        a = pend.pop(0)
                        pv(*a)
                o_sb, bos = osb[it % 2]
                it += 1
                self.A(lambda h: h.copy(out=o_sb[:], in_=o_ps[0:65, :]), [bo], [bos])
                self.A(lambda h: h.activation(out=o_sb[64:65, :], in_=o_sb[64:65, :], func=AF.Ln), [bos], [bos])
                self.A(lambda h: h.activation(out=o_sb[64:65, :], in_=o_sb[64:65, :], func=AF.Exp, scale=-1.0), [bos], [bos])

                def fin(o_sb=o_sb, bos=bos, hd=hd, qs=qs):
                    d_ps, bd = self.ringB.next()
                    self.MM(d_ps[0:64, :], sel[:], o_sb[:], True, True, [bc, bos], [bd])
                    self.V(lambda h: h.tensor_tensor(out=mixT[0:64, hd, qs], in0=o_sb[0:64, :], in1=d_ps[0:64, :],
                                                     op=ALU.mult), [bos, bd], [bmix])
                fin_pending[0] = fin
        if fin_pending[0] is not None:
            fin_pending[0]()

    def mix_mla(self, es, s, l, m, first):
        fw, dr, NT, NG, S = self.fw, self.dr, self.NT, self.NG, self.S
        QT = [fw.sb(es, f"QT{h}", [96, S], BF16) for h in range(4)]
        KT = [fw.sb(es, f"KT{h}", [96, S], BF16) for h in range(4)]
        if self.B1_MLA:
            bq = [[Buf()] for _ in range(NG)]
            bk = [[Buf()] for _ in range(NG)]
        else:
            one = Buf()
            bq = [[one] for _ in range(NG)]
            bk = [[one] for _ in range(NG)]
        Vaug = fw.sb(es, "Vaug", [128, NT, 4, 65], BF16)
        bv = Buf()
        self.G(lambda h: h.memset(Vaug[:], 1.0), [], [bv])
        with ExitStack() as e1:
            wv = dr["mix_w_in"][l].rearrange("(kc p) n -> p kc n", p=128)
            wb = fw.sb(e1, "wB", [128, 8, 384], BF16)
            wkr = fw.sb(e1, "wkr", [128, 8, 192], BF16)
            wuq = fw.sb(e1, "wuq", [128, 2, 384], BF16)
            wuqp = fw.sb(e1, "wuqp", [128, 2, 384], BF16)
            wukv = fw.sb(e1, "wukv", [128, 512], BF16)
            gq = fw.sb(e1, "gq", [128, 256], F32)
            gkv = fw.sb(e1, "gkv", [128, 128], F32)
            bw = Buf()
            ch = fw.new_chan("mlaw")
            self.wdma(wb[:], wv[:, :, 1024:1408], ch, writes=[bw])
            self.wdma(wkr[:], dr["wkr"][l].rearrange("(kc p) n -> p kc n", p=128), ch, writes=[bw])
            self.wdma(wuq[:], dr["mla_w_uq"][l].rearrange("(kc p) n -> p kc n", p=128), ch, writes=[bw])
            self.wdma(wuqp[:], dr["mla_w_uqp"][l].rearrange("(kc p) n -> p kc n", p=128), ch, writes=[bw])
            self.wdma(wukv[:], dr["mla_w_ukv"][l], ch, writes=[bw])
            self.ldma(gq[:], dr["mla_q_norm_g"][l:l + 1, :].broadcast_to([128, 256]), ch, writes=[bw])
            self.ldma(gkv[:], dr["mla_kv_norm_g"][l:l + 1, :].broadcast_to([128, 128]), ch, writes=[bw])
            cqnT = fw.sb(e1, "cqnT", [128, 2, S], BF16)
            ckvT = fw.sb(e1, "ckvT", [128, S], BF16)
            bcT = [Buf() for _ in range(NG)]
            rc = [(fw.sb(e1, f"rC{i}", [96, 512], F32), fw.sb(e1, f"rS{i}", [96, 512], F32), Buf(),
                   fw.new_chan(f"rope{i}")) for i in range(2)]
            st = [(fw.sb(e1, f"mst{i}", [128, 2, 6], F32), fw.sb(e1, f"mmv{i}", [128, 8], F32), Buf()) for i in range(2)]
            cn = [(fw.sb(e1, f"cn{i}", [128, 384], BF16), Buf()) for i in range(2)]
            def stageA(t):
                tok = slice(t * 128, (t + 1) * 128)
                ps, bps = self.ringA.next()
                for kc in range(8):
                    self.MM(ps[:, 0:384], self.hT[:, kc, tok], wb[:, kc, :], kc == 0, kc == 7, [self.bhT[t // 4], bw], [bps])
                s6, m8, bs = st[t % 2]
                cnt, bcn = cn[t % 2]
                self.V(lambda h: h.bn_stats(out=s6[:, 0, :], in_=ps[:, 0:256]), [bps], [bs])
                self.V(lambda h: h.bn_stats(out=s6[:, 1, :], in_=ps[:, 256:384]), [bps], [bs])
                for i in range(2):
                    self.V(lambda h: h.bn_aggr(out=m8[:, 4 * i:4 * i + 2], in_=s6[:, i:i + 1, :]), [bs], [bs])
                    self.V(lambda h: h.scalar_tensor_tensor(out=m8[:, 4 * i + 2:4 * i + 3], in0=m8[:, 4 * i:4 * i + 1],
                                                            scalar=m8[:, 4 * i:4 * i + 1], in1=m8[:, 4 * i + 1:4 * i + 2],
                                                            op0=ALU.mult, op1=ALU.add), [bs], [bs])
                    self.V(lambda h: h.tensor_scalar_add(out=m8[:, 4 * i + 2:4 * i + 3], in0=m8[:, 4 * i + 2:4 * i + 3],
                                                         scalar1=RMS_EPS), [bs], [bs])
                    self.A(lambda h: h.sqrt(out=m8[:, 4 * i + 2:4 * i + 3], in_=m8[:, 4 * i + 2:4 * i + 3]), [bs], [bs])
                    self.V(lambda h: h.reciprocal(out=m8[:, 4 * i + 3:4 * i + 4], in_=m8[:, 4 * i + 2:4 * i + 3]), [bs], [bs])
                self.V(lambda h: h.scalar_tensor_tensor(out=cnt[:, 0:256], in0=ps[:, 0:256], scalar=m8[:, 3:4], in1=gq[:],
                                                        op0=ALU.mult, op1=ALU.mult), [bps, bs, bw], [bcn])
                self.V(lambda h: h.scalar_tensor_tensor(out=cnt[:, 256:384], in0=ps[:, 256:384], scalar=m8[:, 7:8],
                                                        in1=gkv[:], op0=ALU.mult, op1=ALU.mult), [bps, bs, bw], [bcn])

            def stageB(t):
                tok = slice(t * 128, (t + 1) * 128)
                cnt, bcn = cn[t % 2]
                pt, bpt = self.ringT.next()
                for c in range(3):
                    self.TR(pt[:, c * 128:(c + 1) * 128], cnt[:, c * 128:(c + 1) * 128], [bcn], [bpt], inc=(c == 2))
                self.A(lambda h: h.copy(out=cqnT[:, :, tok], in_=pt[:, 0:256].rearrange("p (k c) -> p k c", k=2)),
                       [bpt], [bcT[t // 4]])
                self.A(lambda h: h.copy(out=ckvT[:, tok], in_=pt[:, 256:384]), [bpt], [bcT[t // 4]])
                pv_, bpv = self.ringB.next()
                self.MM(pv_[:, 0:256].rearrange("p (h x) -> p h x", h=4), ckvT[:, tok],
                        wukv[:].rearrange("p (h x) -> p h x", h=4)[:, :, 64:128], True, True,
                        [bcT[t // 4], bw], [bpv])
                self.A(lambda h: h.copy(out=Vaug[:, t, :, 0:64], in_=pv_[:, 0:256].rearrange("p (h x) -> p h x", h=4)),
                       [bpv], [bv])

            stageA(0)
            for t in range(NT):
                if t + 1 < NT:
                    stageA(t + 1)
                stageB(t)
            t1s = [(fw.sb(e1, f"r1{i}", [96, 512], F32), Buf()) for i in range(2)]
            t2s = [(fw.sb(e1, f"r2{i}", [96, 512], F32), Buf()) for i in range(2)]
            k = 0
            for g in range(NG):
                tok = slice(g * 512, (g + 1) * 512)
                rC, rS, brp, chr_ = rc[g % 2]
                self.ldma(rC[:], dr["ropeC"][:, tok], chr_, writes=[brp])
                self.ldma(rS[:], dr["ropeS"][:, tok], chr_, writes=[brp])
                for hd in range(4):
                    pq, bpq = self.ringA.next()
                    pp, bpp = self.ringA.next()
                    for kc in range(2):
                        self.MM(pq[0:96, :], wuq[:, kc, hd * 96:(hd + 1) * 96], cqnT[:, kc, tok], kc == 0, kc == 1,
                                [bw, bcT[g]], [bpq])
                    for kc in range(2):
                        self.MM(pp[0:96, :], wuqp[:, kc, hd * 96:(hd + 1) * 96], cqnT[:, kc, tok], kc == 0, kc == 1,
                                [bw, bcT[g]], [bpp])
                    t1, b1 = t1s[k % 2]
                    t2, b2 = t2s[k % 2]
                    k += 1
                    self.V(lambda h: h.tensor_tensor(out=t1[:], in0=pq[0:96, :], in1=rC[:], op=ALU.mult), [bpq, brp], [b1])
                    self.V(lambda h: h.tensor_tensor(out=t2[:], in0=pp[0:96, :], in1=rS[:], op=ALU.mult), [bpp, brp], [b2])
                    self.V(lambda h: h.tensor_tensor(out=QT[hd][:, tok], in0=t1[:], in1=t2[:], op=ALU.add), [b1, b2], bq[g])
                    pk, bpk = self.ringA.next()
                    self.MM(pk[0:64, :], wukv[:, hd * 128:hd * 128 + 64], ckvT[:, tok], True, True, [bw, bcT[g]], [bpk])
                    self.A(lambda h: h.copy(out=KT[hd][0:64, tok], in_=pk[0:64, :]), [bpk], bk[g])
                pq, bpq = self.ringA.next()
                pp, bpp = self.ringA.next()
                for kc in range(8):
                    self.MM(pq[0:96, :], wkr[:, kc, 0:96], self.hT[:, kc, tok], kc == 0, kc == 7, [bw, self.bhT[g]], [bpq])
                for kc in range(8):
                    self.MM(pp[0:96, :], wkr[:, kc, 96:192], self.hT[:, kc, tok], kc == 0, kc == 7, [bw, self.bhT[g]], [bpp])
                t1, b1 = t1s[k % 2]
                t2, b2 = t2s[k % 2]
                k += 1
                self.V(lambda h: h.tensor_tensor(out=t1[64:96, :], in0=pq[64:96, :], in1=rC[64:96, :], op=ALU.mult),
                       [bpq, brp], [b1])
                self.V(lambda h: h.tensor_tensor(out=t2[64:96, :], in0=pp[64:96, :], in1=rS[64:96, :], op=ALU.mult),
                       [bpp, brp], [b2])
                for hd in range(4):
                    self.V(lambda h: h.tensor_tensor(out=KT[hd][64:96, tok], in0=t1[64:96, :], in1=t2[64:96, :], op=ALU.add),
                           [b1, b2], bk[g])
            fw.barrier()
        with ExitStack() as e2:
            mixT = fw.sb(e2, "mixT", [64, 4, S], BF16)
            bmix = Buf()
            self.attention(e2, QT, KT, bq, bk, 96, Vaug, bv, float(96 ** -0.5), mixT, bmix)
            if "mla_o" in self.dumps:
                self.dump_bf16(e2, "mla_o", mixT[:], [64, 4, S], [bmix])
            self.out_proj(e2, l, m, mixT, bmix, first)
            fw.barrier()

    def dump_bf16(self, es, name, ap, shape, reads):
        if name not in self.dumps or name in self.dump_aps:
            return
        t = self.fw.sb(es, "dmp" + name, shape, F32)
        b = Buf()
        self.V(lambda h: h.tensor_copy(out=t[:], in_=ap), reads, [b])
        self.dump(name, t[:], shape, [b])
        self.fw.barrier()

    def mix_fox(self, es, s, l, m, first):
        fw, dr, NT, NG, S = self.fw, self.dr, self.NT, self.NG, self.S
        QT = [fw.sb(es, f"fQT{h}", [68, S], BF16) for h in range(4)]
        KT = [fw.sb(es, f"fKT{h}", [68, S], BF16) for h in range(4)]
        if self.B1_FOX:
            bq = [[Buf(), Buf()] for _ in range(NG)]
            bk = [[Buf(), Buf()] for _ in range(NG)]
            allaug = [b[1] for b in bq] + [b[1] for b in bk]
        else:
            one = Buf()
            bq = [[one, one] for _ in range(NG)]
            bk = [[one, one] for _ in range(NG)]
            allaug = [one]
        Vaug = fw.sb(es, "fVaug", [128, NT, 4, 65], BF16)
        bv = Buf()
        self.G(lambda h: h.memset(Vaug[:], 1.0), [], [bv])
        for hd in range(4):
            self.G(lambda h: h.memset(QT[hd][64:68, :], 1.0), [], allaug)
            self.G(lambda h: h.memset(KT[hd][64:68, :], 1.0), [], allaug)
        with ExitStack() as e1:
            wv = dr["mix_w_in"][l].rearrange("(kc p) n -> p kc n", p=128)
            wc = fw.sb(e1, "wC", [128, 8, 772], BF16)
            bw = Buf()
            ch = fw.new_chan("foxw")
            self.wdma(wc[:], wv[:, :, 1440:2212], ch, writes=[bw])
            nbf = fw.sb(e1, "nbf", [4, 1], F32)
            self.ldma(nbf[:], dr["fox_b_f"][l:l + 1, :].rearrange("o h -> h o"), ch, writes=[bw])
            self.V(lambda h: h.tensor_scalar_mul(out=nbf[:], in0=nbf[:], scalar1=-1.0), [bw], [bw])
            ones4 = fw.sb(e1, "ones4", [4, 512], F32)
            carry = fw.sb(e1, "carry", [4, 1], F32)
            bcar = Buf()
            self.G(lambda h: h.memset(ones4[:], 1.0), [], [bcar])
            self.G(lambda h: h.memset(carry[:], 0.0), [bcar], [bcar])
            ch_row = [fw.new_chan(f"frow{i}") for i in range(2)]
            for t in range(NT):
                tok = slice(t * 128, (t + 1) * 128)
                pv_, bpv = self.ringB.next()
                for kc in range(8):
                    self.MM(pv_[:, 0:256], self.hT[:, kc, tok], wc[:, kc, 512:768], kc == 0, kc == 7,
                            [self.bhT[t // 4], bw], [bpv])
                self.A(lambda h: h.copy(out=Vaug[:, t, :, 0:64], in_=pv_[:, 0:256].rearrange("p (h x) -> p h x", h=4)),
                       [bpv], [bv])
            ft = [dict(e=fw.sb(e1, f"fe{i}", [4, 512], F32), fn=fw.sb(e1, f"ffn{i}", [4, 512], F32),
                       hi=fw.sb(e1, f"fhi{i}", [4, 512], BF16), hif=fw.sb(e1, f"fhf{i}", [4, 512], F32),
                       lo=fw.sb(e1, f"flo{i}", [4, 512], BF16), nhi=fw.sb(e1, f"fnh{i}", [4, 512], BF16),
                       nlo=fw.sb(e1, f"fnl{i}", [4, 512], BF16), b=Buf()) for i in range(2)]
            for g in range(NG):
                tok = slice(g * 512, (g + 1) * 512)
                for hd in range(4):
                    for which, dst, bdst in ((0, QT, bq[g][0]), (1, KT, bk[g][0])):
                        pq, bpq = self.ringA.next()
                        for kc in range(8):
                            self.MM(pq[0:64, :], wc[:, kc, which * 256 + hd * 64:which * 256 + (hd + 1) * 64],
                                    self.hT[:, kc, tok], kc == 0, kc == 7, [bw, self.bhT[g]], [bpq])
                        self.A(lambda h: h.copy(out=dst[hd][0:64, tok], in_=pq[0:64, :]), [bpq], [bdst])
                pf, bpf = self.ringA.next()
                for kc in range(8):
                    self.MM(pf[0:4, :], wc[:, kc, 768:772], self.hT[:, kc, tok], kc == 0, kc == 7, [bw, self.bhT[g]], [bpf])
                f = ft[g % 2]
                bf_ = f["b"]
                self.A(lambda h: h.activation(out=f["e"][:], in_=pf[0:4, :], func=AF.Exp, bias=nbf[:], scale=-1.0),
                       [bpf, bw], [bf_])
                self.A(lambda h: h.activation(out=f["e"][:], in_=f["e"][:], func=AF.Ln, bias=1.0, scale=1.0), [bf_], [bf_])
                self.V(lambda h: h.tensor_tensor_scan(out=f["fn"][:], data0=ones4[:], data1=f["e"][:], initial=carry[:],
                                                      op0=ALU.mult, op1=ALU.add), [bf_, bcar], [bf_])
                self.V(lambda h: h.tensor_copy(out=carry[:], in_=f["fn"][:, 511:512]), [bf_], [bcar])
                self.V(lambda h: h.tensor_scalar_mul(out=f["fn"][:], in0=f["fn"][:], scalar1=8.0), [bf_], [bf_])
                self.V(lambda h: h.tensor_copy(out=f["hi"][:], in_=f["fn"][:]), [bf_], [bf_])
                self.V(lambda h: h.tensor_copy(out=f["hif"][:], in_=f["hi"][:]), [bf_], [bf_])
                self.V(lambda h: h.tensor_tensor(out=f["lo"][:], in0=f["fn"][:], in1=f["hif"][:], op=ALU.subtract),
                       [bf_], [bf_])
                self.V(lambda h: h.tensor_scalar_mul(out=f["nhi"][:], in0=f["hi"][:], scalar1=-1.0), [bf_], [bf_])
                self.V(lambda h: h.tensor_scalar_mul(out=f["nlo"][:], in0=f["lo"][:], scalar1=-1.0), [bf_], [bf_])
                chr_ = ch_row[g % 2]
                moves = []
                for hd in range(4):
                    moves += [(QT[hd][64:65, tok], f["nhi"][hd:hd + 1, :]), (QT[hd][65:66, tok], f["nlo"][hd:hd + 1, :]),
                              (KT[hd][66:67, tok], f["hi"][hd:hd + 1, :]), (KT[hd][67:68, tok], f["lo"][hd:hd + 1, :])]
                wb_ = list({id(b): b for b in (bq[g][1], bk[g][1])}.values())
                for i, (dst_, src_) in enumerate(moves):
                    last = (i == len(moves) - 1)
                    self.ldma(dst_, src_, chr_, reads=([bf_] if last else [bf_] + wb_), writes=(wb_ if last else []))
            fw.barrier()
        with ExitStack() as e2:
            mixT = fw.sb(e2, "fmixT", [64, 4, S], BF16)
            bmix = Buf()
            self.attention(e2, QT, KT, bq, bk, 68, Vaug, bv, 0.125, mixT, bmix)
            if "fox_o" in self.dumps:
                self.dump_bf16(e2, "fox_o", mixT[:], [64, 4, S], [bmix])
            self.out_proj(e2, l, m, mixT, bmix, first)
            fw.barrier()

    def mix_gmlp(self, es, s, l, m, first):
        fw, dr, NT, NG, S = self.fw, self.dr, self.NT, self.NG, self.S
        wv = dr["mix_w_in"][l].rearrange("(kc p) n -> p kc n", p=128)
        wd = fw.sb(es, "wD", [128, 8, 512], BF16)
        bw = Buf()
        ch = fw.new_chan("gmw")
        self.wdma(wd[:], wv[:, :, 2212:2724], ch, writes=[bw])
        wsf = fw.sb(es, "wsf", [128, 4, 128], F32)
        caus = fw.sb(es, "caus", [128, 128], F32)
        wsm = fw.sb(es, "wsm", [128, 4, 128], BF16)
        bsb = fw.sb(es, "bsb", [64, 4, 128], F32)
        lng = fw.sb(es, "glng", [128, 256], F32)
        lnb = fw.sb(es, "glnb", [128, 256], F32)
        self.ldma(wsf[:], dr["gmlp_wsT"][l].rearrange("g s t -> s g t"), ch, writes=[bw])
        self.ldma(caus[:], dr["caus"], ch, writes=[bw])
        self.ldma(bsb[:], dr["gmlp_b_s"][l:l + 1].broadcast_to([64, 4, 128]), ch, writes=[bw])
        self.ldma(lng[:], dr["gmlp_ln_g"][l:l + 1, :].broadcast_to([128, 256]), ch, writes=[bw])
        self.ldma(lnb[:], dr["gmlp_ln_b"][l:l + 1, :].broadcast_to([128, 256]), ch, writes=[bw])
        self.V(lambda h: h.tensor_tensor(out=wsm[:], in0=wsf[:], in1=caus[:].unsqueeze(1).broadcast_to([128, 4, 128]),
                                         op=ALU.mult), [bw], [bw])
        mixT = fw.sb(es, "gmixT", [64, 4, S], BF16)
        bmix = Buf()
        self.mod_pending = []
        if s == 0 and l == 0 and self.mod_deferred:
            self.mod_alloc(es)
            self.mod_pending = [(ll, cb) for ll in range(1, self.L) for cb in range(18)]
        gv = [(fw.sb(es, f"gv{i}", [128, 256], F32), Buf()) for i in range(2)]
        vl = [(fw.sb(es, f"vl{i}", [128, 256], BF16), Buf()) for i in range(2)]
        st = [(fw.sb(es, f"gst{i}", [128, 6], F32), fw.sb(es, f"gmv{i}", [128, 4], F32), Buf()) for i in range(2)]
        gu = [(fw.sb(es, f"gu{i}", [64, 512], F32), Buf()) for i in range(2)]
        t1s = [(fw.sb(es, f"gt{i}", [64, 512], F32), Buf()) for i in range(2)]
        def stageA(t):
            tok = slice(t * 128, (t + 1) * 128)
            bh = self.bhT[t // 4]
            pv_, bpv = self.ringA.next()
            for kc in range(8):
                self.MM(pv_[:, 0:256], self.hT[:, kc, tok], wd[:, kc, 256:512], kc == 0, kc == 7, [bh, bw], [bpv])
            g_, bg = gv[t % 2]
            v_, bvl = vl[t % 2]
            s6, m4, bs = st[t % 2]
            self.A(lambda h: h.activation(out=g_[:], in_=pv_[:, 0:256], func=AF.Gelu_apprx_tanh), [bpv], [bg])
            self.V(lambda h: h.bn_stats(out=s6[:], in_=g_[:]), [bg], [bs])
            self.V(lambda h: h.bn_aggr(out=m4[:, 0:2], in_=s6[:]), [bs], [bs])
            self.V(lambda h: h.tensor_scalar_add(out=m4[:, 2:3], in0=m4[:, 1:2], scalar1=LN_EPS), [bs], [bs])
            self.A(lambda h: h.sqrt(out=m4[:, 2:3], in_=m4[:, 2:3]), [bs], [bs])
            self.V(lambda h: h.reciprocal(out=m4[:, 2:3], in_=m4[:, 2:3]), [bs], [bs])
            self.V(lambda h: h.scalar_tensor_tensor(out=m4[:, 3:4], in0=m4[:, 0:1], scalar=-1.0, in1=m4[:, 2:3],
                                                    op0=ALU.mult, op1=ALU.mult), [bs], [bs])
            self.A(lambda h: h.activation(out=g_[:], in_=g_[:], func=AF.Identity, bias=m4[:, 3:4], scale=m4[:, 2:3]),
                   [bs, bg], [bg])
            self.V(lambda h: h.tensor_tensor(out=g_[:], in0=g_[:], in1=lng[:], op=ALU.mult), [bg, bw], [bg])
            self.V(lambda h: h.tensor_tensor(out=v_[:], in0=g_[:], in1=lnb[:], op=ALU.add), [bg, bw], [bvl])

        def stageB(t):
            tok = slice(t * 128, (t + 1) * 128)
            bh = self.bhT[t // 4]
            v_, bvl = vl[t % 2]
            pm, bpm = self.ringA.next()
            for g in range(4):
                self.MM(pm[0:64, g * 128:(g + 1) * 128], v_[:, g * 64:(g + 1) * 64], wsm[:, g, :], g == 0, True,
                        [bvl, bw], [bpm], inc=(g == 3), skip=True)
            pu, bpu = self.ringA.next()
            for g in range(4):
                for kc in range(8):
                    self.MM(pu[0:64, g * 128:(g + 1) * 128], wd[:, kc, g * 64:(g + 1) * 64], self.hT[:, kc, tok],
                            kc == 0, kc == 7, [bh, bw], [bpu], inc=(g == 3 and kc == 7), skip=True)
            u_, bu = gu[t % 2]
            t1, b1 = t1s[t % 2]
            self.A(lambda h: h.activation(out=u_[:], in_=pu[0:64, :], func=AF.Gelu_apprx_tanh), [bpu], [bu])
            self.V(lambda h: h.tensor_tensor(out=t1[:], in0=pm[0:64, :], in1=bsb[:].rearrange("p g t -> p (g t)"),
                                             op=ALU.add), [bpm, bw], [b1])
            self.V(lambda h: h.tensor_tensor(out=mixT[0:64, :, tok], in0=t1[:].rearrange("p (g t) -> p g t", g=4),
                                             in1=u_[:].rearrange("p (g t) -> p g t", g=4), op=ALU.mult),
                   [b1, bu], [bmix])
            for _ in range(2):
                if self.mod_pending:
                    self.mod_block(*self.mod_pending.pop(0))

        stageA(0)
        for t in range(NT):
            if t + 1 < NT:
                stageA(t + 1)
            stageB(t)
        while self.mod_pending:
            self.mod_block(*self.mod_pending.pop(0))
        if "gmlp_o" in self.dumps:
            self.dump_bf16(es, "gmlp_o", mixT[:], [64, 4, S], [bmix])
        self.out_proj(es, l, m, mixT, bmix, first)

    def mix_hgrn(self, es, s, l, m, first):
        fw, dr, NT, NG, S = self.fw, self.dr, self.NT, self.NG, self.S
        wv = dr["mix_w_in"][l].rearrange("(kc p) n -> p kc n", p=128)
        wa = fw.sb(es, "wA", [128, 8, 1024], BF16)
        bw = Buf()
        ch = fw.new_chan("hgw")
        self.wdma(wa[:], wv[:, :, 0:1024], ch, writes=[bw])
        tri = fw.sb(es, "tri", [128, 128], F32)
        blk = fw.sb(es, "blk", [128, 128], F32)
        csel = fw.sb(es, "csel", [128, 8], F32)
        cselb = fw.sb(es, "cselb", [128, 8], BF16)
        ones64 = fw.sb(es, "ones64", [64, 64], F32)
        ng = fw.sb(es, "ng", [64, 4], F32)
        lb = fw.sb(es, "lb", [128, 256], F32)
        oml = fw.sb(es, "oml", [128, 256], F32)
        self.ldma(tri[:], dr["tri"], ch, writes=[bw])
        self.ldma(blk[:], dr["blk"], ch, writes=[bw])
        self.ldma(csel[:], dr["csel"], ch, writes=[bw])
        self.ldma(ones64[:], dr["ones64"], ch, writes=[bw])
        self.ldma(ng[:], dr["hgrn_ng"][l], ch, writes=[bw])
        self.V(lambda h: h.tensor_copy(out=cselb[:], in_=csel[:]), [bw], [bw])
        if l == 0:
            self.G(lambda h: h.memset(lb[:], 0.0), [], [bw])
        else:
            assert self.L == 2
            self.ldma(lb[:], dr["hgrn_lb_logits"][1:2, :].broadcast_to([128, 256]), ch, writes=[bw])
            self.ldma(oml[:], dr["hgrn_lb_logits"][0:1, :].broadcast_to([128, 256]), ch, writes=[bw])
            self.V(lambda h: h.tensor_tensor(out=lb[:], in0=lb[:], in1=oml[:], op=ALU.subtract), [bw], [bw])
            self.A(lambda h: h.activation(out=lb[:], in_=lb[:], func=AF.Sigmoid), [bw], [bw])
        self.V(lambda h: h.tensor_scalar(out=oml[:], in0=lb[:], scalar1=-1.0, scalar2=1.0, op0=ALU.mult, op1=ALU.add),
               [bw], [bw])
        mixT = fw.sb(es, "hmixT", [64, 4, S], BF16)
        bmix = Buf()
        Sst = fw.sb(es, "Sst", [64, 4, 64], F32)
        Sbf = fw.sb(es, "Sbf", [64, 4, 64], BF16)
        bS = Buf()
        bSb = Buf()
        self.G(lambda h: h.memset(Sst[:], 0.0), [], [bS])
        self.G(lambda h: h.memset(Sbf[:], 0.0), [], [bSb])

        def T(name, shape, dt, n=2):
            r = [(fw.sb(es, f"{name}{i}", shape, dt), Buf()) for i in range(n)]
            return r * (2 // n)
        f_ = T("hf", [128, 256], F32, 1)
        lf_ = T("hlf", [128, 256], F32)
        kk_ = T("hkk", [128, 256], F32, 1)
        eg_ = T("heg", [128, 768], F32, 1)
        qf_ = T("hqf", [128, 256], F32, 1)
        qt_ = T("hqt", [128, 256], BF16)
        kh_ = T("hkh", [128, 256], F32, 1)
        khb_ = T("hkhb", [128, 256], BF16)
        ke_ = T("hke", [128, 256], BF16)
        kem_ = T("hkem", [128, 8, 256], BF16)
        vb_ = T("hvb", [128, 256], BF16)
        sgb_ = T("hsg", [128, 256], BF16)
        Dt_ = T("hDt", [64, 4, 8], F32)
        qkT_ = T("hqkT", [64, 8, 128], BF16)
        gT_ = T("hgT", [64, 4, 128], BF16)
        scm_ = T("hscm", [128, 4, 128], BF16)
        osq_ = T("hosq", [64, 512], F32, 1)
        rs_ = T("hrs", [64, 512], F32, 1)
        t1_ = T("ht1", [64, 512], F32, 1)
        def bufs(t):
            i2 = t % 2
            return dict(f=f_[i2], lf=lf_[i2], kk=kk_[i2], eg=eg_[i2], qf=qf_[i2], qt=qt_[i2], kh=kh_[i2], khb=khb_[i2],
                        ke=ke_[i2], kem=kem_[i2], vb=vb_[i2], sgb=sgb_[i2], Dt=Dt_[i2], qkT=qkT_[i2], gT=gT_[i2],
                        scm=scm_[i2], osq=osq_[i2], rs=rs_[i2], t1=t1_[i2])

        def front(t):
            tok = slice(t * 128, (t + 1) * 128)
            bh = self.bhT[t // 4]
            i2 = t % 2
            pa, bpa = self.ringA.next()
            pb, bpb = self.ringA.next()
            for kc in range(8):
                self.MM(pa[:], self.hT[:, kc, tok], wa[:, kc, 0:512], kc == 0, kc == 7, [bh, bw], [bpa])
            for kc in range(8):
                self.MM(pb[:], self.hT[:, kc, tok], wa[:, kc, 512:1024], kc == 0, kc == 7, [bh, bw], [bpb])
            f, bf_ = f_[i2]
            lf, blf = lf_[i2]
            kk, bkk = kk_[i2]
            eg, beg = eg_[i2]
            qf, bqf = qf_[i2]
            qt, bqt = qt_[i2]
            kh, bkh = kh_[i2]
            khb, bkhb = khb_[i2]
            ke, bke = ke_[i2]
            kem, bkem = kem_[i2]
            vb, bvb = vb_[i2]
            sgb, bsgb = sgb_[i2]
            Dt, bDt = Dt_[i2]
            qkT, bqkT = qkT_[i2]
            gT, bgT = gT_[i2]
            scm, bscm = scm_[i2]
            osq, bosq = osq_[i2]
            rs, brs = rs_[i2]
            t1, bt1 = t1_[i2]
            self.A(lambda h: h.activation(out=f[:], in_=pa[:, 256:512], func=AF.Sigmoid), [bpa], [bf_])
            self.A(lambda h: h.activation(out=qf[:], in_=pa[:, 0:256], func=AF.Silu), [bpa], [bqf])
            self.A(lambda h: h.copy(out=vb[:], in_=pb[:, 0:256]), [bpb], [bvb])
            self.A(lambda h: h.activation(out=sgb[:], in_=pb[:, 256:512], func=AF.Silu), [bpb], [bsgb])
            yield
            self.V(lambda h: h.tensor_tensor(out=f[:], in0=f[:], in1=oml[:], op=ALU.mult), [bf_, bw], [bf_])
            self.V(lambda h: h.tensor_tensor(out=f[:], in0=f[:], in1=lb[:], op=ALU.add), [bf_, bw], [bf_])
            self.A(lambda h: h.activation(out=lf[:], in_=f[:], func=AF.Ln), [bf_], [blf])
            self.V(lambda h: h.tensor_scalar(out=kk[:], in0=f[:], scalar1=-1.0, scalar2=1.0, op0=ALU.mult, op1=ALU.add),
                   [bf_], [bkk])
            yield
            pg, bpg = self.ringA.next()
            self.MM(pg[:, 0:256], tri[:], lf[:], True, True, [bw, blf], [bpg], inc=False)
            self.MM(pg[:, 256:512], blk[:], lf[:], True, True, [bw, blf], [bpg], skip=True)
            pd, bpd = self.ringB.next()
            for hd in range(4):
                self.MM(pd[0:64, hd * 8:(hd + 1) * 8], lf[:, hd * 64:(hd + 1) * 64], csel[:], hd == 0, True,
                        [bw, blf], [bpd], inc=(hd == 3), skip=True)
            self.A(lambda h: h.activation(out=eg[:, 0:256], in_=pg[:, 0:256], func=AF.Exp), [bpg], [beg])
            self.A(lambda h: h.activation(out=eg[:, 256:512], in_=pg[:, 0:256], func=AF.Exp, scale=-1.0), [bpg], [beg])
            self.A(lambda h: h.activation(out=eg[:, 512:768], in_=pg[:, 256:512], func=AF.Exp), [bpg], [beg])
            self.A(lambda h: h.activation(out=Dt[:].rearrange("p h c -> p (h c)"), in_=pd[0:64, 0:32], func=AF.Exp),
                   [bpd], [bDt])
            yield
            self.V(lambda h: h.tensor_tensor(out=qt[:], in0=qf[:], in1=eg[:, 0:256], op=ALU.mult), [bqf, beg], [bqt])
            self.V(lambda h: h.tensor_tensor(out=kh[:], in0=kk[:], in1=eg[:, 256:512], op=ALU.mult), [bkk, beg], [bkh])
            self.A(lambda h: h.copy(out=khb[:], in_=kh[:]), [bkh], [bkhb])
            self.V(lambda h: h.tensor_tensor(out=ke[:], in0=kh[:], in1=eg[:, 512:768], op=ALU.mult), [bkh, beg], [bke])
            for ci in range(8):
                self.A(lambda h: h.activation(out=kem[:, ci, :], in_=ke[:], func=AF.Copy, scale=csel[:, ci:ci + 1]),
                       [bke, bw], [bkem])
            yield
            pt, bpt = self.ringT.next()
            for hd in range(4):
                self.TR(pt[0:64, hd * 128:(hd + 1) * 128], qt[:, hd * 64:(hd + 1) * 64], [bqt], [bpt], inc=False)
            for hd in range(4):
                self.TR(pt[0:64, 512 + hd * 128:512 + (hd + 1) * 128], khb[:, hd * 64:(hd + 1) * 64], [bkhb], [bpt],
                        inc=(hd == 3))
            self.A(lambda h: h.copy(out=qkT[:].rearrange("p a t -> p (a t)"), in_=pt[0:64, :]), [bpt], [bqkT])
            yield
            pt2, bpt2 = self.ringT.next()
            for hd in range(4):
                self.TR(pt2[0:64, hd * 128:(hd + 1) * 128], sgb[:, hd * 64:(hd + 1) * 64], [bsgb], [bpt2], inc=(hd == 3))
            self.A(lambda h: h.copy(out=gT[:].rearrange("p a t -> p (a t)"), in_=pt2[0:64, 0:512]), [bpt2], [bgT])
            yield
            psc, bpsc = self.ringA.next()
            for hd in range(4):
                self.MM(psc[:, hd * 128:(hd + 1) * 128], qkT[:, 4 + hd, :], qkT[:, hd, :], hd == 0, True,
                        [bqkT], [bpsc], inc=(hd == 3), skip=True)
            self.V(lambda h: h.tensor_tensor(out=scm[:], in0=psc[:].rearrange("p (a t) -> p a t", a=4),
                                             in1=tri[:].unsqueeze(1).broadcast_to([128, 4, 128]), op=ALU.mult),
                   [bpsc, bw], [bscm])
            yield

        def back(t):
            tok = slice(t * 128, (t + 1) * 128)
            B = bufs(t)
            kem, bkem = B["kem"]
            vb, bvb = B["vb"]
            Dt, bDt = B["Dt"]
            qkT, bqkT = B["qkT"]
            gT, bgT = B["gT"]
            scm, bscm = B["scm"]
            osq, bosq = B["osq"]
            rs, brs = B["rs"]
            t1, bt1 = B["t1"]
            po, bpo = self.ringB.next()
            firstmm = True
            for ci in range(8):
                c = t * 8 + ci
                if c > 0:
                    for hd in range(4):
                        self.MM(po[0:64, hd * 128 + ci * 16:hd * 128 + (ci + 1) * 16], Sbf[:, hd, :],
                                qkT[:, hd, ci * 16:(ci + 1) * 16], firstmm, False, [bSb, bqkT], [bpo], inc=False, skip=True)
                        firstmm = False
                pkv, bpkv = self.ringA.next()
                for hd in range(4):
                    self.MM(pkv[0:64, hd * 64:(hd + 1) * 64], kem[:, ci, hd * 64:(hd + 1) * 64], vb[:, hd * 64:(hd + 1) * 64],
                            hd == 0, True, [bkem, bvb], [bpkv], inc=(hd == 3), skip=True)
                self.V(lambda h: h.tensor_tensor(out=Sst[:], in0=Sst[:], in1=Dt[:, :, ci:ci + 1].broadcast_to([64, 4, 64]),
                                                 op=ALU.mult), [bS, bDt], [bS])
                self.V(lambda h: h.tensor_tensor(out=Sst[:], in0=Sst[:], in1=pkv[0:64, 0:256].rearrange("p (a b) -> p a b", a=4),
                                                 op=ALU.add), [bS, bpkv], [bS])
                self.V(lambda h: h.tensor_copy(out=Sbf[:], in_=Sst[:]), [bS], [bSb])
                yield
            for hd in range(4):
                self.MM(po[0:64, hd * 128:(hd + 1) * 128], vb[:, hd * 64:(hd + 1) * 64], scm[:, hd, :], firstmm, True,
                        [bvb, bscm], [bpo], inc=(hd == 3), skip=True)
                firstmm = False
            self.A(lambda h: h.activation(out=osq[:], in_=po[0:64, :], func=AF.Square), [bpo], [bosq])
            pss, bpss = self.ringA.next()
            self.MM(pss[0:64, :], ones64[:], osq[:], True, True, [bw, bosq], [bpss])
            self.V(lambda h: h.tensor_scalar(out=rs[:], in0=pss[0:64, :], scalar1=1.0 / 64.0, scalar2=RMS_EPS,
                                             op0=ALU.mult, op1=ALU.add), [bpss], [brs])
            self.A(lambda h: h.activation(out=rs[:], in_=rs[:], func=AF.Ln), [brs], [brs])
            self.A(lambda h: h.activation(out=rs[:], in_=rs[:], func=AF.Exp, scale=-0.5), [brs], [brs])
            self.V(lambda h: h.tensor_tensor(out=t1[:], in0=po[0:64, :], in1=rs[:], op=ALU.mult), [bpo, brs], [bt1])
            for hd in range(4):
                self.V(lambda h: h.scalar_tensor_tensor(out=mixT[0:64, hd, tok], in0=t1[:, hd * 128:(hd + 1) * 128],
                                                        scalar=ng[:, hd:hd + 1], in1=gT[:, hd, :],
                                                        op0=ALU.mult, op1=ALU.mult), [bt1, bw, bgT], [bmix])

        for _ in front(0):
            pass
        for t in range(NT):
            gb = back(t)
            gf = front(t + 1) if t + 1 < NT else iter(())
            done_b = done_f = False
            while not (done_b and done_f):
                if not done_b:
                    try:
                        next(gb)
                    except StopIteration:
                        done_b = True
                if not done_f:
                    try:
                        next(gf)
                    except StopIteration:
                        done_f = True
        if "hgrn_o" in self.dumps:
            self.dump_bf16(es, "hgrn_o", mixT[:], [64, 4, S], [bmix])
        self.out_proj(es, l, m, mixT, bmix, first)


_NC_CACHE = {}


def _get_nc(S, L):
    key = (S, L)
    if key not in _NC_CACHE:
        _NC_CACHE[key] = Builder(S, L).build()
    return _NC_CACHE[key]


def make_in_maps(inputs, n_cores, S):
    w = host_weights(inputs)
    consts = host_consts(S)
    x = np.asarray(inputs["x"], dtype=np.float32)
    c = np.asarray(inputs["c"], dtype=np.float32)
    maps = []
    for i in range(n_cores):
        cc = c[2 * i:2 * i + 2]
        cT = np.ascontiguousarray(cc.reshape(2, 8, 128).transpose(2, 1, 0))
        mp = {"x": np.ascontiguousarray(x[2 * i:2 * i + 2]), "cT": cT}
        mp.update(w)
        mp.update(consts)
        maps.append(mp)
    return maps


def kernel(**inputs):
    x = np.asarray(inputs["x"])
    B, S, _ = x.shape
    L = np.asarray(inputs["ada_w"]).shape[0]
    n = B // 2
    nc = _get_nc(S, L)
    maps = make_in_maps(inputs, n, S)
    res = run_bass_kernel_spmd(nc, maps, core_ids=list(range(n)))
    out = np.concatenate([r["out"] for r in res.results], axis=0)
    return out.astype(np.float32)
```
